# Optimizing a Trainium2 kernel written in Bass

```python
import math
import jax
import jax.numpy as jnp
from jax import lax
import numpy as np

D_MODEL = 1024
BATCH = 1
SEQ = 16384
DEPTH = 1
DEC_BATCH = 128
DEC_SEQ = 1
PAST_LEN = 16384
PAGE_SIZE = 128

HEAD_DIM = 64
ATTN_HEADS = 8
ATTN_KV_HEADS = 2
WINDOW = 128
N_BUCKETS = 32
MAX_DISTANCE = 128
GLA_HEADS = 4
GLA_DK = 64
GLA_DV = 128
GLA_RANK = 16
GLA_TAU = 16.0
GLA_CHUNK = 64
ATTN_WIDTH = ATTN_HEADS * HEAD_DIM
GLA_WIDTH = GLA_HEADS * GLA_DV
MIX_WIDTH = ATTN_WIDTH + GLA_WIDTH
D_FF = -(-8 * D_MODEL // (3 * 256)) * 256
SPLIT_SIZES = (ATTN_WIDTH, ATTN_KV_HEADS * HEAD_DIM, ATTN_KV_HEADS * HEAD_DIM,
               GLA_HEADS * GLA_DK, GLA_HEADS * GLA_DK, GLA_WIDTH, GLA_WIDTH, GLA_RANK)
IN_WIDTH = sum(SPLIT_SIZES)
EPS = 1e-6

kernel_name = "hymba_swa_sink_gla_decode_step"


def rmsnorm(x, g):
    xf = x.astype(jnp.float32)
    r = lax.rsqrt(jnp.mean(xf * xf, axis=-1, keepdims=True) + EPS)
    return (xf * r * g.astype(jnp.float32)).astype(x.dtype)


def t5_bucket(dist):
    n = jnp.maximum(dist, 0)
    max_exact = N_BUCKETS // 2
    nf = jnp.maximum(n, 1).astype(jnp.float32)
    large = max_exact + (jnp.log(nf / max_exact) / math.log(MAX_DISTANCE / max_exact)
                         * (N_BUCKETS - max_exact)).astype(jnp.int32)
    large = jnp.minimum(large, N_BUCKETS - 1)
    return jnp.where(n < max_exact, n, large)


def project(h, w_in, q_norm_g, k_norm_g, w_gla_gate2, b_gla_gate):
    N, T, _ = h.shape
    idx = [int(s) for s in np.cumsum(SPLIT_SIZES)[:-1]]
    u_qa, u_ka, u_va, u_qg, u_kg, u_vg, u_rg, u_lr = jnp.split(h @ w_in, idx, axis=-1)
    q_a = rmsnorm(u_qa.reshape(N, T, ATTN_HEADS, HEAD_DIM), q_norm_g)
    k_a = rmsnorm(u_ka.reshape(N, T, ATTN_KV_HEADS, HEAD_DIM), k_norm_g)
    v_a = u_va.reshape(N, T, ATTN_KV_HEADS, HEAD_DIM)
    q_g = u_qg.reshape(N, T, GLA_HEADS, GLA_DK) * (GLA_DK ** -0.5)
    k_g = u_kg.reshape(N, T, GLA_HEADS, GLA_DK)
    v_g = u_vg.reshape(N, T, GLA_HEADS, GLA_DV)
    log_a = jax.nn.log_sigmoid((u_lr @ w_gla_gate2 + b_gla_gate).astype(jnp.float32)) / GLA_TAU
    log_a = log_a.reshape(N, T, GLA_HEADS, GLA_DK)
    return q_a, k_a, v_a, q_g, k_g, v_g, u_rg, log_a


def sink_attention(q, k, v, dist, valid, rel_bias, sinks):
    N, Tq, H, hd = q.shape
    Tk, kvh = k.shape[1], k.shape[2]
    G = H // kvh
    qg = q.reshape(N, Tq, kvh, G, hd).astype(jnp.float32)
    s = jnp.einsum('nqkgd,nskd->nkgqs', qg, k.astype(jnp.float32)) * (hd ** -0.5)
    bias = rel_bias[t5_bucket(dist)].astype(jnp.float32)
    bias = jnp.transpose(bias, (2, 0, 1)).reshape(kvh, G, Tq, Tk)
    s = jnp.where(valid[:, None, None], s + bias, -jnp.inf)
    sk = sinks.astype(jnp.float32).reshape(kvh, G, 1, 1)
    m = jnp.maximum(jnp.max(s, axis=-1, keepdims=True), sk)
    p = jnp.exp(s - m)
    denom = jnp.sum(p, axis=-1, keepdims=True) + jnp.exp(sk - m)
    o = jnp.einsum('nkgqs,nskd->nqkgd', p / denom, v.astype(jnp.float32))
    return o.reshape(N, Tq, H * hd)


def prompt_window_attention(q, k, v, rel_bias, sinks):
    B, T, H, hd = q.shape
    kvh = k.shape[2]
    W = WINDOW
    nb = T // W
    qb = q.reshape(B * nb, W, H, hd)

    def pair(t):
        tb = t.reshape(B, nb, W, kvh, hd)
        prev = jnp.concatenate([jnp.zeros_like(tb[:, :1]), tb[:, :-1]], axis=1)
        return jnp.concatenate([prev, tb], axis=2).reshape(B * nb, 2 * W, kvh, hd)

    i = jnp.arange(W)[:, None]
    j = jnp.arange(2 * W)[None, :]
    dist = W + i - j
    band = (dist >= 0) & (dist < WINDOW)
    has_prev = (jnp.arange(nb) > 0)[:, None, None] | (j >= W)[None]
    valid = jnp.tile(band[None] & has_prev, (B, 1, 1))
    o = sink_attention(qb, pair(k), pair(v), dist, valid, rel_bias, sinks)
    return o.reshape(B, T, H * hd)


def sample_window_attention(q, k_new, v_new, k_buf, v_buf, rel_bias, sinks):
    Ts = q.shape[1]
    wb = k_buf.shape[1]
    k_all = jnp.concatenate([k_buf, k_new], axis=1)
    v_all = jnp.concatenate([v_buf, v_new], axis=1)
    dist = wb + jnp.arange(Ts)[:, None] - jnp.arange(wb + Ts)[None, :]
    valid = ((dist >= 0) & (dist < WINDOW))[None]
    o = sink_attention(q, k_all, v_all, dist, valid, rel_bias, sinks)
    return o, k_all[:, -wb:], v_all[:, -wb:]


def gla_chunked(q, k, v, log_a, s0):
    N, T, H, _ = q.shape
    C = min(GLA_CHUNK, T)
    pad = (-T) % C
    nc = (T + pad) // C

    def blocks(t):
        t = jnp.pad(t.astype(jnp.float32), ((0, 0), (0, pad), (0, 0), (0, 0)))
        return t.reshape(N, nc, C, H, t.shape[-1]).transpose(1, 0, 3, 2, 4)

    causal = jnp.arange(C)[:, None] >= jnp.arange(C)[None, :]

    def step(S, inp):
        qc, kc, vc, ac = inp
        b = jnp.cumsum(ac, axis=2)
        o_inter = jnp.einsum('nhtd,nhdv->nhtv', qc * jnp.exp(b), S)
        diff = b[:, :, :, None, :] - b[:, :, None, :, :]
        decay = jnp.exp(jnp.where(causal[:, :, None], diff, -jnp.inf))
        scores = jnp.einsum('nhtd,nhsd,nhtsd->nhts', qc, kc, decay)
        o = o_inter + jnp.einsum('nhts,nhsv->nhtv', scores, vc)
        b_last = b[:, :, -1:, :]
        S = (jnp.exp(b_last[:, :, 0, :, None]) * S
             + jnp.einsum('nhsd,nhsv->nhdv', kc * jnp.exp(b_last - b), vc))
        return S, o

    S, o = lax.scan(step, s0, (blocks(q), blocks(k), blocks(v), blocks(log_a)))
    o = o.transpose(1, 0, 3, 2, 4).reshape(N, nc * C, H, GLA_DV)[:, :T]
    return o, S


def finish(x, o_attn, o_gla, r_gla, gla_norm_g, w_o, ffn_norm_g, w_gate, w_up, w_down):
    N, T, _ = x.shape
    g = rmsnorm(o_gla.astype(x.dtype), gla_norm_g).reshape(N, T, GLA_WIDTH) * jax.nn.silu(r_gla)
    h = x + jnp.concatenate([o_attn.astype(x.dtype), g], axis=-1) @ w_o
    z = rmsnorm(h, ffn_norm_g)
    return h + (jax.nn.silu(z @ w_gate) * (z @ w_up)) @ w_down


def setup_inputs(seed: int = 0) -> dict:
    key = jax.random.key(seed)
    ks = jax.random.split(key, 20)
    wb = min(WINDOW, PAST_LEN)

    def nrm(k, shape, scale):
        return scale * jax.random.normal(k, shape, jnp.float32)

    return {
        "x_prompt": nrm(ks[0], (BATCH, SEQ, D_MODEL), 1.0),
        "x_sample": nrm(ks[1], (DEC_BATCH, DEC_SEQ, D_MODEL), 1.0),
        "cache_k": nrm(ks[2], (DEPTH, DEC_BATCH, wb, ATTN_KV_HEADS, HEAD_DIM), 1.0),
        "cache_v": nrm(ks[3], (DEPTH, DEC_BATCH, wb, ATTN_KV_HEADS, HEAD_DIM), 1.0),
        "state_gla": nrm(ks[4], (DEPTH, DEC_BATCH, GLA_HEADS, GLA_DK, GLA_DV), 0.3),
        "attn_norm_g": 1.0 + nrm(ks[5], (DEPTH, D_MODEL), 0.1),
        "w_in": nrm(ks[6], (DEPTH, D_MODEL, IN_WIDTH), D_MODEL ** -0.5),
        "q_norm_g": 1.0 + nrm(ks[7], (DEPTH, HEAD_DIM), 0.1),
        "k_norm_g": 1.0 + nrm(ks[8], (DEPTH, HEAD_DIM), 0.1),
        "attn_sinks": nrm(ks[9], (DEPTH, ATTN_HEADS), 0.5),
        "rel_bias": nrm(ks[10], (N_BUCKETS, ATTN_HEADS), 0.5),
        "w_gla_gate2": nrm(ks[11], (DEPTH, GLA_RANK, GLA_HEADS * GLA_DK), GLA_RANK ** -0.5),
        "b_gla_gate": nrm(ks[12], (DEPTH, GLA_HEADS * GLA_DK), 0.1),
        "gla_norm_g": 1.0 + nrm(ks[13], (DEPTH, GLA_DV), 0.1),
        "w_o": nrm(ks[14], (DEPTH, MIX_WIDTH, D_MODEL), MIX_WIDTH ** -0.5),
        "ffn_norm_g": 1.0 + nrm(ks[15], (DEPTH, D_MODEL), 0.1),
        "w_gate": nrm(ks[16], (DEPTH, D_MODEL, D_FF), D_MODEL ** -0.5),
        "w_up": nrm(ks[17], (DEPTH, D_MODEL, D_FF), D_MODEL ** -0.5),
        "w_down": nrm(ks[18], (DEPTH, D_FF, D_MODEL), D_FF ** -0.5),
    }


def reference(x_prompt, x_sample, cache_k, cache_v, state_gla, attn_norm_g, w_in, q_norm_g,
              k_norm_g, attn_sinks, rel_bias, w_gla_gate2, b_gla_gate, gla_norm_g, w_o,
              ffn_norm_g, w_gate, w_up, w_down):
    xp, xs = x_prompt, x_sample
    wb = cache_k.shape[2]
    kp_l, vp_l, sp_l, ks_l, vs_l, ss_l = [], [], [], [], [], []
    for l in range(DEPTH):
        proj_w = (w_in[l], q_norm_g[l], k_norm_g[l], w_gla_gate2[l], b_gla_gate[l])
        ffn_w = (gla_norm_g[l], w_o[l], ffn_norm_g[l], w_gate[l], w_up[l], w_down[l])
        q_a, k_a, v_a, q_g, k_g, v_g, r_g, log_a = project(rmsnorm(xp, attn_norm_g[l]), *proj_w)
        o_a = prompt_window_attention(q_a, k_a, v_a, rel_bias, attn_sinks[l])
        s0 = jnp.zeros((xp.shape[0], GLA_HEADS, GLA_DK, GLA_DV), jnp.float32)
        o_g, s_p = gla_chunked(q_g, k_g, v_g, log_a, s0)
        xp = finish(xp, o_a, o_g, r_g, *ffn_w)
        kp_l.append(k_a[:, -wb:])
        vp_l.append(v_a[:, -wb:])
        sp_l.append(s_p.astype(state_gla.dtype))
        q_a, k_a, v_a, q_g, k_g, v_g, r_g, log_a = project(rmsnorm(xs, attn_norm_g[l]), *proj_w)
        o_a, k_buf, v_buf = sample_window_attention(q_a, k_a, v_a, cache_k[l], cache_v[l],
                                                    rel_bias, attn_sinks[l])
        o_g, s_s = gla_chunked(q_g, k_g, v_g, log_a, state_gla[l].astype(jnp.float32))
        xs = finish(xs, o_a, o_g, r_g, *ffn_w)
        ks_l.append(k_buf)
        vs_l.append(v_buf)
        ss_l.append(s_s.astype(state_gla.dtype))
    k_win_prompt = jnp.stack(kp_l)
    v_win_prompt = jnp.stack(vp_l)
    gla_state_prompt = jnp.stack(sp_l)
    k_win_sample = jnp.stack(ks_l)
    v_win_sample = jnp.stack(vs_l)
    gla_state_sample = jnp.stack(ss_l)
    return (xp, xs, k_win_prompt, v_win_prompt, gla_state_prompt,
            k_win_sample, v_win_sample, gla_state_sample)
```

```python
import contextlib
import math
import numpy as np
import concourse.bass as bass
import concourse.mybir as mybir
from concourse.bass_utils import run_bass_kernel_spmd

F32 = mybir.dt.float32
BF16 = mybir.dt.bfloat16
AF = mybir.ActivationFunctionType
ALU = mybir.AluOpType

NCORE = 8
D = 1024
TOK = 2048
NPRE = 7 * 2048
DFF = 2816
NKF = DFF // 128
INW = 2320
ENGS = ("pe", "act", "dve", "pool", "sp")
NDMASEM = 12
NSLOT = 4
EPS = 1e-6
MASKV = -30000.0


class _Ins:
    def then_inc(self, *a, **k):
        return self


class _Mock:
    def __init__(self):
        self.cost = 0.0

    def _free(self, ap):
        n = 1
        for d in list(ap.shape)[1:]:
            n *= int(d)
        return n

    def matmul(self, out, lhsT=None, rhs=None, **k):
        n = max(self._free(rhs), 64)
        self.cost += (n * (4 if rhs.dtype == F32 else 1)) / 2400.0 + 0.01
        return _Ins()

    def transpose(self, out=None, in_=None, identity=None, **k):
        self.cost += (128 * (4 if in_.dtype == F32 else 1)) / 2400.0 + 0.01
        return _Ins()

    def __getattr__(self, name):
        def f(*a, **k):
            o = k.get("out", a[0] if a else None)
            n = self._free(o) if o is not None else 64
            self.cost += 0.2 + n / 1000.0
            return _Ins()
        return f


class Prog:
    def __init__(self, nc):
        self.nc = nc
        self.stack = contextlib.ExitStack()
        self.oplist = []
        self.last_w = {}
        self.readers = {}
        self.base = None
        self.sems = {}
        self.nbuf = 0
        self.banks = []
        self.bank_rr = 0
        self.held = set()
        self.finals = []

    def sb(self, shape, dt, name=None):
        self.nbuf += 1
        return self.stack.enter_context(self.nc.sbuf_tensor(name or f"sb{self.nbuf}", list(shape), dt))

    def ps(self, shape, dt, name=None):
        self.nbuf += 1
        return self.stack.enter_context(self.nc.psum_tensor(name or f"ps{self.nbuf}", list(shape), dt))

    def bank(self, hold=False):
        while True:
            i = self.bank_rr % len(self.banks)
            self.bank_rr += 1
            if i not in self.held:
                break
        if hold:
            self.held.add(i)
        return self.banks[i], ("ps", i)

    def release(self, res):
        self.held.discard(res[1])

    def _sem(self, key):
        if key not in self.sems:
            nm = "s_" + "_".join(str(k) for k in (key if isinstance(key, tuple) else (key,)))
            self.sems[key] = self.stack.enter_context(self.nc.semaphore(nm))
        return self.sems[key]

    def _add(self, eng, fn, reads, writes, kind, cost, dma=None):
        deps = set()
        for r in reads:
            t = self.last_w.get(r, self.base)
            if t is not None:
                deps.add(t)
        for w in writes:
            t = self.last_w.get(w, self.base)
            if t is not None:
                deps.add(t)
            deps.update(self.readers.get(w, ()))
        if not reads and not writes and self.base is not None:
            deps.add(self.base)
        i = len(self.oplist)
        deps.discard(i)
        self.oplist.append(dict(eng=eng, fn=fn, deps=deps, kind=kind, cost=cost, dma=dma))
        for r in reads:
            self.readers.setdefault(r, []).append(i)
        for w in writes:
            self.last_w[w] = i
            self.readers[w] = []
        return i

    def op(self, eng, fn, reads=(), writes=()):
        m = _Mock()
        fn(m)
        return self._add(eng, fn, reads, writes, "c", m.cost)

    def dma(self, eng, out, in_, reads=(), writes=()):
        n = 1
        for d in out.shape:
            n *= int(d)
        nbytes = n * (4 if out.dtype == F32 else 2)
        return self._add(eng, None, reads, writes, "d", 2.0 + nbytes / 150e3, dma=(out, in_))

    def barrier(self, keep=()):
        kept = {k: v for k, v in self.last_w.items() if (k in keep or (isinstance(k, tuple) and k and k[0] in keep))}
        skip = set(getattr(self, "nofence", ()))
        allprev = set(range(len(self.oplist))) - skip
        i = len(self.oplist)
        self.oplist.append(dict(eng="sp", fn=None, deps=allprev, kind="n", cost=0.05, dma=None))
        self.base = i
        self.bars = getattr(self, "bars", []) + [i]
        self.last_w = dict(kept)
        self.readers = {}

    def final_wait(self, eng, toks):
        pass

    def schedule(self):
        import heapq, os
        ops = self.oplist
        n = len(ops)
        succ = [[] for _ in range(n)]
        ndep = [0] * n
        for i, o in enumerate(ops):
            ndep[i] = len(o["deps"])
            for d in o["deps"]:
                succ[d].append(i)
        done = [0.0] * n
        ready_t = [0.0] * n
        efree = {e: 0.0 for e in ENGS}
        waiting = {e: [] for e in ENGS}
        avail = {e: [] for e in ENGS}
        order = {e: [] for e in ENGS}
        for i, o in enumerate(ops):
            if ndep[i] == 0:
                heapq.heappush(waiting[o["eng"]], (0.0, i))
        nsched = 0
        while nsched < n:
            best = None
            for e in ENGS:
                w, a = waiting[e], avail[e]
                while w and w[0][0] <= efree[e]:
                    heapq.heappush(a, heapq.heappop(w)[1])
                if a:
                    cand = (efree[e], a[0], e, True)
                elif w:
                    cand = (w[0][0], w[0][1], e, False)
                else:
                    continue
                if best is None or cand[:2] < best[:2]:
                    best = cand
            st, i, e, from_avail = best
            if from_avail:
                heapq.heappop(avail[e])
            else:
                heapq.heappop(waiting[e])
            o = ops[i]
            if o["kind"] == "d":
                efree[e] = st + 0.15
                done[i] = st + o["cost"]
            else:
                efree[e] = st + o["cost"]
                done[i] = st + o["cost"] + 0.15
            order[e].append(i)
            nsched += 1
            for sidx in succ[i]:
                ndep[sidx] -= 1
                if done[i] > ready_t[sidx]:
                    ready_t[sidx] = done[i]
                if ndep[sidx] == 0:
                    heapq.heappush(waiting[ops[sidx]["eng"]], (ready_t[sidx], sidx))
        self.sim_time = max(done) if n else 0.0
        self.sim_done = done
        if os.environ.get("KSIM"):
            print("SIM total", round(self.sim_time), "barriers", [round(done[b]) for b in getattr(self, "bars", [])])
        return order

    def emit(self):
        nc = self.nc
        import os
        order = self.schedule()
        km = os.environ.get("KSCHED", "mid")
        if km == "0":
            order = {e: [i for i, o in enumerate(self.oplist) if o["eng"] == e] for e in ENGS}
        elif km == "mid" and len(getattr(self, "bars", [])) >= 2:
            B2 = self.bars[-1]
            prog = {e: [i for i, o in enumerate(self.oplist) if o["eng"] == e] for e in ENGS}
            order = {e: [i for i in order[e] if i <= B2] + [i for i in prog[e] if i > B2] for e in ENGS}
        elif km in ("pre", "post") and self.base is not None:
            B = self.base
            prog = {e: [i for i, o in enumerate(self.oplist) if o["eng"] == e] for e in ENGS}
            if km == "post":
                order = {e: [i for i in prog[e] if i <= B] + [i for i in order[e] if i > B] for e in ENGS}
            else:
                order = {e: [i for i in order[e] if i <= B] + [i for i in prog[e] if i > B] for e in ENGS}
        ops = self.oplist
        tok = [None] * len(ops)
        ccnt = {e: 0 for e in ENGS}
        dcnt = {}
        drr = {e: 0 for e in ENGS}
        plan = {e: [] for e in ENGS}
        prevdma = {}
        for e in ENGS:
            for i in order[e]:
                o = ops[i]
                if o["kind"] == "d":
                    j = drr[e] % NDMASEM
                    drr[e] += 1
                    key = ("d", e, j)
                    c = dcnt.get(key, 0)
                    dcnt[key] = c + 1
                    tok[i] = (key, 16 * (c + 1))
                    prevdma[i] = (key, 16 * c) if c > 0 else None
                else:
                    ccnt[e] += 1
                    tok[i] = (e, ccnt[e])
        for e in ENGS:
            waited = {}
            for i in order[e]:
                o = ops[i]
                need = {}
                for d in o["deps"]:
                    k, v = tok[d]
                    if waited.get(k, 0) >= v:
                        continue
                    if need.get(k, 0) < v:
                        need[k] = v
                if o["kind"] == "d" and prevdma.get(i):
                    k, v = prevdma[i]
                    if waited.get(k, 0) < v and need.get(k, 0) < v:
                        need[k] = v
                for k, v in need.items():
                    waited[k] = v
                plan[e].append((list(need.items()), i))
            if e == "sp":
                fin = {}
                for i2, t in enumerate(tok):
                    if t is not None and fin.get(t[0], 0) < t[1]:
                        fin[t[0]] = t[1]
                plan[e].append(([(k, v) for k, v in fin.items() if waited.get(k, 0) < v], None))
        for e in ENGS:
            for (waits, i) in plan[e]:
                for (k, v) in waits:
                    self._sem(k)
                if i is not None:
                    self._sem(tok[i][0])
        if os.environ.get("KCHECK"):
            import collections
            sv = collections.defaultdict(int)
            ptr = {e: 0 for e in ENGS}
            while True:
                prog_ = False
                for e in ENGS:
                    while ptr[e] < len(plan[e]):
                        waits, i = plan[e][ptr[e]]
                        if all(sv[k] >= v for k, v in waits):
                            if i is not None:
                                k, v = tok[i]
                                inc = 16 if ops[i]["kind"] == "d" else 1
                                sv[k] += inc
                                assert sv[k] == v, ("token mismatch", e, i, k, v, sv[k])
                            ptr[e] += 1
                            prog_ = True
                        else:
                            break
                if all(ptr[e] == len(plan[e]) for e in ENGS):
                    print("KCHECK: ok, no deadlock")
                    break
                if not prog_:
                    for e in ENGS:
                        if ptr[e] < len(plan[e]):
                            waits, i = plan[e][ptr[e]]
                            print("KCHECK STUCK", e, ptr[e], i, [(k, v, sv[k]) for k, v in waits if sv[k] < v])
                    break
        block = self.stack.enter_context(nc.Block())

        def run(engname):
            def body(e):
                for (waits, i) in plan[engname]:
                    for (k, v) in waits:
                        e.wait_ge(self.sems[k], v)
                    if i is None:
                        continue
                    o = ops[i]
                    if o["kind"] == "d":
                        ins = e.dma_start(out=o["dma"][0], in_=o["dma"][1], allow_slow_non_contiguous=True)
                        ins.then_inc(self.sems[tok[i][0]], 16)
                    elif o["kind"] == "n":
                        ins = e.nop()
                        ins.then_inc(self.sems[tok[i][0]], 1)
                    else:
                        ins = o["fn"](e)
                        ins.then_inc(self.sems[tok[i][0]], 1)
            return body
        block.tensor(run("pe"))
        block.scalar(run("act"))
        block.vector(run("dve"))
        block.gpsimd(run("pool"))
        block.sync(run("sp"))


def fap(ap, dims):
    return bass.AP(tensor=ap.tensor, offset=ap.offset, ap=[list(ap.ap[0])] + [list(d) for d in dims])


def t5_bucket_np(n):
    n = np.maximum(n, 0)
    nf = np.maximum(n, 1).astype(np.float32)
    large = 16 + (np.log(nf / 16) / math.log(128 / 16) * 16).astype(np.int32)
    large = np.minimum(large, 31)
    return np.where(n < 16, n, large)


def host_consts():
    c = {}
    c["ident"] = np.eye(128, dtype=np.float32)
    s = np.arange(128)[:, None]
    t = np.arange(128)[None, :]
    c["tri"] = np.where(s <= t, -1.0 / 16, 0.0).astype(np.float32)
    c["tris"] = (np.eye(128) * (-1.0 / 16)).astype(np.float32)
    c["cmask"] = (s <= t).astype(np.float32)
    c["jx"] = np.eye(128, dtype=np.float32)[::-1].copy()
    bd = np.zeros((128, 128), np.float32)
    bd[:64, :64] = 1
    bd[64:, 64:] = 1
    c["bd"] = bd
    sw = np.zeros((128, 128), np.float32)
    for m in range(128):
        sw[(m + 64) % 128, m] = 1
    c["sw"] = sw
    oh = np.zeros((128, 2, 256), np.float32)
    for kb in range(2):
        off = 128 if kb == 0 else 0
        for i in range(255):
            dlt = 127 + off - i
            if 0 <= dlt <= 127:
                oh[int(t5_bucket_np(np.array(dlt))), kb, i] = 1.0
            else:
                oh[32, kb, i] = MASKV
        oh[32, kb, 255] = MASKV
    c["oh"] = oh
    sel = np.zeros((128, 16, 128), np.float32)
    for b in range(16):
        sel[b, b, :] = 1
    c["sel"] = sel
    return c


def build_program(debug=False):
    nc = bass.Bass("TRN2", target_bir_lowering=False)
    P = Prog(nc)

    def din(name, shape):
        return nc.dram_tensor(name, list(shape), F32, kind="ExternalInput").ap()

    def dout(name, shape):
        return nc.dram_tensor(name, list(shape), F32, kind="ExternalOutput").ap()

    xp = din("xp", [TOK, D]); xhalo = din("xhalo", [128, D]); xpre = din("xpre", [NPRE, D]); xs = din("xs", [128, D])
    ck = din("ck", [16, 128, 128]); cv = din("cv", [16, 128, 128]); sg = din("sg", [16, 4, 64, 128])
    w_in = din("w_in", [D, INW]); w_o = din("w_o", [D, D]); w_gate = din("w_gate", [D, DFF]); w_up = din("w_up", [D, DFF])
    w_down = din("w_down", [DFF, D])
    attn_g = din("attn_g", [D]); ffn_g = din("ffn_g", [D]); qg_in = din("qng", [64]); kg_in = din("kng", [64])
    sinks = din("sinks", [8]); relb = din("relb", [32, 8]); w2 = din("w2", [16, 256]); bgate = din("bgate", [256])
    glag = din("glag", [128]); flag = din("flag", [128, 1])
    cn = {k: din("c_" + k, v.shape) for k, v in host_consts().items()}

    y_p = dout("y_p", [TOK, D]); y_s = dout("y_s", [128, D])
    kwp = dout("kwp", [128, 128]); vwp = dout("vwp", [128, 128]); gsp = dout("gsp", [4, 64, 128])
    kws = dout("kws", [16, 128, 128]); vws = dout("vws", [16, 128, 128]); gss = dout("gss", [16, 4, 64, 128])
    fscr = nc.dram_tensor("fscr", [2, 8, 256], F32).ap()
    wb = {"w_in": nc.dram_tensor("wb_in", [D, INW], BF16).ap(), "w_o": nc.dram_tensor("wb_o", [D, D], BF16).ap(),
          "w_gate": nc.dram_tensor("wb_gate", [D, DFF], BF16).ap(), "w_up": nc.dram_tensor("wb_up", [D, DFF], BF16).ap(),
          "w_down": nc.dram_tensor("wb_down", [DFF, D], BF16).ap()}
    wf = {"w_in": w_in, "w_o": w_o, "w_gate": w_gate, "w_up": w_up, "w_down": w_down}
    if debug:
        dbg_mix = dout("dbg_mix", [128, 8, 512]); dbg_h = dout("dbg_h", [128, 4, D]); dbg_z = dout("dbg_z", [128, 8, 512]); dbg_a = dout("dbg_a", [128, NKF, 512])
        dbg_q = dout("dbg_q", [128, 4, 512]); dbg_rs = dout("dbg_rs", [128, 4, 512])

    for i in range(6):
        P.banks.append(P.ps([128, 512], F32, f"bank{i}"))
    psT = [P.ps([128, 1024], BF16, f"pst{i}") for i in range(2)]
    pst_rr = [0]

    def tbank():
        i = pst_rr[0] % 2
        pst_rr[0] += 1
        return psT[i], ("pst", i)

    ident_f = P.sb([128, 128], F32); ident_bf = P.sb([128, 128], BF16)
    tri_f = P.sb([128, 128], F32); tris_f = P.sb([128, 128], F32); cmask_bf = P.sb([128, 128], BF16)
    jx_f = P.sb([128, 128], F32); bd_bf = P.sb([128, 128], BF16); sw_bf = P.sb([128, 128], BF16)
    ones_bf = P.sb([128, 128], BF16); zeros_f = P.sb([128, 128], F32); scr_f = P.sb([128, 2048], F32)
    sel_bf = P.sb([128, 16, 128], BF16)
    gaT = P.sb([128, 8], F32); gfT = P.sb([128, 8], F32)
    gq_col = P.sb([128, 1], F32); gk_col = P.sb([128, 1], F32); glag_col = P.sb([128, 1], F32)
    eps_col = P.sb([128, 1], F32); ln8_col = P.sb([128, 1], F32); flag_col = P.sb([128, 1], F32)
    bgate_bc = P.sb([128, 256], F32); w2pad = P.sb([128, 256], BF16)
    relb_pad = P.sb([128, 128], F32)
    hank = scr_f[:, 0:1024].rearrange("p (h s) -> p h s", h=8)
    oh_sb = scr_f[:, 1024:1536].rearrange("p (a b) -> p a b", a=2)
    ftab = scr_f[0:8, 1536:2048].rearrange("p (a b) -> p a b", a=2)
    E = P.sb([128, 2, 2, 2, 2, 128], F32)
    sink_bc = P.sb([128, 8], F32); sinkexp = P.sb([128, 2, 2, 2, 128], F32)
    ring = P.sb([128, NSLOT, 8, 512], BF16)
    xb = P.sb([128, 4, D], F32)
    actT = P.sb([128, 8, 512], BF16)
    nbf = P.sb([128, D], BF16); nbf_b = P.sb([128, D], BF16); nbf_c = P.sb([128, D], BF16)
    zt_b = P.sb([128, 256], F32); sp_b = P.sb([128, 256], F32); ktok_b = P.sb([128, 2, 2, 128], BF16)
    ss_c2 = P.sb([128, 2], F32); rs_c2 = P.sb([128, 2], F32)
    ss_c = P.sb([128, 1], F32); rs_c = P.sb([128, 1], F32)
    qhT = P.sb([128, 4, 512], BF16)
    kX = P.sb([128, 4, 640], BF16)
    khT_bf = P.sb([128, 512], BF16); khT_f = P.sb([128, 128], F32)
    Vdup = P.sb([128, 5, 2, 128], BF16)
    qgT = P.sb([128, 2, 512], BF16); kgA = P.sb([128, 2, 512], BF16); kgB = P.sb([128, 2, 512], BF16)
    kgf = P.sb([128, 2, 128], F32)
    ktok = P.sb([128, 2, 2, 128], BF16)
    vg_tok = P.sb([128, 4, 512], BF16)
    rsT = P.sb([128, 4, 512], BF16)
    ulrT = P.sb([128, 512], BF16)
    zt = P.sb([128, 256], F32); sp_t = P.sb([128, 256], F32)
    ebq = P.sb([128, 2, 512], F32); enb = P.sb([128, 2, 512], F32); elast = P.sb([128, 2, 4], F32)
    elb = P.sb([128, 2, 128], F32)
    S = P.sb([128, 2, 128], F32); Stmp = P.sb([128, 2, 128], F32); SA = P.sb([128, 2, 128], BF16); SB = P.sb([128, 2, 128], BF16)
    ATbf = P.sb([128, 4, 128], BF16)
    sq_bf = P.sb([128, 512], BF16); rstd_f = P.sb([128, 512], F32); tmp_f = P.sb([128, 512], F32)
    pe_f = P.sb([128, 512], F32); rec_f = P.sb([128, 512], F32)
    PT = P.sb([128, 2, 2, 2, 2, 128], BF16)
    mixT = P.sb([128, 8, 512], BF16)
    aT = P.sb([128, NKF, 512], BF16)
    sg_f = P.sb([128, 512], F32)
    vw_f = P.sb([128, 128], F32); kw_f = P.sb([128, 128], F32)
    Kw_bf = P.sb([128, 16, 128], BF16); Kwsw_bf = P.sb([128, 16, 128], BF16)
    KTb = P.sb([128, 2, 2, 128], BF16)
    Vwd = P.sb([128, 16, 2, 128], BF16)
    qsA = P.sb([128, 4, 16], BF16); qsB = P.sb([128, 4, 16], BF16)
    Pts = P.sb([128, 16, 8], BF16); pes = P.sb([128, 16, 8], F32)
    Sb = scr_f[:, 0:512].rearrange("p (a c v) -> p a c v", a=2, c=2)
    Wt = scr_f[:, 512:1024].rearrange("p (a c v) -> p a c v", a=2, c=2)
    Sn = scr_f[:, 1024:1536].rearrange("p (a c v) -> p a c v", a=2, c=2)
    WA = P.sb([128, 2, 2, 128], BF16); WB = P.sb([128, 2, 2, 128], BF16)

    CUR = {"p": 0}
    _vwd32 = Vwd[:].rearrange("p a b c -> p (a b c)").bitcast(F32)
    _kwf = Kw_bf[:].rearrange("p a b -> p (a b)")
    _kwswf = Kwsw_bf[:].rearrange("p a b -> p (a b)")
    nbfs2 = (nbf, nbf_b)

    def X(blk):
        if CUR["p"] == 0:
            return xb[:, blk, :]
        src = _vwd32 if blk < 2 else scr_f[:]
        o = (blk % 2) * 1024
        return src[:, o:o + 1024]

    def A(k):
        if CUR["p"] == 0:
            return actT[:, k, :]
        src = _kwf if k < 4 else _kwswf
        o = (k % 4) * 512
        return src[:, o:o + 512]

    def xr(blk):
        return ("x", CUR["p"], blk)

    def ar(blk):
        return ("actT", CUR["p"], blk)

    def ld(dst, src, name, eng="sp"):
        P.dma(eng, dst, src, writes=[name])

    ld(ident_f[:], cn["ident"], "ident_f"); ld(tri_f[:], cn["tri"], "tri"); ld(tris_f[:], cn["tris"], "tris")
    ld(jx_f[:], cn["jx"], "jx"); ld(oh_sb, cn["oh"], "oh")
    P.dma("pool", ident_bf[:], cn["ident"], writes=["ident_bf"])
    P.dma("pool", cmask_bf[:], cn["cmask"], writes=["cmask"])
    P.dma("pool", bd_bf[:], cn["bd"], writes=["bd"])
    P.dma("pool", sw_bf[:], cn["sw"], writes=["sw"])
    P.dma("pool", sel_bf[:], cn["sel"], writes=["sel"])
    ld(gaT[:], attn_g.rearrange("(k p) -> p k", p=128), "gaT"); ld(gfT[:], ffn_g.rearrange("(k p) -> p k", p=128), "gfT")
    for h in range(2):
        ld(gq_col[h * 64:(h + 1) * 64, :], qg_in.rearrange("(p o) -> p o", o=1), "gq")
        ld(gk_col[h * 64:(h + 1) * 64, :], kg_in.rearrange("(p o) -> p o", o=1), "gk")
    ld(glag_col[:], glag.rearrange("(p o) -> p o", o=1), "glag"); ld(flag_col[:], flag, "flag")
    ld(bgate_bc[:], bass.AP(tensor=bgate.tensor, offset=0, ap=[[0, 128], [1, 256]]), "bgate")
    ld(sink_bc[:], bass.AP(tensor=sinks.tensor, offset=0, ap=[[0, 128], [1, 8]]), "sink_bc")
    P.op("dve", lambda e: e.memset(ones_bf[:], 1.0), writes=["ones"])
    P.op("dve", lambda e: e.memset(zeros_f[:], 0.0), writes=["zeros"])
    P.op("dve", lambda e: e.memset(eps_col[:], EPS), writes=["eps"])
    P.op("dve", lambda e: e.memset(ln8_col[:], math.log(0.125)), writes=["ln8"])
    P.op("dve", lambda e: e.memset(w2pad[:], 0.0), writes=["w2pad"])
    P.dma("pool", w2pad[112:128, :], w2, reads=[], writes=["w2pad"])
    P.op("dve", lambda e: e.tensor_scalar(out=gq_col[:], in0=gq_col[:], scalar1=0.125, scalar2=None, op0=ALU.mult),
         reads=["gq"], writes=["gq"])
    for t_ in (kgA, kgB, SA, SB, qsA, qsB, WA, WB):
        P.op("pool", lambda e, t_=t_: e.memset(t_[:], 0.0), writes=["zinit"])
    P.op("pool", lambda e: e.memset(kX[:], 0.0), writes=["kX"])
    P.op("pool", lambda e: e.memset(S[:], 0.0), writes=["S"])
    P.op("pool", lambda e: e.memset(relb_pad[:], 0.0), writes=["relb_pad"])
    P.op("pool", lambda e: e.memset(relb_pad[32:33, :], 1.0), reads=[], writes=["relb_pad"])
    P.dma("sp", relb_pad[0:32, 0:8], relb, writes=["relb_pad"])

    bk, bkr = P.bank()
    P.op("pe", lambda e: e.matmul(bk[:, 0:512], lhsT=relb_pad[:], rhs=scr_f[:, 1024:1536], start=True, stop=True),
         reads=["relb_pad", "oh"], writes=[bkr])
    P.op("act", lambda e: e.activation(out=scr_f[0:8, 1536:2048], in_=bk[0:8, 0:512], func=AF.Copy), reads=[bkr], writes=["ftab"])
    P.dma("sp", fscr.rearrange("k h i -> h k i"), ftab, reads=["ftab"], writes=["fscr"])
    for kb in range(2):
        src = bass.AP(tensor=fscr.tensor, offset=kb * 8 * 256, ap=[[1, 128], [256, 8], [1, 128]])
        P.dma("sp", hank, src, reads=["fscr"], writes=["hank"])
        for hh in range(0, 8, 4):
            bk, bkr = P.bank()

            def mmj(e, bk=bk, hh=hh):
                ins = None
                for q in range(4):
                    ins = e.matmul(bk[:, q * 128:(q + 1) * 128], lhsT=hank[:, hh + q, :], rhs=jx_f[:], start=True, stop=True)
                return ins
            P.op("pe", mmj, reads=["hank", "jx"], writes=[bkr])
            for q in range(4):
                h = hh + q
                c_, half = h // 2, h % 2
                j, cl = c_ // 2, c_ % 2
                P.op("act", lambda e, bk=bk, q=q, j=j, half=half, kb=kb, cl=cl:
                     e.activation(out=E[:, j, half, kb, cl, :], in_=bk[:, q * 128:(q + 1) * 128], func=AF.Exp),
                     reads=[bkr], writes=["E"])
    P.op("act", lambda e: e.activation(out=sink_bc[:], in_=sink_bc[:], func=AF.Exp), reads=["sink_bc"], writes=["sink_bc"])
    for h in range(8):
        c_, half = h // 2, h % 2
        j, cl = c_ // 2, c_ % 2
        P.op("dve", lambda e, h=h, j=j, half=half, cl=cl: e.tensor_scalar(out=sinkexp[:, j, half, cl, :], in0=zeros_f[:], scalar1=sink_bc[:, h:h + 1],
                                                                         scalar2=None, op0=ALU.add), reads=["sink_bc", "zeros"], writes=["sinkexp"])

    wstate = {"n": 0}

    def wload(parts, fp32=False):
        s = wstate["n"] % NSLOT
        wstate["n"] += 1
        res = ("w", s)
        for (c0, (wn, r0, nrows, cc0, ncols_), nk, ncols) in parts:
            src = (wf if fp32 else wb)[wn][r0:r0 + nrows, cc0:cc0 + ncols_].rearrange("(k p) n -> p k n", p=128)
            q_ = "pool" if (fp32 or wstate["n"] % 2 == 0) else "sp"
            P.dma(q_, ring[:, s, 0:nk, c0:c0 + ncols], src, reads=([] if fp32 else [("wbf", wn)]), writes=[res])
        return s, res

    def wsrc(w, r0, nrows, c0, ncols):
        return (w, r0, nrows, c0, ncols)

    def front(src_fn, nb, gT):
        for blk in range(nb):
            P.dma("sp", X(blk), src_fn(blk), writes=[xr(blk)])
            norm_block(blk, gT)

    def norm_block(blk, gT):
        p = CUR["p"]
        xblk = X(blk); xres = xr(blk); ares = ar(blk)
        nb_ = nbfs2[p]; ss = ss_c2[:, p:p + 1]; rs = rs_c2[:, p:p + 1]
        P.op("act", lambda e: e.activation(out=nb_[:], in_=xblk, func=AF.Square, accum_out=ss),
             reads=[xres], writes=[("nbf", p), ("ss_c", p)])
        P.op("act", lambda e: e.activation(out=rs, in_=ss, func=AF.Ln, scale=1.0 / D, bias=eps_col[:, 0:1]),
             reads=[("ss_c", p), "eps"], writes=[("rs_c", p)])
        P.op("act", lambda e: e.activation(out=rs, in_=rs, func=AF.Exp, scale=-0.5), reads=[("rs_c", p)], writes=[("rs_c", p)])
        P.op("dve", lambda e: e.tensor_scalar(out=nb_[:], in0=xblk, scalar1=rs, scalar2=None, op0=ALU.mult),
             reads=[xres, ("rs_c", p)], writes=[("nbf", p)])
        tb, tbr = tbank()

        def tr(e):
            ins = None
            for k in range(8):
                ins = e.transpose(out=tb[:, k * 128:(k + 1) * 128], in_=nb_[:, k * 128:(k + 1) * 128], identity=ident_bf[:])
            return ins
        P.op("pe", tr, reads=[("nbf", p), "ident_bf"], writes=[tbr])
        if p == 0:
            P.op("dve", lambda e: e.tensor_tensor(out=actT[:, :, blk * 128:(blk + 1) * 128], in0=tb[:].rearrange("p (k t) -> p k t", k=8),
                                                  in1=fap(gT[:], [[1, 8], [0, 128]]), op=ALU.mult),
                 reads=[tbr, "gaT", "gfT"], writes=[ares])
        else:
            for kh, src in enumerate((_kwf, _kwswf)):
                P.op("dve", lambda e, kh=kh, src=src: e.tensor_tensor(
                    out=src.rearrange("p (k t) -> p k t", k=4)[:, :, blk * 128:(blk + 1) * 128],
                    in0=tb[:, kh * 512:(kh + 1) * 512].rearrange("p (k t) -> p k t", k=4),
                    in1=fap(gT[:, kh * 4:kh * 4 + 1], [[1, 4], [0, 128]]), op=ALU.mult),
                    reads=[tbr, "gaT", "gfT", ares], writes=[ares])

    def actT_reads(nb):
        return [ar(b) for b in range(nb)]

    def proj_fm(slot, res, col0, T, nb):
        bk, bkr = P.bank()
        acts = [A(k) for k in range(8)]

        def mm(e):
            ins = None
            for k in range(8):
                ins = e.matmul(bk[:, 0:T], lhsT=ring[:, slot, k, col0:col0 + 128], rhs=acts[k][:, 0:T], start=(k == 0), stop=(k == 7))
            return ins
        P.op("pe", mm, reads=[res] + actT_reads(nb), writes=[bkr])
        return bk, bkr

    def proj_tm(slot, res, col0, ncols, blk):
        bk, bkr = P.bank()
        acts = [A(k) for k in range(8)]

        def mm(e):
            ins = None
            for k in range(8):
                ins = e.matmul(bk[:, 0:ncols], lhsT=acts[k][:, blk * 128:(blk + 1) * 128], rhs=ring[:, slot, k, col0:col0 + ncols],
                               start=(k == 0), stop=(k == 7))
            return ins
        P.op("pe", mm, reads=[res, ar(blk)], writes=[bkr])
        return bk, bkr

    def rstd_fm(src_ap, T, lhs_ones, scale, reads_src):
        P.op("act", lambda e: e.activation(out=sq_bf[:, 0:T], in_=src_ap, func=AF.Square), reads=reads_src, writes=["sq"])
        b2, b2r = P.bank()
        P.op("pe", lambda e: e.matmul(b2[:, 0:T], lhsT=lhs_ones[:], rhs=sq_bf[:, 0:T], start=True, stop=True),
             reads=["sq", "bd", "ones"], writes=[b2r])
        P.op("act", lambda e: e.activation(out=rstd_f[:, 0:T], in_=b2[:, 0:T], func=AF.Ln, scale=scale, bias=eps_col[:, 0:1]),
             reads=[b2r, "eps"], writes=["rstd"])
        P.op("act", lambda e: e.activation(out=rstd_f[:, 0:T], in_=rstd_f[:, 0:T], func=AF.Exp, scale=-0.5), reads=["rstd"], writes=["rstd"])

    def gla_prep(slot_lr, res_lr, lrcol, nb, T, tri_ap, tri_res, full):
        bk, bkr = proj_fm(slot_lr, res_lr, lrcol, T, nb)
        P.op("act", lambda e: e.activation(out=ulrT[:, 0:T], in_=bk[:, 0:T], func=AF.Copy), reads=[bkr], writes=["ulrT"])
        bT = [P.bank(hold=True) for _ in range(2)]
        for blk in range(nb):
            zb, zbr = P.bank()
            P.op("pe", lambda e, zb=zb, blk=blk: e.matmul(zb[:, 0:256], lhsT=ulrT[:, blk * 128:(blk + 1) * 128], rhs=w2pad[:], start=True, stop=True),
                 reads=["ulrT", "w2pad"], writes=[zbr])
            P.op("dve", lambda e, zb=zb: e.tensor_tensor(out=zt[:], in0=zb[:, 0:256], in1=bgate_bc[:], op=ALU.add),
                 reads=[zbr, "bgate"], writes=["zt"])
            P.op("act", lambda e: e.activation(out=zt[:], in_=zt[:], func=AF.Exp, scale=-1.0), reads=["zt"], writes=["zt"])
            P.op("act", lambda e: e.activation(out=sp_t[:], in_=zt[:], func=AF.Ln, bias=1.0), reads=["zt"], writes=["sp_t"])
            for c in range(2):
                P.op("pe", lambda e, c=c, blk=blk: e.matmul(bT[c][0][:, blk * 128:(blk + 1) * 128], lhsT=sp_t[:, c * 128:(c + 1) * 128], rhs=tri_ap,
                                                           start=True, stop=True), reads=["sp_t", tri_res], writes=[bT[c][1]])
        for c in range(2):
            P.release(bT[c][1])
        for c in range(2):
            P.op("act", lambda e, c=c: e.activation(out=enb[:, c, 0:T], in_=bT[c][0][:, 0:T], func=AF.Exp, scale=-1.0), reads=[bT[c][1]], writes=["enb"])
            P.op("act", lambda e, c=c: e.activation(out=elast[:, c, 0:nb], in_=fap(bT[c][0][:, 127:128], [[128, nb]]), func=AF.Exp),
                 reads=[bT[c][1]], writes=["elast"])
            if full:
                P.op("act", lambda e, c=c: e.activation(out=ebq[:, c, 0:T], in_=bT[c][0][:, 0:T], func=AF.Exp, bias=ln8_col[:, 0:1]),
                     reads=[bT[c][1], "ln8"], writes=["ebq"])
                if T == 128:
                    P.op("act", lambda e, c=c: e.activation(out=elb[:, c, :], in_=bT[c][0][:, 0:128], func=AF.Exp), reads=[bT[c][1]], writes=["elb"])

    def kg_evac(slot, res, col0, nb, T, sample=False):
        for c in range(2):
            bk, bkr = proj_fm(slot, res, col0 + c * 128, T, nb)
            P.op("dve", lambda e, bk=bk, c=c: e.tensor_tensor(out=kgA[0:64, c, 0:T], in0=bk[0:64, 0:T], in1=enb[0:64, c, 0:T], op=ALU.mult),
                 reads=[bkr, "enb", "zinit"], writes=[("kgA", c)])
            P.op("dve", lambda e, bk=bk, c=c: e.tensor_tensor(out=kgB[64:128, c, 0:T], in0=bk[64:128, 0:T], in1=enb[64:128, c, 0:T], op=ALU.mult),
                 reads=[bkr, "enb", "zinit"], writes=[("kgB", c)])
            if sample:
                P.op("dve", lambda e, bk=bk, c=c: e.tensor_tensor(out=kgf[:, c, :], in0=bk[:, 0:128], in1=enb[:, c, 0:128], op=ALU.mult),
                     reads=[bkr, "enb"], writes=["kgf"])

    def vg_tm(slot, res, col0, nb):
        for blk in range(nb):
            bk, bkr = proj_tm(slot, res, col0, 512, blk)
            P.op("act", lambda e, bk=bk, blk=blk: e.activation(out=vg_tok[:, blk, :], in_=bk[:, 0:512], func=AF.Copy), reads=[bkr], writes=[("vg", blk)])

    def state_update(blk, masked):
        tb, tbr = tbank()

        def tr(e):
            ins = None
            for c in range(2):
                for ab, src in enumerate((kgA, kgB)):
                    o = (c * 2 + ab) * 128
                    ins = e.transpose(out=tb[:, o:o + 128], in_=src[:, c, blk * 128:(blk + 1) * 128], identity=ident_bf[:])
            return ins
        P.op("pe", tr, reads=[("kgA", 0), ("kgA", 1), ("kgB", 0), ("kgB", 1), "ident_bf"], writes=[tbr])
        P.op("act", lambda e: e.activation(out=ktok[:].rearrange("p c a f -> p (c a f)"), in_=tb[:, 0:512], func=AF.Copy), reads=[tbr], writes=["ktok"])
        ub, ubr = P.bank()

        def mm(e):
            ins = None
            for c in range(2):
                for ab in range(2):
                    h = 2 * c + ab
                    ins = e.matmul(ub[:, c * 128:(c + 1) * 128], lhsT=ktok[:, c, ab, :], rhs=vg_tok[:, blk, h * 128:(h + 1) * 128],
                                   start=(ab == 0), stop=(ab == 1))
            return ins
        P.op("pe", mm, reads=["ktok", ("vg", blk)], writes=[ubr])
        P.op("dve", lambda e: e.tensor_tensor(out=Stmp[:].rearrange("p c v -> p (c v)"), in0=ub[:, 0:256], in1=S[:].rearrange("p c v -> p (c v)"), op=ALU.add),
             reads=[ubr, "S"], writes=["Stmp"])
        P.op("dve", lambda e: e.tensor_tensor(out=S[:], in0=Stmp[:], in1=fap(elast[:, 0, blk:blk + 1], [[4, 2], [0, 128]]), op=ALU.mult),
             reads=["Stmp", "elast", "SA", "SB"], writes=["S"])
        if masked:
            P.op("act", lambda e: e.activation(out=SA[0:64], in_=S[0:64], func=AF.Copy), reads=["S", "zinit"], writes=["SA"])
            P.op("act", lambda e: e.activation(out=SB[64:128], in_=S[64:128], func=AF.Copy), reads=["S", "zinit"], writes=["SB"])

    def gla_out_norm(ob, obr, ncol, W, col0):
        rstd_fm(ob[:, 0:ncol], ncol, ones_bf, 1.0 / 128, [obr])
        P.op("dve", lambda e: e.scalar_tensor_tensor(out=tmp_f[:, 0:ncol], in0=ob[:, 0:ncol], scalar=glag_col[:, 0:1], in1=rstd_f[:, 0:ncol],
                                                     op0=ALU.mult, op1=ALU.mult), reads=[obr, "rstd", "glag"], writes=["tmp_f"])
        P.op("dve", lambda e: e.tensor_tensor(out=mixT[:, 4:8, col0:col0 + W], in0=tmp_f[:, 0:ncol].rearrange("p (h w) -> p h w", h=4),
                                              in1=rsT[:, :, col0:col0 + W], op=ALU.mult), reads=["tmp_f", "rsT"], writes=[("mixg", col0)])

    def gla_block(blk):
        ab_, abr = P.bank()

        def mm1(e):
            ins = None
            for h in range(4):
                c, half = h // 2, h % 2
                src = kgA if half == 0 else kgB
                ins = e.matmul(ab_[:, h * 128:(h + 1) * 128], lhsT=src[:, c, blk * 128:(blk + 1) * 128], rhs=qgT[:, c, blk * 128:(blk + 1) * 128],
                               start=True, stop=True)
            return ins
        P.op("pe", mm1, reads=[("kgA", 0), ("kgA", 1), ("kgB", 0), ("kgB", 1), "qgT"], writes=[abr])
        P.op("dve", lambda e: e.tensor_tensor(out=ATbf[:], in0=ab_[:, 0:512].rearrange("p (h t) -> p h t", h=4), in1=fap(cmask_bf[:], [[0, 4], [1, 128]]),
                                              op=ALU.mult), reads=[abr, "cmask"], writes=["ATbf"])
        ob, obr = P.bank()

        def mm2(e):
            ins = None
            for h in range(4):
                c, half = h // 2, h % 2
                sm = SA if half == 0 else SB
                e.matmul(ob[:, h * 128:(h + 1) * 128], lhsT=vg_tok[:, blk, h * 128:(h + 1) * 128], rhs=ATbf[:, h, :], start=True, stop=False)
                ins = e.matmul(ob[:, h * 128:(h + 1) * 128], lhsT=sm[:, c, :], rhs=qgT[:, c, blk * 128:(blk + 1) * 128], start=False, stop=True)
            return ins
        P.op("pe", mm2, reads=[("vg", blk), "ATbf", "SA", "SB", "qgT"], writes=[obr])
        gla_out_norm(ob, obr, 512, 128, blk * 128)
        state_update(blk, True)

    def attn_block(blk, Etab, Eres, useflag=False):
        for j in range(2):
            sbk = []
            for half in range(2):
                bk, bkr = P.bank()
                sbk.append((bk, bkr))

                def mm(e, bk=bk, half=half, j=j):
                    ins = None
                    for kb in range(2):
                        kc = (blk + kb) * 128
                        ins = e.matmul(bk[:, kb * 256:(kb + 1) * 256], lhsT=kX[:, 2 * j + half, kc:kc + 128],
                                       rhs=qhT[:, 2 * j:2 * j + 2, blk * 128:(blk + 1) * 128], start=True, stop=True)
                    return ins
                P.op("pe", mm, reads=["kX", "qhT"], writes=[bkr])
            for half in range(2):
                bk, bkr = sbk[half]
                P.op("act", lambda e, bk=bk: e.activation(out=pe_f[:], in_=bk[:, 0:512], func=AF.Exp), reads=[bkr], writes=["pe_f"])
                P.op("dve", lambda e, half=half, j=j: e.tensor_tensor(out=PT[:, j, half].rearrange("p a b q -> p (a b q)"), in0=pe_f[:],
                                                                     in1=Etab[:, j, half].rearrange("p a b q -> p (a b q)"), op=ALU.mult),
                     reads=["pe_f", Eres], writes=[("PT", j)])
                if useflag:
                    P.op("dve", lambda e, half=half, j=j: e.tensor_scalar(out=PT[:, j, half, 0], in0=PT[:, j, half, 0], scalar1=flag_col[:, 0:1],
                                                                          scalar2=None, op0=ALU.mult), reads=[("PT", j), "flag"], writes=[("PT", j)])
            ob, obr = P.bank()
            db, dbr = P.bank()

            def mmv(e, ob=ob, db=db, j=j):
                ins = None
                for kb in range(2):
                    rhs = PT[:, j, :, kb, :, :]
                    e.matmul(ob[:, 0:512], lhsT=Vdup[:, blk + kb, j, :], rhs=rhs, start=(kb == 0), stop=(kb == 1))
                for kb in range(2):
                    rhs = PT[:, j, :, kb, :, :]
                    ins = e.matmul(db[:, 0:512], lhsT=ones_bf[:], rhs=rhs, start=(kb == 0), stop=(kb == 1))
                return ins
            P.op("pe", mmv, reads=[("PT", j), "Vdup", "ones"], writes=[obr, dbr])
            P.op("dve", lambda e, db=db, j=j: e.tensor_tensor(out=rec_f[:], in0=db[:, 0:512], in1=sinkexp[:, j].rearrange("p a b q -> p (a b q)"), op=ALU.add),
                 reads=[dbr, "sinkexp"], writes=["rec"])
            P.op("act", lambda e: e.activation(out=rec_f[:], in_=rec_f[:], func=AF.Ln), reads=["rec"], writes=["rec"])
            P.op("act", lambda e: e.activation(out=rec_f[:], in_=rec_f[:], func=AF.Exp, scale=-1.0), reads=["rec"], writes=["rec"])
            for half in range(2):
                r0 = half * 64
                P.op("dve", lambda e, ob=ob, half=half, r0=r0, j=j: e.tensor_tensor(
                    out=mixT[r0:r0 + 64, 2 * j:2 * j + 2, blk * 128:(blk + 1) * 128],
                    in0=ob[r0:r0 + 64, half * 256:(half + 1) * 256].rearrange("p (c q) -> p c q", c=2),
                    in1=rec_f[r0:r0 + 64, half * 256:(half + 1) * 256].rearrange("p (c q) -> p c q", c=2), op=ALU.mult),
                    reads=[obr, "rec"], writes=[("mixa", blk)])

    def qk_norm_chunk(bk, bkr, T, gcol, gres, out_fn):
        rstd_fm(bk[:, 0:T], T, bd_bf, 1.0 / 64, [bkr])
        out_fn(bk, bkr)

    def wo_ffnnorm(nb, s0, r0, s1, r1):
        for blk in range(nb):
            for cg, (s, r) in enumerate(((s0, r0), (s1, r1))):
                bk, bkr = P.bank()

                def mm(e, bk=bk, s=s, blk=blk):
                    ins = None
                    for k in range(8):
                        ins = e.matmul(bk[:, 0:512], lhsT=mixT[:, k, blk * 128:(blk + 1) * 128], rhs=ring[:, s, k, 0:512], start=(k == 0), stop=(k == 7))
                    return ins
                P.op("pe", mm, reads=[r, ("mixa", blk), ("mixg", blk * 128)], writes=[bkr])
                xs_ = X(blk)[:, cg * 512:(cg + 1) * 512]
                P.op("dve", lambda e, bk=bk, xs_=xs_: e.tensor_tensor(out=xs_, in0=bk[:, 0:512], in1=xs_, op=ALU.add), reads=[bkr, xr(blk)], writes=[xr(blk)])
            norm_block(blk, gfT)

    def ffn(nb, T, ydst_fn):
        for s6 in range(6):
            ncols = 512 if s6 < 5 else 256
            sg_, rg_ = wload([(0, wsrc("w_gate", 0, D, s6 * 512, ncols), 8, ncols)])
            su_, ru_ = wload([(0, wsrc("w_up", 0, D, s6 * 512, ncols), 8, ncols)])
            for mi in range(ncols // 128):
                m = s6 * 4 + mi
                gb, gbr = proj_fm(sg_, rg_, mi * 128, T, nb)
                ubk, ubr = proj_fm(su_, ru_, mi * 128, T, nb)
                P.op("act", lambda e, gb=gb: e.activation(out=sg_f[:, 0:T], in_=gb[:, 0:T], func=AF.Silu), reads=[gbr], writes=["sg_f"])
                P.op("dve", lambda e, ubk=ubk, m=m: e.tensor_tensor(out=aT[:, m, 0:T], in0=ubk[:, 0:T], in1=sg_f[:, 0:T], op=ALU.mult),
                     reads=[ubr, "sg_f"], writes=[("aT", m)])
        for cg in range(2):
            bks = [P.bank(hold=True) for _ in range(nb)]
            for kgp in range(3):
                nk = 8 if kgp < 2 else 6
                sl = wload([(0, wsrc("w_down", kgp * 1024, nk * 128, cg * 512, 512), nk, 512)])
                for blk in range(nb):
                    bk, bkr = bks[blk]

                    def mm(e, bk=bk, blk=blk, sl=sl, kgp=kgp, nk=nk):
                        ins = None
                        for kk in range(nk):
                            k = kgp * 8 + kk
                            ins = e.matmul(bk[:, 0:512], lhsT=aT[:, k, blk * 128:(blk + 1) * 128], rhs=ring[:, sl[0], kk, 0:512], start=(k == 0), stop=(k == NKF - 1))
                        return ins
                    P.op("pe", mm, reads=[sl[1]] + [("aT", kgp * 8 + kk) for kk in range(nk)], writes=[bkr])
            for blk in range(nb):
                bk, bkr = bks[blk]
                P.release(bkr)
                xs_ = X(blk)[:, cg * 512:(cg + 1) * 512]
                P.op("dve", lambda e, bk=bk, xs_=xs_: e.tensor_tensor(out=xs_, in0=bk[:, 0:512], in1=xs_, op=ALU.add), reads=[bkr, xr(blk)], writes=[xr(blk)])
                if cg == 1:
                    outs.append(P.dma("sp", ydst_fn(blk), X(blk), reads=[xr(blk)]))

    outs = []

    wstate["n"] = 0
    s_a, r_a = wload([(0, wsrc("w_in", 0, D, 1024, 256), 8, 256), (256, wsrc("w_in", 0, D, 2192, 128), 8, 128)], fp32=True)
    s_b, r_b = wload([(0, wsrc("w_in", 0, D, 1280, 512), 8, 512)], fp32=True)
    for k in range(8):
        P.op("dve", lambda e, k=k: e.tensor_scalar(out=ring[:, s_a, k, 0:384], in0=ring[:, s_a, k, 0:384], scalar1=gaT[:, k:k + 1], scalar2=None, op0=ALU.mult),
             reads=[r_a, "gaT"], writes=[r_a])
        P.op("dve", lambda e, k=k: e.tensor_scalar(out=ring[:, s_b, k, 0:512], in0=ring[:, s_b, k, 0:512], scalar1=gaT[:, k:k + 1], scalar2=None, op0=ALU.mult),
             reads=[r_b, "gaT"], writes=[r_b])
    negs = P.sb([128, 2], F32)
    P.op("dve", lambda e: e.memset(negs[:], -1.0 / 16), writes=["negs"])
    _enbflat = enb[:].rearrange("p c t -> p (c t)")
    _qgflat = qgT[:].rearrange("p c t -> p (c t)")
    for wn in ("w_in", "w_o", "w_gate", "w_up", "w_down"):
        nr = wf[wn].shape[0]
        step = 256
        for r0 in range(0, nr, step):
            r1 = min(nr, r0 + step)
            ci = P.dma("pool", wb[wn][r0:r1, :], wf[wn][r0:r1, :], reads=["wbfchain"], writes=[("wbf", wn), "wbfchain"])
            P.nofence = getattr(P, "nofence", set()) | {ci}
    NBLK = 0 if debug else NPRE // 128
    nbfs = (nbf, nbf_b, nbf_c); zts = (zt, zt_b); sps = (sp_t, sp_b); ktoks = (ktok, ktok_b)
    sbanks = {}

    def sc1(b):
        q = b % 4; p = b % 2; p3 = b % 3
        P.dma("sp", xb[:, q, :], xpre[b * 128:(b + 1) * 128, :], writes=[("sx", q)])
        P.op("act", lambda e: e.activation(out=nbfs[p3][:], in_=xb[:, q, :], func=AF.Square, accum_out=ss_c2[:, p:p + 1]),
             reads=[("sx", q)], writes=[("snbf", p3), ("sss", p)])
        P.op("act", lambda e: e.activation(out=rs_c2[:, p:p + 1], in_=ss_c2[:, p:p + 1], func=AF.Ln, scale=1.0 / D, bias=eps_col[:, 0:1]),
             reads=[("sss", p), "eps"], writes=[("srs", p)])
        P.op("act", lambda e: e.activation(out=rs_c2[:, p:p + 1], in_=rs_c2[:, p:p + 1], func=AF.Exp, scale=-0.5), reads=[("srs", p)], writes=[("srs", p)])
        P.op("dve", lambda e: e.tensor_scalar(out=nbfs[p3][:], in0=xb[:, q, :], scalar1=rs_c2[:, p:p + 1], scalar2=None, op0=ALU.mult),
             reads=[("sx", q), ("srs", p)], writes=[("snbf", p3)])
        tb, tbr = tbank()

        def tr(e):
            ins = None
            for k in range(8):
                ins = e.transpose(out=tb[:, k * 128:(k + 1) * 128], in_=nbfs[p3][:, k * 128:(k + 1) * 128], identity=ident_bf[:])
            return ins
        P.op("pe", tr, reads=[("snbf", p3), "ident_bf"], writes=[tbr])
        if b % 2 == 0:
            P.op("act", lambda e: e.activation(out=actT[:, :, q * 128:(q + 1) * 128], in_=tb[:].rearrange("p (k t) -> p k t", k=8), func=AF.Copy),
                 reads=[tbr], writes=[("sact", q)])
        else:
            P.op("dve", lambda e: e.tensor_copy(out=actT[:, :, q * 128:(q + 1) * 128], in_=tb[:].rearrange("p (k t) -> p k t", k=8)),
                 reads=[tbr], writes=[("sact", q)])

    def sc2(b):
        q = b % 4
        ab, abr = P.bank()
        vb, vbr = P.bank()

        def mm(e):
            ins = None
            for k in range(8):
                ins = e.matmul(ab[:, 0:128], lhsT=ring[:, s_a, k, 256:384], rhs=actT[:, k, q * 128:(q + 1) * 128], start=(k == 0), stop=(k == 7))
            for k in range(8):
                ins = e.matmul(vb[:, 0:512], lhsT=actT[:, k, q * 128:(q + 1) * 128], rhs=ring[:, s_b, k, 0:512], start=(k == 0), stop=(k == 7))
            return ins
        P.op("pe", mm, reads=[r_a, r_b, ("sact", q)], writes=[abr, vbr])
        P.op("dve", lambda e: e.tensor_copy(out=ulrT[:, q * 128:(q + 1) * 128], in_=ab[:, 0:128]), reads=[abr], writes=[("sulr", q)])
        P.op("act", lambda e: e.activation(out=vg_tok[:, q, :], in_=vb[:, 0:512], func=AF.Copy), reads=[vbr], writes=[("svg", q)])

    def sc3a(b):
        q = b % 4; p = b % 2
        zb, zbr = P.bank()
        sbanks[b] = (zb, zbr)
        P.op("pe", lambda e: e.matmul(zb[:, 0:256], lhsT=ulrT[:, q * 128:(q + 1) * 128], rhs=w2pad[:], start=True, stop=True),
             reads=[("sulr", q), "w2pad"], writes=[zbr])
        P.op("dve", lambda e: e.tensor_tensor(out=zts[p][:], in0=zb[:, 0:256], in1=bgate_bc[:], op=ALU.add), reads=[zbr, "bgate"], writes=[("szt", p)])
        P.op("act", lambda e: e.activation(out=zts[p][:], in_=zts[p][:], func=AF.Exp, scale=-1.0), reads=[("szt", p)], writes=[("szt", p)])
        P.op("act", lambda e: e.activation(out=sps[p][:], in_=zts[p][:], func=AF.Ln, bias=1.0), reads=[("szt", p)], writes=[("ssp", p)])

    def sc3b(b):
        q = b % 4; p = b % 2
        kb_, kbr = P.bank()
        cb, cbr = P.bank()
        en_ = _enbflat[:, q * 256:(q + 1) * 256]
        kt_ = _qgflat[:, q * 256:(q + 1) * 256]

        def mm(e):
            ins = None
            for k in range(8):
                ins = e.matmul(kb_[:, 0:256], lhsT=actT[:, k, q * 128:(q + 1) * 128], rhs=ring[:, s_a, k, 0:256], start=(k == 0), stop=(k == 7))
            return ins
        P.op("pe", mm, reads=[r_a, ("sact", q)], writes=[kbr])

        def mmc(e):
            e.matmul(cb[:, 0:256], lhsT=tri_f[:], rhs=sps[p][:], start=True, stop=True)
            ins = None
            for c in range(2):
                ins = e.matmul(cb[:, 256 + 2 * c:258 + 2 * c], lhsT=sps[p][:, c * 128:(c + 1) * 128], rhs=negs[:], start=True, stop=True)
            return ins
        P.op("pe", mmc, reads=[("ssp", p), "tri", "negs"], writes=[cbr])
        P.op("act", lambda e: e.activation(out=en_, in_=cb[:, 0:256], func=AF.Exp, scale=-1.0), reads=[cbr], writes=[("senb", q)])
        P.op("act", lambda e: e.activation(out=elast[:, :, q], in_=fap(cb[:, 256:257], [[2, 2]]), func=AF.Exp), reads=[cbr], writes=[("sel", q)])
        P.op("dve", lambda e: e.tensor_tensor(out=kt_, in0=kb_[:, 0:256], in1=en_, op=ALU.mult),
             reads=[kbr, ("senb", q)], writes=[("sktok", q)])

    def sc4(b):
        q = b % 4; p = b % 2
        kt = _qgflat[:, q * 256:(q + 1) * 256]
        ub, ubr = P.bank()

        def mm(e):
            ins = None
            for c in range(2):
                for ab in range(2):
                    h = 2 * c + ab
                    ins = e.matmul(ub[:, h * 128:(h + 1) * 128], lhsT=kt[:, c * 128:(c + 1) * 128], rhs=vg_tok[:, q, h * 128:(h + 1) * 128], start=True, stop=True)
            return ins
        P.op("pe", mm, reads=[("sktok", q), ("svg", q)], writes=[ubr])
        for ab in range(2):
            r0 = ab * 64
            P.op("dve", lambda e, ab=ab, r0=r0: e.tensor_tensor(out=Stmp[r0:r0 + 64], in0=fap(ub[r0:r0 + 64, ab * 128:ab * 128 + 1], [[256, 2], [1, 128]]),
                                                               in1=S[r0:r0 + 64], op=ALU.add), reads=[ubr, "S", "Stmp"], writes=["Stmp"])
        P.op("dve", lambda e: e.tensor_tensor(out=S[:], in0=Stmp[:], in1=fap(elast[:, 0, q:q + 1], [[4, 2], [0, 128]]), op=ALU.mult),
             reads=["Stmp", ("sel", q)], writes=["S"])

    stages = (sc1, sc2, sc3a, sc3b, sc4)
    for i in range(NBLK + len(stages) - 1):
        for si, st in enumerate(stages):
            b = i - si
            if 0 <= b < NBLK:
                st(b)
    P.barrier(keep=("wbf", "wbfchain"))
    P.op("act", lambda e: e.activation(out=SA[0:64], in_=S[0:64], func=AF.Copy), reads=["S", "zinit"], writes=["SA"])
    P.op("act", lambda e: e.activation(out=SB[64:128], in_=S[64:128], func=AF.Copy), reads=["S", "zinit"], writes=["SB"])

    def main_tile(kind, t):
        sample = kind == "sample"
        nb = 1 if sample else 4
        T = nb * 128
        CUR["p"] = 0 if sample else (t % 2)
        first = (kind == "prompt" and t == 0)
        last = (kind == "prompt" and t == 3)
        L0 = wload([(0, wsrc("w_in", 0, D, 0, 512), 8, 512)])
        L1 = wload([(0, wsrc("w_in", 0, D, 512, 512), 8, 512)])
        L2 = wload([(0, wsrc("w_in", 0, D, 1024, 256), 8, 256), (256, wsrc("w_in", 0, D, 2192, 128), 8, 128)])
        if first:
            front(lambda blk: xhalo, 1, gaT)
            halo_kv = True
            kv_part(L1, 1, 128, 0, False, False)
        if sample:
            front(lambda blk: xs, 1, gaT)
        else:
            front(lambda blk: xp[t * 512 + blk * 128: t * 512 + (blk + 1) * 128, :], 4, gaT)
        gla_prep(L2[0], L2[1], 256, nb, T, (tris_f if sample else tri_f)[:], "tris" if sample else "tri", True)
        for c in range(4):
            bk, bkr = proj_fm(L0[0], L0[1], c * 128, T, nb)
            rstd_fm(bk[:, 0:T], T, bd_bf, 1.0 / 64, [bkr])
            P.op("dve", lambda e, bk=bk, c=c: e.scalar_tensor_tensor(out=qhT[:, c, 0:T], in0=bk[:, 0:T], scalar=gq_col[:, 0:1], in1=rstd_f[:, 0:T],
                                                                    op0=ALU.mult, op1=ALU.mult), reads=[bkr, "rstd", "gq"], writes=["qhT"])
        kv_part(L1, nb, T, 1, last, sample)
        for c in range(2):
            bk, bkr = proj_fm(L1[0], L1[1], 256 + c * 128, T, nb)
            P.op("dve", lambda e, bk=bk, c=c: e.tensor_tensor(out=qgT[:, c, 0:T], in0=bk[:, 0:T], in1=ebq[:, c, 0:T], op=ALU.mult),
                 reads=[bkr, "ebq"], writes=["qgT"])
        kg_evac(L2[0], L2[1], 0, nb, T, sample)
        L3 = wload([(0, wsrc("w_in", 0, D, 1280, 512), 8, 512)])
        vg_tm(L3[0], L3[1], 0, nb)
        L4 = wload([(0, wsrc("w_in", 0, D, 1792, 512), 8, 512)])
        for c in range(4):
            bk, bkr = proj_fm(L4[0], L4[1], c * 128, T, nb)
            P.op("act", lambda e, bk=bk, c=c: e.activation(out=rsT[:, c, 0:T], in_=bk[:, 0:T], func=AF.Silu), reads=[bkr], writes=["rsT"])
        if sample:
            sample_attn()
            sample_gla()
        else:
            for blk in range(nb):
                attn_block(blk, E, "E", first and blk == 0)
                gla_block(blk)
            P.op("pool", lambda e: e.tensor_copy(out=kX[:, :, 0:128], in_=kX[:, :, 512:640]), reads=["kX"], writes=["kX"])
            P.op("pool", lambda e: e.tensor_copy(out=Vdup[:, 0], in_=Vdup[:, 4]), reads=["Vdup"], writes=["Vdup"])
        if debug:
            outs.append(P.dma("pool", dbg_mix, mixT[:], reads=[("mixa", b_) for b_ in range(nb)] + [("mixg", b_ * 128) for b_ in range(nb)]))
            outs.append(P.dma("pool", dbg_q, qhT[:], reads=["qhT"]))
            outs.append(P.dma("pool", dbg_rs, rsT[:], reads=["rsT"]))
        L5 = wload([(0, wsrc("w_o", 0, D, 0, 512), 8, 512)])
        L6 = wload([(0, wsrc("w_o", 0, D, 512, 512), 8, 512)])
        wo_ffnnorm(nb, L5[0], L5[1], L6[0], L6[1])
        if debug:
            outs.append(P.dma("sp", dbg_h, xb[:], reads=[("x", 0, b_) for b_ in range(nb)]))
            outs.append(P.dma("pool", dbg_z, actT[:], reads=[("actT", 0, b_) for b_ in range(nb)]))
        if sample:
            ffn(nb, T, lambda blk: y_s)
        else:
            ffn(nb, T, lambda blk: y_p[t * 512 + blk * 128: t * 512 + (blk + 1) * 128, :])

    def kv_part(L1, nb, T, kblk0, last, sample):
        bk, bkr = proj_fm(L1[0], L1[1], 0, T, nb)
        rstd_fm(bk[:, 0:T], T, bd_bf, 1.0 / 64, [bkr])
        c0 = kblk0 * 128
        P.op("dve", lambda e: e.scalar_tensor_tensor(out=khT_bf[:, 0:T], in0=bk[:, 0:T], scalar=gk_col[:, 0:1], in1=rstd_f[:, 0:T], op0=ALU.mult, op1=ALU.mult),
             reads=[bkr, "rstd", "gk"], writes=["khT"])
        if last or sample:
            lo = T - 128
            P.op("dve", lambda e: e.scalar_tensor_tensor(out=khT_f[:], in0=bk[:, lo:T], scalar=gk_col[:, 0:1], in1=rstd_f[:, lo:T], op0=ALU.mult, op1=ALU.mult),
                 reads=[bkr, "rstd", "gk"], writes=["khT_f"])
        P.op("act", lambda e: e.activation(out=kX[0:64, 0, c0:c0 + T], in_=khT_bf[0:64, 0:T], func=AF.Copy), reads=["khT", "kX"], writes=["kX"])
        P.op("act", lambda e: e.activation(out=kX[64:128, 3, c0:c0 + T], in_=khT_bf[64:128, 0:T], func=AF.Copy), reads=["khT", "kX"], writes=["kX"])
        b2, b2r = P.bank()
        P.op("pe", lambda e: e.matmul(b2[:, 0:T], lhsT=sw_bf[:], rhs=khT_bf[:, 0:T], start=True, stop=True), reads=["khT", "sw"], writes=[b2r])
        P.op("act", lambda e: e.activation(out=kX[0:64, 2, c0:c0 + T], in_=b2[0:64, 0:T], func=AF.Copy), reads=[b2r, "kX"], writes=["kX"])
        P.op("act", lambda e: e.activation(out=kX[64:128, 1, c0:c0 + T], in_=b2[64:128, 0:T], func=AF.Copy), reads=[b2r, "kX"], writes=["kX"])
        for blk in range(nb):
            vb, vbr = proj_tm(L1[0], L1[1], 128, 128, blk)
            P.op("act", lambda e, vb=vb, blk=blk: e.activation(out=Vdup[:, kblk0 + blk].rearrange("p j (u d) -> p j u d", u=2),
                                                               in_=fap(vb[:, 0:1], [[64, 2], [0, 2], [1, 64]]), func=AF.Copy),
                 reads=[vbr, "Vdup"], writes=["Vdup"])
            if (last and blk == nb - 1) or sample:
                P.op("dve", lambda e, vb=vb: e.tensor_copy(out=vw_f[:], in_=vb[:, 0:128]), reads=[vbr], writes=["vw_f"])
        if last or sample:
            kb_, kbr = P.bank()
            P.op("pe", lambda e: e.transpose(out=kb_[:, 0:128], in_=khT_f[:], identity=ident_f[:]), reads=["khT_f", "ident_f"], writes=[kbr])
            P.op("act", lambda e: e.activation(out=kw_f[:], in_=kb_[:, 0:128], func=AF.Copy), reads=[kbr], writes=["kw_f"])
        if last:
            outs.append(P.dma("sp", kwp, kw_f[:], reads=["kw_f"]))
            outs.append(P.dma("sp", vwp, vw_f[:], reads=["vw_f"]))

    def sample_attn():
        for (dst, src, new, nm) in ((kws, ck, kw_f, "kws"), (vws, cv, vw_f, "vws")):
            P.dma("sp", dst[:, 0:127, :], src[:, 1:128, :], writes=[nm])
            P.dma("sp", bass.AP(tensor=dst.tensor, offset=127 * 128, ap=[[128 * 128, 16], [1, 128]]), new[0:16, :], reads=["kw_f", "vw_f"], writes=[nm])
        t1 = P.dma("pool", Kw_bf[:], kws.rearrange("b k f -> k b f"), reads=["kws"], writes=["Kw"])
        for j in range(2):
            P.dma("pool", Kwsw_bf[:, :, (1 - j) * 64:(2 - j) * 64], kws[:, :, j * 64:(j + 1) * 64].rearrange("b k f -> k b f"), reads=["kws"], writes=["Kwsw"])
            for u in range(2):
                P.dma("pool", Vwd[:, :, j, u * 64:(u + 1) * 64], vws[:, :, j * 64:(j + 1) * 64].rearrange("b k f -> k b f"), reads=["vws"], writes=["Vwd"])
        outs.append(t1)
        P.op("dve", lambda e: e.tensor_copy(out=qsA[0:64], in_=qhT[0:64, :, 0:16]), reads=["qhT", "zinit"], writes=["qsA"])
        P.op("dve", lambda e: e.tensor_copy(out=qsB[64:128], in_=qhT[64:128, :, 0:16]), reads=["qhT", "zinit"], writes=["qsB"])
        sb_, sbr = P.bank()
        for b in range(16):
            tb, tbr = tbank()
            bf_ = b % 2

            def tr(e, tb=tb, b=b):
                e.transpose(out=tb[:, 0:128], in_=Kw_bf[:, b, :], identity=ident_bf[:])
                return e.transpose(out=tb[:, 128:256], in_=Kwsw_bf[:, b, :], identity=ident_bf[:])
            P.op("pe", tr, reads=["Kw", "Kwsw", "ident_bf"], writes=[tbr])
            P.op("act", lambda e, tb=tb, bf_=bf_: e.activation(out=KTb[:, bf_].rearrange("p a k -> p (a k)"), in_=tb[:, 0:256], func=AF.Copy),
                 reads=[tbr], writes=[("KTb", bf_)])

            def mm(e, b=b, bf_=bf_):
                ins = None
                for j in range(2):
                    for half in range(2):
                        kt = KTb[:, bf_, 0 if j == half else 1, :]
                        q = (qsA if half == 0 else qsB)[:, 2 * j:2 * j + 2, b]
                        o = b * 8 + (j * 2 + half) * 2
                        ins = e.matmul(sb_[:, o:o + 2], lhsT=kt, rhs=q, start=True, stop=True)
                return ins
            P.op("pe", mm, reads=[("KTb", bf_), "qsA", "qsB"], writes=[sbr])
        P.op("act", lambda e: e.activation(out=pes[:].rearrange("p b h -> p (b h)"), in_=sb_[:, 0:128], func=AF.Exp), reads=[sbr], writes=["pes"])
        P.op("dve", lambda e: e.tensor_tensor(out=Pts[:], in0=pes[:], in1=fap(E[:, 0, 0, 1, 0, 127:128], [[0, 16], [512, 4], [128, 2]]), op=ALU.mult),
             reads=["pes", "E"], writes=["Pts"])
        ob, obr = P.bank()
        db, dbr = P.bank()

        def mmv(e):
            ins = None
            for b in range(16):
                for j in range(2):
                    e.matmul(ob[:, b * 8 + j * 4:b * 8 + j * 4 + 4], lhsT=Vwd[:, b, j, :], rhs=Pts[:, b, j * 4:(j + 1) * 4], start=True, stop=True)
                ins = e.matmul(db[:, b * 8:(b + 1) * 8], lhsT=ones_bf[:], rhs=Pts[:, b, :], start=True, stop=True)
            return ins
        P.op("pe", mmv, reads=["Pts", "Vwd", "ones"], writes=[obr, dbr])
        P.op("dve", lambda e: e.tensor_tensor(out=rec_f[:, 0:128].rearrange("p (b h) -> p b h", b=16), in0=db[:, 0:128].rearrange("p (b h) -> p b h", b=16),
                                              in1=fap(sinkexp[:, 0, 0, 0, 0:1], [[0, 16], [128, 8]]), op=ALU.add), reads=[dbr, "sinkexp"], writes=["rec"])
        P.op("act", lambda e: e.activation(out=rec_f[:, 0:128], in_=rec_f[:, 0:128], func=AF.Ln), reads=["rec"], writes=["rec"])
        P.op("act", lambda e: e.activation(out=rec_f[:, 0:128], in_=rec_f[:, 0:128], func=AF.Exp, scale=-1.0), reads=["rec"], writes=["rec"])
        for j in range(2):
            for half in range(2):
                r0 = half * 64
                o = (j * 2 + half) * 2
                P.op("dve", lambda e, r0=r0, o=o, j=j: e.tensor_tensor(
                    out=mixT[r0:r0 + 64, 2 * j:2 * j + 2, 0:16],
                    in0=fap(ob[r0:r0 + 64, o:o + 1], [[1, 2], [8, 16]]),
                    in1=fap(rec_f[r0:r0 + 64, o:o + 1], [[1, 2], [8, 16]]), op=ALU.mult), reads=[obr, "rec"], writes=[("mixa", 0)])

    def sample_gla():
        ob, obr = P.bank(hold=True)
        for b in range(16):
            bf_ = b % 2
            P.dma("sp", Sb[:, bf_], sg[b].rearrange("(c u) d v -> (u d) c v", u=2), writes=[("Sb", bf_)])
            vb, vbr = P.bank()
            P.op("pe", lambda e, vb=vb, b=b: e.matmul(vb[:, 0:512], lhsT=sel_bf[:, b, :], rhs=vg_tok[:, 0, :], start=True, stop=True),
                 reads=["sel", ("vg", 0)], writes=[vbr])
            for c in range(2):
                for half in range(2):
                    r0 = half * 64
                    h = 2 * c + half
                    P.op("dve", lambda e, vb=vb, b=b, c=c, r0=r0, h=h, bf_=bf_: e.scalar_tensor_tensor(
                        out=Wt[r0:r0 + 64, bf_, c, :], in0=vb[r0:r0 + 64, h * 128:(h + 1) * 128], scalar=kgf[r0:r0 + 64, c, b:b + 1],
                        in1=Sb[r0:r0 + 64, bf_, c, :], op0=ALU.mult, op1=ALU.add), reads=[vbr, "kgf", ("Sb", bf_)], writes=[("Wt", bf_)])
            P.op("act", lambda e, bf_=bf_: e.activation(out=WA[0:64, bf_], in_=Wt[0:64, bf_], func=AF.Copy), reads=[("Wt", bf_), "zinit"], writes=[("WA", bf_)])
            P.op("act", lambda e, bf_=bf_: e.activation(out=WB[64:128, bf_], in_=Wt[64:128, bf_], func=AF.Copy), reads=[("Wt", bf_), "zinit"], writes=[("WB", bf_)])
            P.op("dve", lambda e, b=b, bf_=bf_: e.tensor_tensor(out=Sn[:, bf_], in0=Wt[:, bf_], in1=fap(elb[:, 0, b:b + 1], [[128, 2], [0, 128]]), op=ALU.mult),
                 reads=[("Wt", bf_), "elb"], writes=[("Sn", bf_)])
            outs.append(P.dma("sp", gss[b].rearrange("(c u) d v -> (u d) c v", u=2), Sn[:, bf_], reads=[("Sn", bf_)]))

            def mm(e, b=b, bf_=bf_):
                ins = None
                for h in range(4):
                    c, half = h // 2, h % 2
                    w = (WA if half == 0 else WB)[:, bf_, c, :]
                    ins = e.matmul(ob[:, h * 16 + b:h * 16 + b + 1], lhsT=w, rhs=qgT[:, c, b:b + 1], start=True, stop=True)
                return ins
            P.op("pe", mm, reads=[("WA", bf_), ("WB", bf_), "qgT"], writes=[obr])
        P.release(obr)
        gla_out_norm(ob, obr, 64, 16, 0)

    import os as _os
    _kb = _os.environ.get("KBAR", "")
    for t in range((0 if debug == "sample" else (2 if debug == "two" else (4 if debug == "four" else 1))) if debug else 4):
        main_tile("prompt", t)
        if "t" in _kb:
            P.barrier()
    if debug and debug != "sample":
        outs.append(P.dma("pool", dbg_a, aT[:], reads=[("aT", m) for m in range(NKF)]))
    outs.append(P.dma("sp", gsp.rearrange("(c u) d v -> (u d) c v", u=2), S[:], reads=["S"]))
    P.barrier()
    if (not debug) or debug == "sample":
        main_tile("sample", 0)

    P.emit()
    P.stack.close()
    return nc


_CACHE = {}


def kernel(x_prompt, x_sample, cache_k, cache_v, state_gla, attn_norm_g, w_in, q_norm_g, k_norm_g, attn_sinks,
           rel_bias, w_gla_gate2, b_gla_gate, gla_norm_g, w_o, ffn_norm_g, w_gate, w_up, w_down):
    f = lambda a: np.ascontiguousarray(np.asarray(a, dtype=np.float32))
    xpr = f(x_prompt)[0]
    xsm = f(x_sample)[:, 0, :]
    ckk = f(cache_k)[0].reshape(128, 128, 128)
    cvv = f(cache_v)[0].reshape(128, 128, 128)
    sgg = f(state_gla)[0]
    consts = host_consts()
    shared = dict(w_in=f(w_in)[0], w_o=f(w_o)[0], w_gate=f(w_gate)[0], w_up=f(w_up)[0], w_down=f(w_down)[0],
                  attn_g=f(attn_norm_g)[0], ffn_g=f(ffn_norm_g)[0], qng=f(q_norm_g)[0], kng=f(k_norm_g)[0],
                  sinks=f(attn_sinks)[0], relb=f(rel_bias), w2=f(w_gla_gate2)[0], bgate=f(b_gla_gate)[0], glag=f(gla_norm_g)[0])
    for k, v in consts.items():
        shared["c_" + k] = v
    in_maps = []
    for c in range(NCORE):
        m = dict(shared)
        m["xp"] = xpr[c * TOK:(c + 1) * TOK]
        m["xhalo"] = xpr[c * TOK - 128:c * TOK] if c > 0 else np.zeros((128, D), np.float32)
        pre = np.zeros((NPRE, D), np.float32)
        if c > 0:
            pre[NPRE - c * TOK:] = xpr[:c * TOK]
        m["xpre"] = pre
        xs_ = np.zeros((128, D), np.float32)
        xs_[:16] = xsm[c * 16:(c + 1) * 16]
        m["xs"] = xs_
        m["ck"] = ckk[c * 16:(c + 1) * 16]
        m["cv"] = cvv[c * 16:(c + 1) * 16]
        m["sg"] = sgg[c * 16:(c + 1) * 16]
        m["flag"] = np.full((128, 1), 1.0 if c > 0 else 0.0, np.float32)
        in_maps.append(m)
    if "nc" not in _CACHE:
        _CACHE["nc"] = build_program()
    res = run_bass_kernel_spmd(_CACHE["nc"], in_maps, core_ids=list(range(NCORE)))
    R = res.results
    y_prompt = np.concatenate([R[c]["y_p"] for c in range(NCORE)], axis=0)[None]
    y_sample = np.concatenate([R[c]["y_s"][:16] for c in range(NCORE)], axis=0)[:, None, :]
    kwp = R[7]["kwp"].reshape(1, 1, 128, 2, 64)
    vwp = R[7]["vwp"].reshape(1, 1, 128, 2, 64)
    gsp = R[7]["gsp"].reshape(1, 1, 4, 64, 128)
    kws = np.concatenate([R[c]["kws"] for c in range(NCORE)], axis=0).reshape(1, 128, 128, 2, 64)
    vws = np.concatenate([R[c]["vws"] for c in range(NCORE)], axis=0).reshape(1, 128, 128, 2, 64)
    gss = np.concatenate([R[c]["gss"] for c in range(NCORE)], axis=0).reshape(1, 128, 4, 64, 128)
    return (y_prompt.astype(np.float32), y_sample.astype(np.float32), kwp, vwp, gsp, kws, vws, gss)
```

```python
import contextlib
import math
import numpy as np
import concourse.bass as bass
import concourse.mybir as mybir
from concourse.bass_utils import run_bass_kernel_spmd

F32 = mybir.dt.float32
BF16 = mybir.dt.bfloat16
AF = mybir.ActivationFunctionType
ALU = mybir.AluOpType

NCORE = 8
D = 1024
TOK = 2048
NPRE = 7 * 2048
DFF = 2816
NKF = DFF // 128
INW = 2320
ENGS = ("pe", "act", "dve", "pool", "sp")
NDMASEM = 12
NSLOT = 4
EPS = 1e-6
MASKV = -30000.0


class _Ins:
    def then_inc(self, *a, **k):
        return self


class _Mock:
    def __init__(self):
        self.cost = 0.0

    def _free(self, ap):
        n = 1
        for d in list(ap.shape)[1:]:
            n *= int(d)
        return n

    def matmul(self, out, lhsT=None, rhs=None, **k):
        n = max(self._free(rhs), 64)
        self.cost += (n * (4 if rhs.dtype == F32 else 1)) / 2400.0 + 0.01
        return _Ins()

    def transpose(self, out=None, in_=None, identity=None, **k):
        self.cost += (128 * (4 if in_.dtype == F32 else 1)) / 2400.0 + 0.01
        return _Ins()

    def __getattr__(self, name):
        def f(*a, **k):
            o = k.get("out", a[0] if a else None)
            n = self._free(o) if o is not None else 64
            self.cost += 0.2 + n / 1000.0
            return _Ins()
        return f


class Prog:
    def __init__(self, nc):
        self.nc = nc
        self.stack = contextlib.ExitStack()
        self.oplist = []
        self.last_w = {}
        self.readers = {}
        self.base = None
        self.sems = {}
        self.nbuf = 0
        self.banks = []
        self.bank_rr = 0
        self.held = set()
        self.finals = []

    def sb(self, shape, dt, name=None):
        self.nbuf += 1
        return self.stack.enter_context(self.nc.sbuf_tensor(name or f"sb{self.nbuf}", list(shape), dt))

    def ps(self, shape, dt, name=None):
        self.nbuf += 1
        return self.stack.enter_context(self.nc.psum_tensor(name or f"ps{self.nbuf}", list(shape), dt))

    def bank(self, hold=False):
        while True:
            i = self.bank_rr % len(self.banks)
            self.bank_rr += 1
            if i not in self.held:
                break
        if hold:
            self.held.add(i)
        return self.banks[i], ("ps", i)

    def release(self, res):
        self.held.discard(res[1])

    def _sem(self, key):
        if key not in self.sems:
            nm = "s_" + "_".join(str(k) for k in (key if isinstance(key, tuple) else (key,)))
            self.sems[key] = self.stack.enter_context(self.nc.semaphore(nm))
        return self.sems[key]

    def _add(self, eng, fn, reads, writes, kind, cost, dma=None):
        deps = set()
        for r in reads:
            t = self.last_w.get(r, self.base)
            if t is not None:
                deps.add(t)
        for w in writes:
            t = self.last_w.get(w, self.base)
            if t is not None:
                deps.add(t)
            deps.update(self.readers.get(w, ()))
        if not reads and not writes and self.base is not None:
            deps.add(self.base)
        i = len(self.oplist)
        deps.discard(i)
        self.oplist.append(dict(eng=eng, fn=fn, deps=deps, kind=kind, cost=cost, dma=dma))
        for r in reads:
            self.readers.setdefault(r, []).append(i)
        for w in writes:
            self.last_w[w] = i
            self.readers[w] = []
        return i

    def op(self, eng, fn, reads=(), writes=()):
        m = _Mock()
        fn(m)
        return self._add(eng, fn, reads, writes, "c", m.cost)

    def dma(self, eng, out, in_, reads=(), writes=()):
        n = 1
        for d in out.shape:
            n *= int(d)
        nbytes = n * (4 if out.dtype == F32 else 2)
        return self._add(eng, None, reads, writes, "d", 2.0 + nbytes / 150e3, dma=(out, in_))

    def barrier(self, keep=()):
        kept = {k: v for k, v in self.last_w.items() if (k in keep or (isinstance(k, tuple) and k and k[0] in keep))}
        skip = set(getattr(self, "nofence", ()))
        allprev = set(range(len(self.oplist))) - skip
        i = len(self.oplist)
        self.oplist.append(dict(eng="sp", fn=None, deps=allprev, kind="n", cost=0.05, dma=None))
        self.base = i
        self.bars = getattr(self, "bars", []) + [i]
        self.last_w = dict(kept)
        self.readers = {}

    def final_wait(self, eng, toks):
        pass

    def schedule(self):
        import heapq, os
        ops = self.oplist
        n = len(ops)
        succ = [[] for _ in range(n)]
        ndep = [0] * n
        for i, o in enumerate(ops):
            ndep[i] = len(o["deps"])
            for d in o["deps"]:
                succ[d].append(i)
        done = [0.0] * n
        ready_t = [0.0] * n
        efree = {e: 0.0 for e in ENGS}
        waiting = {e: [] for e in ENGS}
        avail = {e: [] for e in ENGS}
        order = {e: [] for e in ENGS}
        for i, o in enumerate(ops):
            if ndep[i] == 0:
                heapq.heappush(waiting[o["eng"]], (0.0, i))
        nsched = 0
        while nsched < n:
            best = None
            for e in ENGS:
                w, a = waiting[e], avail[e]
                while w and w[0][0] <= efree[e]:
                    heapq.heappush(a, heapq.heappop(w)[1])
                if a:
                    cand = (efree[e], a[0], e, True)
                elif w:
                    cand = (w[0][0], w[0][1], e, False)
                else:
                    continue
                if best is None or cand[:2] < best[:2]:
                    best = cand
            st, i, e, from_avail = best
            if from_avail:
                heapq.heappop(avail[e])
            else:
                heapq.heappop(waiting[e])
            o = ops[i]
            if o["kind"] == "d":
                efree[e] = st + 0.15
                done[i] = st + o["cost"]
            else:
                efree[e] = st + o["cost"]
                done[i] = st + o["cost"] + 0.15
            order[e].append(i)
            nsched += 1
            for sidx in succ[i]:
                ndep[sidx] -= 1
                if done[i] > ready_t[sidx]:
                    ready_t[sidx] = done[i]
                if ndep[sidx] == 0:
                    heapq.heappush(waiting[ops[sidx]["eng"]], (ready_t[sidx], sidx))
        self.sim_time = max(done) if n else 0.0
        self.sim_done = done
        if os.environ.get("KSIM"):
            print("SIM total", round(self.sim_time), "barriers", [round(done[b]) for b in getattr(self, "bars", [])])
        return order

    def emit(self):
        nc = self.nc
        import os
        order = self.schedule()
        km = os.environ.get("KSCHED", "mid")
        if km == "0":
            order = {e: [i for i, o in enumerate(self.oplist) if o["eng"] == e] for e in ENGS}
        elif km == "mid" and len(getattr(self, "bars", [])) >= 2:
            B2 = self.bars[-1]
            prog = {e: [i for i, o in enumerate(self.oplist) if o["eng"] == e] for e in ENGS}
            order = {e: [i for i in order[e] if i <= B2] + [i for i in prog[e] if i > B2] for e in ENGS}
        elif km in ("pre", "post") and self.base is not None:
            B = self.base
            prog = {e: [i for i, o in enumerate(self.oplist) if o["eng"] == e] for e in ENGS}
            if km == "post":
                order = {e: [i for i in prog[e] if i <= B] + [i for i in order[e] if i > B] for e in ENGS}
            else:
                order = {e: [i for i in order[e] if i <= B] + [i for i in prog[e] if i > B] for e in ENGS}
        ops = self.oplist
        tok = [None] * len(ops)
        ccnt = {e: 0 for e in ENGS}
        dcnt = {}
        drr = {e: 0 for e in ENGS}
        plan = {e: [] for e in ENGS}
        prevdma = {}
        for e in ENGS:
            for i in order[e]:
                o = ops[i]
                if o["kind"] == "d":
                    j = drr[e] % NDMASEM
                    drr[e] += 1
                    key = ("d", e, j)
                    c = dcnt.get(key, 0)
                    dcnt[key] = c + 1
                    tok[i] = (key, 16 * (c + 1))
                    prevdma[i] = (key, 16 * c) if c > 0 else None
                else:
                    ccnt[e] += 1
                    tok[i] = (e, ccnt[e])
        for e in ENGS:
            waited = {}
            for i in order[e]:
                o = ops[i]
                need = {}
                for d in o["deps"]:
                    k, v = tok[d]
                    if waited.get(k, 0) >= v:
                        continue
                    if need.get(k, 0) < v:
                        need[k] = v
                if o["kind"] == "d" and prevdma.get(i):
                    k, v = prevdma[i]
                    if waited.get(k, 0) < v and need.get(k, 0) < v:
                        need[k] = v
                for k, v in need.items():
                    waited[k] = v
                plan[e].append((list(need.items()), i))
            if e == "sp":
                fin = {}
                for i2, t in enumerate(tok):
                    if t is not None and fin.get(t[0], 0) < t[1]:
                        fin[t[0]] = t[1]
                plan[e].append(([(k, v) for k, v in fin.items() if waited.get(k, 0) < v], None))
        for e in ENGS:
            for (waits, i) in plan[e]:
                for (k, v) in waits:
                    self._sem(k)
                if i is not None:
                    self._sem(tok[i][0])
        if os.environ.get("KCHECK"):
            import collections
            sv = collections.defaultdict(int)
            ptr = {e: 0 for e in ENGS}
            while True:
                prog_ = False
                for e in ENGS:
                    while ptr[e] < len(plan[e]):
                        waits, i = plan[e][ptr[e]]
                        if all(sv[k] >= v for k, v in waits):
                            if i is not None:
                                k, v = tok[i]
                                inc = 16 if ops[i]["kind"] == "d" else 1
                                sv[k] += inc
                                assert sv[k] == v, ("token mismatch", e, i, k, v, sv[k])
                            ptr[e] += 1
                            prog_ = True
                        else:
                            break
                if all(ptr[e] == len(plan[e]) for e in ENGS):
                    print("KCHECK: ok, no deadlock")
                    break
                if not prog_:
                    for e in ENGS:
                        if ptr[e] < len(plan[e]):
                            waits, i = plan[e][ptr[e]]
                            print("KCHECK STUCK", e, ptr[e], i, [(k, v, sv[k]) for k, v in waits if sv[k] < v])
                    break
        block = self.stack.enter_context(nc.Block())

        def run(engname):
            def body(e):
                for (waits, i) in plan[engname]:
                    for (k, v) in waits:
                        e.wait_ge(self.sems[k], v)
                    if i is None:
                        continue
                    o = ops[i]
                    if o["kind"] == "d":
                        ins = e.dma_start(out=o["dma"][0], in_=o["dma"][1], allow_slow_non_contiguous=True)
                        ins.then_inc(self.sems[tok[i][0]], 16)
                    elif o["kind"] == "n":
                        ins = e.nop()
                        ins.then_inc(self.sems[tok[i][0]], 1)
                    else:
                        ins = o["fn"](e)
                        ins.then_inc(self.sems[tok[i][0]], 1)
            return body
        block.tensor(run("pe"))
        block.scalar(run("act"))
        block.vector(run("dve"))
        block.gpsimd(run("pool"))
        block.sync(run("sp"))


def fap(ap, dims):
    return bass.AP(tensor=ap.tensor, offset=ap.offset, ap=[list(ap.ap[0])] + [list(d) for d in dims])


def t5_bucket_np(n):
    n = np.maximum(n, 0)
    nf = np.maximum(n, 1).astype(np.float32)
    large = 16 + (np.log(nf / 16) / math.log(128 / 16) * 16).astype(np.int32)
    large = np.minimum(large, 31)
    return np.where(n < 16, n, large)


def host_consts():
    c = {}
    c["ident"] = np.eye(128, dtype=np.float32)
    s = np.arange(128)[:, None]
    t = np.arange(128)[None, :]
    c["tri"] = np.where(s <= t, -1.0 / 16, 0.0).astype(np.float32)
    c["tris"] = (np.eye(128) * (-1.0 / 16)).astype(np.float32)
    c["cmask"] = (s <= t).astype(np.float32)
    c["jx"] = np.eye(128, dtype=np.float32)[::-1].copy()
    bd = np.zeros((128, 128), np.float32)
    bd[:64, :64] = 1
    bd[64:, 64:] = 1
    c["bd"] = bd
    sw = np.zeros((128, 128), np.float32)
    for m in range(128):
        sw[(m + 64) % 128, m] = 1
    c["sw"] = sw
    oh = np.zeros((128, 2, 256), np.float32)
    for kb in range(2):
        off = 128 if kb == 0 else 0
        for i in range(255):
            dlt = 127 + off - i
            if 0 <= dlt <= 127:
                oh[int(t5_bucket_np(np.array(dlt))), kb, i] = 1.0
            else:
                oh[32, kb, i] = MASKV
        oh[32, kb, 255] = MASKV
    c["oh"] = oh
    sel = np.zeros((128, 16, 128), np.float32)
    for b in range(16):
        sel[b, b, :] = 1
    c["sel"] = sel
    return c


def build_program(debug=False):
    nc = bass.Bass("TRN2", target_bir_lowering=False)
    P = Prog(nc)

    def din(name, shape):
        return nc.dram_tensor(name, list(shape), F32, kind="ExternalInput").ap()

    def dout(name, shape):
        return nc.dram_tensor(name, list(shape), F32, kind="ExternalOutput").ap()

    xp = din("xp", [TOK, D]); xhalo = din("xhalo", [128, D]); xpre = din("xpre", [NPRE, D]); xs = din("xs", [128, D])
    ck = din("ck", [16, 128, 128]); cv = din("cv", [16, 128, 128]); sg = din("sg", [16, 4, 64, 128])
    w_in = din("w_in", [D, INW]); w_o = din("w_o", [D, D]); w_gate = din("w_gate", [D, DFF]); w_up = din("w_up", [D, DFF])
    w_down = din("w_down", [DFF, D])
    attn_g = din("attn_g", [D]); ffn_g = din("ffn_g", [D]); qg_in = din("qng", [64]); kg_in = din("kng", [64])
    sinks = din("sinks", [8]); relb = din("relb", [32, 8]); w2 = din("w2", [16, 256]); bgate = din("bgate", [256])
    glag = din("glag", [128]); flag = din("flag", [128, 1])
    cn = {k: din("c_" + k, v.shape) for k, v in host_consts().items()}

    y_p = dout("y_p", [TOK, D]); y_s = dout("y_s", [128, D])
    kwp = dout("kwp", [128, 128]); vwp = dout("vwp", [128, 128]); gsp = dout("gsp", [4, 64, 128])
    kws = dout("kws", [16, 128, 128]); vws = dout("vws", [16, 128, 128]); gss = dout("gss", [16, 4, 64, 128])
    fscr = nc.dram_tensor("fscr", [2, 8, 256], F32).ap()
    wb = {"w_in": nc.dram_tensor("wb_in", [D, INW], BF16).ap(), "w_o": nc.dram_tensor("wb_o", [D, D], BF16).ap(),
          "w_gate": nc.dram_tensor("wb_gate", [D, DFF], BF16).ap(), "w_up": nc.dram_tensor("wb_up", [D, DFF], BF16).ap(),
          "w_down": nc.dram_tensor("wb_down", [DFF, D], BF16).ap()}
    wf = {"w_in": w_in, "w_o": w_o, "w_gate": w_gate, "w_up": w_up, "w_down": w_down}
    if debug:
        dbg_mix = dout("dbg_mix", [128, 8, 512]); dbg_h = dout("dbg_h", [128, 4, D]); dbg_z = dout("dbg_z", [128, 8, 512]); dbg_a = dout("dbg_a", [128, NKF, 512])
        dbg_q = dout("dbg_q", [128, 4, 512]); dbg_rs = dout("dbg_rs", [128, 4, 512])

    for i in range(6):
        P.banks.append(P.ps([128, 512], F32, f"bank{i}"))
    psT = [P.ps([128, 1024], BF16, f"pst{i}") for i in range(2)]
    pst_rr = [0]

    def tbank():
        i = pst_rr[0] % 2
        pst_rr[0] += 1
        return psT[i], ("pst", i)

    ident_f = P.sb([128, 128], F32); ident_bf = P.sb([128, 128], BF16)
    tri_f = P.sb([128, 128], F32); tris_f = P.sb([128, 128], F32); cmask_bf = P.sb([128, 128], BF16)
    jx_f = P.sb([128, 128], F32); bd_bf = P.sb([128, 128], BF16); sw_bf = P.sb([128, 128], BF16)
    ones_bf = P.sb([128, 128], BF16); zeros_f = P.sb([128, 128], F32); scr_f = P.sb([128, 2048], F32)
    sel_bf = P.sb([128, 16, 128], BF16)
    gaT = P.sb([128, 8], F32); gfT = P.sb([128, 8], F32)
    gq_col = P.sb([128, 1], F32); gk_col = P.sb([128, 1], F32); glag_col = P.sb([128, 1], F32)
    eps_col = P.sb([128, 1], F32); ln8_col = P.sb([128, 1], F32); flag_col = P.sb([128, 1], F32)
    bgate_bc = P.sb([128, 256], F32); w2pad = P.sb([128, 256], BF16)
    relb_pad = P.sb([128, 128], F32)
    hank = scr_f[:, 0:1024].rearrange("p (h s) -> p h s", h=8)
    oh_sb = scr_f[:, 1024:1536].rearrange("p (a b) -> p a b", a=2)
    ftab = scr_f[0:8, 1536:2048].rearrange("p (a b) -> p a b", a=2)
    E = P.sb([128, 2, 2, 2, 2, 128], F32)
    sink_bc = P.sb([128, 8], F32); sinkexp = P.sb([128, 2, 2, 2, 128], F32)
    ring = P.sb([128, NSLOT, 8, 512], BF16)
    xb = P.sb([128, 4, D], F32)
    actT = P.sb([128, 8, 512], BF16)
    nbf = P.sb([128, D], BF16); nbf_b = P.sb([128, D], BF16)
    zt_b = P.sb([128, 256], F32); sp_b = P.sb([128, 256], F32); ktok_b = P.sb([128, 2, 2, 128], BF16)
    ss_c2 = P.sb([128, 2], F32); rs_c2 = P.sb([128, 2], F32)
    ss_c = P.sb([128, 1], F32); rs_c = P.sb([128, 1], F32)
    qhT = P.sb([128, 4, 512], BF16)
    kX = P.sb([128, 4, 640], BF16)
    khT_bf = P.sb([128, 512], BF16); khT_f = P.sb([128, 128], F32)
    Vdup = P.sb([128, 5, 2, 128], BF16)
    qgT = P.sb([128, 2, 512], BF16); kgA = P.sb([128, 2, 512], BF16); kgB = P.sb([128, 2, 512], BF16)
    kgf = P.sb([128, 2, 128], F32)
    ktok = P.sb([128, 2, 2, 128], BF16)
    vg_tok = P.sb([128, 4, 512], BF16)
    rsT = P.sb([128, 4, 512], BF16)
    ulrT = P.sb([128, 512], BF16)
    zt = P.sb([128, 256], F32); sp_t = P.sb([128, 256], F32)
    ebq = P.sb([128, 2, 512], F32); enb = P.sb([128, 2, 512], F32); elast = P.sb([128, 2, 4], F32)
    elb = P.sb([128, 2, 128], F32)
    S = P.sb([128, 2, 128], F32); Stmp = P.sb([128, 2, 128], F32); SA = P.sb([128, 2, 128], BF16); SB = P.sb([128, 2, 128], BF16)
    ATbf = P.sb([128, 4, 128], BF16)
    sq_bf = P.sb([128, 512], BF16); rstd_f = P.sb([128, 512], F32); tmp_f = P.sb([128, 512], F32)
    pe_f = P.sb([128, 512], F32); rec_f = P.sb([128, 512], F32)
    PT = P.sb([128, 2, 2, 2, 2, 128], BF16)
    mixT = P.sb([128, 8, 512], BF16)
    aT = P.sb([128, NKF, 512], BF16)
    sg_f = P.sb([128, 512], F32)
    vw_f = P.sb([128, 128], F32); kw_f = P.sb([128, 128], F32)
    Kw_bf = P.sb([128, 16, 128], BF16); Kwsw_bf = P.sb([128, 16, 128], BF16)
    KTb = P.sb([128, 2, 2, 128], BF16)
    Vwd = P.sb([128, 16, 2, 128], BF16)
    qsA = P.sb([128, 4, 16], BF16); qsB = P.sb([128, 4, 16], BF16)
    Pts = P.sb([128, 16, 8], BF16); pes = P.sb([128, 16, 8], F32)
    Sb = scr_f[:, 0:512].rearrange("p (a c v) -> p a c v", a=2, c=2)
    Wt = scr_f[:, 512:1024].rearrange("p (a c v) -> p a c v", a=2, c=2)
    Sn = scr_f[:, 1024:1536].rearrange("p (a c v) -> p a c v", a=2, c=2)
    WA = P.sb([128, 2, 2, 128], BF16); WB = P.sb([128, 2, 2, 128], BF16)

    CUR = {"p": 0}
    _vwd32 = Vwd[:].rearrange("p a b c -> p (a b c)").bitcast(F32)
    _kwf = Kw_bf[:].rearrange("p a b -> p (a b)")
    _kwswf = Kwsw_bf[:].rearrange("p a b -> p (a b)")
    nbfs2 = (nbf, nbf_b)

    def X(blk):
        if CUR["p"] == 0:
            return xb[:, blk, :]
        src = _vwd32 if blk < 2 else scr_f[:]
        o = (blk % 2) * 1024
        return src[:, o:o + 1024]

    def A(k):
        if CUR["p"] == 0:
            return actT[:, k, :]
        src = _kwf if k < 4 else _kwswf
        o = (k % 4) * 512
        return src[:, o:o + 512]

    def xr(blk):
        return ("x", CUR["p"], blk)

    def ar(blk):
        return ("actT", CUR["p"], blk)

    def ld(dst, src, name, eng="sp"):
        P.dma(eng, dst, src, writes=[name])

    ld(ident_f[:], cn["ident"], "ident_f"); ld(tri_f[:], cn["tri"], "tri"); ld(tris_f[:], cn["tris"], "tris")
    ld(jx_f[:], cn["jx"], "jx"); ld(oh_sb, cn["oh"], "oh")
    P.dma("pool", ident_bf[:], cn["ident"], writes=["ident_bf"])
    P.dma("pool", cmask_bf[:], cn["cmask"], writes=["cmask"])
    P.dma("pool", bd_bf[:], cn["bd"], writes=["bd"])
    P.dma("pool", sw_bf[:], cn["sw"], writes=["sw"])
    P.dma("pool", sel_bf[:], cn["sel"], writes=["sel"])
    ld(gaT[:], attn_g.rearrange("(k p) -> p k", p=128), "gaT"); ld(gfT[:], ffn_g.rearrange("(k p) -> p k", p=128), "gfT")
    for h in range(2):
        ld(gq_col[h * 64:(h + 1) * 64, :], qg_in.rearrange("(p o) -> p o", o=1), "gq")
        ld(gk_col[h * 64:(h + 1) * 64, :], kg_in.rearrange("(p o) -> p o", o=1), "gk")
    ld(glag_col[:], glag.rearrange("(p o) -> p o", o=1), "glag"); ld(flag_col[:], flag, "flag")
    ld(bgate_bc[:], bass.AP(tensor=bgate.tensor, offset=0, ap=[[0, 128], [1, 256]]), "bgate")
    ld(sink_bc[:], bass.AP(tensor=sinks.tensor, offset=0, ap=[[0, 128], [1, 8]]), "sink_bc")
    P.op("dve", lambda e: e.memset(ones_bf[:], 1.0), writes=["ones"])
    P.op("dve", lambda e: e.memset(zeros_f[:], 0.0), writes=["zeros"])
    P.op("dve", lambda e: e.memset(eps_col[:], EPS), writes=["eps"])
    P.op("dve", lambda e: e.memset(ln8_col[:], math.log(0.125)), writes=["ln8"])
    P.op("dve", lambda e: e.memset(w2pad[:], 0.0), writes=["w2pad"])
    P.dma("pool", w2pad[112:128, :], w2, reads=[], writes=["w2pad"])
    P.op("dve", lambda e: e.tensor_scalar(out=gq_col[:], in0=gq_col[:], scalar1=0.125, scalar2=None, op0=ALU.mult),
         reads=["gq"], writes=["gq"])
    for t_ in (kgA, kgB, SA, SB, qsA, qsB, WA, WB):
        P.op("pool", lambda e, t_=t_: e.memset(t_[:], 0.0), writes=["zinit"])
    P.op("pool", lambda e: e.memset(kX[:], 0.0), writes=["kX"])
    P.op("pool", lambda e: e.memset(S[:], 0.0), writes=["S"])
    P.op("pool", lambda e: e.memset(relb_pad[:], 0.0), writes=["relb_pad"])
    P.op("pool", lambda e: e.memset(relb_pad[32:33, :], 1.0), reads=[], writes=["relb_pad"])
    P.dma("sp", relb_pad[0:32, 0:8], relb, writes=["relb_pad"])

    bk, bkr = P.bank()
    P.op("pe", lambda e: e.matmul(bk[:, 0:512], lhsT=relb_pad[:], rhs=scr_f[:, 1024:1536], start=True, stop=True),
         reads=["relb_pad", "oh"], writes=[bkr])
    P.op("act", lambda e: e.activation(out=scr_f[0:8, 1536:2048], in_=bk[0:8, 0:512], func=AF.Copy), reads=[bkr], writes=["ftab"])
    P.dma("sp", fscr.rearrange("k h i -> h k i"), ftab, reads=["ftab"], writes=["fscr"])
    for kb in range(2):
        src = bass.AP(tensor=fscr.tensor, offset=kb * 8 * 256, ap=[[1, 128], [256, 8], [1, 128]])
        P.dma("sp", hank, src, reads=["fscr"], writes=["hank"])
        for hh in range(0, 8, 4):
            bk, bkr = P.bank()

            def mmj(e, bk=bk, hh=hh):
                ins = None
                for q in range(4):
                    ins = e.matmul(bk[:, q * 128:(q + 1) * 128], lhsT=hank[:, hh + q, :], rhs=jx_f[:], start=True, stop=True)
                return ins
            P.op("pe", mmj, reads=["hank", "jx"], writes=[bkr])
            for q in range(4):
                h = hh + q
                c_, half = h // 2, h % 2
                j, cl = c_ // 2, c_ % 2
                P.op("act", lambda e, bk=bk, q=q, j=j, half=half, kb=kb, cl=cl:
                     e.activation(out=E[:, j, half, kb, cl, :], in_=bk[:, q * 128:(q + 1) * 128], func=AF.Exp),
                     reads=[bkr], writes=["E"])
    P.op("act", lambda e: e.activation(out=sink_bc[:], in_=sink_bc[:], func=AF.Exp), reads=["sink_bc"], writes=["sink_bc"])
    for h in range(8):
        c_, half = h // 2, h % 2
        j, cl = c_ // 2, c_ % 2
        P.op("dve", lambda e, h=h, j=j, half=half, cl=cl: e.tensor_scalar(out=sinkexp[:, j, half, cl, :], in0=zeros_f[:], scalar1=sink_bc[:, h:h + 1],
                                                                         scalar2=None, op0=ALU.add), reads=["sink_bc", "zeros"], writes=["sinkexp"])

    wstate = {"n": 0}

    def wload(parts, fp32=False):
        s = wstate["n"] % NSLOT
        wstate["n"] += 1
        res = ("w", s)
        for (c0, (wn, r0, nrows, cc0, ncols_), nk, ncols) in parts:
            src = (wf if fp32 else wb)[wn][r0:r0 + nrows, cc0:cc0 + ncols_].rearrange("(k p) n -> p k n", p=128)
            q_ = "pool" if (fp32 or wstate["n"] % 2 == 0) else "sp"
            P.dma(q_, ring[:, s, 0:nk, c0:c0 + ncols], src, reads=([] if fp32 else [("wbf", wn)]), writes=[res])
        return s, res

    def wsrc(w, r0, nrows, c0, ncols):
        return (w, r0, nrows, c0, ncols)

    def front(src_fn, nb, gT):
        for blk in range(nb):
            P.dma("sp", X(blk), src_fn(blk), writes=[xr(blk)])
            norm_block(blk, gT)

    def norm_block(blk, gT):
        p = CUR["p"]
        xblk = X(blk); xres = xr(blk); ares = ar(blk)
        nb_ = nbfs2[p]; ss = ss_c2[:, p:p + 1]; rs = rs_c2[:, p:p + 1]
        P.op("act", lambda e: e.activation(out=nb_[:], in_=xblk, func=AF.Square, accum_out=ss),
             reads=[xres], writes=[("nbf", p), ("ss_c", p)])
        P.op("act", lambda e: e.activation(out=rs, in_=ss, func=AF.Ln, scale=1.0 / D, bias=eps_col[:, 0:1]),
             reads=[("ss_c", p), "eps"], writes=[("rs_c", p)])
        P.op("act", lambda e: e.activation(out=rs, in_=rs, func=AF.Exp, scale=-0.5), reads=[("rs_c", p)], writes=[("rs_c", p)])
        P.op("dve", lambda e: e.tensor_scalar(out=nb_[:], in0=xblk, scalar1=rs, scalar2=None, op0=ALU.mult),
             reads=[xres, ("rs_c", p)], writes=[("nbf", p)])
        tb, tbr = tbank()

        def tr(e):
            ins = None
            for k in range(8):
                ins = e.transpose(out=tb[:, k * 128:(k + 1) * 128], in_=nb_[:, k * 128:(k + 1) * 128], identity=ident_bf[:])
            return ins
        P.op("pe", tr, reads=[("nbf", p), "ident_bf"], writes=[tbr])
        if p == 0:
            P.op("dve", lambda e: e.tensor_tensor(out=actT[:, :, blk * 128:(blk + 1) * 128], in0=tb[:].rearrange("p (k t) -> p k t", k=8),
                                                  in1=fap(gT[:], [[1, 8], [0, 128]]), op=ALU.mult),
                 reads=[tbr, "gaT", "gfT"], writes=[ares])
        else:
            for kh, src in enumerate((_kwf, _kwswf)):
                P.op("dve", lambda e, kh=kh, src=src: e.tensor_tensor(
                    out=src.rearrange("p (k t) -> p k t", k=4)[:, :, blk * 128:(blk + 1) * 128],
                    in0=tb[:, kh * 512:(kh + 1) * 512].rearrange("p (k t) -> p k t", k=4),
                    in1=fap(gT[:, kh * 4:kh * 4 + 1], [[1, 4], [0, 128]]), op=ALU.mult),
                    reads=[tbr, "gaT", "gfT", ares], writes=[ares])

    def actT_reads(nb):
        return [ar(b) for b in range(nb)]

    def proj_fm(slot, res, col0, T, nb):
        bk, bkr = P.bank()
        acts = [A(k) for k in range(8)]

        def mm(e):
            ins = None
            for k in range(8):
                ins = e.matmul(bk[:, 0:T], lhsT=ring[:, slot, k, col0:col0 + 128], rhs=acts[k][:, 0:T], start=(k == 0), stop=(k == 7))
            return ins
        P.op("pe", mm, reads=[res] + actT_reads(nb), writes=[bkr])
        return bk, bkr

    def proj_tm(slot, res, col0, ncols, blk):
        bk, bkr = P.bank()
        acts = [A(k) for k in range(8)]

        def mm(e):
            ins = None
            for k in range(8):
                ins = e.matmul(bk[:, 0:ncols], lhsT=acts[k][:, blk * 128:(blk + 1) * 128], rhs=ring[:, slot, k, col0:col0 + ncols],
                               start=(k == 0), stop=(k == 7))
            return ins
        P.op("pe", mm, reads=[res, ar(blk)], writes=[bkr])
        return bk, bkr

    def rstd_fm(src_ap, T, lhs_ones, scale, reads_src):
        P.op("act", lambda e: e.activation(out=sq_bf[:, 0:T], in_=src_ap, func=AF.Square), reads=reads_src, writes=["sq"])
        b2, b2r = P.bank()
        P.op("pe", lambda e: e.matmul(b2[:, 0:T], lhsT=lhs_ones[:], rhs=sq_bf[:, 0:T], start=True, stop=True),
             reads=["sq", "bd", "ones"], writes=[b2r])
        P.op("act", lambda e: e.activation(out=rstd_f[:, 0:T], in_=b2[:, 0:T], func=AF.Ln, scale=scale, bias=eps_col[:, 0:1]),
             reads=[b2r, "eps"], writes=["rstd"])
        P.op("act", lambda e: e.activation(out=rstd_f[:, 0:T], in_=rstd_f[:, 0:T], func=AF.Exp, scale=-0.5), reads=["rstd"], writes=["rstd"])

    def gla_prep(slot_lr, res_lr, lrcol, nb, T, tri_ap, tri_res, full):
        bk, bkr = proj_fm(slot_lr, res_lr, lrcol, T, nb)
        P.op("act", lambda e: e.activation(out=ulrT[:, 0:T], in_=bk[:, 0:T], func=AF.Copy), reads=[bkr], writes=["ulrT"])
        bT = [P.bank(hold=True) for _ in range(2)]
        for blk in range(nb):
            zb, zbr = P.bank()
            P.op("pe", lambda e, zb=zb, blk=blk: e.matmul(zb[:, 0:256], lhsT=ulrT[:, blk * 128:(blk + 1) * 128], rhs=w2pad[:], start=True, stop=True),
                 reads=["ulrT", "w2pad"], writes=[zbr])
            P.op("dve", lambda e, zb=zb: e.tensor_tensor(out=zt[:], in0=zb[:, 0:256], in1=bgate_bc[:], op=ALU.add),
                 reads=[zbr, "bgate"], writes=["zt"])
            P.op("act", lambda e: e.activation(out=zt[:], in_=zt[:], func=AF.Exp, scale=-1.0), reads=["zt"], writes=["zt"])
            P.op("act", lambda e: e.activation(out=sp_t[:], in_=zt[:], func=AF.Ln, bias=1.0), reads=["zt"], writes=["sp_t"])
            for c in range(2):
                P.op("pe", lambda e, c=c, blk=blk: e.matmul(bT[c][0][:, blk * 128:(blk + 1) * 128], lhsT=sp_t[:, c * 128:(c + 1) * 128], rhs=tri_ap,
                                                           start=True, stop=True), reads=["sp_t", tri_res], writes=[bT[c][1]])
        for c in range(2):
            P.release(bT[c][1])
        for c in range(2):
            P.op("act", lambda e, c=c: e.activation(out=enb[:, c, 0:T], in_=bT[c][0][:, 0:T], func=AF.Exp, scale=-1.0), reads=[bT[c][1]], writes=["enb"])
            P.op("act", lambda e, c=c: e.activation(out=elast[:, c, 0:nb], in_=fap(bT[c][0][:, 127:128], [[128, nb]]), func=AF.Exp),
                 reads=[bT[c][1]], writes=["elast"])
            if full:
                P.op("act", lambda e, c=c: e.activation(out=ebq[:, c, 0:T], in_=bT[c][0][:, 0:T], func=AF.Exp, bias=ln8_col[:, 0:1]),
                     reads=[bT[c][1], "ln8"], writes=["ebq"])
                if T == 128:
                    P.op("act", lambda e, c=c: e.activation(out=elb[:, c, :], in_=bT[c][0][:, 0:128], func=AF.Exp), reads=[bT[c][1]], writes=["elb"])

    def kg_evac(slot, res, col0, nb, T, sample=False):
        for c in range(2):
            bk, bkr = proj_fm(slot, res, col0 + c * 128, T, nb)
            P.op("dve", lambda e, bk=bk, c=c: e.tensor_tensor(out=kgA[0:64, c, 0:T], in0=bk[0:64, 0:T], in1=enb[0:64, c, 0:T], op=ALU.mult),
                 reads=[bkr, "enb", "zinit"], writes=[("kgA", c)])
            P.op("dve", lambda e, bk=bk, c=c: e.tensor_tensor(out=kgB[64:128, c, 0:T], in0=bk[64:128, 0:T], in1=enb[64:128, c, 0:T], op=ALU.mult),
                 reads=[bkr, "enb", "zinit"], writes=[("kgB", c)])
            if sample:
                P.op("dve", lambda e, bk=bk, c=c: e.tensor_tensor(out=kgf[:, c, :], in0=bk[:, 0:128], in1=enb[:, c, 0:128], op=ALU.mult),
                     reads=[bkr, "enb"], writes=["kgf"])

    def vg_tm(slot, res, col0, nb):
        for blk in range(nb):
            bk, bkr = proj_tm(slot, res, col0, 512, blk)
            P.op("act", lambda e, bk=bk, blk=blk: e.activation(out=vg_tok[:, blk, :], in_=bk[:, 0:512], func=AF.Copy), reads=[bkr], writes=[("vg", blk)])

    def state_update(blk, masked):
        tb, tbr = tbank()

        def tr(e):
            ins = None
            for c in range(2):
                for ab, src in enumerate((kgA, kgB)):
                    o = (c * 2 + ab) * 128
                    ins = e.transpose(out=tb[:, o:o + 128], in_=src[:, c, blk * 128:(blk + 1) * 128], identity=ident_bf[:])
            return ins
        P.op("pe", tr, reads=[("kgA", 0), ("kgA", 1), ("kgB", 0), ("kgB", 1), "ident_bf"], writes=[tbr])
        P.op("act", lambda e: e.activation(out=ktok[:].rearrange("p c a f -> p (c a f)"), in_=tb[:, 0:512], func=AF.Copy), reads=[tbr], writes=["ktok"])
        ub, ubr = P.bank()

        def mm(e):
            ins = None
            for c in range(2):
                for ab in range(2):
                    h = 2 * c + ab
                    ins = e.matmul(ub[:, c * 128:(c + 1) * 128], lhsT=ktok[:, c, ab, :], rhs=vg_tok[:, blk, h * 128:(h + 1) * 128],
                                   start=(ab == 0), stop=(ab == 1))
            return ins
        P.op("pe", mm, reads=["ktok", ("vg", blk)], writes=[ubr])
        P.op("dve", lambda e: e.tensor_tensor(out=Stmp[:].rearrange("p c v -> p (c v)"), in0=ub[:, 0:256], in1=S[:].rearrange("p c v -> p (c v)"), op=ALU.add),
             reads=[ubr, "S"], writes=["Stmp"])
        P.op("dve", lambda e: e.tensor_tensor(out=S[:], in0=Stmp[:], in1=fap(elast[:, 0, blk:blk + 1], [[4, 2], [0, 128]]), op=ALU.mult),
             reads=["Stmp", "elast", "SA", "SB"], writes=["S"])
        if masked:
            P.op("act", lambda e: e.activation(out=SA[0:64], in_=S[0:64], func=AF.Copy), reads=["S", "zinit"], writes=["SA"])
            P.op("act", lambda e: e.activation(out=SB[64:128], in_=S[64:128], func=AF.Copy), reads=["S", "zinit"], writes=["SB"])

    def gla_out_norm(ob, obr, ncol, W, col0):
        rstd_fm(ob[:, 0:ncol], ncol, ones_bf, 1.0 / 128, [obr])
        P.op("dve", lambda e: e.scalar_tensor_tensor(out=tmp_f[:, 0:ncol], in0=ob[:, 0:ncol], scalar=glag_col[:, 0:1], in1=rstd_f[:, 0:ncol],
                                                     op0=ALU.mult, op1=ALU.mult), reads=[obr, "rstd", "glag"], writes=["tmp_f"])
        P.op("dve", lambda e: e.tensor_tensor(out=mixT[:, 4:8, col0:col0 + W], in0=tmp_f[:, 0:ncol].rearrange("p (h w) -> p h w", h=4),
                                              in1=rsT[:, :, col0:col0 + W], op=ALU.mult), reads=["tmp_f", "rsT"], writes=[("mixg", col0)])

    def gla_block(blk):
        ab_, abr = P.bank()

        def mm1(e):
            ins = None
            for h in range(4):
                c, half = h // 2, h % 2
                src = kgA if half == 0 else kgB
                ins = e.matmul(ab_[:, h * 128:(h + 1) * 128], lhsT=src[:, c, blk * 128:(blk + 1) * 128], rhs=qgT[:, c, blk * 128:(blk + 1) * 128],
                               start=True, stop=True)
            return ins
        P.op("pe", mm1, reads=[("kgA", 0), ("kgA", 1), ("kgB", 0), ("kgB", 1), "qgT"], writes=[abr])
        P.op("dve", lambda e: e.tensor_tensor(out=ATbf[:], in0=ab_[:, 0:512].rearrange("p (h t) -> p h t", h=4), in1=fap(cmask_bf[:], [[0, 4], [1, 128]]),
                                              op=ALU.mult), reads=[abr, "cmask"], writes=["ATbf"])
        ob, obr = P.bank()

        def mm2(e):
            ins = None
            for h in range(4):
                c, half = h // 2, h % 2
                sm = SA if half == 0 else SB
                e.matmul(ob[:, h * 128:(h + 1) * 128], lhsT=vg_tok[:, blk, h * 128:(h + 1) * 128], rhs=ATbf[:, h, :], start=True, stop=False)
                ins = e.matmul(ob[:, h * 128:(h + 1) * 128], lhsT=sm[:, c, :], rhs=qgT[:, c, blk * 128:(blk + 1) * 128], start=False, stop=True)
            return ins
        P.op("pe", mm2, reads=[("vg", blk), "ATbf", "SA", "SB", "qgT"], writes=[obr])
        gla_out_norm(ob, obr, 512, 128, blk * 128)
        state_update(blk, True)

    def attn_block(blk, Etab, Eres, useflag=False):
        for j in range(2):
            sbk = []
            for half in range(2):
                bk, bkr = P.bank()
                sbk.append((bk, bkr))

                def mm(e, bk=bk, half=half, j=j):
                    ins = None
                    for kb in range(2):
                        kc = (blk + kb) * 128
                        ins = e.matmul(bk[:, kb * 256:(kb + 1) * 256], lhsT=kX[:, 2 * j + half, kc:kc + 128],
                                       rhs=qhT[:, 2 * j:2 * j + 2, blk * 128:(blk + 1) * 128], start=True, stop=True)
                    return ins
                P.op("pe", mm, reads=["kX", "qhT"], writes=[bkr])
            for half in range(2):
                bk, bkr = sbk[half]
                P.op("act", lambda e, bk=bk: e.activation(out=pe_f[:], in_=bk[:, 0:512], func=AF.Exp), reads=[bkr], writes=["pe_f"])
                P.op("dve", lambda e, half=half, j=j: e.tensor_tensor(out=PT[:, j, half].rearrange("p a b q -> p (a b q)"), in0=pe_f[:],
                                                                     in1=Etab[:, j, half].rearrange("p a b q -> p (a b q)"), op=ALU.mult),
                     reads=["pe_f", Eres], writes=[("PT", j)])
                if useflag:
                    P.op("dve", lambda e, half=half, j=j: e.tensor_scalar(out=PT[:, j, half, 0], in0=PT[:, j, half, 0], scalar1=flag_col[:, 0:1],
                                                                          scalar2=None, op0=ALU.mult), reads=[("PT", j), "flag"], writes=[("PT", j)])
            ob, obr = P.bank()
            db, dbr = P.bank()

            def mmv(e, ob=ob, db=db, j=j):
                ins = None
                for kb in range(2):
                    rhs = PT[:, j, :, kb, :, :]
                    e.matmul(ob[:, 0:512], lhsT=Vdup[:, blk + kb, j, :], rhs=rhs, start=(kb == 0), stop=(kb == 1))
                for kb in range(2):
                    rhs = PT[:, j, :, kb, :, :]
                    ins = e.matmul(db[:, 0:512], lhsT=ones_bf[:], rhs=rhs, start=(kb == 0), stop=(kb == 1))
                return ins
            P.op("pe", mmv, reads=[("PT", j), "Vdup", "ones"], writes=[obr, dbr])
            P.op("dve", lambda e, db=db, j=j: e.tensor_tensor(out=rec_f[:], in0=db[:, 0:512], in1=sinkexp[:, j].rearrange("p a b q -> p (a b q)"), op=ALU.add),
                 reads=[dbr, "sinkexp"], writes=["rec"])
            P.op("act", lambda e: e.activation(out=rec_f[:], in_=rec_f[:], func=AF.Ln), reads=["rec"], writes=["rec"])
            P.op("act", lambda e: e.activation(out=rec_f[:], in_=rec_f[:], func=AF.Exp, scale=-1.0), reads=["rec"], writes=["rec"])
            for half in range(2):
                r0 = half * 64
                P.op("dve", lambda e, ob=ob, half=half, r0=r0, j=j: e.tensor_tensor(
                    out=mixT[r0:r0 + 64, 2 * j:2 * j + 2, blk * 128:(blk + 1) * 128],
                    in0=ob[r0:r0 + 64, half * 256:(half + 1) * 256].rearrange("p (c q) -> p c q", c=2),
                    in1=rec_f[r0:r0 + 64, half * 256:(half + 1) * 256].rearrange("p (c q) -> p c q", c=2), op=ALU.mult),
                    reads=[obr, "rec"], writes=[("mixa", blk)])

    def qk_norm_chunk(bk, bkr, T, gcol, gres, out_fn):
        rstd_fm(bk[:, 0:T], T, bd_bf, 1.0 / 64, [bkr])
        out_fn(bk, bkr)

    def wo_ffnnorm(nb, s0, r0, s1, r1):
        for blk in range(nb):
            for cg, (s, r) in enumerate(((s0, r0), (s1, r1))):
                bk, bkr = P.bank()

                def mm(e, bk=bk, s=s, blk=blk):
                    ins = None
                    for k in range(8):
                        ins = e.matmul(bk[:, 0:512], lhsT=mixT[:, k, blk * 128:(blk + 1) * 128], rhs=ring[:, s, k, 0:512], start=(k == 0), stop=(k == 7))
                    return ins
                P.op("pe", mm, reads=[r, ("mixa", blk), ("mixg", blk * 128)], writes=[bkr])
                xs_ = X(blk)[:, cg * 512:(cg + 1) * 512]
                P.op("dve", lambda e, bk=bk, xs_=xs_: e.tensor_tensor(out=xs_, in0=bk[:, 0:512], in1=xs_, op=ALU.add), reads=[bkr, xr(blk)], writes=[xr(blk)])
            norm_block(blk, gfT)

    def ffn(nb, T, ydst_fn):
        for s6 in range(6):
            ncols = 512 if s6 < 5 else 256
            sg_, rg_ = wload([(0, wsrc("w_gate", 0, D, s6 * 512, ncols), 8, ncols)])
            su_, ru_ = wload([(0, wsrc("w_up", 0, D, s6 * 512, ncols), 8, ncols)])
            for mi in range(ncols // 128):
                m = s6 * 4 + mi
                gb, gbr = proj_fm(sg_, rg_, mi * 128, T, nb)
                ubk, ubr = proj_fm(su_, ru_, mi * 128, T, nb)
                P.op("act", lambda e, gb=gb: e.activation(out=sg_f[:, 0:T], in_=gb[:, 0:T], func=AF.Silu), reads=[gbr], writes=["sg_f"])
                P.op("dve", lambda e, ubk=ubk, m=m: e.tensor_tensor(out=aT[:, m, 0:T], in0=ubk[:, 0:T], in1=sg_f[:, 0:T], op=ALU.mult),
                     reads=[ubr, "sg_f"], writes=[("aT", m)])
        for cg in range(2):
            bks = [P.bank(hold=True) for _ in range(nb)]
            for kgp in range(3):
                nk = 8 if kgp < 2 else 6
                sl = wload([(0, wsrc("w_down", kgp * 1024, nk * 128, cg * 512, 512), nk, 512)])
                for blk in range(nb):
                    bk, bkr = bks[blk]

                    def mm(e, bk=bk, blk=blk, sl=sl, kgp=kgp, nk=nk):
                        ins = None
                        for kk in range(nk):
                            k = kgp * 8 + kk
                            ins = e.matmul(bk[:, 0:512], lhsT=aT[:, k, blk * 128:(blk + 1) * 128], rhs=ring[:, sl[0], kk, 0:512], start=(k == 0), stop=(k == NKF - 1))
                        return ins
                    P.op("pe", mm, reads=[sl[1]] + [("aT", kgp * 8 + kk) for kk in range(nk)], writes=[bkr])
            for blk in range(nb):
                bk, bkr = bks[blk]
                P.release(bkr)
                xs_ = X(blk)[:, cg * 512:(cg + 1) * 512]
                P.op("dve", lambda e, bk=bk, xs_=xs_: e.tensor_tensor(out=xs_, in0=bk[:, 0:512], in1=xs_, op=ALU.add), reads=[bkr, xr(blk)], writes=[xr(blk)])
                if cg == 1:
                    outs.append(P.dma("sp", ydst_fn(blk), X(blk), reads=[xr(blk)]))

    outs = []

    wstate["n"] = 0
    s_a, r_a = wload([(0, wsrc("w_in", 0, D, 1024, 256), 8, 256), (256, wsrc("w_in", 0, D, 2192, 128), 8, 128)], fp32=True)
    s_b, r_b = wload([(0, wsrc("w_in", 0, D, 1280, 512), 8, 512)], fp32=True)
    for k in range(8):
        P.op("dve", lambda e, k=k: e.tensor_scalar(out=ring[:, s_a, k, 0:384], in0=ring[:, s_a, k, 0:384], scalar1=gaT[:, k:k + 1], scalar2=None, op0=ALU.mult),
             reads=[r_a, "gaT"], writes=[r_a])
        P.op("dve", lambda e, k=k: e.tensor_scalar(out=ring[:, s_b, k, 0:512], in0=ring[:, s_b, k, 0:512], scalar1=gaT[:, k:k + 1], scalar2=None, op0=ALU.mult),
             reads=[r_b, "gaT"], writes=[r_b])
    negs = P.sb([128, 2], BF16)
    P.op("dve", lambda e: e.memset(negs[:], -1.0 / 16), writes=["negs"])
    tri_bf = P.sb([128, 128], BF16)
    P.op("dve", lambda e: e.tensor_copy(out=tri_bf[:], in_=tri_f[:]), reads=["tri"], writes=["tri_bf"])
    sp_h = (P.sb([128, 256], BF16), P.sb([128, 256], BF16))
    _enbflat = enb[:].rearrange("p c t -> p (c t)")
    _qgflat = qgT[:].rearrange("p c t -> p (c t)")
    for wn in ("w_in", "w_o", "w_gate", "w_up", "w_down"):
        nr = wf[wn].shape[0]
        step = 256
        for r0 in range(0, nr, step):
            r1 = min(nr, r0 + step)
            ci = P.dma("pool", wb[wn][r0:r1, :], wf[wn][r0:r1, :], reads=["wbfchain"], writes=[("wbf", wn), "wbfchain"])
            P.nofence = getattr(P, "nofence", set()) | {ci}
    NBLK = (16 if debug == "scan" else 0) if debug else NPRE // 128
    nbfs = (nbf, nbf_b); zts = (zt, zt_b); sps = (sp_t, sp_b); ktoks = (ktok, ktok_b)
    sbanks = {}

    def sc1(b):
        q = b % 4; p = b % 2
        P.dma("sp", xb[:, q, :], xpre[b * 128:(b + 1) * 128, :], writes=[("sx", q)])
        P.op("act", lambda e: e.activation(out=nbfs[p][:], in_=xb[:, q, :], func=AF.Square, accum_out=ss_c2[:, p:p + 1]),
             reads=[("sx", q)], writes=[("snbf", p), ("sss", p)])
        P.op("act", lambda e: e.activation(out=rs_c2[:, p:p + 1], in_=ss_c2[:, p:p + 1], func=AF.Ln, scale=1.0 / D, bias=eps_col[:, 0:1]),
             reads=[("sss", p), "eps"], writes=[("srs", p)])
        P.op("act", lambda e: e.activation(out=rs_c2[:, p:p + 1], in_=rs_c2[:, p:p + 1], func=AF.Exp, scale=-0.5), reads=[("srs", p)], writes=[("srs", p)])
        P.op("dve", lambda e: e.tensor_scalar(out=nbfs[p][:], in0=xb[:, q, :], scalar1=rs_c2[:, p:p + 1], scalar2=None, op0=ALU.mult),
             reads=[("sx", q), ("srs", p)], writes=[("snbf", p)])
        tb, tbr = tbank()

        def tr(e):
            ins = None
            for k in range(8):
                ins = e.transpose(out=tb[:, k * 128:(k + 1) * 128], in_=nbfs[p][:, k * 128:(k + 1) * 128], identity=ident_bf[:])
            return ins
        P.op("pe", tr, reads=[("snbf", p), "ident_bf"], writes=[tbr])
        if b % 2 == 0:
            P.op("act", lambda e: e.activation(out=actT[:, :, q * 128:(q + 1) * 128], in_=tb[:].rearrange("p (k t) -> p k t", k=8), func=AF.Copy),
                 reads=[tbr], writes=[("sact", q)])
        else:
            P.op("dve", lambda e: e.tensor_copy(out=actT[:, :, q * 128:(q + 1) * 128], in_=tb[:].rearrange("p (k t) -> p k t", k=8)),
                 reads=[tbr], writes=[("sact", q)])

    def sc2(b):
        q = b % 4
        ab, abr = P.bank()
        vb, vbr = P.bank()

        def mm(e):
            ins = None
            for k in range(8):
                ins = e.matmul(ab[:, 0:128], lhsT=ring[:, s_a, k, 256:384], rhs=actT[:, k, q * 128:(q + 1) * 128], start=(k == 0), stop=(k == 7))
            for k in range(8):
                ins = e.matmul(vb[:, 0:512], lhsT=actT[:, k, q * 128:(q + 1) * 128], rhs=ring[:, s_b, k, 0:512], start=(k == 0), stop=(k == 7))
            return ins
        P.op("pe", mm, reads=[r_a, r_b, ("sact", q)], writes=[abr, vbr])
        P.op("act", lambda e: e.activation(out=ulrT[:, q * 128:(q + 1) * 128], in_=ab[:, 0:128], func=AF.Copy), reads=[abr], writes=[("sulr", q)])
        P.op("act", lambda e: e.activation(out=vg_tok[:, q, :], in_=vb[:, 0:512], func=AF.Copy), reads=[vbr], writes=[("svg", q)])

    def sc3a(b):
        q = b % 4; p = b % 2
        zb, zbr = P.bank()
        sbanks[b] = (zb, zbr)
        P.op("pe", lambda e: e.matmul(zb[:, 0:256], lhsT=ulrT[:, q * 128:(q + 1) * 128], rhs=w2pad[:], start=True, stop=True),
             reads=[("sulr", q), "w2pad"], writes=[zbr])
        P.op("dve", lambda e: e.tensor_tensor(out=zts[p][:], in0=zb[:, 0:256], in1=bgate_bc[:], op=ALU.add), reads=[zbr, "bgate"], writes=[("szt", p)])
        P.op("act", lambda e: e.activation(out=zts[p][:], in_=zts[p][:], func=AF.Exp, scale=-1.0), reads=[("szt", p)], writes=[("szt", p)])
        P.op("act", lambda e: e.activation(out=sp_h[p][:], in_=zts[p][:], func=AF.Ln, bias=1.0), reads=[("szt", p)], writes=[("ssp", p)])

    def sc3b(b):
        q = b % 4; p = b % 2
        kb_, kbr = P.bank()
        cb, cbr = P.bank()
        en_ = _enbflat[:, q * 256:(q + 1) * 256]
        kt_ = _qgflat[:, q * 256:(q + 1) * 256]

        def mm(e):
            ins = None
            for k in range(8):
                ins = e.matmul(kb_[:, 0:256], lhsT=actT[:, k, q * 128:(q + 1) * 128], rhs=ring[:, s_a, k, 0:256], start=(k == 0), stop=(k == 7))
            return ins
        P.op("pe", mm, reads=[r_a, ("sact", q)], writes=[kbr])

        def mmc(e):
            e.matmul(cb[:, 0:256], lhsT=tri_bf[:], rhs=sp_h[p][:], start=True, stop=True)
            ins = None
            for c in range(2):
                ins = e.matmul(cb[:, 256 + 2 * c:258 + 2 * c], lhsT=sp_h[p][:, c * 128:(c + 1) * 128], rhs=negs[:], start=True, stop=True)
            return ins
        P.op("pe", mmc, reads=[("ssp", p), "tri_bf", "negs"], writes=[cbr])
        P.op("act", lambda e: e.activation(out=en_, in_=cb[:, 0:256], func=AF.Exp, scale=-1.0), reads=[cbr], writes=[("senb", q)])
        P.op("act", lambda e: e.activation(out=elast[:, :, q], in_=fap(cb[:, 256:257], [[2, 2]]), func=AF.Exp), reads=[cbr], writes=[("sel", q)])
        P.op("dve", lambda e: e.tensor_tensor(out=kt_, in0=kb_[:, 0:256], in1=en_, op=ALU.mult),
             reads=[kbr, ("senb", q)], writes=[("sktok", q)])

    def sc4(b):
        q = b % 4; p = b % 2
        kt = _qgflat[:, q * 256:(q + 1) * 256]
        ub, ubr = P.bank()

        def mm(e):
            ins = None
            for c in range(2):
                for ab in range(2):
                    h = 2 * c + ab
                    ins = e.matmul(ub[:, h * 128:(h + 1) * 128], lhsT=kt[:, c * 128:(c + 1) * 128], rhs=vg_tok[:, q, h * 128:(h + 1) * 128], start=True, stop=True)
            return ins
        P.op("pe", mm, reads=[("sktok", q), ("svg", q)], writes=[ubr])
        for ab in range(2):
            r0 = ab * 64
            P.op("dve", lambda e, ab=ab, r0=r0: e.tensor_tensor(out=Stmp[r0:r0 + 64], in0=fap(ub[r0:r0 + 64, ab * 128:ab * 128 + 1], [[256, 2], [1, 128]]),
                                                               in1=S[r0:r0 + 64], op=ALU.add), reads=[ubr, "S", "Stmp"], writes=["Stmp"])
        P.op("dve", lambda e: e.tensor_tensor(out=S[:], in0=Stmp[:], in1=fap(elast[:, 0, q:q + 1], [[4, 2], [0, 128]]), op=ALU.mult),
             reads=["Stmp", ("sel", q)], writes=["S"])

    stages = (sc1, sc2, sc3a, sc3b, sc4)
    for i in range(NBLK + len(stages) - 1):
        for si, st in enumerate(stages):
            b = i - si
            if 0 <= b < NBLK:
                st(b)
    P.barrier(keep=("wbf", "wbfchain"))
    P.op("act", lambda e: e.activation(out=SA[0:64], in_=S[0:64], func=AF.Copy), reads=["S", "zinit"], writes=["SA"])
    P.op("act", lambda e: e.activation(out=SB[64:128], in_=S[64:128], func=AF.Copy), reads=["S", "zinit"], writes=["SB"])

    def main_tile(kind, t):
        sample = kind == "sample"
        nb = 1 if sample else 4
        T = nb * 128
        CUR["p"] = 0 if sample else (t % 2)
        first = (kind == "prompt" and t == 0)
        last = (kind == "prompt" and t == 3)
        L0 = wload([(0, wsrc("w_in", 0, D, 0, 512), 8, 512)])
        L1 = wload([(0, wsrc("w_in", 0, D, 512, 512), 8, 512)])
        L2 = wload([(0, wsrc("w_in", 0, D, 1024, 256), 8, 256), (256, wsrc("w_in", 0, D, 2192, 128), 8, 128)])
        if first:
            front(lambda blk: xhalo, 1, gaT)
            halo_kv = True
            kv_part(L1, 1, 128, 0, False, False)
        if sample:
            front(lambda blk: xs, 1, gaT)
        else:
            front(lambda blk: xp[t * 512 + blk * 128: t * 512 + (blk + 1) * 128, :], 4, gaT)
        gla_prep(L2[0], L2[1], 256, nb, T, (tris_f if sample else tri_f)[:], "tris" if sample else "tri", True)
        for c in range(4):
            bk, bkr = proj_fm(L0[0], L0[1], c * 128, T, nb)
            rstd_fm(bk[:, 0:T], T, bd_bf, 1.0 / 64, [bkr])
            P.op("dve", lambda e, bk=bk, c=c: e.scalar_tensor_tensor(out=qhT[:, c, 0:T], in0=bk[:, 0:T], scalar=gq_col[:, 0:1], in1=rstd_f[:, 0:T],
                                                                    op0=ALU.mult, op1=ALU.mult), reads=[bkr, "rstd", "gq"], writes=["qhT"])
        kv_part(L1, nb, T, 1, last, sample)
        for c in range(2):
            bk, bkr = proj_fm(L1[0], L1[1], 256 + c * 128, T, nb)
            P.op("dve", lambda e, bk=bk, c=c: e.tensor_tensor(out=qgT[:, c, 0:T], in0=bk[:, 0:T], in1=ebq[:, c, 0:T], op=ALU.mult),
                 reads=[bkr, "ebq"], writes=["qgT"])
        kg_evac(L2[0], L2[1], 0, nb, T, sample)
        L3 = wload([(0, wsrc("w_in", 0, D, 1280, 512), 8, 512)])
        vg_tm(L3[0], L3[1], 0, nb)
        L4 = wload([(0, wsrc("w_in", 0, D, 1792, 512), 8, 512)])
        for c in range(4):
            bk, bkr = proj_fm(L4[0], L4[1], c * 128, T, nb)
            P.op("act", lambda e, bk=bk, c=c: e.activation(out=rsT[:, c, 0:T], in_=bk[:, 0:T], func=AF.Silu), reads=[bkr], writes=["rsT"])
        if sample:
            sample_attn()
            sample_gla()
        else:
            for blk in range(nb):
                attn_block(blk, E, "E", first and blk == 0)
                gla_block(blk)
            P.op("pool", lambda e: e.tensor_copy(out=kX[:, :, 0:128], in_=kX[:, :, 512:640]), reads=["kX"], writes=["kX"])
            P.op("pool", lambda e: e.tensor_copy(out=Vdup[:, 0], in_=Vdup[:, 4]), reads=["Vdup"], writes=["Vdup"])
        if debug:
            outs.append(P.dma("pool", dbg_mix, mixT[:], reads=[("mixa", b_) for b_ in range(nb)] + [("mixg", b_ * 128) for b_ in range(nb)]))
            outs.append(P.dma("pool", dbg_q, qhT[:], reads=["qhT"]))
            outs.append(P.dma("pool", dbg_rs, rsT[:], reads=["rsT"]))
        L5 = wload([(0, wsrc("w_o", 0, D, 0, 512), 8, 512)])
        L6 = wload([(0, wsrc("w_o", 0, D, 512, 512), 8, 512)])
        wo_ffnnorm(nb, L5[0], L5[1], L6[0], L6[1])
        if debug:
            outs.append(P.dma("sp", dbg_h, xb[:], reads=[("x", 0, b_) for b_ in range(nb)]))
            outs.append(P.dma("pool", dbg_z, actT[:], reads=[("actT", 0, b_) for b_ in range(nb)]))
        if sample:
            ffn(nb, T, lambda blk: y_s)
        else:
            ffn(nb, T, lambda blk: y_p[t * 512 + blk * 128: t * 512 + (blk + 1) * 128, :])

    def kv_part(L1, nb, T, kblk0, last, sample):
        bk, bkr = proj_fm(L1[0], L1[1], 0, T, nb)
        rstd_fm(bk[:, 0:T], T, bd_bf, 1.0 / 64, [bkr])
        c0 = kblk0 * 128
        P.op("dve", lambda e: e.scalar_tensor_tensor(out=khT_bf[:, 0:T], in0=bk[:, 0:T], scalar=gk_col[:, 0:1], in1=rstd_f[:, 0:T], op0=ALU.mult, op1=ALU.mult),
             reads=[bkr, "rstd", "gk"], writes=["khT"])
        if last or sample:
            lo = T - 128
            P.op("dve", lambda e: e.scalar_tensor_tensor(out=khT_f[:], in0=bk[:, lo:T], scalar=gk_col[:, 0:1], in1=rstd_f[:, lo:T], op0=ALU.mult, op1=ALU.mult),
                 reads=[bkr, "rstd", "gk"], writes=["khT_f"])
        P.op("act", lambda e: e.activation(out=kX[0:64, 0, c0:c0 + T], in_=khT_bf[0:64, 0:T], func=AF.Copy), reads=["khT", "kX"], writes=["kX"])
        P.op("act", lambda e: e.activation(out=kX[64:128, 3, c0:c0 + T], in_=khT_bf[64:128, 0:T], func=AF.Copy), reads=["khT", "kX"], writes=["kX"])
        b2, b2r = P.bank()
        P.op("pe", lambda e: e.matmul(b2[:, 0:T], lhsT=sw_bf[:], rhs=khT_bf[:, 0:T], start=True, stop=True), reads=["khT", "sw"], writes=[b2r])
        P.op("act", lambda e: e.activation(out=kX[0:64, 2, c0:c0 + T], in_=b2[0:64, 0:T], func=AF.Copy), reads=[b2r, "kX"], writes=["kX"])
        P.op("act", lambda e: e.activation(out=kX[64:128, 1, c0:c0 + T], in_=b2[64:128, 0:T], func=AF.Copy), reads=[b2r, "kX"], writes=["kX"])
        for blk in range(nb):
            vb, vbr = proj_tm(L1[0], L1[1], 128, 128, blk)
            P.op("act", lambda e, vb=vb, blk=blk: e.activation(out=Vdup[:, kblk0 + blk].rearrange("p j (u d) -> p j u d", u=2),
                                                               in_=fap(vb[:, 0:1], [[64, 2], [0, 2], [1, 64]]), func=AF.Copy),
                 reads=[vbr, "Vdup"], writes=["Vdup"])
            if (last and blk == nb - 1) or sample:
                P.op("dve", lambda e, vb=vb: e.tensor_copy(out=vw_f[:], in_=vb[:, 0:128]), reads=[vbr], writes=["vw_f"])
        if last or sample:
            kb_, kbr = P.bank()
            P.op("pe", lambda e: e.transpose(out=kb_[:, 0:128], in_=khT_f[:], identity=ident_f[:]), reads=["khT_f", "ident_f"], writes=[kbr])
            P.op("act", lambda e: e.activation(out=kw_f[:], in_=kb_[:, 0:128], func=AF.Copy), reads=[kbr], writes=["kw_f"])
        if last:
            outs.append(P.dma("sp", kwp, kw_f[:], reads=["kw_f"]))
            outs.append(P.dma("sp", vwp, vw_f[:], reads=["vw_f"]))

    def sample_attn():
        for (dst, src, new, nm) in ((kws, ck, kw_f, "kws"), (vws, cv, vw_f, "vws")):
            P.dma("sp", dst[:, 0:127, :], src[:, 1:128, :], writes=[nm])
            P.dma("sp", bass.AP(tensor=dst.tensor, offset=127 * 128, ap=[[128 * 128, 16], [1, 128]]), new[0:16, :], reads=["kw_f", "vw_f"], writes=[nm])
        t1 = P.dma("pool", Kw_bf[:], kws.rearrange("b k f -> k b f"), reads=["kws"], writes=["Kw"])
        for j in range(2):
            P.dma("pool", Kwsw_bf[:, :, (1 - j) * 64:(2 - j) * 64], kws[:, :, j * 64:(j + 1) * 64].rearrange("b k f -> k b f"), reads=["kws"], writes=["Kwsw"])
            for u in range(2):
                P.dma("pool", Vwd[:, :, j, u * 64:(u + 1) * 64], vws[:, :, j * 64:(j + 1) * 64].rearrange("b k f -> k b f"), reads=["vws"], writes=["Vwd"])
        outs.append(t1)
        P.op("dve", lambda e: e.tensor_copy(out=qsA[0:64], in_=qhT[0:64, :, 0:16]), reads=["qhT", "zinit"], writes=["qsA"])
        P.op("dve", lambda e: e.tensor_copy(out=qsB[64:128], in_=qhT[64:128, :, 0:16]), reads=["qhT", "zinit"], writes=["qsB"])
        sb_, sbr = P.bank()
        for b in range(16):
            tb, tbr = tbank()
            bf_ = b % 2

            def tr(e, tb=tb, b=b):
                e.transpose(out=tb[:, 0:128], in_=Kw_bf[:, b, :], identity=ident_bf[:])
                return e.transpose(out=tb[:, 128:256], in_=Kwsw_bf[:, b, :], identity=ident_bf[:])
            P.op("pe", tr, reads=["Kw", "Kwsw", "ident_bf"], writes=[tbr])
            P.op("act", lambda e, tb=tb, bf_=bf_: e.activation(out=KTb[:, bf_].rearrange("p a k -> p (a k)"), in_=tb[:, 0:256], func=AF.Copy),
                 reads=[tbr], writes=[("KTb", bf_)])

            def mm(e, b=b, bf_=bf_):
                ins = None
                for j in range(2):
                    for half in range(2):
                        kt = KTb[:, bf_, 0 if j == half else 1, :]
                        q = (qsA if half == 0 else qsB)[:, 2 * j:2 * j + 2, b]
                        o = b * 8 + (j * 2 + half) * 2
                        ins = e.matmul(sb_[:, o:o + 2], lhsT=kt, rhs=q, start=True, stop=True)
                return ins
            P.op("pe", mm, reads=[("KTb", bf_), "qsA", "qsB"], writes=[sbr])
        P.op("act", lambda e: e.activation(out=pes[:].rearrange("p b h -> p (b h)"), in_=sb_[:, 0:128], func=AF.Exp), reads=[sbr], writes=["pes"])
        P.op("dve", lambda e: e.tensor_tensor(out=Pts[:], in0=pes[:], in1=fap(E[:, 0, 0, 1, 0, 127:128], [[0, 16], [512, 4], [128, 2]]), op=ALU.mult),
             reads=["pes", "E"], writes=["Pts"])
        ob, obr = P.bank()
        db, dbr = P.bank()

        def mmv(e):
            ins = None
            for b in range(16):
                for j in range(2):
                    e.matmul(ob[:, b * 8 + j * 4:b * 8 + j * 4 + 4], lhsT=Vwd[:, b, j, :], rhs=Pts[:, b, j * 4:(j + 1) * 4], start=True, stop=True)
                ins = e.matmul(db[:, b * 8:(b + 1) * 8], lhsT=ones_bf[:], rhs=Pts[:, b, :], start=True, stop=True)
            return ins
        P.op("pe", mmv, reads=["Pts", "Vwd", "ones"], writes=[obr, dbr])
        P.op("dve", lambda e: e.tensor_tensor(out=rec_f[:, 0:128].rearrange("p (b h) -> p b h", b=16), in0=db[:, 0:128].rearrange("p (b h) -> p b h", b=16),
                                              in1=fap(sinkexp[:, 0, 0, 0, 0:1], [[0, 16], [128, 8]]), op=ALU.add), reads=[dbr, "sinkexp"], writes=["rec"])
        P.op("act", lambda e: e.activation(out=rec_f[:, 0:128], in_=rec_f[:, 0:128], func=AF.Ln), reads=["rec"], writes=["rec"])
        P.op("act", lambda e: e.activation(out=rec_f[:, 0:128], in_=rec_f[:, 0:128], func=AF.Exp, scale=-1.0), reads=["rec"], writes=["rec"])
        for j in range(2):
            for half in range(2):
                r0 = half * 64
                o = (j * 2 + half) * 2
                P.op("dve", lambda e, r0=r0, o=o, j=j: e.tensor_tensor(
                    out=mixT[r0:r0 + 64, 2 * j:2 * j + 2, 0:16],
                    in0=fap(ob[r0:r0 + 64, o:o + 1], [[1, 2], [8, 16]]),
                    in1=fap(rec_f[r0:r0 + 64, o:o + 1], [[1, 2], [8, 16]]), op=ALU.mult), reads=[obr, "rec"], writes=[("mixa", 0)])

    def sample_gla():
        ob, obr = P.bank(hold=True)
        for b in range(16):
            bf_ = b % 2
            P.dma("sp", Sb[:, bf_], sg[b].rearrange("(c u) d v -> (u d) c v", u=2), writes=[("Sb", bf_)])
            vb, vbr = P.bank()
            P.op("pe", lambda e, vb=vb, b=b: e.matmul(vb[:, 0:512], lhsT=sel_bf[:, b, :], rhs=vg_tok[:, 0, :], start=True, stop=True),
                 reads=["sel", ("vg", 0)], writes=[vbr])
            for c in range(2):
                for half in range(2):
                    r0 = half * 64
                    h = 2 * c + half
                    P.op("dve", lambda e, vb=vb, b=b, c=c, r0=r0, h=h, bf_=bf_: e.scalar_tensor_tensor(
                        out=Wt[r0:r0 + 64, bf_, c, :], in0=vb[r0:r0 + 64, h * 128:(h + 1) * 128], scalar=kgf[r0:r0 + 64, c, b:b + 1],
                        in1=Sb[r0:r0 + 64, bf_, c, :], op0=ALU.mult, op1=ALU.add), reads=[vbr, "kgf", ("Sb", bf_)], writes=[("Wt", bf_)])
            P.op("act", lambda e, bf_=bf_: e.activation(out=WA[0:64, bf_], in_=Wt[0:64, bf_], func=AF.Copy), reads=[("Wt", bf_), "zinit"], writes=[("WA", bf_)])
            P.op("act", lambda e, bf_=bf_: e.activation(out=WB[64:128, bf_], in_=Wt[64:128, bf_], func=AF.Copy), reads=[("Wt", bf_), "zinit"], writes=[("WB", bf_)])
            P.op("dve", lambda e, b=b, bf_=bf_: e.tensor_tensor(out=Sn[:, bf_], in0=Wt[:, bf_], in1=fap(elb[:, 0, b:b + 1], [[128, 2], [0, 128]]), op=ALU.mult),
                 reads=[("Wt", bf_), "elb"], writes=[("Sn", bf_)])
            outs.append(P.dma("sp", gss[b].rearrange("(c u) d v -> (u d) c v", u=2), Sn[:, bf_], reads=[("Sn", bf_)]))

            def mm(e, b=b, bf_=bf_):
                ins = None
                for h in range(4):
                    c, half = h // 2, h % 2
                    w = (WA if half == 0 else WB)[:, bf_, c, :]
                    ins = e.matmul(ob[:, h * 16 + b:h * 16 + b + 1], lhsT=w, rhs=qgT[:, c, b:b + 1], start=True, stop=True)
                return ins
            P.op("pe", mm, reads=[("WA", bf_), ("WB", bf_), "qgT"], writes=[obr])
        P.release(obr)
        gla_out_norm(ob, obr, 64, 16, 0)

    import os as _os
    _kb = _os.environ.get("KBAR", "")
    for t in range((0 if debug in ("sample", "scan") else (2 if debug == "two" else (4 if debug == "four" else 1))) if debug else 4):
        main_tile("prompt", t)
        if "t" in _kb:
            P.barrier()
    if debug and debug != "sample":
        outs.append(P.dma("pool", dbg_a, aT[:], reads=[("aT", m) for m in range(NKF)]))
    outs.append(P.dma("sp", gsp.rearrange("(c u) d v -> (u d) c v", u=2), S[:], reads=["S"]))
    P.barrier()
    if (not debug) or debug == "sample":
        main_tile("sample", 0)

    P.emit()
    P.stack.close()
    return nc


_CACHE = {}


def kernel(x_prompt, x_sample, cache_k, cache_v, state_gla, attn_norm_g, w_in, q_norm_g, k_norm_g, attn_sinks,
           rel_bias, w_gla_gate2, b_gla_gate, gla_norm_g, w_o, ffn_norm_g, w_gate, w_up, w_down):
    f = lambda a: np.ascontiguousarray(np.asarray(a, dtype=np.float32))
    xpr = f(x_prompt)[0]
    xsm = f(x_sample)[:, 0, :]
    ckk = f(cache_k)[0].reshape(128, 128, 128)
    cvv = f(cache_v)[0].reshape(128, 128, 128)
    sgg = f(state_gla)[0]
    consts = host_consts()
    shared = dict(w_in=f(w_in)[0], w_o=f(w_o)[0], w_gate=f(w_gate)[0], w_up=f(w_up)[0], w_down=f(w_down)[0],
                  attn_g=f(attn_norm_g)[0], ffn_g=f(ffn_norm_g)[0], qng=f(q_norm_g)[0], kng=f(k_norm_g)[0],
                  sinks=f(attn_sinks)[0], relb=f(rel_bias), w2=f(w_gla_gate2)[0], bgate=f(b_gla_gate)[0], glag=f(gla_norm_g)[0])
    for k, v in consts.items():
        shared["c_" + k] = v
    in_maps = []
    for c in range(NCORE):
        m = dict(shared)
        m["xp"] = xpr[c * TOK:(c + 1) * TOK]
        m["xhalo"] = xpr[c * TOK - 128:c * TOK] if c > 0 else np.zeros((128, D), np.float32)
        pre = np.zeros((NPRE, D), np.float32)
        if c > 0:
            pre[NPRE - c * TOK:] = xpr[:c * TOK]
        m["xpre"] = pre
        xs_ = np.zeros((128, D), np.float32)
        xs_[:16] = xsm[c * 16:(c + 1) * 16]
        m["xs"] = xs_
        m["ck"] = ckk[c * 16:(c + 1) * 16]
        m["cv"] = cvv[c * 16:(c + 1) * 16]
        m["sg"] = sgg[c * 16:(c + 1) * 16]
        m["flag"] = np.full((128, 1), 1.0 if c > 0 else 0.0, np.float32)
        in_maps.append(m)
    if "nc" not in _CACHE:
        _CACHE["nc"] = build_program()
    res = run_bass_kernel_spmd(_CACHE["nc"], in_maps, core_ids=list(range(NCORE)))
    R = res.results
    y_prompt = np.concatenate([R[c]["y_p"] for c in range(NCORE)], axis=0)[None]
    y_sample = np.concatenate([R[c]["y_s"][:16] for c in range(NCORE)], axis=0)[:, None, :]
    kwp = R[7]["kwp"].reshape(1, 1, 128, 2, 64)
    vwp = R[7]["vwp"].reshape(1, 1, 128, 2, 64)
    gsp = R[7]["gsp"].reshape(1, 1, 4, 64, 128)
    kws = np.concatenate([R[c]["kws"] for c in range(NCORE)], axis=0).reshape(1, 128, 128, 2, 64)
    vws = np.concatenate([R[c]["vws"] for c in range(NCORE)], axis=0).reshape(1, 128, 128, 2, 64)
    gss = np.concatenate([R[c]["gss"] for c in range(NCORE)], axis=0).reshape(1, 128, 4, 64, 128)
    return (y_prompt.astype(np.float32), y_sample.astype(np.float32), kwp, vwp, gsp, kws, vws, gss)
```

```python
import contextlib
import math
import numpy as np
import concourse.bass as bass
import concourse.mybir as mybir
from concourse.bass_utils import run_bass_kernel_spmd

F32 = mybir.dt.float32
BF16 = mybir.dt.bfloat16
AF = mybir.ActivationFunctionType
ALU = mybir.AluOpType

NCORE = 8
D = 1024
TOK = 2048
NPRE = 7 * 2048
DFF = 2816
NKF = DFF // 128
INW = 2320
ENGS = ("pe", "act", "dve", "pool", "sp")
NDMASEM = 12
NSLOT = 4
EPS = 1e-6
MASKV = -30000.0


class _Ins:
    def then_inc(self, *a, **k):
        return self


class _Mock:
    def __init__(self):
        self.cost = 0.0

    def _free(self, ap):
        n = 1
        for d in list(ap.shape)[1:]:
            n *= int(d)
        return n

    def matmul(self, out, lhsT=None, rhs=None, **k):
        n = max(self._free(rhs), 64)
        self.cost += (n * (4 if rhs.dtype == F32 else 1)) / 2400.0 + 0.01
        return _Ins()

    def transpose(self, out=None, in_=None, identity=None, **k):
        self.cost += (128 * (4 if in_.dtype == F32 else 1)) / 2400.0 + 0.01
        return _Ins()

    def __getattr__(self, name):
        def f(*a, **k):
            o = k.get("out", a[0] if a else None)
            n = self._free(o) if o is not None else 64
            self.cost += 0.2 + n / 1000.0
            return _Ins()
        return f


class Prog:
    def __init__(self, nc):
        self.nc = nc
        self.stack = contextlib.ExitStack()
        self.oplist = []
        self.last_w = {}
        self.readers = {}
        self.base = None
        self.sems = {}
        self.nbuf = 0
        self.banks = []
        self.bank_rr = 0
        self.held = set()
        self.finals = []

    def sb(self, shape, dt, name=None):
        self.nbuf += 1
        return self.stack.enter_context(self.nc.sbuf_tensor(name or f"sb{self.nbuf}", list(shape), dt))

    def ps(self, shape, dt, name=None):
        self.nbuf += 1
        return self.stack.enter_context(self.nc.psum_tensor(name or f"ps{self.nbuf}", list(shape), dt))

    def bank(self, hold=False):
        while True:
            i = self.bank_rr % len(self.banks)
            self.bank_rr += 1
            if i not in self.held:
                break
        if hold:
            self.held.add(i)
        return self.banks[i], ("ps", i)

    def release(self, res):
        self.held.discard(res[1])

    def _sem(self, key):
        if key not in self.sems:
            nm = "s_" + "_".join(str(k) for k in (key if isinstance(key, tuple) else (key,)))
            self.sems[key] = self.stack.enter_context(self.nc.semaphore(nm))
        return self.sems[key]

    def _add(self, eng, fn, reads, writes, kind, cost, dma=None):
        deps = set()
        for r in reads:
            t = self.last_w.get(r, self.base)
            if t is not None:
                deps.add(t)
        for w in writes:
            t = self.last_w.get(w, self.base)
            if t is not None:
                deps.add(t)
            deps.update(self.readers.get(w, ()))
        if not reads and not writes and self.base is not None:
            deps.add(self.base)
        i = len(self.oplist)
        deps.discard(i)
        self.oplist.append(dict(eng=eng, fn=fn, deps=deps, kind=kind, cost=cost, dma=dma))
        for r in reads:
            self.readers.setdefault(r, []).append(i)
        for w in writes:
            self.last_w[w] = i
            self.readers[w] = []
        return i

    def op(self, eng, fn, reads=(), writes=()):
        m = _Mock()
        fn(m)
        return self._add(eng, fn, reads, writes, "c", m.cost)

    def dma(self, eng, out, in_, reads=(), writes=()):
        n = 1
        for d in out.shape:
            n *= int(d)
        nbytes = n * (4 if out.dtype == F32 else 2)
        return self._add(eng, None, reads, writes, "d", 2.0 + nbytes / 150e3, dma=(out, in_))

    def barrier(self, keep=()):
        kept = {k: v for k, v in self.last_w.items() if (k in keep or (isinstance(k, tuple) and k and k[0] in keep))}
        skip = set(getattr(self, "nofence", ()))
        allprev = set(range(len(self.oplist))) - skip
        i = len(self.oplist)
        self.oplist.append(dict(eng="sp", fn=None, deps=allprev, kind="n", cost=0.05, dma=None))
        self.base = i
        self.bars = getattr(self, "bars", []) + [i]
        self.last_w = dict(kept)
        self.readers = {}

    def final_wait(self, eng, toks):
        pass

    def schedule(self):
        import heapq, os
        ops = self.oplist
        n = len(ops)
        succ = [[] for _ in range(n)]
        ndep = [0] * n
        for i, o in enumerate(ops):
            ndep[i] = len(o["deps"])
            for d in o["deps"]:
                succ[d].append(i)
        done = [0.0] * n
        ready_t = [0.0] * n
        efree = {e: 0.0 for e in ENGS}
        waiting = {e: [] for e in ENGS}
        avail = {e: [] for e in ENGS}
        order = {e: [] for e in ENGS}
        for i, o in enumerate(ops):
            if ndep[i] == 0:
                heapq.heappush(waiting[o["eng"]], (0.0, i))
        nsched = 0
        while nsched < n:
            best = None
            for e in ENGS:
                w, a = waiting[e], avail[e]
                while w and w[0][0] <= efree[e]:
                    heapq.heappush(a, heapq.heappop(w)[1])
                if a:
                    cand = (efree[e], a[0], e, True)
                elif w:
                    cand = (w[0][0], w[0][1], e, False)
                else:
                    continue
                if best is None or cand[:2] < best[:2]:
                    best = cand
            st, i, e, from_avail = best
            if from_avail:
                heapq.heappop(avail[e])
            else:
                heapq.heappop(waiting[e])
            o = ops[i]
            if o["kind"] == "d":
                efree[e] = st + 0.15
                done[i] = st + o["cost"]
            else:
                efree[e] = st + o["cost"]
                done[i] = st + o["cost"] + 0.15
            order[e].append(i)
            nsched += 1
            for sidx in succ[i]:
                ndep[sidx] -= 1
                if done[i] > ready_t[sidx]:
                    ready_t[sidx] = done[i]
                if ndep[sidx] == 0:
                    heapq.heappush(waiting[ops[sidx]["eng"]], (ready_t[sidx], sidx))
        self.sim_time = max(done) if n else 0.0
        self.sim_done = done
        if os.environ.get("KSIM"):
            print("SIM total", round(self.sim_time), "barriers", [round(done[b]) for b in getattr(self, "bars", [])])
        return order

    def emit(self):
        nc = self.nc
        import os
        order = self.schedule()
        km = os.environ.get("KSCHED", "mid")
        if km == "0":
            order = {e: [i for i, o in enumerate(self.oplist) if o["eng"] == e] for e in ENGS}
        elif km == "mid" and len(getattr(self, "bars", [])) >= 2:
            B2 = self.bars[-1]
            prog = {e: [i for i, o in enumerate(self.oplist) if o["eng"] == e] for e in ENGS}
            order = {e: [i for i in order[e] if i <= B2] + [i for i in prog[e] if i > B2] for e in ENGS}
        elif km in ("pre", "post") and self.base is not None:
            B = self.base
            prog = {e: [i for i, o in enumerate(self.oplist) if o["eng"] == e] for e in ENGS}
            if km == "post":
                order = {e: [i for i in prog[e] if i <= B] + [i for i in order[e] if i > B] for e in ENGS}
            else:
                order = {e: [i for i in order[e] if i <= B] + [i for i in prog[e] if i > B] for e in ENGS}
        ops = self.oplist
        tok = [None] * len(ops)
        ccnt = {e: 0 for e in ENGS}
        dcnt = {}
        drr = {e: 0 for e in ENGS}
        plan = {e: [] for e in ENGS}
        prevdma = {}
        for e in ENGS:
            for i in order[e]:
                o = ops[i]
                if o["kind"] == "d":
                    j = drr[e] % NDMASEM
                    drr[e] += 1
                    key = ("d", e, j)
                    c = dcnt.get(key, 0)
                    dcnt[key] = c + 1
                    tok[i] = (key, 16 * (c + 1))
                    prevdma[i] = (key, 16 * c) if c > 0 else None
                else:
                    ccnt[e] += 1
                    tok[i] = (e, ccnt[e])
        for e in ENGS:
            waited = {}
            for i in order[e]:
                o = ops[i]
                need = {}
                for d in o["deps"]:
                    k, v = tok[d]
                    if waited.get(k, 0) >= v:
                        continue
                    if need.get(k, 0) < v:
                        need[k] = v
                if o["kind"] == "d" and prevdma.get(i):
                    k, v = prevdma[i]
                    if waited.get(k, 0) < v and need.get(k, 0) < v:
                        need[k] = v
                for k, v in need.items():
                    waited[k] = v
                plan[e].append((list(need.items()), i))
            if e == "sp":
                fin = {}
                for i2, t in enumerate(tok):
                    if t is not None and fin.get(t[0], 0) < t[1]:
                        fin[t[0]] = t[1]
                plan[e].append(([(k, v) for k, v in fin.items() if waited.get(k, 0) < v], None))
        for e in ENGS:
            for (waits, i) in plan[e]:
                for (k, v) in waits:
                    self._sem(k)
                if i is not None:
                    self._sem(tok[i][0])
        if os.environ.get("KCHECK"):
            import collections
            sv = collections.defaultdict(int)
            ptr = {e: 0 for e in ENGS}
            while True:
                prog_ = False
                for e in ENGS:
                    while ptr[e] < len(plan[e]):
                        waits, i = plan[e][ptr[e]]
                        if all(sv[k] >= v for k, v in waits):
                            if i is not None:
                                k, v = tok[i]
                                inc = 16 if ops[i]["kind"] == "d" else 1
                                sv[k] += inc
                                assert sv[k] == v, ("token mismatch", e, i, k, v, sv[k])
                            ptr[e] += 1
                            prog_ = True
                        else:
                            break
                if all(ptr[e] == len(plan[e]) for e in ENGS):
                    print("KCHECK: ok, no deadlock")
                    break
                if not prog_:
                    for e in ENGS:
                        if ptr[e] < len(plan[e]):
                            waits, i = plan[e][ptr[e]]
                            print("KCHECK STUCK", e, ptr[e], i, [(k, v, sv[k]) for k, v in waits if sv[k] < v])
                    break
        block = self.stack.enter_context(nc.Block())

        def run(engname):
            def body(e):
                for (waits, i) in plan[engname]:
                    for (k, v) in waits:
                        e.wait_ge(self.sems[k], v)
                    if i is None:
                        continue
                    o = ops[i]
                    if o["kind"] == "d":
                        ins = e.dma_start(out=o["dma"][0], in_=o["dma"][1], allow_slow_non_contiguous=True)
                        ins.then_inc(self.sems[tok[i][0]], 16)
                    elif o["kind"] == "n":
                        ins = e.nop()
                        ins.then_inc(self.sems[tok[i][0]], 1)
                    else:
                        ins = o["fn"](e)
                        ins.then_inc(self.sems[tok[i][0]], 1)
            return body
        block.tensor(run("pe"))
        block.scalar(run("act"))
        block.vector(run("dve"))
        block.gpsimd(run("pool"))
        block.sync(run("sp"))


def fap(ap, dims):
    return bass.AP(tensor=ap.tensor, offset=ap.offset, ap=[list(ap.ap[0])] + [list(d) for d in dims])


def t5_bucket_np(n):
    n = np.maximum(n, 0)
    nf = np.maximum(n, 1).astype(np.float32)
    large = 16 + (np.log(nf / 16) / math.log(128 / 16) * 16).astype(np.int32)
    large = np.minimum(large, 31)
    return np.where(n < 16, n, large)


def host_consts():
    c = {}
    c["ident"] = np.eye(128, dtype=np.float32)
    s = np.arange(128)[:, None]
    t = np.arange(128)[None, :]
    c["tri"] = np.where(s <= t, -1.0 / 16, 0.0).astype(np.float32)
    c["tris"] = (np.eye(128) * (-1.0 / 16)).astype(np.float32)
    c["cmask"] = (s <= t).astype(np.float32)
    c["jx"] = np.eye(128, dtype=np.float32)[::-1].copy()
    bd = np.zeros((128, 128), np.float32)
    bd[:64, :64] = 1
    bd[64:, 64:] = 1
    c["bd"] = bd
    sw = np.zeros((128, 128), np.float32)
    for m in range(128):
        sw[(m + 64) % 128, m] = 1
    c["sw"] = sw
    oh = np.zeros((128, 2, 256), np.float32)
    for kb in range(2):
        off = 128 if kb == 0 else 0
        for i in range(255):
            dlt = 127 + off - i
            if 0 <= dlt <= 127:
                oh[int(t5_bucket_np(np.array(dlt))), kb, i] = 1.0
            else:
                oh[32, kb, i] = MASKV
        oh[32, kb, 255] = MASKV
    c["oh"] = oh
    sel = np.zeros((128, 16, 128), np.float32)
    for b in range(16):
        sel[b, b, :] = 1
    c["sel"] = sel
    return c


def build_program(debug=False):
    nc = bass.Bass("TRN2", target_bir_lowering=False)
    P = Prog(nc)

    def din(name, shape):
        return nc.dram_tensor(name, list(shape), F32, kind="ExternalInput").ap()

    def dout(name, shape):
        return nc.dram_tensor(name, list(shape), F32, kind="ExternalOutput").ap()

    xp = din("xp", [TOK, D]); xhalo = din("xhalo", [128, D]); xpre = din("xpre", [NPRE, D]); xs = din("xs", [128, D])
    ck = din("ck", [16, 128, 128]); cv = din("cv", [16, 128, 128]); sg = din("sg", [16, 4, 64, 128])
    w_in = din("w_in", [D, INW]); w_o = din("w_o", [D, D]); w_gate = din("w_gate", [D, DFF]); w_up = din("w_up", [D, DFF])
    w_down = din("w_down", [DFF, D])
    attn_g = din("attn_g", [D]); ffn_g = din("ffn_g", [D]); qg_in = din("qng", [64]); kg_in = din("kng", [64])
    sinks = din("sinks", [8]); relb = din("relb", [32, 8]); w2 = din("w2", [16, 256]); bgate = din("bgate", [256])
    glag = din("glag", [128]); flag = din("flag", [128, 1])
    cn = {k: din("c_" + k, v.shape) for k, v in host_consts().items()}

    y_p = dout("y_p", [TOK, D]); y_s = dout("y_s", [128, D])
    kwp = dout("kwp", [128, 128]); vwp = dout("vwp", [128, 128]); gsp = dout("gsp", [4, 64, 128])
    kws = dout("kws", [16, 128, 128]); vws = dout("vws", [16, 128, 128]); gss = dout("gss", [16, 4, 64, 128])
    fscr = nc.dram_tensor("fscr", [2, 8, 256], F32).ap()
    wb = {"w_in": nc.dram_tensor("wb_in", [D, INW], BF16).ap(), "w_o": nc.dram_tensor("wb_o", [D, D], BF16).ap(),
          "w_gate": nc.dram_tensor("wb_gate", [D, DFF], BF16).ap(), "w_up": nc.dram_tensor("wb_up", [D, DFF], BF16).ap(),
          "w_down": nc.dram_tensor("wb_down", [DFF, D], BF16).ap()}
    wf = {"w_in": w_in, "w_o": w_o, "w_gate": w_gate, "w_up": w_up, "w_down": w_down}
    if debug:
        dbg_mix = dout("dbg_mix", [128, 8, 512]); dbg_h = dout("dbg_h", [128, 4, D]); dbg_z = dout("dbg_z", [128, 8, 512]); dbg_a = dout("dbg_a", [128, NKF, 512])
        dbg_q = dout("dbg_q", [128, 4, 512]); dbg_rs = dout("dbg_rs", [128, 4, 512])

    for i in range(6):
        P.banks.append(P.ps([128, 512], F32, f"bank{i}"))
    psT = [P.ps([128, 1024], BF16, f"pst{i}") for i in range(2)]
    pst_rr = [0]

    def tbank():
        i = pst_rr[0] % 2
        pst_rr[0] += 1
        return psT[i], ("pst", i)

    ident_f = P.sb([128, 128], F32); ident_bf = P.sb([128, 128], BF16)
    tri_f = P.sb([128, 128], F32); tris_f = P.sb([128, 128], F32); cmask_bf = P.sb([128, 128], BF16)
    jx_f = P.sb([128, 128], F32); bd_bf = P.sb([128, 128], BF16); sw_bf = P.sb([128, 128], BF16)
    ones_bf = P.sb([128, 128], BF16); zeros_f = P.sb([128, 128], F32); scr_f = P.sb([128, 2048], F32)
    sel_bf = P.sb([128, 16, 128], BF16)
    gaT = P.sb([128, 8], F32); gfT = P.sb([128, 8], F32)
    gq_col = P.sb([128, 1], F32); gk_col = P.sb([128, 1], F32); glag_col = P.sb([128, 1], F32)
    eps_col = P.sb([128, 1], F32); ln8_col = P.sb([128, 1], F32); flag_col = P.sb([128, 1], F32)
    bgate_bc = P.sb([128, 256], F32); w2pad = P.sb([128, 256], BF16)
    relb_pad = P.sb([128, 128], F32)
    hank = scr_f[:, 0:1024].rearrange("p (h s) -> p h s", h=8)
    oh_sb = scr_f[:, 1024:1536].rearrange("p (a b) -> p a b", a=2)
    ftab = scr_f[0:8, 1536:2048].rearrange("p (a b) -> p a b", a=2)
    E = P.sb([128, 2, 2, 2, 2, 128], F32)
    sink_bc = P.sb([128, 8], F32); sinkexp = P.sb([128, 2, 2, 2, 128], F32)
    ring = P.sb([128, NSLOT, 8, 512], BF16)
    xb = P.sb([128, 4, D], F32)
    actT = P.sb([128, 8, 512], BF16)
    nbf = P.sb([128, D], BF16); nbf_b = P.sb([128, D], BF16)
    zt_b = P.sb([128, 256], F32); sp_b = P.sb([128, 256], F32); ktok_b = P.sb([128, 2, 2, 128], BF16)
    ss_c2 = P.sb([128, 2], F32); rs_c2 = P.sb([128, 2], F32)
    ss_c = P.sb([128, 1], F32); rs_c = P.sb([128, 1], F32)
    qhT = P.sb([128, 4, 512], BF16)
    kX = P.sb([128, 4, 640], BF16)
    khT_bf = P.sb([128, 512], BF16); khT_f = P.sb([128, 128], F32)
    Vdup = P.sb([128, 5, 2, 128], BF16)
    qgT = P.sb([128, 2, 512], BF16); kgA = P.sb([128, 2, 512], BF16); kgB = P.sb([128, 2, 512], BF16)
    kgf = P.sb([128, 2, 128], F32)
    ktok = P.sb([128, 2, 2, 128], BF16)
    vg_tok = P.sb([128, 4, 512], BF16)
    rsT = P.sb([128, 4, 512], BF16)
    ulrT = P.sb([128, 512], BF16)
    zt = P.sb([128, 256], F32); sp_t = P.sb([128, 256], F32)
    ebq = P.sb([128, 2, 512], F32); enb = P.sb([128, 2, 512], F32); elast = P.sb([128, 2, 4], F32)
    elb = P.sb([128, 2, 128], F32)
    S = P.sb([128, 2, 128], F32); Stmp = P.sb([128, 2, 128], F32); SA = P.sb([128, 2, 128], BF16); SB = P.sb([128, 2, 128], BF16)
    ATbf = P.sb([128, 4, 128], BF16)
    sq_bf = P.sb([128, 512], BF16); rstd_f = P.sb([128, 512], F32); tmp_f = P.sb([128, 512], F32)
    pe_f = P.sb([128, 512], F32); rec_f = P.sb([128, 512], F32)
    PT = P.sb([128, 2, 2, 2, 2, 128], BF16)
    mixT = P.sb([128, 8, 512], BF16)
    aT = P.sb([128, NKF, 512], BF16)
    sg_f = P.sb([128, 512], F32)
    vw_f = P.sb([128, 128], F32); kw_f = P.sb([128, 128], F32)
    Kw_bf = P.sb([128, 16, 128], BF16); Kwsw_bf = P.sb([128, 16, 128], BF16)
    KTb = P.sb([128, 2, 2, 128], BF16)
    Vwd = P.sb([128, 16, 2, 128], BF16)
    qsA = P.sb([128, 4, 16], BF16); qsB = P.sb([128, 4, 16], BF16)
    Pts = P.sb([128, 16, 8], BF16); pes = P.sb([128, 16, 8], F32)
    Sb = scr_f[:, 0:512].rearrange("p (a c v) -> p a c v", a=2, c=2)
    Wt = scr_f[:, 512:1024].rearrange("p (a c v) -> p a c v", a=2, c=2)
    Sn = scr_f[:, 1024:1536].rearrange("p (a c v) -> p a c v", a=2, c=2)
    WA = P.sb([128, 2, 2, 128], BF16); WB = P.sb([128, 2, 2, 128], BF16)

    CUR = {"p": 0}
    _vwd32 = Vwd[:].rearrange("p a b c -> p (a b c)").bitcast(F32)
    _kwf = Kw_bf[:].rearrange("p a b -> p (a b)")
    _kwswf = Kwsw_bf[:].rearrange("p a b -> p (a b)")
    nbfs2 = (nbf, nbf_b)

    def X(blk):
        if CUR["p"] == 0:
            return xb[:, blk, :]
        src = _vwd32 if blk < 2 else scr_f[:]
        o = (blk % 2) * 1024
        return src[:, o:o + 1024]

    def A(k):
        if CUR["p"] == 0:
            return actT[:, k, :]
        src = _kwf if k < 4 else _kwswf
        o = (k % 4) * 512
        return src[:, o:o + 512]

    def xr(blk):
        return ("x", CUR["p"], blk)

    def ar(blk):
        return ("actT", CUR["p"], blk)

    def ld(dst, src, name, eng="sp"):
        P.dma(eng, dst, src, writes=[name])

    ld(ident_f[:], cn["ident"], "ident_f"); ld(tri_f[:], cn["tri"], "tri"); ld(tris_f[:], cn["tris"], "tris")
    ld(jx_f[:], cn["jx"], "jx"); ld(oh_sb, cn["oh"], "oh")
    P.dma("pool", ident_bf[:], cn["ident"], writes=["ident_bf"])
    P.dma("pool", cmask_bf[:], cn["cmask"], writes=["cmask"])
    P.dma("pool", bd_bf[:], cn["bd"], writes=["bd"])
    P.dma("pool", sw_bf[:], cn["sw"], writes=["sw"])
    P.dma("pool", sel_bf[:], cn["sel"], writes=["sel"])
    ld(gaT[:], attn_g.rearrange("(k p) -> p k", p=128), "gaT"); ld(gfT[:], ffn_g.rearrange("(k p) -> p k", p=128), "gfT")
    for h in range(2):
        ld(gq_col[h * 64:(h + 1) * 64, :], qg_in.rearrange("(p o) -> p o", o=1), "gq")
        ld(gk_col[h * 64:(h + 1) * 64, :], kg_in.rearrange("(p o) -> p o", o=1), "gk")
    ld(glag_col[:], glag.rearrange("(p o) -> p o", o=1), "glag"); ld(flag_col[:], flag, "flag")
    ld(bgate_bc[:], bass.AP(tensor=bgate.tensor, offset=0, ap=[[0, 128], [1, 256]]), "bgate")
    ld(sink_bc[:], bass.AP(tensor=sinks.tensor, offset=0, ap=[[0, 128], [1, 8]]), "sink_bc")
    P.op("dve", lambda e: e.memset(ones_bf[:], 1.0), writes=["ones"])
    P.op("dve", lambda e: e.memset(zeros_f[:], 0.0), writes=["zeros"])
    P.op("dve", lambda e: e.memset(eps_col[:], EPS), writes=["eps"])
    P.op("dve", lambda e: e.memset(ln8_col[:], math.log(0.125)), writes=["ln8"])
    P.op("dve", lambda e: e.memset(w2pad[:], 0.0), writes=["w2pad"])
    P.dma("pool", w2pad[112:128, :], w2, reads=[], writes=["w2pad"])
    P.op("dve", lambda e: e.tensor_scalar(out=gq_col[:], in0=gq_col[:], scalar1=0.125, scalar2=None, op0=ALU.mult),
         reads=["gq"], writes=["gq"])
    for t_ in (kgA, kgB, SA, SB, qsA, qsB, WA, WB):
        P.op("pool", lambda e, t_=t_: e.memset(t_[:], 0.0), writes=["zinit"])
    P.op("pool", lambda e: e.memset(kX[:], 0.0), writes=["kX"])
    P.op("pool", lambda e: e.memset(S[:], 0.0), writes=["S"])
    P.op("pool", lambda e: e.memset(relb_pad[:], 0.0), writes=["relb_pad"])
    P.op("pool", lambda e: e.memset(relb_pad[32:33, :], 1.0), reads=[], writes=["relb_pad"])
    P.dma("sp", relb_pad[0:32, 0:8], relb, writes=["relb_pad"])

    bk, bkr = P.bank()
    P.op("pe", lambda e: e.matmul(bk[:, 0:512], lhsT=relb_pad[:], rhs=scr_f[:, 1024:1536], start=True, stop=True),
         reads=["relb_pad", "oh"], writes=[bkr])
    P.op("act", lambda e: e.activation(out=scr_f[0:8, 1536:2048], in_=bk[0:8, 0:512], func=AF.Copy), reads=[bkr], writes=["ftab"])
    P.dma("sp", fscr.rearrange("k h i -> h k i"), ftab, reads=["ftab"], writes=["fscr"])
    for kb in range(2):
        src = bass.AP(tensor=fscr.tensor, offset=kb * 8 * 256, ap=[[1, 128], [256, 8], [1, 128]])
        P.dma("sp", hank, src, reads=["fscr"], writes=["hank"])
        for hh in range(0, 8, 4):
            bk, bkr = P.bank()

            def mmj(e, bk=bk, hh=hh):
                ins = None
                for q in range(4):
                    ins = e.matmul(bk[:, q * 128:(q + 1) * 128], lhsT=hank[:, hh + q, :], rhs=jx_f[:], start=True, stop=True)
                return ins
            P.op("pe", mmj, reads=["hank", "jx"], writes=[bkr])
            for q in range(4):
                h = hh + q
                c_, half = h // 2, h % 2
                j, cl = c_ // 2, c_ % 2
                P.op("act", lambda e, bk=bk, q=q, j=j, half=half, kb=kb, cl=cl:
                     e.activation(out=E[:, j, half, kb, cl, :], in_=bk[:, q * 128:(q + 1) * 128], func=AF.Exp),
                     reads=[bkr], writes=["E"])
    P.op("act", lambda e: e.activation(out=sink_bc[:], in_=sink_bc[:], func=AF.Exp), reads=["sink_bc"], writes=["sink_bc"])
    for h in range(8):
        c_, half = h // 2, h % 2
        j, cl = c_ // 2, c_ % 2
        P.op("dve", lambda e, h=h, j=j, half=half, cl=cl: e.tensor_scalar(out=sinkexp[:, j, half, cl, :], in0=zeros_f[:], scalar1=sink_bc[:, h:h + 1],
                                                                         scalar2=None, op0=ALU.add), reads=["sink_bc", "zeros"], writes=["sinkexp"])

    wstate = {"n": 0}

    def wload(parts, fp32=False):
        s = wstate["n"] % NSLOT
        wstate["n"] += 1
        res = ("w", s)
        for (c0, (wn, r0, nrows, cc0, ncols_), nk, ncols) in parts:
            src = (wf if fp32 else wb)[wn][r0:r0 + nrows, cc0:cc0 + ncols_].rearrange("(k p) n -> p k n", p=128)
            q_ = "pool" if (fp32 or wstate["n"] % 2 == 0) else "sp"
            P.dma(q_, ring[:, s, 0:nk, c0:c0 + ncols], src, reads=([] if fp32 else [("wbf", wn)]), writes=[res])
        return s, res

    def wsrc(w, r0, nrows, c0, ncols):
        return (w, r0, nrows, c0, ncols)

    def front(src_fn, nb, gT):
        for blk in range(nb):
            P.dma("sp", X(blk), src_fn(blk), writes=[xr(blk)])
            norm_block(blk, gT)

    def norm_block(blk, gT):
        p = CUR["p"]
        xblk = X(blk); xres = xr(blk); ares = ar(blk)
        nb_ = nbfs2[p]; ss = ss_c2[:, p:p + 1]; rs = rs_c2[:, p:p + 1]
        P.op("act", lambda e: e.activation(out=nb_[:], in_=xblk, func=AF.Square, accum_out=ss),
             reads=[xres], writes=[("nbf", p), ("ss_c", p)])
        P.op("act", lambda e: e.activation(out=rs, in_=ss, func=AF.Ln, scale=1.0 / D, bias=eps_col[:, 0:1]),
             reads=[("ss_c", p), "eps"], writes=[("rs_c", p)])
        P.op("act", lambda e: e.activation(out=rs, in_=rs, func=AF.Exp, scale=-0.5), reads=[("rs_c", p)], writes=[("rs_c", p)])
        P.op("dve", lambda e: e.tensor_scalar(out=nb_[:], in0=xblk, scalar1=rs, scalar2=None, op0=ALU.mult),
             reads=[xres, ("rs_c", p)], writes=[("nbf", p)])
        tb, tbr = tbank()

        def tr(e):
            ins = None
            for k in range(8):
                ins = e.transpose(out=tb[:, k * 128:(k + 1) * 128], in_=nb_[:, k * 128:(k + 1) * 128], identity=ident_bf[:])
            return ins
        P.op("pe", tr, reads=[("nbf", p), "ident_bf"], writes=[tbr])
        if p == 0:
            P.op("dve", lambda e: e.tensor_tensor(out=actT[:, :, blk * 128:(blk + 1) * 128], in0=tb[:].rearrange("p (k t) -> p k t", k=8),
                                                  in1=fap(gT[:], [[1, 8], [0, 128]]), op=ALU.mult),
                 reads=[tbr, "gaT", "gfT"], writes=[ares])
        else:
            for kh, src in enumerate((_kwf, _kwswf)):
                P.op("dve", lambda e, kh=kh, src=src: e.tensor_tensor(
                    out=src.rearrange("p (k t) -> p k t", k=4)[:, :, blk * 128:(blk + 1) * 128],
                    in0=tb[:, kh * 512:(kh + 1) * 512].rearrange("p (k t) -> p k t", k=4),
                    in1=fap(gT[:, kh * 4:kh * 4 + 1], [[1, 4], [0, 128]]), op=ALU.mult),
                    reads=[tbr, "gaT", "gfT", ares], writes=[ares])

    def actT_reads(nb):
        return [ar(b) for b in range(nb)]

    def proj_fm(slot, res, col0, T, nb):
        bk, bkr = P.bank()
        acts = [A(k) for k in range(8)]

        def mm(e):
            ins = None
            for k in range(8):
                ins = e.matmul(bk[:, 0:T], lhsT=ring[:, slot, k, col0:col0 + 128], rhs=acts[k][:, 0:T], start=(k == 0), stop=(k == 7))
            return ins
        P.op("pe", mm, reads=[res] + actT_reads(nb), writes=[bkr])
        return bk, bkr

    def proj_tm(slot, res, col0, ncols, blk):
        bk, bkr = P.bank()
        acts = [A(k) for k in range(8)]

        def mm(e):
            ins = None
            for k in range(8):
                ins = e.matmul(bk[:, 0:ncols], lhsT=acts[k][:, blk * 128:(blk + 1) * 128], rhs=ring[:, slot, k, col0:col0 + ncols],
                               start=(k == 0), stop=(k == 7))
            return ins
        P.op("pe", mm, reads=[res, ar(blk)], writes=[bkr])
        return bk, bkr

    def rstd_fm(src_ap, T, lhs_ones, scale, reads_src):
        P.op("act", lambda e: e.activation(out=sq_bf[:, 0:T], in_=src_ap, func=AF.Square), reads=reads_src, writes=["sq"])
        b2, b2r = P.bank()
        P.op("pe", lambda e: e.matmul(b2[:, 0:T], lhsT=lhs_ones[:], rhs=sq_bf[:, 0:T], start=True, stop=True),
             reads=["sq", "bd", "ones"], writes=[b2r])
        P.op("act", lambda e: e.activation(out=rstd_f[:, 0:T], in_=b2[:, 0:T], func=AF.Ln, scale=scale, bias=eps_col[:, 0:1]),
             reads=[b2r, "eps"], writes=["rstd"])
        P.op("act", lambda e: e.activation(out=rstd_f[:, 0:T], in_=rstd_f[:, 0:T], func=AF.Exp, scale=-0.5), reads=["rstd"], writes=["rstd"])

    def gla_prep(slot_lr, res_lr, lrcol, nb, T, tri_ap, tri_res, full):
        bk, bkr = proj_fm(slot_lr, res_lr, lrcol, T, nb)
        P.op("act", lambda e: e.activation(out=ulrT[:, 0:T], in_=bk[:, 0:T], func=AF.Copy), reads=[bkr], writes=["ulrT"])
        bT = [P.bank(hold=True) for _ in range(2)]
        for blk in range(nb):
            zb, zbr = P.bank()
            P.op("pe", lambda e, zb=zb, blk=blk: e.matmul(zb[:, 0:256], lhsT=ulrT[:, blk * 128:(blk + 1) * 128], rhs=w2pad[:], start=True, stop=True),
                 reads=["ulrT", "w2pad"], writes=[zbr])
            P.op("dve", lambda e, zb=zb: e.tensor_tensor(out=zt[:], in0=zb[:, 0:256], in1=bgate_bc[:], op=ALU.add),
                 reads=[zbr, "bgate"], writes=["zt"])
            P.op("act", lambda e: e.activation(out=zt[:], in_=zt[:], func=AF.Exp, scale=-1.0), reads=["zt"], writes=["zt"])
            P.op("act", lambda e: e.activation(out=sp_t[:], in_=zt[:], func=AF.Ln, bias=1.0), reads=["zt"], writes=["sp_t"])
            for c in range(2):
                P.op("pe", lambda e, c=c, blk=blk: e.matmul(bT[c][0][:, blk * 128:(blk + 1) * 128], lhsT=sp_t[:, c * 128:(c + 1) * 128], rhs=tri_ap,
                                                           start=True, stop=True), reads=["sp_t", tri_res], writes=[bT[c][1]])
        for c in range(2):
            P.release(bT[c][1])
        for c in range(2):
            P.op("act", lambda e, c=c: e.activation(out=enb[:, c, 0:T], in_=bT[c][0][:, 0:T], func=AF.Exp, scale=-1.0), reads=[bT[c][1]], writes=["enb"])
            P.op("act", lambda e, c=c: e.activation(out=elast[:, c, 0:nb], in_=fap(bT[c][0][:, 127:128], [[128, nb]]), func=AF.Exp),
                 reads=[bT[c][1]], writes=["elast"])
            if full:
                P.op("act", lambda e, c=c: e.activation(out=ebq[:, c, 0:T], in_=bT[c][0][:, 0:T], func=AF.Exp, bias=ln8_col[:, 0:1]),
                     reads=[bT[c][1], "ln8"], writes=["ebq"])
                if T == 128:
                    P.op("act", lambda e, c=c: e.activation(out=elb[:, c, :], in_=bT[c][0][:, 0:128], func=AF.Exp), reads=[bT[c][1]], writes=["elb"])

    def kg_evac(slot, res, col0, nb, T, sample=False):
        for c in range(2):
            bk, bkr = proj_fm(slot, res, col0 + c * 128, T, nb)
            P.op("dve", lambda e, bk=bk, c=c: e.tensor_tensor(out=kgA[0:64, c, 0:T], in0=bk[0:64, 0:T], in1=enb[0:64, c, 0:T], op=ALU.mult),
                 reads=[bkr, "enb", "zinit"], writes=[("kgA", c)])
            P.op("dve", lambda e, bk=bk, c=c: e.tensor_tensor(out=kgB[64:128, c, 0:T], in0=bk[64:128, 0:T], in1=enb[64:128, c, 0:T], op=ALU.mult),
                 reads=[bkr, "enb", "zinit"], writes=[("kgB", c)])
            if sample:
                P.op("dve", lambda e, bk=bk, c=c: e.tensor_tensor(out=kgf[:, c, :], in0=bk[:, 0:128], in1=enb[:, c, 0:128], op=ALU.mult),
                     reads=[bkr, "enb"], writes=["kgf"])

    def vg_tm(slot, res, col0, nb):
        for blk in range(nb):
            bk, bkr = proj_tm(slot, res, col0, 512, blk)
            P.op("act", lambda e, bk=bk, blk=blk: e.activation(out=vg_tok[:, blk, :], in_=bk[:, 0:512], func=AF.Copy), reads=[bkr], writes=[("vg", blk)])

    def state_update(blk, masked):
        tb, tbr = tbank()

        def tr(e):
            ins = None
            for c in range(2):
                for ab, src in enumerate((kgA, kgB)):
                    o = (c * 2 + ab) * 128
                    ins = e.transpose(out=tb[:, o:o + 128], in_=src[:, c, blk * 128:(blk + 1) * 128], identity=ident_bf[:])
            return ins
        P.op("pe", tr, reads=[("kgA", 0), ("kgA", 1), ("kgB", 0), ("kgB", 1), "ident_bf"], writes=[tbr])
        P.op("act", lambda e: e.activation(out=ktok[:].rearrange("p c a f -> p (c a f)"), in_=tb[:, 0:512], func=AF.Copy), reads=[tbr], writes=["ktok"])
        ub, ubr = P.bank()

        def mm(e):
            ins = None
            for c in range(2):
                for ab in range(2):
                    h = 2 * c + ab
                    ins = e.matmul(ub[:, c * 128:(c + 1) * 128], lhsT=ktok[:, c, ab, :], rhs=vg_tok[:, blk, h * 128:(h + 1) * 128],
                                   start=(ab == 0), stop=(ab == 1))
            return ins
        P.op("pe", mm, reads=["ktok", ("vg", blk)], writes=[ubr])
        P.op("dve", lambda e: e.tensor_tensor(out=Stmp[:].rearrange("p c v -> p (c v)"), in0=ub[:, 0:256], in1=S[:].rearrange("p c v -> p (c v)"), op=ALU.add),
             reads=[ubr, "S"], writes=["Stmp"])
        P.op("dve", lambda e: e.tensor_tensor(out=S[:], in0=Stmp[:], in1=fap(elast[:, 0, blk:blk + 1], [[4, 2], [0, 128]]), op=ALU.mult),
             reads=["Stmp", "elast", "SA", "SB"], writes=["S"])
        if masked:
            P.op("act", lambda e: e.activation(out=SA[0:64], in_=S[0:64], func=AF.Copy), reads=["S", "zinit"], writes=["SA"])
            P.op("act", lambda e: e.activation(out=SB[64:128], in_=S[64:128], func=AF.Copy), reads=["S", "zinit"], writes=["SB"])

    def gla_out_norm(ob, obr, ncol, W, col0):
        rstd_fm(ob[:, 0:ncol], ncol, ones_bf, 1.0 / 128, [obr])
        P.op("dve", lambda e: e.scalar_tensor_tensor(out=tmp_f[:, 0:ncol], in0=ob[:, 0:ncol], scalar=glag_col[:, 0:1], in1=rstd_f[:, 0:ncol],
                                                     op0=ALU.mult, op1=ALU.mult), reads=[obr, "rstd", "glag"], writes=["tmp_f"])
        P.op("dve", lambda e: e.tensor_tensor(out=mixT[:, 4:8, col0:col0 + W], in0=tmp_f[:, 0:ncol].rearrange("p (h w) -> p h w", h=4),
                                              in1=rsT[:, :, col0:col0 + W], op=ALU.mult), reads=["tmp_f", "rsT"], writes=[("mixg", col0)])

    def gla_block(blk):
        ab_, abr = P.bank()

        def mm1(e):
            ins = None
            for h in range(4):
                c, half = h // 2, h % 2
                src = kgA if half == 0 else kgB
                ins = e.matmul(ab_[:, h * 128:(h + 1) * 128], lhsT=src[:, c, blk * 128:(blk + 1) * 128], rhs=qgT[:, c, blk * 128:(blk + 1) * 128],
                               start=True, stop=True)
            return ins
        P.op("pe", mm1, reads=[("kgA", 0), ("kgA", 1), ("kgB", 0), ("kgB", 1), "qgT"], writes=[abr])
        P.op("dve", lambda e: e.tensor_tensor(out=ATbf[:], in0=ab_[:, 0:512].rearrange("p (h t) -> p h t", h=4), in1=fap(cmask_bf[:], [[0, 4], [1, 128]]),
                                              op=ALU.mult), reads=[abr, "cmask"], writes=["ATbf"])
        ob, obr = P.bank()

        def mm2(e):
            ins = None
            for h in range(4):
                c, half = h // 2, h % 2
                sm = SA if half == 0 else SB
                e.matmul(ob[:, h * 128:(h + 1) * 128], lhsT=vg_tok[:, blk, h * 128:(h + 1) * 128], rhs=ATbf[:, h, :], start=True, stop=False)
                ins = e.matmul(ob[:, h * 128:(h + 1) * 128], lhsT=sm[:, c, :], rhs=qgT[:, c, blk * 128:(blk + 1) * 128], start=False, stop=True)
            return ins
        P.op("pe", mm2, reads=[("vg", blk), "ATbf", "SA", "SB", "qgT"], writes=[obr])
        gla_out_norm(ob, obr, 512, 128, blk * 128)
        state_update(blk, True)

    def attn_block(blk, Etab, Eres, useflag=False):
        for j in range(2):
            sbk = []
            for half in range(2):
                bk, bkr = P.bank()
                sbk.append((bk, bkr))

                def mm(e, bk=bk, half=half, j=j):
                    ins = None
                    for kb in range(2):
                        kc = (blk + kb) * 128
                        ins = e.matmul(bk[:, kb * 256:(kb + 1) * 256], lhsT=kX[:, 2 * j + half, kc:kc + 128],
                                       rhs=qhT[:, 2 * j:2 * j + 2, blk * 128:(blk + 1) * 128], start=True, stop=True)
                    return ins
                P.op("pe", mm, reads=["kX", "qhT"], writes=[bkr])
            for half in range(2):
                bk, bkr = sbk[half]
                P.op("act", lambda e, bk=bk: e.activation(out=pe_f[:], in_=bk[:, 0:512], func=AF.Exp), reads=[bkr], writes=["pe_f"])
                P.op("dve", lambda e, half=half, j=j: e.tensor_tensor(out=PT[:, j, half].rearrange("p a b q -> p (a b q)"), in0=pe_f[:],
                                                                     in1=Etab[:, j, half].rearrange("p a b q -> p (a b q)"), op=ALU.mult),
                     reads=["pe_f", Eres], writes=[("PT", j)])
                if useflag:
                    P.op("dve", lambda e, half=half, j=j: e.tensor_scalar(out=PT[:, j, half, 0], in0=PT[:, j, half, 0], scalar1=flag_col[:, 0:1],
                                                                          scalar2=None, op0=ALU.mult), reads=[("PT", j), "flag"], writes=[("PT", j)])
            ob, obr = P.bank()
            db, dbr = P.bank()

            def mmv(e, ob=ob, db=db, j=j):
                ins = None
                for kb in range(2):
                    rhs = PT[:, j, :, kb, :, :]
                    e.matmul(ob[:, 0:512], lhsT=Vdup[:, blk + kb, j, :], rhs=rhs, start=(kb == 0), stop=(kb == 1))
                for kb in range(2):
                    rhs = PT[:, j, :, kb, :, :]
                    ins = e.matmul(db[:, 0:512], lhsT=ones_bf[:], rhs=rhs, start=(kb == 0), stop=(kb == 1))
                return ins
            P.op("pe", mmv, reads=[("PT", j), "Vdup", "ones"], writes=[obr, dbr])
            P.op("dve", lambda e, db=db, j=j: e.tensor_tensor(out=rec_f[:], in0=db[:, 0:512], in1=sinkexp[:, j].rearrange("p a b q -> p (a b q)"), op=ALU.add),
                 reads=[dbr, "sinkexp"], writes=["rec"])
            P.op("act", lambda e: e.activation(out=rec_f[:], in_=rec_f[:], func=AF.Ln), reads=["rec"], writes=["rec"])
            P.op("act", lambda e: e.activation(out=rec_f[:], in_=rec_f[:], func=AF.Exp, scale=-1.0), reads=["rec"], writes=["rec"])
            for half in range(2):
                r0 = half * 64
                P.op("dve", lambda e, ob=ob, half=half, r0=r0, j=j: e.tensor_tensor(
                    out=mixT[r0:r0 + 64, 2 * j:2 * j + 2, blk * 128:(blk + 1) * 128],
                    in0=ob[r0:r0 + 64, half * 256:(half + 1) * 256].rearrange("p (c q) -> p c q", c=2),
                    in1=rec_f[r0:r0 + 64, half * 256:(half + 1) * 256].rearrange("p (c q) -> p c q", c=2), op=ALU.mult),
                    reads=[obr, "rec"], writes=[("mixa", blk)])

    def qk_norm_chunk(bk, bkr, T, gcol, gres, out_fn):
        rstd_fm(bk[:, 0:T], T, bd_bf, 1.0 / 64, [bkr])
        out_fn(bk, bkr)

    def wo_ffnnorm(nb, s0, r0, s1, r1):
        for blk in range(nb):
            for cg, (s, r) in enumerate(((s0, r0), (s1, r1))):
                bk, bkr = P.bank()

                def mm(e, bk=bk, s=s, blk=blk):
                    ins = None
                    for k in range(8):
                        ins = e.matmul(bk[:, 0:512], lhsT=mixT[:, k, blk * 128:(blk + 1) * 128], rhs=ring[:, s, k, 0:512], start=(k == 0), stop=(k == 7))
                    return ins
                P.op("pe", mm, reads=[r, ("mixa", blk), ("mixg", blk * 128)], writes=[bkr])
                xs_ = X(blk)[:, cg * 512:(cg + 1) * 512]
                P.op("dve", lambda e, bk=bk, xs_=xs_: e.tensor_tensor(out=xs_, in0=bk[:, 0:512], in1=xs_, op=ALU.add), reads=[bkr, xr(blk)], writes=[xr(blk)])
            norm_block(blk, gfT)

    def ffn(nb, T, ydst_fn):
        for s6 in range(6):
            ncols = 512 if s6 < 5 else 256
            sg_, rg_ = wload([(0, wsrc("w_gate", 0, D, s6 * 512, ncols), 8, ncols)])
            su_, ru_ = wload([(0, wsrc("w_up", 0, D, s6 * 512, ncols), 8, ncols)])
            for mi in range(ncols // 128):
                m = s6 * 4 + mi
                gb, gbr = proj_fm(sg_, rg_, mi * 128, T, nb)
                ubk, ubr = proj_fm(su_, ru_, mi * 128, T, nb)
                P.op("act", lambda e, gb=gb: e.activation(out=sg_f[:, 0:T], in_=gb[:, 0:T], func=AF.Silu), reads=[gbr], writes=["sg_f"])
                P.op("dve", lambda e, ubk=ubk, m=m: e.tensor_tensor(out=aT[:, m, 0:T], in0=ubk[:, 0:T], in1=sg_f[:, 0:T], op=ALU.mult),
                     reads=[ubr, "sg_f"], writes=[("aT", m)])
        for cg in range(2):
            bks = [P.bank(hold=True) for _ in range(nb)]
            for kgp in range(3):
                nk = 8 if kgp < 2 else 6
                sl = wload([(0, wsrc("w_down", kgp * 1024, nk * 128, cg * 512, 512), nk, 512)])
                for blk in range(nb):
                    bk, bkr = bks[blk]

                    def mm(e, bk=bk, blk=blk, sl=sl, kgp=kgp, nk=nk):
                        ins = None
                        for kk in range(nk):
                            k = kgp * 8 + kk
                            ins = e.matmul(bk[:, 0:512], lhsT=aT[:, k, blk * 128:(blk + 1) * 128], rhs=ring[:, sl[0], kk, 0:512], start=(k == 0), stop=(k == NKF - 1))
                        return ins
                    P.op("pe", mm, reads=[sl[1]] + [("aT", kgp * 8 + kk) for kk in range(nk)], writes=[bkr])
            for blk in range(nb):
                bk, bkr = bks[blk]
                P.release(bkr)
                xs_ = X(blk)[:, cg * 512:(cg + 1) * 512]
                P.op("dve", lambda e, bk=bk, xs_=xs_: e.tensor_tensor(out=xs_, in0=bk[:, 0:512], in1=xs_, op=ALU.add), reads=[bkr, xr(blk)], writes=[xr(blk)])
                if cg == 1:
                    outs.append(P.dma("sp", ydst_fn(blk), X(blk), reads=[xr(blk)]))

    outs = []

    wstate["n"] = 0
    s_a, r_a = wload([(0, wsrc("w_in", 0, D, 1024, 256), 8, 256), (256, wsrc("w_in", 0, D, 2192, 128), 8, 128)], fp32=True)
    s_b, r_b = wload([(0, wsrc("w_in", 0, D, 1280, 512), 8, 512)], fp32=True)
    for k in range(8):
        P.op("dve", lambda e, k=k: e.tensor_scalar(out=ring[:, s_a, k, 0:384], in0=ring[:, s_a, k, 0:384], scalar1=gaT[:, k:k + 1], scalar2=None, op0=ALU.mult),
             reads=[r_a, "gaT"], writes=[r_a])
        P.op("dve", lambda e, k=k: e.tensor_scalar(out=ring[:, s_b, k, 0:512], in0=ring[:, s_b, k, 0:512], scalar1=gaT[:, k:k + 1], scalar2=None, op0=ALU.mult),
             reads=[r_b, "gaT"], writes=[r_b])
    negs = P.sb([128, 2], BF16)
    P.op("dve", lambda e: e.memset(negs[:], -1.0 / 16), writes=["negs"])
    tri_bf = P.sb([128, 128], BF16)
    P.op("dve", lambda e: e.tensor_copy(out=tri_bf[:], in_=tri_f[:]), reads=["tri"], writes=["tri_bf"])
    sp_h = (P.sb([128, 256], BF16), P.sb([128, 256], BF16))
    _enbflat = enb[:].rearrange("p c t -> p (c t)")
    _qgflat = qgT[:].rearrange("p c t -> p (c t)")
    for wn in ("w_in", "w_o", "w_gate", "w_up", "w_down"):
        nr = wf[wn].shape[0]
        step = 256
        for r0 in range(0, nr, step):
            r1 = min(nr, r0 + step)
            ci = P.dma("pool", wb[wn][r0:r1, :], wf[wn][r0:r1, :], reads=["wbfchain"], writes=[("wbf", wn), "wbfchain"])
            P.nofence = getattr(P, "nofence", set()) | {ci}
    NBLK = (16 if debug == "scan" else 0) if debug else NPRE // 128
    nbfs = (nbf, nbf_b); zts = (zt, zt_b); sps = (sp_t, sp_b); ktoks = (ktok, ktok_b)
    sbanks = {}

    def sc1(b):
        q = b % 4; p = b % 2
        P.dma("sp", xb[:, q, :], xpre[b * 128:(b + 1) * 128, :], writes=[("sx", q)])
        P.op("act", lambda e: e.activation(out=nbfs[p][:], in_=xb[:, q, :], func=AF.Square, accum_out=ss_c2[:, p:p + 1]),
             reads=[("sx", q)], writes=[("snbf", p), ("sss", p)])
        P.op("act", lambda e: e.activation(out=rs_c2[:, p:p + 1], in_=ss_c2[:, p:p + 1], func=AF.Ln, scale=1.0 / D, bias=eps_col[:, 0:1]),
             reads=[("sss", p), "eps"], writes=[("srs", p)])
        P.op("act", lambda e: e.activation(out=rs_c2[:, p:p + 1], in_=rs_c2[:, p:p + 1], func=AF.Exp, scale=-0.5), reads=[("srs", p)], writes=[("srs", p)])
        P.op("dve", lambda e: e.tensor_scalar(out=nbfs[p][:], in0=xb[:, q, :], scalar1=rs_c2[:, p:p + 1], scalar2=None, op0=ALU.mult),
             reads=[("sx", q), ("srs", p)], writes=[("snbf", p)])
        tb, tbr = tbank()

        def tr(e):
            ins = None
            for k in range(8):
                ins = e.transpose(out=tb[:, k * 128:(k + 1) * 128], in_=nbfs[p][:, k * 128:(k + 1) * 128], identity=ident_bf[:])
            return ins
        P.op("pe", tr, reads=[("snbf", p), "ident_bf"], writes=[tbr])
        if b % 2 == 0:
            P.op("act", lambda e: e.activation(out=actT[:, :, q * 128:(q + 1) * 128], in_=tb[:].rearrange("p (k t) -> p k t", k=8), func=AF.Copy),
                 reads=[tbr], writes=[("sact", q)])
        else:
            P.op("dve", lambda e: e.tensor_copy(out=actT[:, :, q * 128:(q + 1) * 128], in_=tb[:].rearrange("p (k t) -> p k t", k=8)),
                 reads=[tbr], writes=[("sact", q)])

    def sc2(b):
        q = b % 4
        ab, abr = P.bank()
        vb, vbr = P.bank()

        def mm(e):
            ins = None
            for k in range(8):
                ins = e.matmul(ab[:, 0:128], lhsT=ring[:, s_a, k, 256:384], rhs=actT[:, k, q * 128:(q + 1) * 128], start=(k == 0), stop=(k == 7))
            for k in range(8):
                ins = e.matmul(vb[:, 0:512], lhsT=actT[:, k, q * 128:(q + 1) * 128], rhs=ring[:, s_b, k, 0:512], start=(k == 0), stop=(k == 7))
            return ins
        P.op("pe", mm, reads=[r_a, r_b, ("sact", q)], writes=[abr, vbr])
        P.op("act", lambda e: e.activation(out=ulrT[:, q * 128:(q + 1) * 128], in_=ab[:, 0:128], func=AF.Copy), reads=[abr], writes=[("sulr", q)])
        P.op("act", lambda e: e.activation(out=vg_tok[:, q, :], in_=vb[:, 0:512], func=AF.Copy), reads=[vbr], writes=[("svg", q)])

    def sc3a(b):
        q = b % 4; p = b % 2
        zb, zbr = P.bank()
        sbanks[b] = (zb, zbr)
        P.op("pe", lambda e: e.matmul(zb[:, 0:256], lhsT=ulrT[:, q * 128:(q + 1) * 128], rhs=w2pad[:], start=True, stop=True),
             reads=[("sulr", q), "w2pad"], writes=[zbr])
        P.op("dve", lambda e: e.tensor_tensor(out=zts[p][:], in0=zb[:, 0:256], in1=bgate_bc[:], op=ALU.add), reads=[zbr, "bgate"], writes=[("szt", p)])
        P.op("act", lambda e: e.activation(out=zts[p][:], in_=zts[p][:], func=AF.Exp, scale=-1.0), reads=[("szt", p)], writes=[("szt", p)])
        P.op("act", lambda e: e.activation(out=sp_h[p][:], in_=zts[p][:], func=AF.Ln, bias=1.0), reads=[("szt", p)], writes=[("ssp", p)])

    def sc3b(b):
        q = b % 4; p = b % 2
        kb_, kbr = P.bank()
        cb, cbr = P.bank()
        en_ = _enbflat[:, q * 256:(q + 1) * 256]
        kt_ = _qgflat[:, q * 256:(q + 1) * 256]

        def mm(e):
            ins = None
            for k in range(8):
                ins = e.matmul(kb_[:, 0:256], lhsT=actT[:, k, q * 128:(q + 1) * 128], rhs=ring[:, s_a, k, 0:256], start=(k == 0), stop=(k == 7))
            return ins
        P.op("pe", mm, reads=[r_a, ("sact", q)], writes=[kbr])

        def mmc(e):
            e.matmul(cb[:, 0:256], lhsT=tri_bf[:], rhs=sp_h[p][:], start=True, stop=True)
            ins = None
            for c in range(2):
                ins = e.matmul(cb[:, 256 + 2 * c:258 + 2 * c], lhsT=sp_h[p][:, c * 128:(c + 1) * 128], rhs=negs[:], start=True, stop=True)
            return ins
        P.op("pe", mmc, reads=[("ssp", p), "tri_bf", "negs"], writes=[cbr])
        P.op("act", lambda e: e.activation(out=en_, in_=cb[:, 0:256], func=AF.Exp, scale=-1.0), reads=[cbr], writes=[("senb", q)])
        P.op("act", lambda e: e.activation(out=elast[:, :, q], in_=fap(cb[:, 256:257], [[2, 2]]), func=AF.Exp), reads=[cbr], writes=[("sel", q)])
        P.op("dve", lambda e: e.tensor_tensor(out=kt_, in0=kb_[:, 0:256], in1=en_, op=ALU.mult),
             reads=[kbr, ("senb", q)], writes=[("sktok", q)])

    def sc4(b):
        q = b % 4; p = b % 2
        kt = _qgflat[:, q * 256:(q + 1) * 256]
        ub, ubr = P.bank()

        def mm(e):
            ins = None
            for c in range(2):
                ins = e.matmul(ub[:, c * 256:(c + 1) * 256], lhsT=kt[:, c * 128:(c + 1) * 128], rhs=vg_tok[:, q, c * 256:(c + 1) * 256], start=True, stop=True)
            return ins
        P.op("pe", mm, reads=[("sktok", q), ("svg", q)], writes=[ubr])
        for ab in range(2):
            r0 = ab * 64
            P.op("dve", lambda e, ab=ab, r0=r0: e.tensor_tensor(out=Stmp[r0:r0 + 64], in0=fap(ub[r0:r0 + 64, ab * 128:ab * 128 + 1], [[256, 2], [1, 128]]),
                                                               in1=S[r0:r0 + 64], op=ALU.add), reads=[ubr, "S", "Stmp"], writes=["Stmp"])
        P.op("dve", lambda e: e.tensor_tensor(out=S[:], in0=Stmp[:], in1=fap(elast[:, 0, q:q + 1], [[4, 2], [0, 128]]), op=ALU.mult),
             reads=["Stmp", ("sel", q)], writes=["S"])

    stages = (sc1, sc2, sc3a, sc3b, sc4)
    for i in range(NBLK + len(stages) - 1):
        for si, st in enumerate(stages):
            b = i - si
            if 0 <= b < NBLK:
                st(b)
    P.barrier(keep=("wbf", "wbfchain"))
    P.op("act", lambda e: e.activation(out=SA[0:64], in_=S[0:64], func=AF.Copy), reads=["S", "zinit"], writes=["SA"])
    P.op("act", lambda e: e.activation(out=SB[64:128], in_=S[64:128], func=AF.Copy), reads=["S", "zinit"], writes=["SB"])

    def main_tile(kind, t):
        sample = kind == "sample"
        nb = 1 if sample else 4
        T = nb * 128
        CUR["p"] = 0 if sample else (t % 2)
        first = (kind == "prompt" and t == 0)
        last = (kind == "prompt" and t == 3)
        L0 = wload([(0, wsrc("w_in", 0, D, 0, 512), 8, 512)])
        L1 = wload([(0, wsrc("w_in", 0, D, 512, 512), 8, 512)])
        L2 = wload([(0, wsrc("w_in", 0, D, 1024, 256), 8, 256), (256, wsrc("w_in", 0, D, 2192, 128), 8, 128)])
        if first:
            front(lambda blk: xhalo, 1, gaT)
            halo_kv = True
            kv_part(L1, 1, 128, 0, False, False)
        if sample:
            front(lambda blk: xs, 1, gaT)
        else:
            front(lambda blk: xp[t * 512 + blk * 128: t * 512 + (blk + 1) * 128, :], 4, gaT)
        gla_prep(L2[0], L2[1], 256, nb, T, (tris_f if sample else tri_f)[:], "tris" if sample else "tri", True)
        for c in range(4):
            bk, bkr = proj_fm(L0[0], L0[1], c * 128, T, nb)
            rstd_fm(bk[:, 0:T], T, bd_bf, 1.0 / 64, [bkr])
            P.op("dve", lambda e, bk=bk, c=c: e.scalar_tensor_tensor(out=qhT[:, c, 0:T], in0=bk[:, 0:T], scalar=gq_col[:, 0:1], in1=rstd_f[:, 0:T],
                                                                    op0=ALU.mult, op1=ALU.mult), reads=[bkr, "rstd", "gq"], writes=["qhT"])
        kv_part(L1, nb, T, 1, last, sample)
        for c in range(2):
            bk, bkr = proj_fm(L1[0], L1[1], 256 + c * 128, T, nb)
            P.op("dve", lambda e, bk=bk, c=c: e.tensor_tensor(out=qgT[:, c, 0:T], in0=bk[:, 0:T], in1=ebq[:, c, 0:T], op=ALU.mult),
                 reads=[bkr, "ebq"], writes=["qgT"])
        kg_evac(L2[0], L2[1], 0, nb, T, sample)
        L3 = wload([(0, wsrc("w_in", 0, D, 1280, 512), 8, 512)])
        vg_tm(L3[0], L3[1], 0, nb)
        L4 = wload([(0, wsrc("w_in", 0, D, 1792, 512), 8, 512)])
        for c in range(4):
            bk, bkr = proj_fm(L4[0], L4[1], c * 128, T, nb)
            P.op("act", lambda e, bk=bk, c=c: e.activation(out=rsT[:, c, 0:T], in_=bk[:, 0:T], func=AF.Silu), reads=[bkr], writes=["rsT"])
        if sample:
            sample_attn()
            sample_gla()
        else:
            for blk in range(nb):
                attn_block(blk, E, "E", first and blk == 0)
                gla_block(blk)
            P.op("pool", lambda e: e.tensor_copy(out=kX[:, :, 0:128], in_=kX[:, :, 512:640]), reads=["kX"], writes=["kX"])
            P.op("pool", lambda e: e.tensor_copy(out=Vdup[:, 0], in_=Vdup[:, 4]), reads=["Vdup"], writes=["Vdup"])
        if debug:
            outs.append(P.dma("pool", dbg_mix, mixT[:], reads=[("mixa", b_) for b_ in range(nb)] + [("mixg", b_ * 128) for b_ in range(nb)]))
            outs.append(P.dma("pool", dbg_q, qhT[:], reads=["qhT"]))
            outs.append(P.dma("pool", dbg_rs, rsT[:], reads=["rsT"]))
        L5 = wload([(0, wsrc("w_o", 0, D, 0, 512), 8, 512)])
        L6 = wload([(0, wsrc("w_o", 0, D, 512, 512), 8, 512)])
        wo_ffnnorm(nb, L5[0], L5[1], L6[0], L6[1])
        if debug:
            outs.append(P.dma("sp", dbg_h, xb[:], reads=[("x", 0, b_) for b_ in range(nb)]))
            outs.append(P.dma("pool", dbg_z, actT[:], reads=[("actT", 0, b_) for b_ in range(nb)]))
        if sample:
            ffn(nb, T, lambda blk: y_s)
        else:
            ffn(nb, T, lambda blk: y_p[t * 512 + blk * 128: t * 512 + (blk + 1) * 128, :])

    def kv_part(L1, nb, T, kblk0, last, sample):
        bk, bkr = proj_fm(L1[0], L1[1], 0, T, nb)
        rstd_fm(bk[:, 0:T], T, bd_bf, 1.0 / 64, [bkr])
        c0 = kblk0 * 128
        P.op("dve", lambda e: e.scalar_tensor_tensor(out=khT_bf[:, 0:T], in0=bk[:, 0:T], scalar=gk_col[:, 0:1], in1=rstd_f[:, 0:T], op0=ALU.mult, op1=ALU.mult),
             reads=[bkr, "rstd", "gk"], writes=["khT"])
        if last or sample:
            lo = T - 128
            P.op("dve", lambda e: e.scalar_tensor_tensor(out=khT_f[:], in0=bk[:, lo:T], scalar=gk_col[:, 0:1], in1=rstd_f[:, lo:T], op0=ALU.mult, op1=ALU.mult),
                 reads=[bkr, "rstd", "gk"], writes=["khT_f"])
        P.op("act", lambda e: e.activation(out=kX[0:64, 0, c0:c0 + T], in_=khT_bf[0:64, 0:T], func=AF.Copy), reads=["khT", "kX"], writes=["kX"])
        P.op("act", lambda e: e.activation(out=kX[64:128, 3, c0:c0 + T], in_=khT_bf[64:128, 0:T], func=AF.Copy), reads=["khT", "kX"], writes=["kX"])
        b2, b2r = P.bank()
        P.op("pe", lambda e: e.matmul(b2[:, 0:T], lhsT=sw_bf[:], rhs=khT_bf[:, 0:T], start=True, stop=True), reads=["khT", "sw"], writes=[b2r])
        P.op("act", lambda e: e.activation(out=kX[0:64, 2, c0:c0 + T], in_=b2[0:64, 0:T], func=AF.Copy), reads=[b2r, "kX"], writes=["kX"])
        P.op("act", lambda e: e.activation(out=kX[64:128, 1, c0:c0 + T], in_=b2[64:128, 0:T], func=AF.Copy), reads=[b2r, "kX"], writes=["kX"])
        for blk in range(nb):
            vb, vbr = proj_tm(L1[0], L1[1], 128, 128, blk)
            P.op("act", lambda e, vb=vb, blk=blk: e.activation(out=Vdup[:, kblk0 + blk].rearrange("p j (u d) -> p j u d", u=2),
                                                               in_=fap(vb[:, 0:1], [[64, 2], [0, 2], [1, 64]]), func=AF.Copy),
                 reads=[vbr, "Vdup"], writes=["Vdup"])
            if (last and blk == nb - 1) or sample:
                P.op("dve", lambda e, vb=vb: e.tensor_copy(out=vw_f[:], in_=vb[:, 0:128]), reads=[vbr], writes=["vw_f"])
        if last or sample:
            kb_, kbr = P.bank()
            P.op("pe", lambda e: e.transpose(out=kb_[:, 0:128], in_=khT_f[:], identity=ident_f[:]), reads=["khT_f", "ident_f"], writes=[kbr])
            P.op("act", lambda e: e.activation(out=kw_f[:], in_=kb_[:, 0:128], func=AF.Copy), reads=[kbr], writes=["kw_f"])
        if last:
            outs.append(P.dma("sp", kwp, kw_f[:], reads=["kw_f"]))
            outs.append(P.dma("sp", vwp, vw_f[:], reads=["vw_f"]))

    def sample_attn():
        for (dst, src, new, nm) in ((kws, ck, kw_f, "kws"), (vws, cv, vw_f, "vws")):
            P.dma("sp", dst[:, 0:127, :], src[:, 1:128, :], writes=[nm])
            P.dma("sp", bass.AP(tensor=dst.tensor, offset=127 * 128, ap=[[128 * 128, 16], [1, 128]]), new[0:16, :], reads=["kw_f", "vw_f"], writes=[nm])
        t1 = P.dma("pool", Kw_bf[:], kws.rearrange("b k f -> k b f"), reads=["kws"], writes=["Kw"])
        for j in range(2):
            P.dma("pool", Kwsw_bf[:, :, (1 - j) * 64:(2 - j) * 64], kws[:, :, j * 64:(j + 1) * 64].rearrange("b k f -> k b f"), reads=["kws"], writes=["Kwsw"])
            for u in range(2):
                P.dma("pool", Vwd[:, :, j, u * 64:(u + 1) * 64], vws[:, :, j * 64:(j + 1) * 64].rearrange("b k f -> k b f"), reads=["vws"], writes=["Vwd"])
        outs.append(t1)
        P.op("dve", lambda e: e.tensor_copy(out=qsA[0:64], in_=qhT[0:64, :, 0:16]), reads=["qhT", "zinit"], writes=["qsA"])
        P.op("dve", lambda e: e.tensor_copy(out=qsB[64:128], in_=qhT[64:128, :, 0:16]), reads=["qhT", "zinit"], writes=["qsB"])
        sb_, sbr = P.bank()
        for b in range(16):
            tb, tbr = tbank()
            bf_ = b % 2

            def tr(e, tb=tb, b=b):
                e.transpose(out=tb[:, 0:128], in_=Kw_bf[:, b, :], identity=ident_bf[:])
                return e.transpose(out=tb[:, 128:256], in_=Kwsw_bf[:, b, :], identity=ident_bf[:])
            P.op("pe", tr, reads=["Kw", "Kwsw", "ident_bf"], writes=[tbr])
            P.op("act", lambda e, tb=tb, bf_=bf_: e.activation(out=KTb[:, bf_].rearrange("p a k -> p (a k)"), in_=tb[:, 0:256], func=AF.Copy),
                 reads=[tbr], writes=[("KTb", bf_)])

            def mm(e, b=b, bf_=bf_):
                ins = None
                for j in range(2):
                    for half in range(2):
                        kt = KTb[:, bf_, 0 if j == half else 1, :]
                        q = (qsA if half == 0 else qsB)[:, 2 * j:2 * j + 2, b]
                        o = b * 8 + (j * 2 + half) * 2
                        ins = e.matmul(sb_[:, o:o + 2], lhsT=kt, rhs=q, start=True, stop=True)
                return ins
            P.op("pe", mm, reads=[("KTb", bf_), "qsA", "qsB"], writes=[sbr])
        P.op("act", lambda e: e.activation(out=pes[:].rearrange("p b h -> p (b h)"), in_=sb_[:, 0:128], func=AF.Exp), reads=[sbr], writes=["pes"])
        P.op("dve", lambda e: e.tensor_tensor(out=Pts[:], in0=pes[:], in1=fap(E[:, 0, 0, 1, 0, 127:128], [[0, 16], [512, 4], [128, 2]]), op=ALU.mult),
             reads=["pes", "E"], writes=["Pts"])
        ob, obr = P.bank()
        db, dbr = P.bank()

        def mmv(e):
            ins = None
            for b in range(16):
                for j in range(2):
                    e.matmul(ob[:, b * 8 + j * 4:b * 8 + j * 4 + 4], lhsT=Vwd[:, b, j, :], rhs=Pts[:, b, j * 4:(j + 1) * 4], start=True, stop=True)
                ins = e.matmul(db[:, b * 8:(b + 1) * 8], lhsT=ones_bf[:], rhs=Pts[:, b, :], start=True, stop=True)
            return ins
        P.op("pe", mmv, reads=["Pts", "Vwd", "ones"], writes=[obr, dbr])
        P.op("dve", lambda e: e.tensor_tensor(out=rec_f[:, 0:128].rearrange("p (b h) -> p b h", b=16), in0=db[:, 0:128].rearrange("p (b h) -> p b h", b=16),
                                              in1=fap(sinkexp[:, 0, 0, 0, 0:1], [[0, 16], [128, 8]]), op=ALU.add), reads=[dbr, "sinkexp"], writes=["rec"])
        P.op("act", lambda e: e.activation(out=rec_f[:, 0:128], in_=rec_f[:, 0:128], func=AF.Ln), reads=["rec"], writes=["rec"])
        P.op("act", lambda e: e.activation(out=rec_f[:, 0:128], in_=rec_f[:, 0:128], func=AF.Exp, scale=-1.0), reads=["rec"], writes=["rec"])
        for j in range(2):
            for half in range(2):
                r0 = half * 64
                o = (j * 2 + half) * 2
                P.op("dve", lambda e, r0=r0, o=o, j=j: e.tensor_tensor(
                    out=mixT[r0:r0 + 64, 2 * j:2 * j + 2, 0:16],
                    in0=fap(ob[r0:r0 + 64, o:o + 1], [[1, 2], [8, 16]]),
                    in1=fap(rec_f[r0:r0 + 64, o:o + 1], [[1, 2], [8, 16]]), op=ALU.mult), reads=[obr, "rec"], writes=[("mixa", 0)])

    def sample_gla():
        ob, obr = P.bank(hold=True)
        for b in range(16):
            bf_ = b % 2
            P.dma("sp", Sb[:, bf_], sg[b].rearrange("(c u) d v -> (u d) c v", u=2), writes=[("Sb", bf_)])
            vb, vbr = P.bank()
            P.op("pe", lambda e, vb=vb, b=b: e.matmul(vb[:, 0:512], lhsT=sel_bf[:, b, :], rhs=vg_tok[:, 0, :], start=True, stop=True),
                 reads=["sel", ("vg", 0)], writes=[vbr])
            for c in range(2):
                for half in range(2):
                    r0 = half * 64
                    h = 2 * c + half
                    P.op("dve", lambda e, vb=vb, b=b, c=c, r0=r0, h=h, bf_=bf_: e.scalar_tensor_tensor(
                        out=Wt[r0:r0 + 64, bf_, c, :], in0=vb[r0:r0 + 64, h * 128:(h + 1) * 128], scalar=kgf[r0:r0 + 64, c, b:b + 1],
                        in1=Sb[r0:r0 + 64, bf_, c, :], op0=ALU.mult, op1=ALU.add), reads=[vbr, "kgf", ("Sb", bf_)], writes=[("Wt", bf_)])
            P.op("act", lambda e, bf_=bf_: e.activation(out=WA[0:64, bf_], in_=Wt[0:64, bf_], func=AF.Copy), reads=[("Wt", bf_), "zinit"], writes=[("WA", bf_)])
            P.op("act", lambda e, bf_=bf_: e.activation(out=WB[64:128, bf_], in_=Wt[64:128, bf_], func=AF.Copy), reads=[("Wt", bf_), "zinit"], writes=[("WB", bf_)])
            P.op("dve", lambda e, b=b, bf_=bf_: e.tensor_tensor(out=Sn[:, bf_], in0=Wt[:, bf_], in1=fap(elb[:, 0, b:b + 1], [[128, 2], [0, 128]]), op=ALU.mult),
                 reads=[("Wt", bf_), "elb"], writes=[("Sn", bf_)])
            outs.append(P.dma("sp", gss[b].rearrange("(c u) d v -> (u d) c v", u=2), Sn[:, bf_], reads=[("Sn", bf_)]))

            def mm(e, b=b, bf_=bf_):
                ins = None
                for h in range(4):
                    c, half = h // 2, h % 2
                    w = (WA if half == 0 else WB)[:, bf_, c, :]
                    ins = e.matmul(ob[:, h * 16 + b:h * 16 + b + 1], lhsT=w, rhs=qgT[:, c, b:b + 1], start=True, stop=True)
                return ins
            P.op("pe", mm, reads=[("WA", bf_), ("WB", bf_), "qgT"], writes=[obr])
        P.release(obr)
        gla_out_norm(ob, obr, 64, 16, 0)

    import os as _os
    _kb = _os.environ.get("KBAR", "")
    for t in range((0 if debug in ("sample", "scan") else (2 if debug == "two" else (4 if debug == "four" else 1))) if debug else 4):
        main_tile("prompt", t)
        if "t" in _kb:
            P.barrier()
    if debug and debug != "sample":
        outs.append(P.dma("pool", dbg_a, aT[:], reads=[("aT", m) for m in range(NKF)]))
    outs.append(P.dma("sp", gsp.rearrange("(c u) d v -> (u d) c v", u=2), S[:], reads=["S"]))
    P.barrier()
    if (not debug) or debug == "sample":
        main_tile("sample", 0)

    P.emit()
    P.stack.close()
    return nc


_CACHE = {}


def kernel(x_prompt, x_sample, cache_k, cache_v, state_gla, attn_norm_g, w_in, q_norm_g, k_norm_g, attn_sinks,
           rel_bias, w_gla_gate2, b_gla_gate, gla_norm_g, w_o, ffn_norm_g, w_gate, w_up, w_down):
    f = lambda a: np.ascontiguousarray(np.asarray(a, dtype=np.float32))
    xpr = f(x_prompt)[0]
    xsm = f(x_sample)[:, 0, :]
    ckk = f(cache_k)[0].reshape(128, 128, 128)
    cvv = f(cache_v)[0].reshape(128, 128, 128)
    sgg = f(state_gla)[0]
    consts = host_consts()
    shared = dict(w_in=f(w_in)[0], w_o=f(w_o)[0], w_gate=f(w_gate)[0], w_up=f(w_up)[0], w_down=f(w_down)[0],
                  attn_g=f(attn_norm_g)[0], ffn_g=f(ffn_norm_g)[0], qng=f(q_norm_g)[0], kng=f(k_norm_g)[0],
                  sinks=f(attn_sinks)[0], relb=f(rel_bias), w2=f(w_gla_gate2)[0], bgate=f(b_gla_gate)[0], glag=f(gla_norm_g)[0])
    for k, v in consts.items():
        shared["c_" + k] = v
    in_maps = []
    for c in range(NCORE):
        m = dict(shared)
        m["xp"] = xpr[c * TOK:(c + 1) * TOK]
        m["xhalo"] = xpr[c * TOK - 128:c * TOK] if c > 0 else np.zeros((128, D), np.float32)
        pre = np.zeros((NPRE, D), np.float32)
        if c > 0:
            pre[NPRE - c * TOK:] = xpr[:c * TOK]
        m["xpre"] = pre
        xs_ = np.zeros((128, D), np.float32)
        xs_[:16] = xsm[c * 16:(c + 1) * 16]
        m["xs"] = xs_
        m["ck"] = ckk[c * 16:(c + 1) * 16]
        m["cv"] = cvv[c * 16:(c + 1) * 16]
        m["sg"] = sgg[c * 16:(c + 1) * 16]
        m["flag"] = np.full((128, 1), 1.0 if c > 0 else 0.0, np.float32)
        in_maps.append(m)
    if "nc" not in _CACHE:
        _CACHE["nc"] = build_program()
    res = run_bass_kernel_spmd(_CACHE["nc"], in_maps, core_ids=list(range(NCORE)))
    R = res.results
    y_prompt = np.concatenate([R[c]["y_p"] for c in range(NCORE)], axis=0)[None]
    y_sample = np.concatenate([R[c]["y_s"][:16] for c in range(NCORE)], axis=0)[:, None, :]
    kwp = R[7]["kwp"].reshape(1, 1, 128, 2, 64)
    vwp = R[7]["vwp"].reshape(1, 1, 128, 2, 64)
    gsp = R[7]["gsp"].reshape(1, 1, 4, 64, 128)
    kws = np.concatenate([R[c]["kws"] for c in range(NCORE)], axis=0).reshape(1, 128, 128, 2, 64)
    vws = np.concatenate([R[c]["vws"] for c in range(NCORE)], axis=0).reshape(1, 128, 128, 2, 64)
    gss = np.concatenate([R[c]["gss"] for c in range(NCORE)], axis=0).reshape(1, 128, 4, 64, 128)
    return (y_prompt.astype(np.float32), y_sample.astype(np.float32), kwp, vwp, gsp, kws, vws, gss)
```

```python
import contextlib
import math
import numpy as np
import concourse.bass as bass
import concourse.mybir as mybir
from concourse.bass_utils import run_bass_kernel_spmd

F32 = mybir.dt.float32
BF16 = mybir.dt.bfloat16
AF = mybir.ActivationFunctionType
ALU = mybir.AluOpType

NCORE = 8
D = 1024
TOK = 2048
NPRE = 7 * 2048
DFF = 2816
NKF = DFF // 128
INW = 2320
ENGS = ("pe", "act", "dve", "pool", "sp")
NDMASEM = 12
NSLOT = 4
EPS = 1e-6
MASKV = -30000.0


class _Ins:
    def then_inc(self, *a, **k):
        return self


class _Mock:
    def __init__(self):
        self.cost = 0.0

    def _free(self, ap):
        n = 1
        for d in list(ap.shape)[1:]:
            n *= int(d)
        return n

    def matmul(self, out, lhsT=None, rhs=None, **k):
        n = max(self._free(rhs), 64)
        self.cost += (n * (4 if rhs.dtype == F32 else 1)) / 2400.0 + 0.01
        return _Ins()

    def transpose(self, out=None, in_=None, identity=None, **k):
        self.cost += (128 * (4 if in_.dtype == F32 else 1)) / 2400.0 + 0.01
        return _Ins()

    def __getattr__(self, name):
        def f(*a, **k):
            o = k.get("out", a[0] if a else None)
            n = self._free(o) if o is not None else 64
            self.cost += 0.2 + n / 1000.0
            return _Ins()
        return f


class Prog:
    def __init__(self, nc):
        self.nc = nc
        self.stack = contextlib.ExitStack()
        self.oplist = []
        self.last_w = {}
        self.readers = {}
        self.base = None
        self.sems = {}
        self.nbuf = 0
        self.banks = []
        self.bank_rr = 0
        self.held = set()
        self.finals = []

    def sb(self, shape, dt, name=None):
        self.nbuf += 1
        return self.stack.enter_context(self.nc.sbuf_tensor(name or f"sb{self.nbuf}", list(shape), dt))

    def ps(self, shape, dt, name=None):
        self.nbuf += 1
        return self.stack.enter_context(self.nc.psum_tensor(name or f"ps{self.nbuf}", list(shape), dt))

    def bank(self, hold=False):
        while True:
            i = self.bank_rr % len(self.banks)
            self.bank_rr += 1
            if i not in self.held:
                break
        if hold:
            self.held.add(i)
        return self.banks[i], ("ps", i)

    def release(self, res):
        self.held.discard(res[1])

    def _sem(self, key):
        if key not in self.sems:
            nm = "s_" + "_".join(str(k) for k in (key if isinstance(key, tuple) else (key,)))
            self.sems[key] = self.stack.enter_context(self.nc.semaphore(nm))
        return self.sems[key]

    def _add(self, eng, fn, reads, writes, kind, cost, dma=None):
        deps = set()
        for r in reads:
            t = self.last_w.get(r, self.base)
            if t is not None:
                deps.add(t)
        for w in writes:
            t = self.last_w.get(w, self.base)
            if t is not None:
                deps.add(t)
            deps.update(self.readers.get(w, ()))
        if not reads and not writes and self.base is not None:
            deps.add(self.base)
        i = len(self.oplist)
        deps.discard(i)
        self.oplist.append(dict(eng=eng, fn=fn, deps=deps, kind=kind, cost=cost, dma=dma))
        for r in reads:
            self.readers.setdefault(r, []).append(i)
        for w in writes:
            self.last_w[w] = i
            self.readers[w] = []
        return i

    def op(self, eng, fn, reads=(), writes=()):
        m = _Mock()
        fn(m)
        return self._add(eng, fn, reads, writes, "c", m.cost)

    def dma(self, eng, out, in_, reads=(), writes=()):
        n = 1
        for d in out.shape:
            n *= int(d)
        nbytes = n * (4 if out.dtype == F32 else 2)
        return self._add(eng, None, reads, writes, "d", 2.0 + nbytes / 150e3, dma=(out, in_))

    def barrier(self, keep=()):
        kept = {k: v for k, v in self.last_w.items() if (k in keep or (isinstance(k, tuple) and k and k[0] in keep))}
        skip = set(getattr(self, "nofence", ()))
        allprev = set(range(len(self.oplist))) - skip
        i = len(self.oplist)
        self.oplist.append(dict(eng="sp", fn=None, deps=allprev, kind="n", cost=0.05, dma=None))
        self.base = i
        self.bars = getattr(self, "bars", []) + [i]
        self.last_w = dict(kept)
        self.readers = {}

    def final_wait(self, eng, toks):
        pass

    def schedule(self):
        import heapq, os
        ops = self.oplist
        n = len(ops)
        succ = [[] for _ in range(n)]
        ndep = [0] * n
        for i, o in enumerate(ops):
            ndep[i] = len(o["deps"])
            for d in o["deps"]:
                succ[d].append(i)
        done = [0.0] * n
        ready_t = [0.0] * n
        efree = {e: 0.0 for e in ENGS}
        waiting = {e: [] for e in ENGS}
        avail = {e: [] for e in ENGS}
        order = {e: [] for e in ENGS}
        for i, o in enumerate(ops):
            if ndep[i] == 0:
                heapq.heappush(waiting[o["eng"]], (0.0, i))
        nsched = 0
        while nsched < n:
            best = None
            for e in ENGS:
                w, a = waiting[e], avail[e]
                while w and w[0][0] <= efree[e]:
                    heapq.heappush(a, heapq.heappop(w)[1])
                if a:
                    cand = (efree[e], a[0], e, True)
                elif w:
                    cand = (w[0][0], w[0][1], e, False)
                else:
                    continue
                if best is None or cand[:2] < best[:2]:
                    best = cand
            st, i, e, from_avail = best
            if from_avail:
                heapq.heappop(avail[e])
            else:
                heapq.heappop(waiting[e])
            o = ops[i]
            if o["kind"] == "d":
                efree[e] = st + 0.15
                done[i] = st + o["cost"]
            else:
                efree[e] = st + o["cost"]
                done[i] = st + o["cost"] + 0.15
            order[e].append(i)
            nsched += 1
            for sidx in succ[i]:
                ndep[sidx] -= 1
                if done[i] > ready_t[sidx]:
                    ready_t[sidx] = done[i]
                if ndep[sidx] == 0:
                    heapq.heappush(waiting[ops[sidx]["eng"]], (ready_t[sidx], sidx))
        self.sim_time = max(done) if n else 0.0
        self.sim_done = done
        if os.environ.get("KSIM"):
            print("SIM total", round(self.sim_time), "barriers", [round(done[b]) for b in getattr(self, "bars", [])])
        return order

    def emit(self):
        nc = self.nc
        import os
        order = self.schedule()
        km = os.environ.get("KSCHED", "mid")
        if km == "0":
            order = {e: [i for i, o in enumerate(self.oplist) if o["eng"] == e] for e in ENGS}
        elif km == "mid" and len(getattr(self, "bars", [])) >= 2:
            B2 = self.bars[-1]
            prog = {e: [i for i, o in enumerate(self.oplist) if o["eng"] == e] for e in ENGS}
            order = {e: [i for i in order[e] if i <= B2] + [i for i in prog[e] if i > B2] for e in ENGS}
        elif km in ("pre", "post") and self.base is not None:
            B = self.base
            prog = {e: [i for i, o in enumerate(self.oplist) if o["eng"] == e] for e in ENGS}
            if km == "post":
                order = {e: [i for i in prog[e] if i <= B] + [i for i in order[e] if i > B] for e in ENGS}
            else:
                order = {e: [i for i in order[e] if i <= B] + [i for i in prog[e] if i > B] for e in ENGS}
        ops = self.oplist
        tok = [None] * len(ops)
        ccnt = {e: 0 for e in ENGS}
        dcnt = {}
        drr = {e: 0 for e in ENGS}
        plan = {e: [] for e in ENGS}
        prevdma = {}
        for e in ENGS:
            for i in order[e]:
                o = ops[i]
                if o["kind"] == "d":
                    j = drr[e] % NDMASEM
                    drr[e] += 1
                    key = ("d", e, j)
                    c = dcnt.get(key, 0)
                    dcnt[key] = c + 1
                    tok[i] = (key, 16 * (c + 1))
                    prevdma[i] = (key, 16 * c) if c > 0 else None
                else:
                    ccnt[e] += 1
                    tok[i] = (e, ccnt[e])
        for e in ENGS:
            waited = {}
            for i in order[e]:
                o = ops[i]
                need = {}
                for d in o["deps"]:
                    k, v = tok[d]
                    if waited.get(k, 0) >= v:
                        continue
                    if need.get(k, 0) < v:
                        need[k] = v
                if o["kind"] == "d" and prevdma.get(i):
                    k, v = prevdma[i]
                    if waited.get(k, 0) < v and need.get(k, 0) < v:
                        need[k] = v
                for k, v in need.items():
                    waited[k] = v
                plan[e].append((list(need.items()), i))
            if e == "sp":
                fin = {}
                for i2, t in enumerate(tok):
                    if t is not None and fin.get(t[0], 0) < t[1]:
                        fin[t[0]] = t[1]
                plan[e].append(([(k, v) for k, v in fin.items() if waited.get(k, 0) < v], None))
        for e in ENGS:
            for (waits, i) in plan[e]:
                for (k, v) in waits:
                    self._sem(k)
                if i is not None:
                    self._sem(tok[i][0])
        if os.environ.get("KCHECK"):
            import collections
            sv = collections.defaultdict(int)
            ptr = {e: 0 for e in ENGS}
            while True:
                prog_ = False
                for e in ENGS:
                    while ptr[e] < len(plan[e]):
                        waits, i = plan[e][ptr[e]]
                        if all(sv[k] >= v for k, v in waits):
                            if i is not None:
                                k, v = tok[i]
                                inc = 16 if ops[i]["kind"] == "d" else 1
                                sv[k] += inc
                                assert sv[k] == v, ("token mismatch", e, i, k, v, sv[k])
                            ptr[e] += 1
                            prog_ = True
                        else:
                            break
                if all(ptr[e] == len(plan[e]) for e in ENGS):
                    print("KCHECK: ok, no deadlock")
                    break
                if not prog_:
                    for e in ENGS:
                        if ptr[e] < len(plan[e]):
                            waits, i = plan[e][ptr[e]]
                            print("KCHECK STUCK", e, ptr[e], i, [(k, v, sv[k]) for k, v in waits if sv[k] < v])
                    break
        block = self.stack.enter_context(nc.Block())

        def run(engname):
            def body(e):
                for (waits, i) in plan[engname]:
                    for (k, v) in waits:
                        e.wait_ge(self.sems[k], v)
                    if i is None:
                        continue
                    o = ops[i]
                    if o["kind"] == "d":
                        ins = e.dma_start(out=o["dma"][0], in_=o["dma"][1], allow_slow_non_contiguous=True)
                        ins.then_inc(self.sems[tok[i][0]], 16)
                    elif o["kind"] == "n":
                        ins = e.nop()
                        ins.then_inc(self.sems[tok[i][0]], 1)
                    else:
                        ins = o["fn"](e)
                        ins.then_inc(self.sems[tok[i][0]], 1)
            return body
        block.tensor(run("pe"))
        block.scalar(run("act"))
        block.vector(run("dve"))
        block.gpsimd(run("pool"))
        block.sync(run("sp"))


def fap(ap, dims):
    return bass.AP(tensor=ap.tensor, offset=ap.offset, ap=[list(ap.ap[0])] + [list(d) for d in dims])


def t5_bucket_np(n):
    n = np.maximum(n, 0)
    nf = np.maximum(n, 1).astype(np.float32)
    large = 16 + (np.log(nf / 16) / math.log(128 / 16) * 16).astype(np.int32)
    large = np.minimum(large, 31)
    return np.where(n < 16, n, large)


def host_consts():
    c = {}
    c["ident"] = np.eye(128, dtype=np.float32)
    s = np.arange(128)[:, None]
    t = np.arange(128)[None, :]
    c["tri"] = np.where(s <= t, -1.0 / 16, 0.0).astype(np.float32)
    c["tris"] = (np.eye(128) * (-1.0 / 16)).astype(np.float32)
    c["cmask"] = (s <= t).astype(np.float32)
    c["jx"] = np.eye(128, dtype=np.float32)[::-1].copy()
    bd = np.zeros((128, 128), np.float32)
    bd[:64, :64] = 1
    bd[64:, 64:] = 1
    c["bd"] = bd
    sw = np.zeros((128, 128), np.float32)
    for m in range(128):
        sw[(m + 64) % 128, m] = 1
    c["sw"] = sw
    oh = np.zeros((128, 2, 256), np.float32)
    for kb in range(2):
        off = 128 if kb == 0 else 0
        for i in range(255):
            dlt = 127 + off - i
            if 0 <= dlt <= 127:
                oh[int(t5_bucket_np(np.array(dlt))), kb, i] = 1.0
            else:
                oh[32, kb, i] = MASKV
        oh[32, kb, 255] = MASKV
    c["oh"] = oh
    sel = np.zeros((128, 16, 128), np.float32)
    for b in range(16):
        sel[b, b, :] = 1
    c["sel"] = sel
    return c


def build_program(debug=False):
    nc = bass.Bass("TRN2", target_bir_lowering=False)
    P = Prog(nc)

    def din(name, shape):
        return nc.dram_tensor(name, list(shape), F32, kind="ExternalInput").ap()

    def dout(name, shape):
        return nc.dram_tensor(name, list(shape), F32, kind="ExternalOutput").ap()

    xp = din("xp", [TOK, D]); xhalo = din("xhalo", [128, D]); xpre = din("xpre", [NPRE, D]); xs = din("xs", [128, D])
    ck = din("ck", [16, 128, 128]); cv = din("cv", [16, 128, 128]); sg = din("sg", [16, 4, 64, 128])
    w_in = din("w_in", [D, INW]); w_o = din("w_o", [D, D]); w_gate = din("w_gate", [D, DFF]); w_up = din("w_up", [D, DFF])
    w_down = din("w_down", [DFF, D])
    attn_g = din("attn_g", [D]); ffn_g = din("ffn_g", [D]); qg_in = din("qng", [64]); kg_in = din("kng", [64])
    sinks = din("sinks", [8]); relb = din("relb", [32, 8]); w2 = din("w2", [16, 256]); bgate = din("bgate", [256])
    glag = din("glag", [128]); flag = din("flag", [128, 1])
    cn = {k: din("c_" + k, v.shape) for k, v in host_consts().items()}

    y_p = dout("y_p", [TOK, D]); y_s = dout("y_s", [128, D])
    kwp = dout("kwp", [128, 128]); vwp = dout("vwp", [128, 128]); gsp = dout("gsp", [4, 64, 128])
    kws = dout("kws", [16, 128, 128]); vws = dout("vws", [16, 128, 128]); gss = dout("gss", [16, 4, 64, 128])
    fscr = nc.dram_tensor("fscr", [2, 8, 256], F32).ap()
    wb = {"w_in": nc.dram_tensor("wb_in", [D, INW], BF16).ap(), "w_o": nc.dram_tensor("wb_o", [D, D], BF16).ap(),
          "w_gate": nc.dram_tensor("wb_gate", [D, DFF], BF16).ap(), "w_up": nc.dram_tensor("wb_up", [D, DFF], BF16).ap(),
          "w_down": nc.dram_tensor("wb_down", [DFF, D], BF16).ap()}
    wf = {"w_in": w_in, "w_o": w_o, "w_gate": w_gate, "w_up": w_up, "w_down": w_down}
    if debug:
        dbg_mix = dout("dbg_mix", [128, 8, 512]); dbg_h = dout("dbg_h", [128, 4, D]); dbg_z = dout("dbg_z", [128, 8, 512]); dbg_a = dout("dbg_a", [128, NKF, 512])
        dbg_q = dout("dbg_q", [128, 4, 512]); dbg_rs = dout("dbg_rs", [128, 4, 512])

    for i in range(6):
        P.banks.append(P.ps([128, 512], F32, f"bank{i}"))
    psT = [P.ps([128, 1024], BF16, f"pst{i}") for i in range(2)]
    pst_rr = [0]

    def tbank():
        i = pst_rr[0] % 2
        pst_rr[0] += 1
        return psT[i], ("pst", i)

    ident_f = P.sb([128, 128], F32); ident_bf = P.sb([128, 128], BF16)
    tri_f = P.sb([128, 128], F32); tris_f = P.sb([128, 128], F32); cmask_bf = P.sb([128, 128], BF16)
    jx_f = P.sb([128, 128], F32); bd_bf = P.sb([128, 128], BF16); sw_bf = P.sb([128, 128], BF16)
    ones_bf = P.sb([128, 128], BF16); zeros_f = P.sb([128, 128], F32); scr_f = P.sb([128, 2048], F32)
    sel_bf = P.sb([128, 16, 128], BF16)
    gaT = P.sb([128, 8], F32); gfT = P.sb([128, 8], F32)
    gq_col = P.sb([128, 1], F32); gk_col = P.sb([128, 1], F32); glag_col = P.sb([128, 1], F32)
    eps_col = P.sb([128, 1], F32); ln8_col = P.sb([128, 1], F32); flag_col = P.sb([128, 1], F32)
    bgate_bc = P.sb([128, 256], F32); w2pad = P.sb([128, 256], BF16)
    relb_pad = P.sb([128, 128], F32)
    hank = scr_f[:, 0:1024].rearrange("p (h s) -> p h s", h=8)
    oh_sb = scr_f[:, 1024:1536].rearrange("p (a b) -> p a b", a=2)
    ftab = scr_f[0:8, 1536:2048].rearrange("p (a b) -> p a b", a=2)
    E = P.sb([128, 2, 2, 2, 2, 128], F32)
    sink_bc = P.sb([128, 8], F32); sinkexp = P.sb([128, 2, 2, 2, 128], F32)
    ring = P.sb([128, NSLOT, 8, 512], BF16)
    xb = P.sb([128, 4, D], F32)
    actT = P.sb([128, 8, 512], BF16)
    nbf = P.sb([128, D], BF16); nbf_b = P.sb([128, D], BF16)
    zt_b = P.sb([128, 256], F32); sp_b = P.sb([128, 256], F32); ktok_b = P.sb([128, 2, 2, 128], BF16)
    ss_c2 = P.sb([128, 2], F32); rs_c2 = P.sb([128, 2], F32)
    ss_c = P.sb([128, 1], F32); rs_c = P.sb([128, 1], F32)
    qhT = P.sb([128, 4, 512], BF16)
    kX = P.sb([128, 4, 640], BF16)
    khT_bf = P.sb([128, 512], BF16); khT_f = P.sb([128, 128], F32)
    Vdup = P.sb([128, 5, 2, 128], BF16)
    qgT = P.sb([128, 2, 512], BF16); kgA = P.sb([128, 2, 512], BF16); kgB = P.sb([128, 2, 512], BF16)
    kgf = P.sb([128, 2, 128], F32)
    ktok = P.sb([128, 2, 2, 128], BF16)
    vg_tok = P.sb([128, 4, 512], BF16)
    rsT = P.sb([128, 4, 512], BF16)
    ulrT = P.sb([128, 512], BF16)
    zt = P.sb([128, 256], F32); sp_t = P.sb([128, 256], F32)
    ebq = P.sb([128, 2, 512], F32); enb = P.sb([128, 2, 512], F32); elast = P.sb([128, 2, 4], F32)
    elb = P.sb([128, 2, 128], F32)
    S = P.sb([128, 2, 128], F32); Stmp = P.sb([128, 2, 128], F32); SA = P.sb([128, 2, 128], BF16); SB = P.sb([128, 2, 128], BF16)
    ATbf = P.sb([128, 4, 128], BF16)
    sq_bf = P.sb([128, 512], BF16); rstd_f = P.sb([128, 512], F32); tmp_f = P.sb([128, 512], F32)
    pe_f = P.sb([128, 512], F32); rec_f = P.sb([128, 512], F32)
    PT = P.sb([128, 2, 2, 2, 2, 128], BF16)
    mixT = P.sb([128, 8, 512], BF16)
    aT = P.sb([128, NKF, 512], BF16)
    sg_f = P.sb([128, 512], F32)
    vw_f = P.sb([128, 128], F32); kw_f = P.sb([128, 128], F32)
    Kw_bf = P.sb([128, 16, 128], BF16); Kwsw_bf = P.sb([128, 16, 128], BF16)
    KTb = P.sb([128, 2, 2, 128], BF16)
    Vwd = P.sb([128, 16, 2, 128], BF16)
    qsA = P.sb([128, 4, 16], BF16); qsB = P.sb([128, 4, 16], BF16)
    Pts = P.sb([128, 16, 8], BF16); pes = P.sb([128, 16, 8], F32)
    Sb = scr_f[:, 0:512].rearrange("p (a c v) -> p a c v", a=2, c=2)
    Wt = scr_f[:, 512:1024].rearrange("p (a c v) -> p a c v", a=2, c=2)
    Sn = scr_f[:, 1024:1536].rearrange("p (a c v) -> p a c v", a=2, c=2)
    WA = P.sb([128, 2, 2, 128], BF16); WB = P.sb([128, 2, 2, 128], BF16)

    CUR = {"p": 0}
    _vwd32 = Vwd[:].rearrange("p a b c -> p (a b c)").bitcast(F32)
    _kwf = Kw_bf[:].rearrange("p a b -> p (a b)")
    _kwswf = Kwsw_bf[:].rearrange("p a b -> p (a b)")
    nbfs2 = (nbf, nbf_b)

    def X(blk):
        if CUR["p"] == 0:
            return xb[:, blk, :]
        src = _vwd32 if blk < 2 else scr_f[:]
        o = (blk % 2) * 1024
        return src[:, o:o + 1024]

    def A(k):
        if CUR["p"] == 0:
            return actT[:, k, :]
        src = _kwf if k < 4 else _kwswf
        o = (k % 4) * 512
        return src[:, o:o + 512]

    def xr(blk):
        return ("x", CUR["p"], blk)

    def ar(blk):
        return ("actT", CUR["p"], blk)

    def ld(dst, src, name, eng="sp"):
        P.dma(eng, dst, src, writes=[name])

    ld(ident_f[:], cn["ident"], "ident_f"); ld(tri_f[:], cn["tri"], "tri"); ld(tris_f[:], cn["tris"], "tris")
    ld(jx_f[:], cn["jx"], "jx"); ld(oh_sb, cn["oh"], "oh")
    P.dma("pool", ident_bf[:], cn["ident"], writes=["ident_bf"])
    P.dma("pool", cmask_bf[:], cn["cmask"], writes=["cmask"])
    P.dma("pool", bd_bf[:], cn["bd"], writes=["bd"])
    P.dma("pool", sw_bf[:], cn["sw"], writes=["sw"])
    P.dma("pool", sel_bf[:], cn["sel"], writes=["sel"])
    ld(gaT[:], attn_g.rearrange("(k p) -> p k", p=128), "gaT"); ld(gfT[:], ffn_g.rearrange("(k p) -> p k", p=128), "gfT")
    for h in range(2):
        ld(gq_col[h * 64:(h + 1) * 64, :], qg_in.rearrange("(p o) -> p o", o=1), "gq")
        ld(gk_col[h * 64:(h + 1) * 64, :], kg_in.rearrange("(p o) -> p o", o=1), "gk")
    ld(glag_col[:], glag.rearrange("(p o) -> p o", o=1), "glag"); ld(flag_col[:], flag, "flag")
    ld(bgate_bc[:], bass.AP(tensor=bgate.tensor, offset=0, ap=[[0, 128], [1, 256]]), "bgate")
    ld(sink_bc[:], bass.AP(tensor=sinks.tensor, offset=0, ap=[[0, 128], [1, 8]]), "sink_bc")
    P.op("dve", lambda e: e.memset(ones_bf[:], 1.0), writes=["ones"])
    P.op("dve", lambda e: e.memset(zeros_f[:], 0.0), writes=["zeros"])
    P.op("dve", lambda e: e.memset(eps_col[:], EPS), writes=["eps"])
    P.op("dve", lambda e: e.memset(ln8_col[:], math.log(0.125)), writes=["ln8"])
    P.op("dve", lambda e: e.memset(w2pad[:], 0.0), writes=["w2pad"])
    P.dma("pool", w2pad[112:128, :], w2, reads=[], writes=["w2pad"])
    P.op("dve", lambda e: e.tensor_scalar(out=gq_col[:], in0=gq_col[:], scalar1=0.125, scalar2=None, op0=ALU.mult),
         reads=["gq"], writes=["gq"])
    for t_ in (kgA, kgB, SA, SB, qsA, qsB, WA, WB):
        P.op("dve", lambda e, t_=t_: e.memset(t_[:], 0.0), writes=["zinit"])
    P.op("dve", lambda e: e.memset(kX[:], 0.0), writes=["kX"])
    P.op("dve", lambda e: e.memset(S[:], 0.0), writes=["S"])
    P.op("dve", lambda e: e.memset(relb_pad[:], 0.0), writes=["relb_pad"])
    P.op("dve", lambda e: e.memset(relb_pad[32:33, :], 1.0), reads=[], writes=["relb_pad"])
    P.dma("sp", relb_pad[0:32, 0:8], relb, writes=["relb_pad"])

    bk, bkr = P.bank()
    P.op("pe", lambda e: e.matmul(bk[:, 0:512], lhsT=relb_pad[:], rhs=scr_f[:, 1024:1536], start=True, stop=True),
         reads=["relb_pad", "oh"], writes=[bkr])
    P.op("act", lambda e: e.activation(out=scr_f[0:8, 1536:2048], in_=bk[0:8, 0:512], func=AF.Copy), reads=[bkr], writes=["ftab"])
    P.dma("sp", fscr.rearrange("k h i -> h k i"), ftab, reads=["ftab"], writes=["fscr"])
    for kb in range(2):
        src = bass.AP(tensor=fscr.tensor, offset=kb * 8 * 256, ap=[[1, 128], [256, 8], [1, 128]])
        P.dma("sp", hank, src, reads=["fscr"], writes=["hank"])
        for hh in range(0, 8, 4):
            bk, bkr = P.bank()

            def mmj(e, bk=bk, hh=hh):
                ins = None
                for q in range(4):
                    ins = e.matmul(bk[:, q * 128:(q + 1) * 128], lhsT=hank[:, hh + q, :], rhs=jx_f[:], start=True, stop=True)
                return ins
            P.op("pe", mmj, reads=["hank", "jx"], writes=[bkr])
            for q in range(4):
                h = hh + q
                c_, half = h // 2, h % 2
                j, cl = c_ // 2, c_ % 2
                P.op("act", lambda e, bk=bk, q=q, j=j, half=half, kb=kb, cl=cl:
                     e.activation(out=E[:, j, half, kb, cl, :], in_=bk[:, q * 128:(q + 1) * 128], func=AF.Exp),
                     reads=[bkr], writes=["E"])
    P.op("act", lambda e: e.activation(out=sink_bc[:], in_=sink_bc[:], func=AF.Exp), reads=["sink_bc"], writes=["sink_bc"])
    for h in range(8):
        c_, half = h // 2, h % 2
        j, cl = c_ // 2, c_ % 2
        P.op("dve", lambda e, h=h, j=j, half=half, cl=cl: e.tensor_scalar(out=sinkexp[:, j, half, cl, :], in0=zeros_f[:], scalar1=sink_bc[:, h:h + 1],
                                                                         scalar2=None, op0=ALU.add), reads=["sink_bc", "zeros"], writes=["sinkexp"])

    wstate = {"n": 0}

    def wload(parts, fp32=False):
        s = wstate["n"] % NSLOT
        wstate["n"] += 1
        res = ("w", s)
        for (c0, (wn, r0, nrows, cc0, ncols_), nk, ncols) in parts:
            src = (wf if fp32 else wb)[wn][r0:r0 + nrows, cc0:cc0 + ncols_].rearrange("(k p) n -> p k n", p=128)
            q_ = "pool" if (fp32 or wstate["n"] % 2 == 0) else "sp"
            P.dma(q_, ring[:, s, 0:nk, c0:c0 + ncols], src, reads=([] if fp32 else [("wbf", wn)]), writes=[res])
        return s, res

    def wsrc(w, r0, nrows, c0, ncols):
        return (w, r0, nrows, c0, ncols)

    def front(src_fn, nb, gT):
        for blk in range(nb):
            P.dma("sp", X(blk), src_fn(blk), writes=[xr(blk)])
            norm_block(blk, gT)

    def norm_block(blk, gT):
        p = CUR["p"]
        xblk = X(blk); xres = xr(blk); ares = ar(blk)
        nb_ = nbfs2[p]; ss = ss_c2[:, p:p + 1]; rs = rs_c2[:, p:p + 1]
        P.op("act", lambda e: e.activation(out=nb_[:], in_=xblk, func=AF.Square, accum_out=ss),
             reads=[xres], writes=[("nbf", p), ("ss_c", p)])
        P.op("act", lambda e: e.activation(out=rs, in_=ss, func=AF.Ln, scale=1.0 / D, bias=eps_col[:, 0:1]),
             reads=[("ss_c", p), "eps"], writes=[("rs_c", p)])
        P.op("act", lambda e: e.activation(out=rs, in_=rs, func=AF.Exp, scale=-0.5), reads=[("rs_c", p)], writes=[("rs_c", p)])
        P.op("dve", lambda e: e.tensor_scalar(out=nb_[:], in0=xblk, scalar1=rs, scalar2=None, op0=ALU.mult),
             reads=[xres, ("rs_c", p)], writes=[("nbf", p)])
        tb, tbr = tbank()

        def tr(e):
            ins = None
            for k in range(8):
                ins = e.transpose(out=tb[:, k * 128:(k + 1) * 128], in_=nb_[:, k * 128:(k + 1) * 128], identity=ident_bf[:])
            return ins
        P.op("pe", tr, reads=[("nbf", p), "ident_bf"], writes=[tbr])
        if p == 0:
            P.op("dve", lambda e: e.tensor_tensor(out=actT[:, :, blk * 128:(blk + 1) * 128], in0=tb[:].rearrange("p (k t) -> p k t", k=8),
                                                  in1=fap(gT[:], [[1, 8], [0, 128]]), op=ALU.mult),
                 reads=[tbr, "gaT", "gfT"], writes=[ares])
        else:
            for kh, src in enumerate((_kwf, _kwswf)):
                P.op("dve", lambda e, kh=kh, src=src: e.tensor_tensor(
                    out=src.rearrange("p (k t) -> p k t", k=4)[:, :, blk * 128:(blk + 1) * 128],
                    in0=tb[:, kh * 512:(kh + 1) * 512].rearrange("p (k t) -> p k t", k=4),
                    in1=fap(gT[:, kh * 4:kh * 4 + 1], [[1, 4], [0, 128]]), op=ALU.mult),
                    reads=[tbr, "gaT", "gfT", ares], writes=[ares])

    def actT_reads(nb):
        return [ar(b) for b in range(nb)]

    def proj_fm(slot, res, col0, T, nb):
        bk, bkr = P.bank()
        acts = [A(k) for k in range(8)]

        def mm(e):
            ins = None
            for k in range(8):
                ins = e.matmul(bk[:, 0:T], lhsT=ring[:, slot, k, col0:col0 + 128], rhs=acts[k][:, 0:T], start=(k == 0), stop=(k == 7))
            return ins
        P.op("pe", mm, reads=[res] + actT_reads(nb), writes=[bkr])
        return bk, bkr

    def proj_tm(slot, res, col0, ncols, blk):
        bk, bkr = P.bank()
        acts = [A(k) for k in range(8)]

        def mm(e):
            ins = None
            for k in range(8):
                ins = e.matmul(bk[:, 0:ncols], lhsT=acts[k][:, blk * 128:(blk + 1) * 128], rhs=ring[:, slot, k, col0:col0 + ncols],
                               start=(k == 0), stop=(k == 7))
            return ins
        P.op("pe", mm, reads=[res, ar(blk)], writes=[bkr])
        return bk, bkr

    def rstd_fm(src_ap, T, lhs_ones, scale, reads_src):
        P.op("act", lambda e: e.activation(out=sq_bf[:, 0:T], in_=src_ap, func=AF.Square), reads=reads_src, writes=["sq"])
        b2, b2r = P.bank()
        P.op("pe", lambda e: e.matmul(b2[:, 0:T], lhsT=lhs_ones[:], rhs=sq_bf[:, 0:T], start=True, stop=True),
             reads=["sq", "bd", "ones"], writes=[b2r])
        P.op("act", lambda e: e.activation(out=rstd_f[:, 0:T], in_=b2[:, 0:T], func=AF.Ln, scale=scale, bias=eps_col[:, 0:1]),
             reads=[b2r, "eps"], writes=["rstd"])
        P.op("act", lambda e: e.activation(out=rstd_f[:, 0:T], in_=rstd_f[:, 0:T], func=AF.Exp, scale=-0.5), reads=["rstd"], writes=["rstd"])

    def gla_prep(slot_lr, res_lr, lrcol, nb, T, tri_ap, tri_res, full):
        bk, bkr = proj_fm(slot_lr, res_lr, lrcol, T, nb)
        P.op("act", lambda e: e.activation(out=ulrT[:, 0:T], in_=bk[:, 0:T], func=AF.Copy), reads=[bkr], writes=["ulrT"])
        bT = [P.bank(hold=True) for _ in range(2)]
        for blk in range(nb):
            zb, zbr = P.bank()
            P.op("pe", lambda e, zb=zb, blk=blk: e.matmul(zb[:, 0:256], lhsT=ulrT[:, blk * 128:(blk + 1) * 128], rhs=w2pad[:], start=True, stop=True),
                 reads=["ulrT", "w2pad"], writes=[zbr])
            P.op("dve", lambda e, zb=zb: e.tensor_tensor(out=zt[:], in0=zb[:, 0:256], in1=bgate_bc[:], op=ALU.add),
                 reads=[zbr, "bgate"], writes=["zt"])
            P.op("act", lambda e: e.activation(out=zt[:], in_=zt[:], func=AF.Exp, scale=-1.0), reads=["zt"], writes=["zt"])
            P.op("act", lambda e: e.activation(out=sp_t[:], in_=zt[:], func=AF.Ln, bias=1.0), reads=["zt"], writes=["sp_t"])
            for c in range(2):
                P.op("pe", lambda e, c=c, blk=blk: e.matmul(bT[c][0][:, blk * 128:(blk + 1) * 128], lhsT=sp_t[:, c * 128:(c + 1) * 128], rhs=tri_ap,
                                                           start=True, stop=True), reads=["sp_t", tri_res], writes=[bT[c][1]])
        for c in range(2):
            P.release(bT[c][1])
        for c in range(2):
            P.op("act", lambda e, c=c: e.activation(out=enb[:, c, 0:T], in_=bT[c][0][:, 0:T], func=AF.Exp, scale=-1.0), reads=[bT[c][1]], writes=["enb"])
            P.op("act", lambda e, c=c: e.activation(out=elast[:, c, 0:nb], in_=fap(bT[c][0][:, 127:128], [[128, nb]]), func=AF.Exp),
                 reads=[bT[c][1]], writes=["elast"])
            if full:
                P.op("act", lambda e, c=c: e.activation(out=ebq[:, c, 0:T], in_=bT[c][0][:, 0:T], func=AF.Exp, bias=ln8_col[:, 0:1]),
                     reads=[bT[c][1], "ln8"], writes=["ebq"])
                if T == 128:
                    P.op("act", lambda e, c=c: e.activation(out=elb[:, c, :], in_=bT[c][0][:, 0:128], func=AF.Exp), reads=[bT[c][1]], writes=["elb"])

    def kg_evac(slot, res, col0, nb, T, sample=False):
        for c in range(2):
            bk, bkr = proj_fm(slot, res, col0 + c * 128, T, nb)
            P.op("dve", lambda e, bk=bk, c=c: e.tensor_tensor(out=kgA[0:64, c, 0:T], in0=bk[0:64, 0:T], in1=enb[0:64, c, 0:T], op=ALU.mult),
                 reads=[bkr, "enb", "zinit"], writes=[("kgA", c)])
            P.op("dve", lambda e, bk=bk, c=c: e.tensor_tensor(out=kgB[64:128, c, 0:T], in0=bk[64:128, 0:T], in1=enb[64:128, c, 0:T], op=ALU.mult),
                 reads=[bkr, "enb", "zinit"], writes=[("kgB", c)])
            if sample:
                P.op("dve", lambda e, bk=bk, c=c: e.tensor_tensor(out=kgf[:, c, :], in0=bk[:, 0:128], in1=enb[:, c, 0:128], op=ALU.mult),
                     reads=[bkr, "enb"], writes=["kgf"])

    def vg_tm(slot, res, col0, nb):
        for blk in range(nb):
            bk, bkr = proj_tm(slot, res, col0, 512, blk)
            P.op("act", lambda e, bk=bk, blk=blk: e.activation(out=vg_tok[:, blk, :], in_=bk[:, 0:512], func=AF.Copy), reads=[bkr], writes=[("vg", blk)])

    def state_update(blk, masked):
        tb, tbr = tbank()

        def tr(e):
            ins = None
            for c in range(2):
                for ab, src in enumerate((kgA, kgB)):
                    o = (c * 2 + ab) * 128
                    ins = e.transpose(out=tb[:, o:o + 128], in_=src[:, c, blk * 128:(blk + 1) * 128], identity=ident_bf[:])
            return ins
        P.op("pe", tr, reads=[("kgA", 0), ("kgA", 1), ("kgB", 0), ("kgB", 1), "ident_bf"], writes=[tbr])
        P.op("act", lambda e: e.activation(out=ktok[:].rearrange("p c a f -> p (c a f)"), in_=tb[:, 0:512], func=AF.Copy), reads=[tbr], writes=["ktok"])
        ub, ubr = P.bank()

        def mm(e):
            ins = None
            for c in range(2):
                for ab in range(2):
                    h = 2 * c + ab
                    ins = e.matmul(ub[:, c * 128:(c + 1) * 128], lhsT=ktok[:, c, ab, :], rhs=vg_tok[:, blk, h * 128:(h + 1) * 128],
                                   start=(ab == 0), stop=(ab == 1))
            return ins
        P.op("pe", mm, reads=["ktok", ("vg", blk)], writes=[ubr])
        P.op("dve", lambda e: e.tensor_tensor(out=Stmp[:].rearrange("p c v -> p (c v)"), in0=ub[:, 0:256], in1=S[:].rearrange("p c v -> p (c v)"), op=ALU.add),
             reads=[ubr, "S"], writes=["Stmp"])
        P.op("dve", lambda e: e.tensor_tensor(out=S[:], in0=Stmp[:], in1=fap(elast[:, 0, blk:blk + 1], [[4, 2], [0, 128]]), op=ALU.mult),
             reads=["Stmp", "elast", "SA", "SB"], writes=["S"])
        if masked:
            P.op("act", lambda e: e.activation(out=SA[0:64], in_=S[0:64], func=AF.Copy), reads=["S", "zinit"], writes=["SA"])
            P.op("act", lambda e: e.activation(out=SB[64:128], in_=S[64:128], func=AF.Copy), reads=["S", "zinit"], writes=["SB"])

    def gla_out_norm(ob, obr, ncol, W, col0):
        rstd_fm(ob[:, 0:ncol], ncol, ones_bf, 1.0 / 128, [obr])
        P.op("dve", lambda e: e.scalar_tensor_tensor(out=tmp_f[:, 0:ncol], in0=ob[:, 0:ncol], scalar=glag_col[:, 0:1], in1=rstd_f[:, 0:ncol],
                                                     op0=ALU.mult, op1=ALU.mult), reads=[obr, "rstd", "glag"], writes=["tmp_f"])
        P.op("dve", lambda e: e.tensor_tensor(out=mixT[:, 4:8, col0:col0 + W], in0=tmp_f[:, 0:ncol].rearrange("p (h w) -> p h w", h=4),
                                              in1=rsT[:, :, col0:col0 + W], op=ALU.mult), reads=["tmp_f", "rsT"], writes=[("mixg", col0)])

    def gla_block(blk):
        ab_, abr = P.bank()

        def mm1(e):
            ins = None
            for h in range(4):
                c, half = h // 2, h % 2
                src = kgA if half == 0 else kgB
                ins = e.matmul(ab_[:, h * 128:(h + 1) * 128], lhsT=src[:, c, blk * 128:(blk + 1) * 128], rhs=qgT[:, c, blk * 128:(blk + 1) * 128],
                               start=True, stop=True)
            return ins
        P.op("pe", mm1, reads=[("kgA", 0), ("kgA", 1), ("kgB", 0), ("kgB", 1), "qgT"], writes=[abr])
        P.op("dve", lambda e: e.tensor_tensor(out=ATbf[:], in0=ab_[:, 0:512].rearrange("p (h t) -> p h t", h=4), in1=fap(cmask_bf[:], [[0, 4], [1, 128]]),
                                              op=ALU.mult), reads=[abr, "cmask"], writes=["ATbf"])
        ob, obr = P.bank()

        def mm2(e):
            ins = None
            for h in range(4):
                c, half = h // 2, h % 2
                sm = SA if half == 0 else SB
                e.matmul(ob[:, h * 128:(h + 1) * 128], lhsT=vg_tok[:, blk, h * 128:(h + 1) * 128], rhs=ATbf[:, h, :], start=True, stop=False)
                ins = e.matmul(ob[:, h * 128:(h + 1) * 128], lhsT=sm[:, c, :], rhs=qgT[:, c, blk * 128:(blk + 1) * 128], start=False, stop=True)
            return ins
        P.op("pe", mm2, reads=[("vg", blk), "ATbf", "SA", "SB", "qgT"], writes=[obr])
        gla_out_norm(ob, obr, 512, 128, blk * 128)
        state_update(blk, True)

    def attn_block(blk, Etab, Eres, useflag=False):
        for j in range(2):
            sbk = []
            for half in range(2):
                bk, bkr = P.bank()
                sbk.append((bk, bkr))

                def mm(e, bk=bk, half=half, j=j):
                    ins = None
                    for kb in range(2):
                        kc = (blk + kb) * 128
                        ins = e.matmul(bk[:, kb * 256:(kb + 1) * 256], lhsT=kX[:, 2 * j + half, kc:kc + 128],
                                       rhs=qhT[:, 2 * j:2 * j + 2, blk * 128:(blk + 1) * 128], start=True, stop=True)
                    return ins
                P.op("pe", mm, reads=["kX", "qhT"], writes=[bkr])
            for half in range(2):
                bk, bkr = sbk[half]
                P.op("act", lambda e, bk=bk: e.activation(out=pe_f[:], in_=bk[:, 0:512], func=AF.Exp), reads=[bkr], writes=["pe_f"])
                P.op("dve", lambda e, half=half, j=j: e.tensor_tensor(out=PT[:, j, half].rearrange("p a b q -> p (a b q)"), in0=pe_f[:],
                                                                     in1=Etab[:, j, half].rearrange("p a b q -> p (a b q)"), op=ALU.mult),
                     reads=["pe_f", Eres], writes=[("PT", j)])
                if useflag:
                    P.op("dve", lambda e, half=half, j=j: e.tensor_scalar(out=PT[:, j, half, 0], in0=PT[:, j, half, 0], scalar1=flag_col[:, 0:1],
                                                                          scalar2=None, op0=ALU.mult), reads=[("PT", j), "flag"], writes=[("PT", j)])
            ob, obr = P.bank()
            db, dbr = P.bank()

            def mmv(e, ob=ob, db=db, j=j):
                ins = None
                for kb in range(2):
                    rhs = PT[:, j, :, kb, :, :]
                    e.matmul(ob[:, 0:512], lhsT=Vdup[:, blk + kb, j, :], rhs=rhs, start=(kb == 0), stop=(kb == 1))
                for kb in range(2):
                    rhs = PT[:, j, :, kb, :, :]
                    ins = e.matmul(db[:, 0:512], lhsT=ones_bf[:], rhs=rhs, start=(kb == 0), stop=(kb == 1))
                return ins
            P.op("pe", mmv, reads=[("PT", j), "Vdup", "ones"], writes=[obr, dbr])
            P.op("dve", lambda e, db=db, j=j: e.tensor_tensor(out=rec_f[:], in0=db[:, 0:512], in1=sinkexp[:, j].rearrange("p a b q -> p (a b q)"), op=ALU.add),
                 reads=[dbr, "sinkexp"], writes=["rec"])
            P.op("act", lambda e: e.activation(out=rec_f[:], in_=rec_f[:], func=AF.Ln), reads=["rec"], writes=["rec"])
            P.op("act", lambda e: e.activation(out=rec_f[:], in_=rec_f[:], func=AF.Exp, scale=-1.0), reads=["rec"], writes=["rec"])
            for half in range(2):
                r0 = half * 64
                P.op("dve", lambda e, ob=ob, half=half, r0=r0, j=j: e.tensor_tensor(
                    out=mixT[r0:r0 + 64, 2 * j:2 * j + 2, blk * 128:(blk + 1) * 128],
                    in0=ob[r0:r0 + 64, half * 256:(half + 1) * 256].rearrange("p (c q) -> p c q", c=2),
                    in1=rec_f[r0:r0 + 64, half * 256:(half + 1) * 256].rearrange("p (c q) -> p c q", c=2), op=ALU.mult),
                    reads=[obr, "rec"], writes=[("mixa", blk)])

    def qk_norm_chunk(bk, bkr, T, gcol, gres, out_fn):
        rstd_fm(bk[:, 0:T], T, bd_bf, 1.0 / 64, [bkr])
        out_fn(bk, bkr)

    def wo_ffnnorm(nb, s0, r0, s1, r1):
        for blk in range(nb):
            for cg, (s, r) in enumerate(((s0, r0), (s1, r1))):
                bk, bkr = P.bank()

                def mm(e, bk=bk, s=s, blk=blk):
                    ins = None
                    for k in range(8):
                        ins = e.matmul(bk[:, 0:512], lhsT=mixT[:, k, blk * 128:(blk + 1) * 128], rhs=ring[:, s, k, 0:512], start=(k == 0), stop=(k == 7))
                    return ins
                P.op("pe", mm, reads=[r, ("mixa", blk), ("mixg", blk * 128)], writes=[bkr])
                xs_ = X(blk)[:, cg * 512:(cg + 1) * 512]
                P.op("dve", lambda e, bk=bk, xs_=xs_: e.tensor_tensor(out=xs_, in0=bk[:, 0:512], in1=xs_, op=ALU.add), reads=[bkr, xr(blk)], writes=[xr(blk)])
            norm_block(blk, gfT)

    def ffn(nb, T, ydst_fn):
        for s6 in range(6):
            ncols = 512 if s6 < 5 else 256
            sg_, rg_ = wload([(0, wsrc("w_gate", 0, D, s6 * 512, ncols), 8, ncols)])
            su_, ru_ = wload([(0, wsrc("w_up", 0, D, s6 * 512, ncols), 8, ncols)])
            for mi in range(ncols // 128):
                m = s6 * 4 + mi
                gb, gbr = proj_fm(sg_, rg_, mi * 128, T, nb)
                ubk, ubr = proj_fm(su_, ru_, mi * 128, T, nb)
                P.op("act", lambda e, gb=gb: e.activation(out=sg_f[:, 0:T], in_=gb[:, 0:T], func=AF.Silu), reads=[gbr], writes=["sg_f"])
                P.op("dve", lambda e, ubk=ubk, m=m: e.tensor_tensor(out=aT[:, m, 0:T], in0=ubk[:, 0:T], in1=sg_f[:, 0:T], op=ALU.mult),
                     reads=[ubr, "sg_f"], writes=[("aT", m)])
        for cg in range(2):
            bks = [P.bank(hold=True) for _ in range(nb)]
            for kgp in range(3):
                nk = 8 if kgp < 2 else 6
                sl = wload([(0, wsrc("w_down", kgp * 1024, nk * 128, cg * 512, 512), nk, 512)])
                for blk in range(nb):
                    bk, bkr = bks[blk]

                    def mm(e, bk=bk, blk=blk, sl=sl, kgp=kgp, nk=nk):
                        ins = None
                        for kk in range(nk):
                            k = kgp * 8 + kk
                            ins = e.matmul(bk[:, 0:512], lhsT=aT[:, k, blk * 128:(blk + 1) * 128], rhs=ring[:, sl[0], kk, 0:512], start=(k == 0), stop=(k == NKF - 1))
                        return ins
                    P.op("pe", mm, reads=[sl[1]] + [("aT", kgp * 8 + kk) for kk in range(nk)], writes=[bkr])
            for blk in range(nb):
                bk, bkr = bks[blk]
                P.release(bkr)
                xs_ = X(blk)[:, cg * 512:(cg + 1) * 512]
                P.op("dve", lambda e, bk=bk, xs_=xs_: e.tensor_tensor(out=xs_, in0=bk[:, 0:512], in1=xs_, op=ALU.add), reads=[bkr, xr(blk)], writes=[xr(blk)])
                if cg == 1:
                    outs.append(P.dma("sp", ydst_fn(blk), X(blk), reads=[xr(blk)]))

    outs = []

    wstate["n"] = 0
    s_a, r_a = wload([(0, wsrc("w_in", 0, D, 1024, 256), 8, 256), (256, wsrc("w_in", 0, D, 2192, 128), 8, 128)], fp32=True)
    s_b, r_b = wload([(0, wsrc("w_in", 0, D, 1280, 512), 8, 512)], fp32=True)
    for k in range(8):
        P.op("dve", lambda e, k=k: e.tensor_scalar(out=ring[:, s_a, k, 0:384], in0=ring[:, s_a, k, 0:384], scalar1=gaT[:, k:k + 1], scalar2=None, op0=ALU.mult),
             reads=[r_a, "gaT"], writes=[r_a])
        P.op("dve", lambda e, k=k: e.tensor_scalar(out=ring[:, s_b, k, 0:512], in0=ring[:, s_b, k, 0:512], scalar1=gaT[:, k:k + 1], scalar2=None, op0=ALU.mult),
             reads=[r_b, "gaT"], writes=[r_b])
    negs = P.sb([128, 2], BF16)
    P.op("dve", lambda e: e.memset(negs[:], -1.0 / 16), writes=["negs"])
    tri_bf = P.sb([128, 128], BF16)
    P.op("dve", lambda e: e.tensor_copy(out=tri_bf[:], in_=tri_f[:]), reads=["tri"], writes=["tri_bf"])
    sp_h = (P.sb([128, 256], BF16), P.sb([128, 256], BF16))
    _enbflat = enb[:].rearrange("p c t -> p (c t)")
    _qgflat = qgT[:].rearrange("p c t -> p (c t)")
    for wn in ("w_in", "w_o", "w_gate", "w_up", "w_down"):
        nr = wf[wn].shape[0]
        step = 256
        for r0 in range(0, nr, step):
            r1 = min(nr, r0 + step)
            ci = P.dma("pool", wb[wn][r0:r1, :], wf[wn][r0:r1, :], reads=["wbfchain"], writes=[("wbf", wn), "wbfchain"])
            P.nofence = getattr(P, "nofence", set()) | {ci}
    NBLK = (16 if debug == "scan" else 0) if debug else NPRE // 128
    nbfs = (nbf, nbf_b); zts = (zt, zt_b); sps = (sp_t, sp_b); ktoks = (ktok, ktok_b)
    sbanks = {}

    def sc1(b):
        q = b % 4; p = b % 2
        P.dma("sp", xb[:, q, :], xpre[b * 128:(b + 1) * 128, :], writes=[("sx", q)])
        P.op("act", lambda e: e.activation(out=nbfs[p][:], in_=xb[:, q, :], func=AF.Square, accum_out=ss_c2[:, p:p + 1]),
             reads=[("sx", q)], writes=[("snbf", p), ("sss", p)])
        P.op("act", lambda e: e.activation(out=rs_c2[:, p:p + 1], in_=ss_c2[:, p:p + 1], func=AF.Ln, scale=1.0 / D, bias=eps_col[:, 0:1]),
             reads=[("sss", p), "eps"], writes=[("srs", p)])
        P.op("act", lambda e: e.activation(out=rs_c2[:, p:p + 1], in_=rs_c2[:, p:p + 1], func=AF.Exp, scale=-0.5), reads=[("srs", p)], writes=[("srs", p)])
        P.op("dve", lambda e: e.tensor_scalar(out=nbfs[p][:], in0=xb[:, q, :], scalar1=rs_c2[:, p:p + 1], scalar2=None, op0=ALU.mult),
             reads=[("sx", q), ("srs", p)], writes=[("snbf", p)])
        tb, tbr = tbank()

        def tr(e):
            ins = None
            for k in range(8):
                ins = e.transpose(out=tb[:, k * 128:(k + 1) * 128], in_=nbfs[p][:, k * 128:(k + 1) * 128], identity=ident_bf[:])
            return ins
        P.op("pe", tr, reads=[("snbf", p), "ident_bf"], writes=[tbr])
        if b % 2 == 0:
            P.op("act", lambda e: e.activation(out=actT[:, :, q * 128:(q + 1) * 128], in_=tb[:].rearrange("p (k t) -> p k t", k=8), func=AF.Copy),
                 reads=[tbr], writes=[("sact", q)])
        else:
            P.op("dve", lambda e: e.tensor_copy(out=actT[:, :, q * 128:(q + 1) * 128], in_=tb[:].rearrange("p (k t) -> p k t", k=8)),
                 reads=[tbr], writes=[("sact", q)])

    def sc2(b):
        q = b % 4
        ab, abr = P.bank()
        vb, vbr = P.bank()

        def mm(e):
            ins = None
            for k in range(8):
                ins = e.matmul(ab[:, 0:128], lhsT=ring[:, s_a, k, 256:384], rhs=actT[:, k, q * 128:(q + 1) * 128], start=(k == 0), stop=(k == 7))
            for k in range(8):
                ins = e.matmul(vb[:, 0:512], lhsT=actT[:, k, q * 128:(q + 1) * 128], rhs=ring[:, s_b, k, 0:512], start=(k == 0), stop=(k == 7))
            return ins
        P.op("pe", mm, reads=[r_a, r_b, ("sact", q)], writes=[abr, vbr])
        P.op("act", lambda e: e.activation(out=ulrT[:, q * 128:(q + 1) * 128], in_=ab[:, 0:128], func=AF.Copy), reads=[abr], writes=[("sulr", q)])
        P.op("act", lambda e: e.activation(out=vg_tok[:, q, :], in_=vb[:, 0:512], func=AF.Copy), reads=[vbr], writes=[("svg", q)])

    def sc3a(b):
        q = b % 4; p = b % 2
        zb, zbr = P.bank()
        sbanks[b] = (zb, zbr)
        P.op("pe", lambda e: e.matmul(zb[:, 0:256], lhsT=ulrT[:, q * 128:(q + 1) * 128], rhs=w2pad[:], start=True, stop=True),
             reads=[("sulr", q), "w2pad"], writes=[zbr])
        P.op("dve", lambda e: e.tensor_tensor(out=zts[p][:], in0=zb[:, 0:256], in1=bgate_bc[:], op=ALU.add), reads=[zbr, "bgate"], writes=[("szt", p)])
        P.op("act", lambda e: e.activation(out=zts[p][:], in_=zts[p][:], func=AF.Exp, scale=-1.0), reads=[("szt", p)], writes=[("szt", p)])
        P.op("act", lambda e: e.activation(out=sp_h[p][:], in_=zts[p][:], func=AF.Ln, bias=1.0), reads=[("szt", p)], writes=[("ssp", p)])

    def sc3b(b):
        q = b % 4; p = b % 2
        kb_, kbr = P.bank()
        cb, cbr = P.bank()
        en_ = _enbflat[:, q * 256:(q + 1) * 256]
        kt_ = _qgflat[:, q * 256:(q + 1) * 256]

        def mm(e):
            ins = None
            for k in range(8):
                ins = e.matmul(kb_[:, 0:256], lhsT=actT[:, k, q * 128:(q + 1) * 128], rhs=ring[:, s_a, k, 0:256], start=(k == 0), stop=(k == 7))
            return ins
        P.op("pe", mm, reads=[r_a, ("sact", q)], writes=[kbr])

        def mmc(e):
            e.matmul(cb[:, 0:256], lhsT=tri_bf[:], rhs=sp_h[p][:], start=True, stop=True)
            ins = None
            for c in range(2):
                ins = e.matmul(cb[:, 256 + 2 * c:258 + 2 * c], lhsT=sp_h[p][:, c * 128:(c + 1) * 128], rhs=negs[:], start=True, stop=True)
            return ins
        P.op("pe", mmc, reads=[("ssp", p), "tri_bf", "negs"], writes=[cbr])
        P.op("act", lambda e: e.activation(out=en_, in_=cb[:, 0:256], func=AF.Exp, scale=-1.0), reads=[cbr], writes=[("senb", q)])
        P.op("act", lambda e: e.activation(out=elast[:, :, q], in_=fap(cb[:, 256:257], [[2, 2]]), func=AF.Exp), reads=[cbr], writes=[("sel", q)])
        P.op("dve", lambda e: e.tensor_tensor(out=kt_, in0=kb_[:, 0:256], in1=en_, op=ALU.mult),
             reads=[kbr, ("senb", q)], writes=[("sktok", q)])

    def sc4(b):
        q = b % 4; p = b % 2
        kt = _qgflat[:, q * 256:(q + 1) * 256]
        ub, ubr = P.bank()

        def mm(e):
            ins = None
            for c in range(2):
                for ab in range(2):
                    h = 2 * c + ab
                    ins = e.matmul(ub[:, h * 128:(h + 1) * 128], lhsT=kt[:, c * 128:(c + 1) * 128], rhs=vg_tok[:, q, h * 128:(h + 1) * 128], start=True, stop=True)
            return ins
        P.op("pe", mm, reads=[("sktok", q), ("svg", q)], writes=[ubr])
        for ab in range(2):
            r0 = ab * 64
            P.op("dve", lambda e, ab=ab, r0=r0: e.tensor_tensor(out=Stmp[r0:r0 + 64], in0=fap(ub[r0:r0 + 64, ab * 128:ab * 128 + 1], [[256, 2], [1, 128]]),
                                                               in1=S[r0:r0 + 64], op=ALU.add), reads=[ubr, "S", "Stmp"], writes=["Stmp"])
        P.op("dve", lambda e: e.tensor_tensor(out=S[:], in0=Stmp[:], in1=fap(elast[:, 0, q:q + 1], [[4, 2], [0, 128]]), op=ALU.mult),
             reads=["Stmp", ("sel", q)], writes=["S"])

    stages = (sc1, sc2, sc3a, sc3b, sc4)
    for i in range(NBLK + len(stages) - 1):
        for si, st in enumerate(stages):
            b = i - si
            if 0 <= b < NBLK:
                st(b)
    P.barrier(keep=("wbf", "wbfchain"))
    P.op("act", lambda e: e.activation(out=SA[0:64], in_=S[0:64], func=AF.Copy), reads=["S", "zinit"], writes=["SA"])
    P.op("act", lambda e: e.activation(out=SB[64:128], in_=S[64:128], func=AF.Copy), reads=["S", "zinit"], writes=["SB"])

    def main_tile(kind, t):
        sample = kind == "sample"
        nb = 1 if sample else 4
        T = nb * 128
        CUR["p"] = 0 if sample else (t % 2)
        first = (kind == "prompt" and t == 0)
        last = (kind == "prompt" and t == 3)
        L0 = wload([(0, wsrc("w_in", 0, D, 0, 512), 8, 512)])
        L1 = wload([(0, wsrc("w_in", 0, D, 512, 512), 8, 512)])
        L2 = wload([(0, wsrc("w_in", 0, D, 1024, 256), 8, 256), (256, wsrc("w_in", 0, D, 2192, 128), 8, 128)])
        if first:
            front(lambda blk: xhalo, 1, gaT)
            halo_kv = True
            kv_part(L1, 1, 128, 0, False, False)
        if sample:
            front(lambda blk: xs, 1, gaT)
        else:
            front(lambda blk: xp[t * 512 + blk * 128: t * 512 + (blk + 1) * 128, :], 4, gaT)
        gla_prep(L2[0], L2[1], 256, nb, T, (tris_f if sample else tri_f)[:], "tris" if sample else "tri", True)
        for c in range(4):
            bk, bkr = proj_fm(L0[0], L0[1], c * 128, T, nb)
            rstd_fm(bk[:, 0:T], T, bd_bf, 1.0 / 64, [bkr])
            P.op("dve", lambda e, bk=bk, c=c: e.scalar_tensor_tensor(out=qhT[:, c, 0:T], in0=bk[:, 0:T], scalar=gq_col[:, 0:1], in1=rstd_f[:, 0:T],
                                                                    op0=ALU.mult, op1=ALU.mult), reads=[bkr, "rstd", "gq"], writes=["qhT"])
        kv_part(L1, nb, T, 1, last, sample)
        for c in range(2):
            bk, bkr = proj_fm(L1[0], L1[1], 256 + c * 128, T, nb)
            P.op("dve", lambda e, bk=bk, c=c: e.tensor_tensor(out=qgT[:, c, 0:T], in0=bk[:, 0:T], in1=ebq[:, c, 0:T], op=ALU.mult),
                 reads=[bkr, "ebq"], writes=["qgT"])
        kg_evac(L2[0], L2[1], 0, nb, T, sample)
        L3 = wload([(0, wsrc("w_in", 0, D, 1280, 512), 8, 512)])
        vg_tm(L3[0], L3[1], 0, nb)
        L4 = wload([(0, wsrc("w_in", 0, D, 1792, 512), 8, 512)])
        for c in range(4):
            bk, bkr = proj_fm(L4[0], L4[1], c * 128, T, nb)
            P.op("act", lambda e, bk=bk, c=c: e.activation(out=rsT[:, c, 0:T], in_=bk[:, 0:T], func=AF.Silu), reads=[bkr], writes=["rsT"])
        if sample:
            sample_attn()
            sample_gla()
        else:
            for blk in range(nb):
                attn_block(blk, E, "E", first and blk == 0)
                gla_block(blk)
            P.op("pool", lambda e: e.tensor_copy(out=kX[:, :, 0:128], in_=kX[:, :, 512:640]), reads=["kX"], writes=["kX"])
            P.op("pool", lambda e: e.tensor_copy(out=Vdup[:, 0], in_=Vdup[:, 4]), reads=["Vdup"], writes=["Vdup"])
        if debug:
            outs.append(P.dma("pool", dbg_mix, mixT[:], reads=[("mixa", b_) for b_ in range(nb)] + [("mixg", b_ * 128) for b_ in range(nb)]))
            outs.append(P.dma("pool", dbg_q, qhT[:], reads=["qhT"]))
            outs.append(P.dma("pool", dbg_rs, rsT[:], reads=["rsT"]))
        L5 = wload([(0, wsrc("w_o", 0, D, 0, 512), 8, 512)])
        L6 = wload([(0, wsrc("w_o", 0, D, 512, 512), 8, 512)])
        wo_ffnnorm(nb, L5[0], L5[1], L6[0], L6[1])
        if debug:
            outs.append(P.dma("sp", dbg_h, xb[:], reads=[("x", 0, b_) for b_ in range(nb)]))
            outs.append(P.dma("pool", dbg_z, actT[:], reads=[("actT", 0, b_) for b_ in range(nb)]))
        if sample:
            ffn(nb, T, lambda blk: y_s)
        else:
            ffn(nb, T, lambda blk: y_p[t * 512 + blk * 128: t * 512 + (blk + 1) * 128, :])

    def kv_part(L1, nb, T, kblk0, last, sample):
        bk, bkr = proj_fm(L1[0], L1[1], 0, T, nb)
        rstd_fm(bk[:, 0:T], T, bd_bf, 1.0 / 64, [bkr])
        c0 = kblk0 * 128
        P.op("dve", lambda e: e.scalar_tensor_tensor(out=khT_bf[:, 0:T], in0=bk[:, 0:T], scalar=gk_col[:, 0:1], in1=rstd_f[:, 0:T], op0=ALU.mult, op1=ALU.mult),
             reads=[bkr, "rstd", "gk"], writes=["khT"])
        if last or sample:
            lo = T - 128
            P.op("dve", lambda e: e.scalar_tensor_tensor(out=khT_f[:], in0=bk[:, lo:T], scalar=gk_col[:, 0:1], in1=rstd_f[:, lo:T], op0=ALU.mult, op1=ALU.mult),
                 reads=[bkr, "rstd", "gk"], writes=["khT_f"])
        P.op("act", lambda e: e.activation(out=kX[0:64, 0, c0:c0 + T], in_=khT_bf[0:64, 0:T], func=AF.Copy), reads=["khT", "kX"], writes=["kX"])
        P.op("act", lambda e: e.activation(out=kX[64:128, 3, c0:c0 + T], in_=khT_bf[64:128, 0:T], func=AF.Copy), reads=["khT", "kX"], writes=["kX"])
        b2, b2r = P.bank()
        P.op("pe", lambda e: e.matmul(b2[:, 0:T], lhsT=sw_bf[:], rhs=khT_bf[:, 0:T], start=True, stop=True), reads=["khT", "sw"], writes=[b2r])
        P.op("act", lambda e: e.activation(out=kX[0:64, 2, c0:c0 + T], in_=b2[0:64, 0:T], func=AF.Copy), reads=[b2r, "kX"], writes=["kX"])
        P.op("act", lambda e: e.activation(out=kX[64:128, 1, c0:c0 + T], in_=b2[64:128, 0:T], func=AF.Copy), reads=[b2r, "kX"], writes=["kX"])
        for blk in range(nb):
            vb, vbr = proj_tm(L1[0], L1[1], 128, 128, blk)
            P.op("act", lambda e, vb=vb, blk=blk: e.activation(out=Vdup[:, kblk0 + blk].rearrange("p j (u d) -> p j u d", u=2),
                                                               in_=fap(vb[:, 0:1], [[64, 2], [0, 2], [1, 64]]), func=AF.Copy),
                 reads=[vbr, "Vdup"], writes=["Vdup"])
            if (last and blk == nb - 1) or sample:
                P.op("dve", lambda e, vb=vb: e.tensor_copy(out=vw_f[:], in_=vb[:, 0:128]), reads=[vbr], writes=["vw_f"])
        if last or sample:
            kb_, kbr = P.bank()
            P.op("pe", lambda e: e.transpose(out=kb_[:, 0:128], in_=khT_f[:], identity=ident_f[:]), reads=["khT_f", "ident_f"], writes=[kbr])
            P.op("act", lambda e: e.activation(out=kw_f[:], in_=kb_[:, 0:128], func=AF.Copy), reads=[kbr], writes=["kw_f"])
        if last:
            outs.append(P.dma("sp", kwp, kw_f[:], reads=["kw_f"]))
            outs.append(P.dma("sp", vwp, vw_f[:], reads=["vw_f"]))

    def sample_attn():
        for (dst, src, new, nm) in ((kws, ck, kw_f, "kws"), (vws, cv, vw_f, "vws")):
            P.dma("sp", dst[:, 0:127, :], src[:, 1:128, :], writes=[nm])
            P.dma("sp", bass.AP(tensor=dst.tensor, offset=127 * 128, ap=[[128 * 128, 16], [1, 128]]), new[0:16, :], reads=["kw_f", "vw_f"], writes=[nm])
        t1 = P.dma("pool", Kw_bf[:], kws.rearrange("b k f -> k b f"), reads=["kws"], writes=["Kw"])
        for j in range(2):
            P.dma("pool", Kwsw_bf[:, :, (1 - j) * 64:(2 - j) * 64], kws[:, :, j * 64:(j + 1) * 64].rearrange("b k f -> k b f"), reads=["kws"], writes=["Kwsw"])
            for u in range(2):
                P.dma("pool", Vwd[:, :, j, u * 64:(u + 1) * 64], vws[:, :, j * 64:(j + 1) * 64].rearrange("b k f -> k b f"), reads=["vws"], writes=["Vwd"])
        outs.append(t1)
        P.op("dve", lambda e: e.tensor_copy(out=qsA[0:64], in_=qhT[0:64, :, 0:16]), reads=["qhT", "zinit"], writes=["qsA"])
        P.op("dve", lambda e: e.tensor_copy(out=qsB[64:128], in_=qhT[64:128, :, 0:16]), reads=["qhT", "zinit"], writes=["qsB"])
        sb_, sbr = P.bank()
        for b in range(16):
            tb, tbr = tbank()
            bf_ = b % 2

            def tr(e, tb=tb, b=b):
                e.transpose(out=tb[:, 0:128], in_=Kw_bf[:, b, :], identity=ident_bf[:])
                return e.transpose(out=tb[:, 128:256], in_=Kwsw_bf[:, b, :], identity=ident_bf[:])
            P.op("pe", tr, reads=["Kw", "Kwsw", "ident_bf"], writes=[tbr])
            P.op("act", lambda e, tb=tb, bf_=bf_: e.activation(out=KTb[:, bf_].rearrange("p a k -> p (a k)"), in_=tb[:, 0:256], func=AF.Copy),
                 reads=[tbr], writes=[("KTb", bf_)])

            def mm(e, b=b, bf_=bf_):
                ins = None
                for j in range(2):
                    for half in range(2):
                        kt = KTb[:, bf_, 0 if j == half else 1, :]
                        q = (qsA if half == 0 else qsB)[:, 2 * j:2 * j + 2, b]
                        o = b * 8 + (j * 2 + half) * 2
                        ins = e.matmul(sb_[:, o:o + 2], lhsT=kt, rhs=q, start=True, stop=True)
                return ins
            P.op("pe", mm, reads=[("KTb", bf_), "qsA", "qsB"], writes=[sbr])
        P.op("act", lambda e: e.activation(out=pes[:].rearrange("p b h -> p (b h)"), in_=sb_[:, 0:128], func=AF.Exp), reads=[sbr], writes=["pes"])
        P.op("dve", lambda e: e.tensor_tensor(out=Pts[:], in0=pes[:], in1=fap(E[:, 0, 0, 1, 0, 127:128], [[0, 16], [512, 4], [128, 2]]), op=ALU.mult),
             reads=["pes", "E"], writes=["Pts"])
        ob, obr = P.bank()
        db, dbr = P.bank()

        def mmv(e):
            ins = None
            for b in range(16):
                for j in range(2):
                    e.matmul(ob[:, b * 8 + j * 4:b * 8 + j * 4 + 4], lhsT=Vwd[:, b, j, :], rhs=Pts[:, b, j * 4:(j + 1) * 4], start=True, stop=True)
                ins = e.matmul(db[:, b * 8:(b + 1) * 8], lhsT=ones_bf[:], rhs=Pts[:, b, :], start=True, stop=True)
            return ins
        P.op("pe", mmv, reads=["Pts", "Vwd", "ones"], writes=[obr, dbr])
        P.op("dve", lambda e: e.tensor_tensor(out=rec_f[:, 0:128].rearrange("p (b h) -> p b h", b=16), in0=db[:, 0:128].rearrange("p (b h) -> p b h", b=16),
                                              in1=fap(sinkexp[:, 0, 0, 0, 0:1], [[0, 16], [128, 8]]), op=ALU.add), reads=[dbr, "sinkexp"], writes=["rec"])
        P.op("act", lambda e: e.activation(out=rec_f[:, 0:128], in_=rec_f[:, 0:128], func=AF.Ln), reads=["rec"], writes=["rec"])
        P.op("act", lambda e: e.activation(out=rec_f[:, 0:128], in_=rec_f[:, 0:128], func=AF.Exp, scale=-1.0), reads=["rec"], writes=["rec"])
        for j in range(2):
            for half in range(2):
                r0 = half * 64
                o = (j * 2 + half) * 2
                P.op("dve", lambda e, r0=r0, o=o, j=j: e.tensor_tensor(
                    out=mixT[r0:r0 + 64, 2 * j:2 * j + 2, 0:16],
                    in0=fap(ob[r0:r0 + 64, o:o + 1], [[1, 2], [8, 16]]),
                    in1=fap(rec_f[r0:r0 + 64, o:o + 1], [[1, 2], [8, 16]]), op=ALU.mult), reads=[obr, "rec"], writes=[("mixa", 0)])

    def sample_gla():
        ob, obr = P.bank(hold=True)
        for b in range(16):
            bf_ = b % 2
            P.dma("sp", Sb[:, bf_], sg[b].rearrange("(c u) d v -> (u d) c v", u=2), writes=[("Sb", bf_)])
            vb, vbr = P.bank()
            P.op("pe", lambda e, vb=vb, b=b: e.matmul(vb[:, 0:512], lhsT=sel_bf[:, b, :], rhs=vg_tok[:, 0, :], start=True, stop=True),
                 reads=["sel", ("vg", 0)], writes=[vbr])
            for c in range(2):
                for half in range(2):
                    r0 = half * 64
                    h = 2 * c + half
                    P.op("dve", lambda e, vb=vb, b=b, c=c, r0=r0, h=h, bf_=bf_: e.scalar_tensor_tensor(
                        out=Wt[r0:r0 + 64, bf_, c, :], in0=vb[r0:r0 + 64, h * 128:(h + 1) * 128], scalar=kgf[r0:r0 + 64, c, b:b + 1],
                        in1=Sb[r0:r0 + 64, bf_, c, :], op0=ALU.mult, op1=ALU.add), reads=[vbr, "kgf", ("Sb", bf_)], writes=[("Wt", bf_)])
            P.op("act", lambda e, bf_=bf_: e.activation(out=WA[0:64, bf_], in_=Wt[0:64, bf_], func=AF.Copy), reads=[("Wt", bf_), "zinit"], writes=[("WA", bf_)])
            P.op("act", lambda e, bf_=bf_: e.activation(out=WB[64:128, bf_], in_=Wt[64:128, bf_], func=AF.Copy), reads=[("Wt", bf_), "zinit"], writes=[("WB", bf_)])
            P.op("dve", lambda e, b=b, bf_=bf_: e.tensor_tensor(out=Sn[:, bf_], in0=Wt[:, bf_], in1=fap(elb[:, 0, b:b + 1], [[128, 2], [0, 128]]), op=ALU.mult),
                 reads=[("Wt", bf_), "elb"], writes=[("Sn", bf_)])
            outs.append(P.dma("sp", gss[b].rearrange("(c u) d v -> (u d) c v", u=2), Sn[:, bf_], reads=[("Sn", bf_)]))

            def mm(e, b=b, bf_=bf_):
                ins = None
                for h in range(4):
                    c, half = h // 2, h % 2
                    w = (WA if half == 0 else WB)[:, bf_, c, :]
                    ins = e.matmul(ob[:, h * 16 + b:h * 16 + b + 1], lhsT=w, rhs=qgT[:, c, b:b + 1], start=True, stop=True)
                return ins
            P.op("pe", mm, reads=[("WA", bf_), ("WB", bf_), "qgT"], writes=[obr])
        P.release(obr)
        gla_out_norm(ob, obr, 64, 16, 0)

    import os as _os
    _kb = _os.environ.get("KBAR", "")
    for t in range((0 if debug in ("sample", "scan") else (2 if debug == "two" else (4 if debug == "four" else 1))) if debug else 4):
        main_tile("prompt", t)
        if "t" in _kb:
            P.barrier()
    if debug and debug != "sample":
        outs.append(P.dma("pool", dbg_a, aT[:], reads=[("aT", m) for m in range(NKF)]))
    outs.append(P.dma("sp", gsp.rearrange("(c u) d v -> (u d) c v", u=2), S[:], reads=["S"]))
    P.barrier()
    if (not debug) or debug == "sample":
        main_tile("sample", 0)

    P.emit()
    P.stack.close()
    return nc


_CACHE = {}


def kernel(x_prompt, x_sample, cache_k, cache_v, state_gla, attn_norm_g, w_in, q_norm_g, k_norm_g, attn_sinks,
           rel_bias, w_gla_gate2, b_gla_gate, gla_norm_g, w_o, ffn_norm_g, w_gate, w_up, w_down):
    f = lambda a: np.ascontiguousarray(np.asarray(a, dtype=np.float32))
    xpr = f(x_prompt)[0]
    xsm = f(x_sample)[:, 0, :]
    ckk = f(cache_k)[0].reshape(128, 128, 128)
    cvv = f(cache_v)[0].reshape(128, 128, 128)
    sgg = f(state_gla)[0]
    consts = host_consts()
    shared = dict(w_in=f(w_in)[0], w_o=f(w_o)[0], w_gate=f(w_gate)[0], w_up=f(w_up)[0], w_down=f(w_down)[0],
                  attn_g=f(attn_norm_g)[0], ffn_g=f(ffn_norm_g)[0], qng=f(q_norm_g)[0], kng=f(k_norm_g)[0],
                  sinks=f(attn_sinks)[0], relb=f(rel_bias), w2=f(w_gla_gate2)[0], bgate=f(b_gla_gate)[0], glag=f(gla_norm_g)[0])
    for k, v in consts.items():
        shared["c_" + k] = v
    in_maps = []
    for c in range(NCORE):
        m = dict(shared)
        m["xp"] = xpr[c * TOK:(c + 1) * TOK]
        m["xhalo"] = xpr[c * TOK - 128:c * TOK] if c > 0 else np.zeros((128, D), np.float32)
        pre = np.zeros((NPRE, D), np.float32)
        if c > 0:
            pre[NPRE - c * TOK:] = xpr[:c * TOK]
        m["xpre"] = pre
        xs_ = np.zeros((128, D), np.float32)
        xs_[:16] = xsm[c * 16:(c + 1) * 16]
        m["xs"] = xs_
        m["ck"] = ckk[c * 16:(c + 1) * 16]
        m["cv"] = cvv[c * 16:(c + 1) * 16]
        m["sg"] = sgg[c * 16:(c + 1) * 16]
        m["flag"] = np.full((128, 1), 1.0 if c > 0 else 0.0, np.float32)
        in_maps.append(m)
    if "nc" not in _CACHE:
        _CACHE["nc"] = build_program()
    res = run_bass_kernel_spmd(_CACHE["nc"], in_maps, core_ids=list(range(NCORE)))
    R = res.results
    y_prompt = np.concatenate([R[c]["y_p"] for c in range(NCORE)], axis=0)[None]
    y_sample = np.concatenate([R[c]["y_s"][:16] for c in range(NCORE)], axis=0)[:, None, :]
    kwp = R[7]["kwp"].reshape(1, 1, 128, 2, 64)
    vwp = R[7]["vwp"].reshape(1, 1, 128, 2, 64)
    gsp = R[7]["gsp"].reshape(1, 1, 4, 64, 128)
    kws = np.concatenate([R[c]["kws"] for c in range(NCORE)], axis=0).reshape(1, 128, 128, 2, 64)
    vws = np.concatenate([R[c]["vws"] for c in range(NCORE)], axis=0).reshape(1, 128, 128, 2, 64)
    gss = np.concatenate([R[c]["gss"] for c in range(NCORE)], axis=0).reshape(1, 128, 4, 64, 128)
    return (y_prompt.astype(np.float32), y_sample.astype(np.float32), kwp, vwp, gsp, kws, vws, gss)
```

```python
import contextlib
import math
import numpy as np
import concourse.bass as bass
import concourse.mybir as mybir
from concourse.bass_utils import run_bass_kernel_spmd

F32 = mybir.dt.float32
BF16 = mybir.dt.bfloat16
AF = mybir.ActivationFunctionType
ALU = mybir.AluOpType

NCORE = 8
D = 1024
TOK = 2048
NPRE = 7 * 2048
DFF = 2816
NKF = DFF // 128
INW = 2320
ENGS = ("pe", "act", "dve", "pool", "sp")
NDMASEM = 12
NSLOT = 4
EPS = 1e-6
MASKV = -30000.0


class _Ins:
    def then_inc(self, *a, **k):
        return self


class _Mock:
    def __init__(self):
        self.cost = 0.0

    def _free(self, ap):
        n = 1
        for d in list(ap.shape)[1:]:
            n *= int(d)
        return n

    def matmul(self, out, lhsT=None, rhs=None, **k):
        n = max(self._free(rhs), 64)
        self.cost += (n * (4 if rhs.dtype == F32 else 1)) / 2400.0 + 0.01
        return _Ins()

    def transpose(self, out=None, in_=None, identity=None, **k):
        self.cost += (128 * (4 if in_.dtype == F32 else 1)) / 2400.0 + 0.01
        return _Ins()

    def __getattr__(self, name):
        def f(*a, **k):
            o = k.get("out", a[0] if a else None)
            n = self._free(o) if o is not None else 64
            self.cost += 0.2 + n / 1000.0
            return _Ins()
        return f


class Prog:
    def __init__(self, nc):
        self.nc = nc
        self.stack = contextlib.ExitStack()
        self.oplist = []
        self.last_w = {}
        self.readers = {}
        self.base = None
        self.sems = {}
        self.nbuf = 0
        self.banks = []
        self.bank_rr = 0
        self.held = set()
        self.finals = []

    def sb(self, shape, dt, name=None):
        self.nbuf += 1
        return self.stack.enter_context(self.nc.sbuf_tensor(name or f"sb{self.nbuf}", list(shape), dt))

    def ps(self, shape, dt, name=None):
        self.nbuf += 1
        return self.stack.enter_context(self.nc.psum_tensor(name or f"ps{self.nbuf}", list(shape), dt))

    def bank(self, hold=False):
        while True:
            i = self.bank_rr % len(self.banks)
            self.bank_rr += 1
            if i not in self.held:
                break
        if hold:
            self.held.add(i)
        return self.banks[i], ("ps", i)

    def release(self, res):
        self.held.discard(res[1])

    def _sem(self, key):
        if key not in self.sems:
            nm = "s_" + "_".join(str(k) for k in (key if isinstance(key, tuple) else (key,)))
            self.sems[key] = self.stack.enter_context(self.nc.semaphore(nm))
        return self.sems[key]

    def _add(self, eng, fn, reads, writes, kind, cost, dma=None):
        deps = set()
        for r in reads:
            t = self.last_w.get(r, self.base)
            if t is not None:
                deps.add(t)
        for w in writes:
            t = self.last_w.get(w, self.base)
            if t is not None:
                deps.add(t)
            deps.update(self.readers.get(w, ()))
        if not reads and not writes and self.base is not None:
            deps.add(self.base)
        i = len(self.oplist)
        deps.discard(i)
        self.oplist.append(dict(eng=eng, fn=fn, deps=deps, kind=kind, cost=cost, dma=dma))
        for r in reads:
            self.readers.setdefault(r, []).append(i)
        for w in writes:
            self.last_w[w] = i
            self.readers[w] = []
        return i

    def op(self, eng, fn, reads=(), writes=()):
        m = _Mock()
        fn(m)
        return self._add(eng, fn, reads, writes, "c", m.cost)

    def dma(self, eng, out, in_, reads=(), writes=()):
        n = 1
        for d in out.shape:
            n *= int(d)
        nbytes = n * (4 if out.dtype == F32 else 2)
        return self._add(eng, None, reads, writes, "d", 2.0 + nbytes / 150e3, dma=(out, in_))

    def barrier(self, keep=()):
        kept = {k: v for k, v in self.last_w.items() if (k in keep or (isinstance(k, tuple) and k and k[0] in keep))}
        skip = set(getattr(self, "nofence", ()))
        allprev = set(range(len(self.oplist))) - skip
        i = len(self.oplist)
        self.oplist.append(dict(eng="sp", fn=None, deps=allprev, kind="n", cost=0.05, dma=None))
        self.base = i
        self.bars = getattr(self, "bars", []) + [i]
        self.last_w = dict(kept)
        self.readers = {}

    def final_wait(self, eng, toks):
        pass

    def schedule(self):
        import heapq, os
        ops = self.oplist
        n = len(ops)
        succ = [[] for _ in range(n)]
        ndep = [0] * n
        for i, o in enumerate(ops):
            ndep[i] = len(o["deps"])
            for d in o["deps"]:
                succ[d].append(i)
        done = [0.0] * n
        ready_t = [0.0] * n
        efree = {e: 0.0 for e in ENGS}
        waiting = {e: [] for e in ENGS}
        avail = {e: [] for e in ENGS}
        order = {e: [] for e in ENGS}
        for i, o in enumerate(ops):
            if ndep[i] == 0:
                heapq.heappush(waiting[o["eng"]], (0.0, i))
        nsched = 0
        while nsched < n:
            best = None
            for e in ENGS:
                w, a = waiting[e], avail[e]
                while w and w[0][0] <= efree[e]:
                    heapq.heappush(a, heapq.heappop(w)[1])
                if a:
                    cand = (efree[e], a[0], e, True)
                elif w:
                    cand = (w[0][0], w[0][1], e, False)
                else:
                    continue
                if best is None or cand[:2] < best[:2]:
                    best = cand
            st, i, e, from_avail = best
            if from_avail:
                heapq.heappop(avail[e])
            else:
                heapq.heappop(waiting[e])
            o = ops[i]
            if o["kind"] == "d":
                efree[e] = st + 0.15
                done[i] = st + o["cost"]
            else:
                efree[e] = st + o["cost"]
                done[i] = st + o["cost"] + 0.15
            order[e].append(i)
            nsched += 1
            for sidx in succ[i]:
                ndep[sidx] -= 1
                if done[i] > ready_t[sidx]:
                    ready_t[sidx] = done[i]
                if ndep[sidx] == 0:
                    heapq.heappush(waiting[ops[sidx]["eng"]], (ready_t[sidx], sidx))
        self.sim_time = max(done) if n else 0.0
        self.sim_done = done
        if os.environ.get("KSIM"):
            print("SIM total", round(self.sim_time), "barriers", [round(done[b]) for b in getattr(self, "bars", [])])
        return order

    def emit(self):
        nc = self.nc
        import os
        order = self.schedule()
        km = os.environ.get("KSCHED", "mid")
        if km == "0":
            order = {e: [i for i, o in enumerate(self.oplist) if o["eng"] == e] for e in ENGS}
        elif km == "mid" and len(getattr(self, "bars", [])) >= 2:
            B2 = self.bars[-1]
            prog = {e: [i for i, o in enumerate(self.oplist) if o["eng"] == e] for e in ENGS}
            order = {e: [i for i in order[e] if i <= B2] + [i for i in prog[e] if i > B2] for e in ENGS}
        elif km in ("pre", "post") and self.base is not None:
            B = self.base
            prog = {e: [i for i, o in enumerate(self.oplist) if o["eng"] == e] for e in ENGS}
            if km == "post":
                order = {e: [i for i in prog[e] if i <= B] + [i for i in order[e] if i > B] for e in ENGS}
            else:
                order = {e: [i for i in order[e] if i <= B] + [i for i in prog[e] if i > B] for e in ENGS}
        ops = self.oplist
        tok = [None] * len(ops)
        ccnt = {e: 0 for e in ENGS}
        dcnt = {}
        drr = {e: 0 for e in ENGS}
        plan = {e: [] for e in ENGS}
        prevdma = {}
        for e in ENGS:
            for i in order[e]:
                o = ops[i]
                if o["kind"] == "d":
                    j = drr[e] % NDMASEM
                    drr[e] += 1
                    key = ("d", e, j)
                    c = dcnt.get(key, 0)
                    dcnt[key] = c + 1
                    tok[i] = (key, 16 * (c + 1))
                    prevdma[i] = (key, 16 * c) if c > 0 else None
                else:
                    ccnt[e] += 1
                    tok[i] = (e, ccnt[e])
        for e in ENGS:
            waited = {}
            for i in order[e]:
                o = ops[i]
                need = {}
                for d in o["deps"]:
                    k, v = tok[d]
                    if waited.get(k, 0) >= v:
                        continue
                    if need.get(k, 0) < v:
                        need[k] = v
                if o["kind"] == "d" and prevdma.get(i):
                    k, v = prevdma[i]
                    if waited.get(k, 0) < v and need.get(k, 0) < v:
                        need[k] = v
                for k, v in need.items():
                    waited[k] = v
                plan[e].append((list(need.items()), i))
            if e == "sp":
                fin = {}
                for i2, t in enumerate(tok):
                    if t is not None and fin.get(t[0], 0) < t[1]:
                        fin[t[0]] = t[1]
                plan[e].append(([(k, v) for k, v in fin.items() if waited.get(k, 0) < v], None))
        for e in ENGS:
            for (waits, i) in plan[e]:
                for (k, v) in waits:
                    self._sem(k)
                if i is not None:
                    self._sem(tok[i][0])
        if os.environ.get("KCHECK"):
            import collections
            sv = collections.defaultdict(int)
            ptr = {e: 0 for e in ENGS}
            while True:
                prog_ = False
                for e in ENGS:
                    while ptr[e] < len(plan[e]):
                        waits, i = plan[e][ptr[e]]
                        if all(sv[k] >= v for k, v in waits):
                            if i is not None:
                                k, v = tok[i]
                                inc = 16 if ops[i]["kind"] == "d" else 1
                                sv[k] += inc
                                assert sv[k] == v, ("token mismatch", e, i, k, v, sv[k])
                            ptr[e] += 1
                            prog_ = True
                        else:
                            break
                if all(ptr[e] == len(plan[e]) for e in ENGS):
                    print("KCHECK: ok, no deadlock")
                    break
                if not prog_:
                    for e in ENGS:
                        if ptr[e] < len(plan[e]):
                            waits, i = plan[e][ptr[e]]
                            print("KCHECK STUCK", e, ptr[e], i, [(k, v, sv[k]) for k, v in waits if sv[k] < v])
                    break
        block = self.stack.enter_context(nc.Block())

        def run(engname):
            def body(e):
                for (waits, i) in plan[engname]:
                    for (k, v) in waits:
                        e.wait_ge(self.sems[k], v)
                    if i is None:
                        continue
                    o = ops[i]
                    if o["kind"] == "d":
                        ins = e.dma_start(out=o["dma"][0], in_=o["dma"][1], allow_slow_non_contiguous=True)
                        ins.then_inc(self.sems[tok[i][0]], 16)
                    elif o["kind"] == "n":
                        ins = e.nop()
                        ins.then_inc(self.sems[tok[i][0]], 1)
                    else:
                        ins = o["fn"](e)
                        ins.then_inc(self.sems[tok[i][0]], 1)
            return body
        block.tensor(run("pe"))
        block.scalar(run("act"))
        block.vector(run("dve"))
        block.gpsimd(run("pool"))
        block.sync(run("sp"))


def fap(ap, dims):
    return bass.AP(tensor=ap.tensor, offset=ap.offset, ap=[list(ap.ap[0])] + [list(d) for d in dims])


def t5_bucket_np(n):
    n = np.maximum(n, 0)
    nf = np.maximum(n, 1).astype(np.float32)
    large = 16 + (np.log(nf / 16) / math.log(128 / 16) * 16).astype(np.int32)
    large = np.minimum(large, 31)
    return np.where(n < 16, n, large)


def host_consts():
    c = {}
    c["ident"] = np.eye(128, dtype=np.float32)
    s = np.arange(128)[:, None]
    t = np.arange(128)[None, :]
    c["tri"] = np.where(s <= t, -1.0 / 16, 0.0).astype(np.float32)
    c["tris"] = (np.eye(128) * (-1.0 / 16)).astype(np.float32)
    c["cmask"] = (s <= t).astype(np.float32)
    c["jx"] = np.eye(128, dtype=np.float32)[::-1].copy()
    bd = np.zeros((128, 128), np.float32)
    bd[:64, :64] = 1
    bd[64:, 64:] = 1
    c["bd"] = bd
    sw = np.zeros((128, 128), np.float32)
    for m in range(128):
        sw[(m + 64) % 128, m] = 1
    c["sw"] = sw
    oh = np.zeros((128, 2, 256), np.float32)
    for kb in range(2):
        off = 128 if kb == 0 else 0
        for i in range(255):
            dlt = 127 + off - i
            if 0 <= dlt <= 127:
                oh[int(t5_bucket_np(np.array(dlt))), kb, i] = 1.0
            else:
                oh[32, kb, i] = MASKV
        oh[32, kb, 255] = MASKV
    c["oh"] = oh
    sel = np.zeros((128, 16, 128), np.float32)
    for b in range(16):
        sel[b, b, :] = 1
    c["sel"] = sel
    return c


def build_program(debug=False):
    nc = bass.Bass("TRN2", target_bir_lowering=False)
    P = Prog(nc)

    def din(name, shape):
        return nc.dram_tensor(name, list(shape), F32, kind="ExternalInput").ap()

    def dout(name, shape):
        return nc.dram_tensor(name, list(shape), F32, kind="ExternalOutput").ap()

    xp = din("xp", [TOK, D]); xhalo = din("xhalo", [128, D]); xpre = din("xpre", [NPRE, D]); xs = din("xs", [128, D])
    ck = din("ck", [16, 128, 128]); cv = din("cv", [16, 128, 128]); sg = din("sg", [16, 4, 64, 128])
    w_in = din("w_in", [D, INW]); w_o = din("w_o", [D, D]); w_gate = din("w_gate", [D, DFF]); w_up = din("w_up", [D, DFF])
    w_down = din("w_down", [DFF, D])
    attn_g = din("attn_g", [D]); ffn_g = din("ffn_g", [D]); qg_in = din("qng", [64]); kg_in = din("kng", [64])
    sinks = din("sinks", [8]); relb = din("relb", [32, 8]); w2 = din("w2", [16, 256]); bgate = din("bgate", [256])
    glag = din("glag", [128]); flag = din("flag", [128, 1])
    cn = {k: din("c_" + k, v.shape) for k, v in host_consts().items()}

    y_p = dout("y_p", [TOK, D]); y_s = dout("y_s", [128, D])
    kwp = dout("kwp", [128, 128]); vwp = dout("vwp", [128, 128]); gsp = dout("gsp", [4, 64, 128])
    kws = dout("kws", [16, 128, 128]); vws = dout("vws", [16, 128, 128]); gss = dout("gss", [16, 4, 64, 128])
    fscr = nc.dram_tensor("fscr", [2, 8, 256], F32).ap()
    wb = {"w_in": nc.dram_tensor("wb_in", [D, INW], BF16).ap(), "w_o": nc.dram_tensor("wb_o", [D, D], BF16).ap(),
          "w_gate": nc.dram_tensor("wb_gate", [D, DFF], BF16).ap(), "w_up": nc.dram_tensor("wb_up", [D, DFF], BF16).ap(),
          "w_down": nc.dram_tensor("wb_down", [DFF, D], BF16).ap()}
    wf = {"w_in": w_in, "w_o": w_o, "w_gate": w_gate, "w_up": w_up, "w_down": w_down}
    if debug:
        dbg_mix = dout("dbg_mix", [128, 8, 512]); dbg_h = dout("dbg_h", [128, 4, D]); dbg_z = dout("dbg_z", [128, 8, 512]); dbg_a = dout("dbg_a", [128, NKF, 512])
        dbg_q = dout("dbg_q", [128, 4, 512]); dbg_rs = dout("dbg_rs", [128, 4, 512])

    for i in range(6):
        P.banks.append(P.ps([128, 512], F32, f"bank{i}"))
    psT = [P.ps([128, 1024], BF16, f"pst{i}") for i in range(2)]
    pst_rr = [0]

    def tbank():
        i = pst_rr[0] % 2
        pst_rr[0] += 1
        return psT[i], ("pst", i)

    ident_f = P.sb([128, 128], F32); ident_bf = P.sb([128, 128], BF16)
    tri_f = P.sb([128, 128], F32); tris_f = P.sb([128, 128], F32); cmask_bf = P.sb([128, 128], BF16)
    jx_f = P.sb([128, 128], F32); bd_bf = P.sb([128, 128], BF16); sw_bf = P.sb([128, 128], BF16)
    ones_bf = P.sb([128, 128], BF16); zeros_f = P.sb([128, 128], F32); scr_f = P.sb([128, 2048], F32)
    sel_bf = P.sb([128, 16, 128], BF16)
    gaT = P.sb([128, 8], F32); gfT = P.sb([128, 8], F32)
    gq_col = P.sb([128, 1], F32); gk_col = P.sb([128, 1], F32); glag_col = P.sb([128, 1], F32)
    eps_col = P.sb([128, 1], F32); ln8_col = P.sb([128, 1], F32); flag_col = P.sb([128, 1], F32)
    bgate_bc = P.sb([128, 256], F32); w2pad = P.sb([128, 256], BF16)
    relb_pad = P.sb([128, 128], F32)
    hank = scr_f[:, 0:1024].rearrange("p (h s) -> p h s", h=8)
    oh_sb = scr_f[:, 1024:1536].rearrange("p (a b) -> p a b", a=2)
    ftab = scr_f[0:8, 1536:2048].rearrange("p (a b) -> p a b", a=2)
    E = P.sb([128, 2, 2, 2, 2, 128], F32)
    sink_bc = P.sb([128, 8], F32); sinkexp = P.sb([128, 2, 2, 2, 128], F32)
    ring = P.sb([128, NSLOT, 8, 512], BF16)
    xb = P.sb([128, 4, D], F32)
    actT = P.sb([128, 8, 512], BF16)
    nbf = P.sb([128, D], BF16); nbf_b = P.sb([128, D], BF16)
    zt_b = P.sb([128, 256], F32); sp_b = P.sb([128, 256], F32); ktok_b = P.sb([128, 2, 2, 128], BF16)
    ss_c2 = P.sb([128, 2], F32); rs_c2 = P.sb([128, 2], F32)
    ss_c = P.sb([128, 1], F32); rs_c = P.sb([128, 1], F32)
    qhT = P.sb([128, 4, 512], BF16)
    kX = P.sb([128, 4, 640], BF16)
    khT_bf = P.sb([128, 512], BF16); khT_f = P.sb([128, 128], F32)
    Vdup = P.sb([128, 5, 2, 128], BF16)
    qgT = P.sb([128, 2, 512], BF16); kgA = P.sb([128, 2, 512], BF16); kgB = P.sb([128, 2, 512], BF16)
    kgf = P.sb([128, 2, 128], F32)
    ktok = P.sb([128, 2, 2, 128], BF16)
    vg_tok = P.sb([128, 4, 512], BF16)
    rsT = P.sb([128, 4, 512], BF16)
    ulrT = P.sb([128, 512], BF16)
    zt = P.sb([128, 256], F32); sp_t = P.sb([128, 256], F32)
    ebq = P.sb([128, 2, 512], F32); enb = P.sb([128, 2, 512], F32); elast = P.sb([128, 2, 4], F32)
    elb = P.sb([128, 2, 128], F32)
    S = P.sb([128, 2, 128], F32); Stmp = P.sb([128, 2, 128], F32); SA = P.sb([128, 2, 128], BF16); SB = P.sb([128, 2, 128], BF16)
    ATbf = P.sb([128, 4, 128], BF16)
    sq_bf = P.sb([128, 512], BF16); rstd_f = P.sb([128, 512], F32); tmp_f = P.sb([128, 512], F32)
    pe_f = P.sb([128, 512], F32); rec_f = P.sb([128, 512], F32)
    PT = P.sb([128, 2, 2, 2, 2, 128], BF16)
    mixT = P.sb([128, 8, 512], BF16)
    aT = P.sb([128, NKF, 512], BF16)
    sg_f = P.sb([128, 512], F32)
    vw_f = P.sb([128, 128], F32); kw_f = P.sb([128, 128], F32)
    Kw_bf = P.sb([128, 16, 128], BF16); Kwsw_bf = P.sb([128, 16, 128], BF16)
    KTb = P.sb([128, 2, 2, 128], BF16)
    Vwd = P.sb([128, 16, 2, 128], BF16)
    qsA = P.sb([128, 4, 16], BF16); qsB = P.sb([128, 4, 16], BF16)
    Pts = P.sb([128, 16, 8], BF16); pes = P.sb([128, 16, 8], F32)
    Sb = scr_f[:, 0:512].rearrange("p (a c v) -> p a c v", a=2, c=2)
    Wt = scr_f[:, 512:1024].rearrange("p (a c v) -> p a c v", a=2, c=2)
    Sn = scr_f[:, 1024:1536].rearrange("p (a c v) -> p a c v", a=2, c=2)
    WA = P.sb([128, 2, 2, 128], BF16); WB = P.sb([128, 2, 2, 128], BF16)

    CUR = {"p": 0}
    _vwd32 = Vwd[:].rearrange("p a b c -> p (a b c)").bitcast(F32)
    _kwf = Kw_bf[:].rearrange("p a b -> p (a b)")
    _kwswf = Kwsw_bf[:].rearrange("p a b -> p (a b)")
    nbfs2 = (nbf, nbf_b)

    def X(blk):
        if CUR["p"] == 0:
            return xb[:, blk, :]
        src = _vwd32 if blk < 2 else scr_f[:]
        o = (blk % 2) * 1024
        return src[:, o:o + 1024]

    def A(k):
        if CUR["p"] == 0:
            return actT[:, k, :]
        src = _kwf if k < 4 else _kwswf
        o = (k % 4) * 512
        return src[:, o:o + 512]

    def xr(blk):
        return ("x", CUR["p"], blk)

    def ar(blk):
        return ("actT", CUR["p"], blk)

    def ld(dst, src, name, eng="sp"):
        P.dma(eng, dst, src, writes=[name])

    ld(ident_f[:], cn["ident"], "ident_f"); ld(tri_f[:], cn["tri"], "tri"); ld(tris_f[:], cn["tris"], "tris")
    ld(jx_f[:], cn["jx"], "jx"); ld(oh_sb, cn["oh"], "oh")
    P.dma("pool", ident_bf[:], cn["ident"], writes=["ident_bf"])
    P.dma("pool", cmask_bf[:], cn["cmask"], writes=["cmask"])
    P.dma("pool", bd_bf[:], cn["bd"], writes=["bd"])
    P.dma("pool", sw_bf[:], cn["sw"], writes=["sw"])
    P.dma("pool", sel_bf[:], cn["sel"], writes=["sel"])
    ld(gaT[:], attn_g.rearrange("(k p) -> p k", p=128), "gaT"); ld(gfT[:], ffn_g.rearrange("(k p) -> p k", p=128), "gfT")
    for h in range(2):
        ld(gq_col[h * 64:(h + 1) * 64, :], qg_in.rearrange("(p o) -> p o", o=1), "gq")
        ld(gk_col[h * 64:(h + 1) * 64, :], kg_in.rearrange("(p o) -> p o", o=1), "gk")
    ld(glag_col[:], glag.rearrange("(p o) -> p o", o=1), "glag"); ld(flag_col[:], flag, "flag")
    ld(bgate_bc[:], bass.AP(tensor=bgate.tensor, offset=0, ap=[[0, 128], [1, 256]]), "bgate")
    ld(sink_bc[:], bass.AP(tensor=sinks.tensor, offset=0, ap=[[0, 128], [1, 8]]), "sink_bc")
    P.op("dve", lambda e: e.memset(ones_bf[:], 1.0), writes=["ones"])
    P.op("dve", lambda e: e.memset(zeros_f[:], 0.0), writes=["zeros"])
    P.op("dve", lambda e: e.memset(eps_col[:], EPS), writes=["eps"])
    P.op("dve", lambda e: e.memset(ln8_col[:], math.log(0.125)), writes=["ln8"])
    P.op("dve", lambda e: e.memset(w2pad[:], 0.0), writes=["w2pad"])
    P.dma("pool", w2pad[112:128, :], w2, reads=[], writes=["w2pad"])
    P.op("dve", lambda e: e.tensor_scalar(out=gq_col[:], in0=gq_col[:], scalar1=0.125, scalar2=None, op0=ALU.mult),
         reads=["gq"], writes=["gq"])
    for t_ in (kgA, kgB, SA, SB, qsA, qsB, WA, WB):
        P.op("dve", lambda e, t_=t_: e.memset(t_[:], 0.0), writes=["zinit"])
    P.op("dve", lambda e: e.memset(kX[:], 0.0), writes=["kX"])
    P.op("dve", lambda e: e.memset(S[:], 0.0), writes=["S"])
    P.op("dve", lambda e: e.memset(relb_pad[:], 0.0), writes=["relb_pad"])
    P.op("dve", lambda e: e.memset(relb_pad[32:33, :], 1.0), reads=[], writes=["relb_pad"])
    P.dma("sp", relb_pad[0:32, 0:8], relb, writes=["relb_pad"])

    bk, bkr = P.bank()
    P.op("pe", lambda e: e.matmul(bk[:, 0:512], lhsT=relb_pad[:], rhs=scr_f[:, 1024:1536], start=True, stop=True),
         reads=["relb_pad", "oh"], writes=[bkr])
    P.op("act", lambda e: e.activation(out=scr_f[0:8, 1536:2048], in_=bk[0:8, 0:512], func=AF.Copy), reads=[bkr], writes=["ftab"])
    P.dma("sp", fscr.rearrange("k h i -> h k i"), ftab, reads=["ftab"], writes=["fscr"])
    for kb in range(2):
        src = bass.AP(tensor=fscr.tensor, offset=kb * 8 * 256, ap=[[1, 128], [256, 8], [1, 128]])
        P.dma("sp", hank, src, reads=["fscr"], writes=["hank"])
        for hh in range(0, 8, 4):
            bk, bkr = P.bank()

            def mmj(e, bk=bk, hh=hh):
                ins = None
                for q in range(4):
                    ins = e.matmul(bk[:, q * 128:(q + 1) * 128], lhsT=hank[:, hh + q, :], rhs=jx_f[:], start=True, stop=True)
                return ins
            P.op("pe", mmj, reads=["hank", "jx"], writes=[bkr])
            for q in range(4):
                h = hh + q
                c_, half = h // 2, h % 2
                j, cl = c_ // 2, c_ % 2
                P.op("act", lambda e, bk=bk, q=q, j=j, half=half, kb=kb, cl=cl:
                     e.activation(out=E[:, j, half, kb, cl, :], in_=bk[:, q * 128:(q + 1) * 128], func=AF.Exp),
                     reads=[bkr], writes=["E"])
    P.op("act", lambda e: e.activation(out=sink_bc[:], in_=sink_bc[:], func=AF.Exp), reads=["sink_bc"], writes=["sink_bc"])
    for h in range(8):
        c_, half = h // 2, h % 2
        j, cl = c_ // 2, c_ % 2
        P.op("dve", lambda e, h=h, j=j, half=half, cl=cl: e.tensor_scalar(out=sinkexp[:, j, half, cl, :], in0=zeros_f[:], scalar1=sink_bc[:, h:h + 1],
                                                                         scalar2=None, op0=ALU.add), reads=["sink_bc", "zeros"], writes=["sinkexp"])

    wstate = {"n": 0}

    def wload(parts, fp32=False):
        s = wstate["n"] % NSLOT
        wstate["n"] += 1
        res = ("w", s)
        for (c0, (wn, r0, nrows, cc0, ncols_), nk, ncols) in parts:
            src = (wf if fp32 else wb)[wn][r0:r0 + nrows, cc0:cc0 + ncols_].rearrange("(k p) n -> p k n", p=128)
            q_ = "pool" if (fp32 or wstate["n"] % 2 == 0) else "sp"
            P.dma(q_, ring[:, s, 0:nk, c0:c0 + ncols], src, reads=([] if fp32 else [("wbf", wn)]), writes=[res])
        return s, res

    def wsrc(w, r0, nrows, c0, ncols):
        return (w, r0, nrows, c0, ncols)

    def front(src_fn, nb, gT):
        for blk in range(nb):
            P.dma("sp", X(blk), src_fn(blk), writes=[xr(blk)])
            norm_block(blk, gT)

    def norm_block(blk, gT):
        p = CUR["p"]
        xblk = X(blk); xres = xr(blk); ares = ar(blk)
        nb_ = nbfs2[p]; ss = ss_c2[:, p:p + 1]; rs = rs_c2[:, p:p + 1]
        P.op("act", lambda e: e.activation(out=nb_[:], in_=xblk, func=AF.Square, accum_out=ss),
             reads=[xres], writes=[("nbf", p), ("ss_c", p)])
        P.op("act", lambda e: e.activation(out=rs, in_=ss, func=AF.Ln, scale=1.0 / D, bias=eps_col[:, 0:1]),
             reads=[("ss_c", p), "eps"], writes=[("rs_c", p)])
        P.op("act", lambda e: e.activation(out=rs, in_=rs, func=AF.Exp, scale=-0.5), reads=[("rs_c", p)], writes=[("rs_c", p)])
        P.op("dve", lambda e: e.tensor_scalar(out=nb_[:], in0=xblk, scalar1=rs, scalar2=None, op0=ALU.mult),
             reads=[xres, ("rs_c", p)], writes=[("nbf", p)])
        tb, tbr = tbank()

        def tr(e):
            ins = None
            for k in range(8):
                ins = e.transpose(out=tb[:, k * 128:(k + 1) * 128], in_=nb_[:, k * 128:(k + 1) * 128], identity=ident_bf[:])
            return ins
        P.op("pe", tr, reads=[("nbf", p), "ident_bf"], writes=[tbr])
        if p == 0:
            P.op("dve", lambda e: e.tensor_tensor(out=actT[:, :, blk * 128:(blk + 1) * 128], in0=tb[:].rearrange("p (k t) -> p k t", k=8),
                                                  in1=fap(gT[:], [[1, 8], [0, 128]]), op=ALU.mult),
                 reads=[tbr, "gaT", "gfT"], writes=[ares])
        else:
            for kh, src in enumerate((_kwf, _kwswf)):
                P.op("dve", lambda e, kh=kh, src=src: e.tensor_tensor(
                    out=src.rearrange("p (k t) -> p k t", k=4)[:, :, blk * 128:(blk + 1) * 128],
                    in0=tb[:, kh * 512:(kh + 1) * 512].rearrange("p (k t) -> p k t", k=4),
                    in1=fap(gT[:, kh * 4:kh * 4 + 1], [[1, 4], [0, 128]]), op=ALU.mult),
                    reads=[tbr, "gaT", "gfT", ares], writes=[ares])

    def actT_reads(nb):
        return [ar(b) for b in range(nb)]

    def proj_fm(slot, res, col0, T, nb):
        bk, bkr = P.bank()
        acts = [A(k) for k in range(8)]

        def mm(e):
            ins = None
            for k in range(8):
                ins = e.matmul(bk[:, 0:T], lhsT=ring[:, slot, k, col0:col0 + 128], rhs=acts[k][:, 0:T], start=(k == 0), stop=(k == 7))
            return ins
        P.op("pe", mm, reads=[res] + actT_reads(nb), writes=[bkr])
        return bk, bkr

    def proj_tm(slot, res, col0, ncols, blk):
        bk, bkr = P.bank()
        acts = [A(k) for k in range(8)]

        def mm(e):
            ins = None
            for k in range(8):
                ins = e.matmul(bk[:, 0:ncols], lhsT=acts[k][:, blk * 128:(blk + 1) * 128], rhs=ring[:, slot, k, col0:col0 + ncols],
                               start=(k == 0), stop=(k == 7))
            return ins
        P.op("pe", mm, reads=[res, ar(blk)], writes=[bkr])
        return bk, bkr

    def rstd_fm(src_ap, T, lhs_ones, scale, reads_src):
        P.op("act", lambda e: e.activation(out=sq_bf[:, 0:T], in_=src_ap, func=AF.Square), reads=reads_src, writes=["sq"])
        b2, b2r = P.bank()
        P.op("pe", lambda e: e.matmul(b2[:, 0:T], lhsT=lhs_ones[:], rhs=sq_bf[:, 0:T], start=True, stop=True),
             reads=["sq", "bd", "ones"], writes=[b2r])
        P.op("act", lambda e: e.activation(out=rstd_f[:, 0:T], in_=b2[:, 0:T], func=AF.Ln, scale=scale, bias=eps_col[:, 0:1]),
             reads=[b2r, "eps"], writes=["rstd"])
        P.op("act", lambda e: e.activation(out=rstd_f[:, 0:T], in_=rstd_f[:, 0:T], func=AF.Exp, scale=-0.5), reads=["rstd"], writes=["rstd"])

    def gla_prep(slot_lr, res_lr, lrcol, nb, T, tri_ap, tri_res, full):
        bk, bkr = proj_fm(slot_lr, res_lr, lrcol, T, nb)
        P.op("act", lambda e: e.activation(out=ulrT[:, 0:T], in_=bk[:, 0:T], func=AF.Copy), reads=[bkr], writes=["ulrT"])
        bT = [P.bank(hold=True) for _ in range(2)]
        for blk in range(nb):
            zb, zbr = P.bank()
            P.op("pe", lambda e, zb=zb, blk=blk: e.matmul(zb[:, 0:256], lhsT=ulrT[:, blk * 128:(blk + 1) * 128], rhs=w2pad[:], start=True, stop=True),
                 reads=["ulrT", "w2pad"], writes=[zbr])
            P.op("dve", lambda e, zb=zb: e.tensor_tensor(out=zt[:], in0=zb[:, 0:256], in1=bgate_bc[:], op=ALU.add),
                 reads=[zbr, "bgate"], writes=["zt"])
            P.op("act", lambda e: e.activation(out=zt[:], in_=zt[:], func=AF.Exp, scale=-1.0), reads=["zt"], writes=["zt"])
            P.op("act", lambda e: e.activation(out=sp_t[:], in_=zt[:], func=AF.Ln, bias=1.0), reads=["zt"], writes=["sp_t"])
            for c in range(2):
                P.op("pe", lambda e, c=c, blk=blk: e.matmul(bT[c][0][:, blk * 128:(blk + 1) * 128], lhsT=sp_t[:, c * 128:(c + 1) * 128], rhs=tri_ap,
                                                           start=True, stop=True), reads=["sp_t", tri_res], writes=[bT[c][1]])
        for c in range(2):
            P.release(bT[c][1])
        for c in range(2):
            P.op("act", lambda e, c=c: e.activation(out=enb[:, c, 0:T], in_=bT[c][0][:, 0:T], func=AF.Exp, scale=-1.0), reads=[bT[c][1]], writes=["enb"])
            P.op("act", lambda e, c=c: e.activation(out=elast[:, c, 0:nb], in_=fap(bT[c][0][:, 127:128], [[128, nb]]), func=AF.Exp),
                 reads=[bT[c][1]], writes=["elast"])
            if full:
                P.op("act", lambda e, c=c: e.activation(out=ebq[:, c, 0:T], in_=bT[c][0][:, 0:T], func=AF.Exp, bias=ln8_col[:, 0:1]),
                     reads=[bT[c][1], "ln8"], writes=["ebq"])
                if T == 128:
                    P.op("act", lambda e, c=c: e.activation(out=elb[:, c, :], in_=bT[c][0][:, 0:128], func=AF.Exp), reads=[bT[c][1]], writes=["elb"])

    def kg_evac(slot, res, col0, nb, T, sample=False):
        for c in range(2):
            bk, bkr = proj_fm(slot, res, col0 + c * 128, T, nb)
            P.op("dve", lambda e, bk=bk, c=c: e.tensor_tensor(out=kgA[0:64, c, 0:T], in0=bk[0:64, 0:T], in1=enb[0:64, c, 0:T], op=ALU.mult),
                 reads=[bkr, "enb", "zinit"], writes=[("kgA", c)])
            P.op("dve", lambda e, bk=bk, c=c: e.tensor_tensor(out=kgB[64:128, c, 0:T], in0=bk[64:128, 0:T], in1=enb[64:128, c, 0:T], op=ALU.mult),
                 reads=[bkr, "enb", "zinit"], writes=[("kgB", c)])
            if sample:
                P.op("dve", lambda e, bk=bk, c=c: e.tensor_tensor(out=kgf[:, c, :], in0=bk[:, 0:128], in1=enb[:, c, 0:128], op=ALU.mult),
                     reads=[bkr, "enb"], writes=["kgf"])

    def vg_tm(slot, res, col0, nb):
        for blk in range(nb):
            bk, bkr = proj_tm(slot, res, col0, 512, blk)
            P.op("act", lambda e, bk=bk, blk=blk: e.activation(out=vg_tok[:, blk, :], in_=bk[:, 0:512], func=AF.Copy), reads=[bkr], writes=[("vg", blk)])

    def state_update(blk, masked):
        tb, tbr = tbank()

        def tr(e):
            ins = None
            for c in range(2):
                for ab, src in enumerate((kgA, kgB)):
                    o = (c * 2 + ab) * 128
                    ins = e.transpose(out=tb[:, o:o + 128], in_=src[:, c, blk * 128:(blk + 1) * 128], identity=ident_bf[:])
            return ins
        P.op("pe", tr, reads=[("kgA", 0), ("kgA", 1), ("kgB", 0), ("kgB", 1), "ident_bf"], writes=[tbr])
        P.op("act", lambda e: e.activation(out=ktok[:].rearrange("p c a f -> p (c a f)"), in_=tb[:, 0:512], func=AF.Copy), reads=[tbr], writes=["ktok"])
        ub, ubr = P.bank()

        def mm(e):
            ins = None
            for c in range(2):
                for ab in range(2):
                    h = 2 * c + ab
                    ins = e.matmul(ub[:, c * 128:(c + 1) * 128], lhsT=ktok[:, c, ab, :], rhs=vg_tok[:, blk, h * 128:(h + 1) * 128],
                                   start=(ab == 0), stop=(ab == 1))
            return ins
        P.op("pe", mm, reads=["ktok", ("vg", blk)], writes=[ubr])
        P.op("dve", lambda e: e.tensor_tensor(out=Stmp[:].rearrange("p c v -> p (c v)"), in0=ub[:, 0:256], in1=S[:].rearrange("p c v -> p (c v)"), op=ALU.add),
             reads=[ubr, "S"], writes=["Stmp"])
        P.op("dve", lambda e: e.tensor_tensor(out=S[:], in0=Stmp[:], in1=fap(elast[:, 0, blk:blk + 1], [[4, 2], [0, 128]]), op=ALU.mult),
             reads=["Stmp", "elast", "SA", "SB"], writes=["S"])
        if masked:
            P.op("act", lambda e: e.activation(out=SA[0:64], in_=S[0:64], func=AF.Copy), reads=["S", "zinit"], writes=["SA"])
            P.op("act", lambda e: e.activation(out=SB[64:128], in_=S[64:128], func=AF.Copy), reads=["S", "zinit"], writes=["SB"])

    def gla_out_norm(ob, obr, ncol, W, col0):
        rstd_fm(ob[:, 0:ncol], ncol, ones_bf, 1.0 / 128, [obr])
        P.op("dve", lambda e: e.scalar_tensor_tensor(out=tmp_f[:, 0:ncol], in0=ob[:, 0:ncol], scalar=glag_col[:, 0:1], in1=rstd_f[:, 0:ncol],
                                                     op0=ALU.mult, op1=ALU.mult), reads=[obr, "rstd", "glag"], writes=["tmp_f"])
        P.op("dve", lambda e: e.tensor_tensor(out=mixT[:, 4:8, col0:col0 + W], in0=tmp_f[:, 0:ncol].rearrange("p (h w) -> p h w", h=4),
                                              in1=rsT[:, :, col0:col0 + W], op=ALU.mult), reads=["tmp_f", "rsT"], writes=[("mixg", col0)])

    def gla_block(blk):
        ab_, abr = P.bank()

        def mm1(e):
            ins = None
            for h in range(4):
                c, half = h // 2, h % 2
                src = kgA if half == 0 else kgB
                ins = e.matmul(ab_[:, h * 128:(h + 1) * 128], lhsT=src[:, c, blk * 128:(blk + 1) * 128], rhs=qgT[:, c, blk * 128:(blk + 1) * 128],
                               start=True, stop=True)
            return ins
        P.op("pe", mm1, reads=[("kgA", 0), ("kgA", 1), ("kgB", 0), ("kgB", 1), "qgT"], writes=[abr])
        P.op("dve", lambda e: e.tensor_tensor(out=ATbf[:], in0=ab_[:, 0:512].rearrange("p (h t) -> p h t", h=4), in1=fap(cmask_bf[:], [[0, 4], [1, 128]]),
                                              op=ALU.mult), reads=[abr, "cmask"], writes=["ATbf"])
        ob, obr = P.bank()

        def mm2(e):
            ins = None
            for h in range(4):
                c, half = h // 2, h % 2
                sm = SA if half == 0 else SB
                e.matmul(ob[:, h * 128:(h + 1) * 128], lhsT=vg_tok[:, blk, h * 128:(h + 1) * 128], rhs=ATbf[:, h, :], start=True, stop=False)
                ins = e.matmul(ob[:, h * 128:(h + 1) * 128], lhsT=sm[:, c, :], rhs=qgT[:, c, blk * 128:(blk + 1) * 128], start=False, stop=True)
            return ins
        P.op("pe", mm2, reads=[("vg", blk), "ATbf", "SA", "SB", "qgT"], writes=[obr])
        gla_out_norm(ob, obr, 512, 128, blk * 128)
        state_update(blk, True)

    def attn_block(blk, Etab, Eres, useflag=False):
        for j in range(2):
            sbk = []
            for half in range(2):
                bk, bkr = P.bank()
                sbk.append((bk, bkr))

                def mm(e, bk=bk, half=half, j=j):
                    ins = None
                    for kb in range(2):
                        kc = (blk + kb) * 128
                        ins = e.matmul(bk[:, kb * 256:(kb + 1) * 256], lhsT=kX[:, 2 * j + half, kc:kc + 128],
                                       rhs=qhT[:, 2 * j:2 * j + 2, blk * 128:(blk + 1) * 128], start=True, stop=True)
                    return ins
                P.op("pe", mm, reads=["kX", "qhT"], writes=[bkr])
            for half in range(2):
                bk, bkr = sbk[half]
                P.op("act", lambda e, bk=bk: e.activation(out=pe_f[:], in_=bk[:, 0:512], func=AF.Exp), reads=[bkr], writes=["pe_f"])
                P.op("dve", lambda e, half=half, j=j: e.tensor_tensor(out=PT[:, j, half].rearrange("p a b q -> p (a b q)"), in0=pe_f[:],
                                                                     in1=Etab[:, j, half].rearrange("p a b q -> p (a b q)"), op=ALU.mult),
                     reads=["pe_f", Eres], writes=[("PT", j)])
                if useflag:
                    P.op("dve", lambda e, half=half, j=j: e.tensor_scalar(out=PT[:, j, half, 0], in0=PT[:, j, half, 0], scalar1=flag_col[:, 0:1],
                                                                          scalar2=None, op0=ALU.mult), reads=[("PT", j), "flag"], writes=[("PT", j)])
            ob, obr = P.bank()
            db, dbr = P.bank()

            def mmv(e, ob=ob, db=db, j=j):
                ins = None
                for kb in range(2):
                    rhs = PT[:, j, :, kb, :, :]
                    e.matmul(ob[:, 0:512], lhsT=Vdup[:, blk + kb, j, :], rhs=rhs, start=(kb == 0), stop=(kb == 1))
                for kb in range(2):
                    rhs = PT[:, j, :, kb, :, :]
                    ins = e.matmul(db[:, 0:512], lhsT=ones_bf[:], rhs=rhs, start=(kb == 0), stop=(kb == 1))
                return ins
            P.op("pe", mmv, reads=[("PT", j), "Vdup", "ones"], writes=[obr, dbr])
            P.op("dve", lambda e, db=db, j=j: e.tensor_tensor(out=rec_f[:], in0=db[:, 0:512], in1=sinkexp[:, j].rearrange("p a b q -> p (a b q)"), op=ALU.add),
                 reads=[dbr, "sinkexp"], writes=["rec"])
            P.op("act", lambda e: e.activation(out=rec_f[:], in_=rec_f[:], func=AF.Ln), reads=["rec"], writes=["rec"])
            P.op("act", lambda e: e.activation(out=rec_f[:], in_=rec_f[:], func=AF.Exp, scale=-1.0), reads=["rec"], writes=["rec"])
            for half in range(2):
                r0 = half * 64
                P.op("dve", lambda e, ob=ob, half=half, r0=r0, j=j: e.tensor_tensor(
                    out=mixT[r0:r0 + 64, 2 * j:2 * j + 2, blk * 128:(blk + 1) * 128],
                    in0=ob[r0:r0 + 64, half * 256:(half + 1) * 256].rearrange("p (c q) -> p c q", c=2),
                    in1=rec_f[r0:r0 + 64, half * 256:(half + 1) * 256].rearrange("p (c q) -> p c q", c=2), op=ALU.mult),
                    reads=[obr, "rec"], writes=[("mixa", blk)])

    def qk_norm_chunk(bk, bkr, T, gcol, gres, out_fn):
        rstd_fm(bk[:, 0:T], T, bd_bf, 1.0 / 64, [bkr])
        out_fn(bk, bkr)

    def wo_ffnnorm(nb, s0, r0, s1, r1):
        for blk in range(nb):
            for cg, (s, r) in enumerate(((s0, r0), (s1, r1))):
                bk, bkr = P.bank()

                def mm(e, bk=bk, s=s, blk=blk):
                    ins = None
                    for k in range(8):
                        ins = e.matmul(bk[:, 0:512], lhsT=mixT[:, k, blk * 128:(blk + 1) * 128], rhs=ring[:, s, k, 0:512], start=(k == 0), stop=(k == 7))
                    return ins
                P.op("pe", mm, reads=[r, ("mixa", blk), ("mixg", blk * 128)], writes=[bkr])
                xs_ = X(blk)[:, cg * 512:(cg + 1) * 512]
                P.op("dve", lambda e, bk=bk, xs_=xs_: e.tensor_tensor(out=xs_, in0=bk[:, 0:512], in1=xs_, op=ALU.add), reads=[bkr, xr(blk)], writes=[xr(blk)])
            norm_block(blk, gfT)

    def ffn(nb, T, ydst_fn):
        for s6 in range(6):
            ncols = 512 if s6 < 5 else 256
            sg_, rg_ = wload([(0, wsrc("w_gate", 0, D, s6 * 512, ncols), 8, ncols)])
            su_, ru_ = wload([(0, wsrc("w_up", 0, D, s6 * 512, ncols), 8, ncols)])
            for mi in range(ncols // 128):
                m = s6 * 4 + mi
                gb, gbr = proj_fm(sg_, rg_, mi * 128, T, nb)
                ubk, ubr = proj_fm(su_, ru_, mi * 128, T, nb)
                P.op("act", lambda e, gb=gb: e.activation(out=sg_f[:, 0:T], in_=gb[:, 0:T], func=AF.Silu), reads=[gbr], writes=["sg_f"])
                P.op("dve", lambda e, ubk=ubk, m=m: e.tensor_tensor(out=aT[:, m, 0:T], in0=ubk[:, 0:T], in1=sg_f[:, 0:T], op=ALU.mult),
                     reads=[ubr, "sg_f"], writes=[("aT", m)])
        for cg in range(2):
            bks = [P.bank(hold=True) for _ in range(nb)]
            for kgp in range(3):
                nk = 8 if kgp < 2 else 6
                sl = wload([(0, wsrc("w_down", kgp * 1024, nk * 128, cg * 512, 512), nk, 512)])
                for blk in range(nb):
                    bk, bkr = bks[blk]

                    def mm(e, bk=bk, blk=blk, sl=sl, kgp=kgp, nk=nk):
                        ins = None
                        for kk in range(nk):
                            k = kgp * 8 + kk
                            ins = e.matmul(bk[:, 0:512], lhsT=aT[:, k, blk * 128:(blk + 1) * 128], rhs=ring[:, sl[0], kk, 0:512], start=(k == 0), stop=(k == NKF - 1))
                        return ins
                    P.op("pe", mm, reads=[sl[1]] + [("aT", kgp * 8 + kk) for kk in range(nk)], writes=[bkr])
            for blk in range(nb):
                bk, bkr = bks[blk]
                P.release(bkr)
                xs_ = X(blk)[:, cg * 512:(cg + 1) * 512]
                P.op("dve", lambda e, bk=bk, xs_=xs_: e.tensor_tensor(out=xs_, in0=bk[:, 0:512], in1=xs_, op=ALU.add), reads=[bkr, xr(blk)], writes=[xr(blk)])
                if cg == 1:
                    outs.append(P.dma("sp", ydst_fn(blk), X(blk), reads=[xr(blk)]))

    outs = []

    wstate["n"] = 0
    s_a, r_a = wload([(0, wsrc("w_in", 0, D, 1024, 256), 8, 256), (256, wsrc("w_in", 0, D, 2192, 128), 8, 128)], fp32=True)
    s_b, r_b = wload([(0, wsrc("w_in", 0, D, 1280, 512), 8, 512)], fp32=True)
    for k in range(8):
        P.op("dve", lambda e, k=k: e.tensor_scalar(out=ring[:, s_a, k, 0:384], in0=ring[:, s_a, k, 0:384], scalar1=gaT[:, k:k + 1], scalar2=None, op0=ALU.mult),
             reads=[r_a, "gaT"], writes=[r_a])
        P.op("act", lambda e, k=k: e.activation(out=ring[:, s_b, k, 0:512], in_=ring[:, s_b, k, 0:512], func=AF.Copy, scale=gaT[:, k:k + 1]),
             reads=[r_b, "gaT"], writes=[r_b])
    negs = P.sb([128, 2], BF16)
    P.op("dve", lambda e: e.memset(negs[:], -1.0 / 16), writes=["negs"])
    tri_bf = P.sb([128, 128], BF16)
    P.op("dve", lambda e: e.tensor_copy(out=tri_bf[:], in_=tri_f[:]), reads=["tri"], writes=["tri_bf"])
    sp_h = (P.sb([128, 256], BF16), P.sb([128, 256], BF16))
    _enbflat = enb[:].rearrange("p c t -> p (c t)")
    _qgflat = qgT[:].rearrange("p c t -> p (c t)")
    for wn in ("w_in", "w_o", "w_gate", "w_up", "w_down"):
        nr = wf[wn].shape[0]
        step = 256
        for r0 in range(0, nr, step):
            r1 = min(nr, r0 + step)
            ci = P.dma("pool", wb[wn][r0:r1, :], wf[wn][r0:r1, :], reads=["wbfchain"] + ([r_a, r_b] if (wn == "w_in" and r0 == 0) else []),
                       writes=[("wbf", wn), "wbfchain"])
            P.nofence = getattr(P, "nofence", set()) | {ci}
    NBLK = (16 if debug == "scan" else 0) if debug else NPRE // 128
    nbfs = (nbf, nbf_b); zts = (zt, zt_b); sps = (sp_t, sp_b); ktoks = (ktok, ktok_b)
    sbanks = {}

    def sc1(b):
        q = b % 4; p = b % 2
        P.dma("sp", xb[:, q, :], xpre[b * 128:(b + 1) * 128, :], writes=[("sx", q)])
        P.op("act", lambda e: e.activation(out=nbfs[p][:], in_=xb[:, q, :], func=AF.Square, accum_out=ss_c2[:, p:p + 1]),
             reads=[("sx", q)], writes=[("snbf", p), ("sss", p)])
        P.op("act", lambda e: e.activation(out=rs_c2[:, p:p + 1], in_=ss_c2[:, p:p + 1], func=AF.Ln, scale=1.0 / D, bias=eps_col[:, 0:1]),
             reads=[("sss", p), "eps"], writes=[("srs", p)])
        P.op("act", lambda e: e.activation(out=rs_c2[:, p:p + 1], in_=rs_c2[:, p:p + 1], func=AF.Exp, scale=-0.5), reads=[("srs", p)], writes=[("srs", p)])
        P.op("dve", lambda e: e.tensor_scalar(out=nbfs[p][:], in0=xb[:, q, :], scalar1=rs_c2[:, p:p + 1], scalar2=None, op0=ALU.mult),
             reads=[("sx", q), ("srs", p)], writes=[("snbf", p)])
        tb, tbr = tbank()

        def tr(e):
            ins = None
            for k in range(8):
                ins = e.transpose(out=tb[:, k * 128:(k + 1) * 128], in_=nbfs[p][:, k * 128:(k + 1) * 128], identity=ident_bf[:])
            return ins
        P.op("pe", tr, reads=[("snbf", p), "ident_bf"], writes=[tbr])
        if b % 2 == 0:
            P.op("act", lambda e: e.activation(out=actT[:, :, q * 128:(q + 1) * 128], in_=tb[:].rearrange("p (k t) -> p k t", k=8), func=AF.Copy),
                 reads=[tbr], writes=[("sact", q)])
        else:
            P.op("dve", lambda e: e.tensor_copy(out=actT[:, :, q * 128:(q + 1) * 128], in_=tb[:].rearrange("p (k t) -> p k t", k=8)),
                 reads=[tbr], writes=[("sact", q)])

    def sc2(b):
        q = b % 4
        ab, abr = P.bank()
        vb, vbr = P.bank()

        def mm(e):
            ins = None
            for k in range(8):
                ins = e.matmul(ab[:, 0:128], lhsT=ring[:, s_a, k, 256:384], rhs=actT[:, k, q * 128:(q + 1) * 128], start=(k == 0), stop=(k == 7))
            for k in range(8):
                ins = e.matmul(vb[:, 0:512], lhsT=actT[:, k, q * 128:(q + 1) * 128], rhs=ring[:, s_b, k, 0:512], start=(k == 0), stop=(k == 7))
            return ins
        P.op("pe", mm, reads=[r_a, r_b, ("sact", q)], writes=[abr, vbr])
        P.op("act", lambda e: e.activation(out=ulrT[:, q * 128:(q + 1) * 128], in_=ab[:, 0:128], func=AF.Copy), reads=[abr], writes=[("sulr", q)])
        P.op("act", lambda e: e.activation(out=vg_tok[:, q, :], in_=vb[:, 0:512], func=AF.Copy), reads=[vbr], writes=[("svg", q)])

    def sc3a(b):
        q = b % 4; p = b % 2
        zb, zbr = P.bank()
        sbanks[b] = (zb, zbr)
        P.op("pe", lambda e: e.matmul(zb[:, 0:256], lhsT=ulrT[:, q * 128:(q + 1) * 128], rhs=w2pad[:], start=True, stop=True),
             reads=[("sulr", q), "w2pad"], writes=[zbr])
        P.op("dve", lambda e: e.tensor_tensor(out=zts[p][:], in0=zb[:, 0:256], in1=bgate_bc[:], op=ALU.add), reads=[zbr, "bgate"], writes=[("szt", p)])
        P.op("act", lambda e: e.activation(out=zts[p][:], in_=zts[p][:], func=AF.Exp, scale=-1.0), reads=[("szt", p)], writes=[("szt", p)])
        P.op("act", lambda e: e.activation(out=sp_h[p][:], in_=zts[p][:], func=AF.Ln, bias=1.0), reads=[("szt", p)], writes=[("ssp", p)])

    def sc3b(b):
        q = b % 4; p = b % 2
        kb_, kbr = P.bank()
        cb, cbr = P.bank()
        en_ = _enbflat[:, q * 256:(q + 1) * 256]
        kt_ = _qgflat[:, q * 256:(q + 1) * 256]

        def mm(e):
            ins = None
            for k in range(8):
                ins = e.matmul(kb_[:, 0:256], lhsT=actT[:, k, q * 128:(q + 1) * 128], rhs=ring[:, s_a, k, 0:256], start=(k == 0), stop=(k == 7))
            return ins
        P.op("pe", mm, reads=[r_a, ("sact", q)], writes=[kbr])

        def mmc(e):
            e.matmul(cb[:, 0:256], lhsT=tri_bf[:], rhs=sp_h[p][:], start=True, stop=True)
            ins = None
            for c in range(2):
                ins = e.matmul(cb[:, 256 + 2 * c:258 + 2 * c], lhsT=sp_h[p][:, c * 128:(c + 1) * 128], rhs=negs[:], start=True, stop=True)
            return ins
        P.op("pe", mmc, reads=[("ssp", p), "tri_bf", "negs"], writes=[cbr])
        P.op("act", lambda e: e.activation(out=en_, in_=cb[:, 0:256], func=AF.Exp, scale=-1.0), reads=[cbr], writes=[("senb", q)])
        P.op("act", lambda e: e.activation(out=elast[:, :, q], in_=fap(cb[:, 256:257], [[2, 2]]), func=AF.Exp), reads=[cbr], writes=[("sel", q)])
        P.op("dve", lambda e: e.tensor_tensor(out=kt_, in0=kb_[:, 0:256], in1=en_, op=ALU.mult),
             reads=[kbr, ("senb", q)], writes=[("sktok", q)])

    def sc4(b):
        q = b % 4; p = b % 2
        kt = _qgflat[:, q * 256:(q + 1) * 256]
        ub, ubr = P.bank()

        def mm(e):
            ins = None
            for c in range(2):
                for ab in range(2):
                    h = 2 * c + ab
                    ins = e.matmul(ub[:, h * 128:(h + 1) * 128], lhsT=kt[:, c * 128:(c + 1) * 128], rhs=vg_tok[:, q, h * 128:(h + 1) * 128], start=True, stop=True)
            return ins
        P.op("pe", mm, reads=[("sktok", q), ("svg", q)], writes=[ubr])
        for ab in range(2):
            r0 = ab * 64
            P.op("dve", lambda e, ab=ab, r0=r0: e.tensor_tensor(out=Stmp[r0:r0 + 64], in0=fap(ub[r0:r0 + 64, ab * 128:ab * 128 + 1], [[256, 2], [1, 128]]),
                                                               in1=S[r0:r0 + 64], op=ALU.add), reads=[ubr, "S", "Stmp"], writes=["Stmp"])
        P.op("dve", lambda e: e.tensor_tensor(out=S[:], in0=Stmp[:], in1=fap(elast[:, 0, q:q + 1], [[4, 2], [0, 128]]), op=ALU.mult),
             reads=["Stmp", ("sel", q)], writes=["S"])

    stages = (sc1, sc2, sc3a, sc3b, sc4)
    for i in range(NBLK + len(stages) - 1):
        for si, st in enumerate(stages):
            b = i - si
            if 0 <= b < NBLK:
                st(b)
    P.barrier(keep=("wbf", "wbfchain"))
    P.op("act", lambda e: e.activation(out=SA[0:64], in_=S[0:64], func=AF.Copy), reads=["S", "zinit"], writes=["SA"])
    P.op("act", lambda e: e.activation(out=SB[64:128], in_=S[64:128], func=AF.Copy), reads=["S", "zinit"], writes=["SB"])

    def main_tile(kind, t):
        sample = kind == "sample"
        nb = 1 if sample else 4
        T = nb * 128
        CUR["p"] = 0 if sample else (t % 2)
        first = (kind == "prompt" and t == 0)
        last = (kind == "prompt" and t == 3)
        L0 = wload([(0, wsrc("w_in", 0, D, 0, 512), 8, 512)])
        L1 = wload([(0, wsrc("w_in", 0, D, 512, 512), 8, 512)])
        L2 = wload([(0, wsrc("w_in", 0, D, 1024, 256), 8, 256), (256, wsrc("w_in", 0, D, 2192, 128), 8, 128)])
        if first:
            front(lambda blk: xhalo, 1, gaT)
            halo_kv = True
            kv_part(L1, 1, 128, 0, False, False)
        if sample:
            front(lambda blk: xs, 1, gaT)
        else:
            front(lambda blk: xp[t * 512 + blk * 128: t * 512 + (blk + 1) * 128, :], 4, gaT)
        gla_prep(L2[0], L2[1], 256, nb, T, (tris_f if sample else tri_f)[:], "tris" if sample else "tri", True)
        for c in range(4):
            bk, bkr = proj_fm(L0[0], L0[1], c * 128, T, nb)
            rstd_fm(bk[:, 0:T], T, bd_bf, 1.0 / 64, [bkr])
            P.op("dve", lambda e, bk=bk, c=c: e.scalar_tensor_tensor(out=qhT[:, c, 0:T], in0=bk[:, 0:T], scalar=gq_col[:, 0:1], in1=rstd_f[:, 0:T],
                                                                    op0=ALU.mult, op1=ALU.mult), reads=[bkr, "rstd", "gq"], writes=["qhT"])
        kv_part(L1, nb, T, 1, last, sample)
        for c in range(2):
            bk, bkr = proj_fm(L1[0], L1[1], 256 + c * 128, T, nb)
            P.op("dve", lambda e, bk=bk, c=c: e.tensor_tensor(out=qgT[:, c, 0:T], in0=bk[:, 0:T], in1=ebq[:, c, 0:T], op=ALU.mult),
                 reads=[bkr, "ebq"], writes=["qgT"])
        kg_evac(L2[0], L2[1], 0, nb, T, sample)
        L3 = wload([(0, wsrc("w_in", 0, D, 1280, 512), 8, 512)])
        vg_tm(L3[0], L3[1], 0, nb)
        L4 = wload([(0, wsrc("w_in", 0, D, 1792, 512), 8, 512)])
        for c in range(4):
            bk, bkr = proj_fm(L4[0], L4[1], c * 128, T, nb)
            P.op("act", lambda e, bk=bk, c=c: e.activation(out=rsT[:, c, 0:T], in_=bk[:, 0:T], func=AF.Silu), reads=[bkr], writes=["rsT"])
        if sample:
            sample_attn()
            sample_gla()
        else:
            for blk in range(nb):
                attn_block(blk, E, "E", first and blk == 0)
                gla_block(blk)
            P.op("pool", lambda e: e.tensor_copy(out=kX[:, :, 0:128], in_=kX[:, :, 512:640]), reads=["kX"], writes=["kX"])
            P.op("pool", lambda e: e.tensor_copy(out=Vdup[:, 0], in_=Vdup[:, 4]), reads=["Vdup"], writes=["Vdup"])
        if debug:
            outs.append(P.dma("pool", dbg_mix, mixT[:], reads=[("mixa", b_) for b_ in range(nb)] + [("mixg", b_ * 128) for b_ in range(nb)]))
            outs.append(P.dma("pool", dbg_q, qhT[:], reads=["qhT"]))
            outs.append(P.dma("pool", dbg_rs, rsT[:], reads=["rsT"]))
        L5 = wload([(0, wsrc("w_o", 0, D, 0, 512), 8, 512)])
        L6 = wload([(0, wsrc("w_o", 0, D, 512, 512), 8, 512)])
        wo_ffnnorm(nb, L5[0], L5[1], L6[0], L6[1])
        if debug:
            outs.append(P.dma("sp", dbg_h, xb[:], reads=[("x", 0, b_) for b_ in range(nb)]))
            outs.append(P.dma("pool", dbg_z, actT[:], reads=[("actT", 0, b_) for b_ in range(nb)]))
        if sample:
            ffn(nb, T, lambda blk: y_s)
        else:
            ffn(nb, T, lambda blk: y_p[t * 512 + blk * 128: t * 512 + (blk + 1) * 128, :])

    def kv_part(L1, nb, T, kblk0, last, sample):
        bk, bkr = proj_fm(L1[0], L1[1], 0, T, nb)
        rstd_fm(bk[:, 0:T], T, bd_bf, 1.0 / 64, [bkr])
        c0 = kblk0 * 128
        P.op("dve", lambda e: e.scalar_tensor_tensor(out=khT_bf[:, 0:T], in0=bk[:, 0:T], scalar=gk_col[:, 0:1], in1=rstd_f[:, 0:T], op0=ALU.mult, op1=ALU.mult),
             reads=[bkr, "rstd", "gk"], writes=["khT"])
        if last or sample:
            lo = T - 128
            P.op("dve", lambda e: e.scalar_tensor_tensor(out=khT_f[:], in0=bk[:, lo:T], scalar=gk_col[:, 0:1], in1=rstd_f[:, lo:T], op0=ALU.mult, op1=ALU.mult),
                 reads=[bkr, "rstd", "gk"], writes=["khT_f"])
        P.op("act", lambda e: e.activation(out=kX[0:64, 0, c0:c0 + T], in_=khT_bf[0:64, 0:T], func=AF.Copy), reads=["khT", "kX"], writes=["kX"])
        P.op("act", lambda e: e.activation(out=kX[64:128, 3, c0:c0 + T], in_=khT_bf[64:128, 0:T], func=AF.Copy), reads=["khT", "kX"], writes=["kX"])
        b2, b2r = P.bank()
        P.op("pe", lambda e: e.matmul(b2[:, 0:T], lhsT=sw_bf[:], rhs=khT_bf[:, 0:T], start=True, stop=True), reads=["khT", "sw"], writes=[b2r])
        P.op("act", lambda e: e.activation(out=kX[0:64, 2, c0:c0 + T], in_=b2[0:64, 0:T], func=AF.Copy), reads=[b2r, "kX"], writes=["kX"])
        P.op("act", lambda e: e.activation(out=kX[64:128, 1, c0:c0 + T], in_=b2[64:128, 0:T], func=AF.Copy), reads=[b2r, "kX"], writes=["kX"])
        for blk in range(nb):
            vb, vbr = proj_tm(L1[0], L1[1], 128, 128, blk)
            P.op("act", lambda e, vb=vb, blk=blk: e.activation(out=Vdup[:, kblk0 + blk].rearrange("p j (u d) -> p j u d", u=2),
                                                               in_=fap(vb[:, 0:1], [[64, 2], [0, 2], [1, 64]]), func=AF.Copy),
                 reads=[vbr, "Vdup"], writes=["Vdup"])
            if (last and blk == nb - 1) or sample:
                P.op("dve", lambda e, vb=vb: e.tensor_copy(out=vw_f[:], in_=vb[:, 0:128]), reads=[vbr], writes=["vw_f"])
        if last or sample:
            kb_, kbr = P.bank()
            P.op("pe", lambda e: e.transpose(out=kb_[:, 0:128], in_=khT_f[:], identity=ident_f[:]), reads=["khT_f", "ident_f"], writes=[kbr])
            P.op("act", lambda e: e.activation(out=kw_f[:], in_=kb_[:, 0:128], func=AF.Copy), reads=[kbr], writes=["kw_f"])
        if last:
            outs.append(P.dma("sp", kwp, kw_f[:], reads=["kw_f"]))
            outs.append(P.dma("sp", vwp, vw_f[:], reads=["vw_f"]))

    def sample_attn():
        for (dst, src, new, nm) in ((kws, ck, kw_f, "kws"), (vws, cv, vw_f, "vws")):
            P.dma("sp", dst[:, 0:127, :], src[:, 1:128, :], writes=[nm])
            P.dma("sp", bass.AP(tensor=dst.tensor, offset=127 * 128, ap=[[128 * 128, 16], [1, 128]]), new[0:16, :], reads=["kw_f", "vw_f"], writes=[nm])
        t1 = P.dma("pool", Kw_bf[:], kws.rearrange("b k f -> k b f"), reads=["kws"], writes=["Kw"])
        for j in range(2):
            P.dma("pool", Kwsw_bf[:, :, (1 - j) * 64:(2 - j) * 64], kws[:, :, j * 64:(j + 1) * 64].rearrange("b k f -> k b f"), reads=["kws"], writes=["Kwsw"])
            for u in range(2):
                P.dma("pool", Vwd[:, :, j, u * 64:(u + 1) * 64], vws[:, :, j * 64:(j + 1) * 64].rearrange("b k f -> k b f"), reads=["vws"], writes=["Vwd"])
        outs.append(t1)
        P.op("dve", lambda e: e.tensor_copy(out=qsA[0:64], in_=qhT[0:64, :, 0:16]), reads=["qhT", "zinit"], writes=["qsA"])
        P.op("dve", lambda e: e.tensor_copy(out=qsB[64:128], in_=qhT[64:128, :, 0:16]), reads=["qhT", "zinit"], writes=["qsB"])
        sb_, sbr = P.bank()
        for b in range(16):
            tb, tbr = tbank()
            bf_ = b % 2

            def tr(e, tb=tb, b=b):
                e.transpose(out=tb[:, 0:128], in_=Kw_bf[:, b, :], identity=ident_bf[:])
                return e.transpose(out=tb[:, 128:256], in_=Kwsw_bf[:, b, :], identity=ident_bf[:])
            P.op("pe", tr, reads=["Kw", "Kwsw", "ident_bf"], writes=[tbr])
            P.op("act", lambda e, tb=tb, bf_=bf_: e.activation(out=KTb[:, bf_].rearrange("p a k -> p (a k)"), in_=tb[:, 0:256], func=AF.Copy),
                 reads=[tbr], writes=[("KTb", bf_)])

            def mm(e, b=b, bf_=bf_):
                ins = None
                for j in range(2):
                    for half in range(2):
                        kt = KTb[:, bf_, 0 if j == half else 1, :]
                        q = (qsA if half == 0 else qsB)[:, 2 * j:2 * j + 2, b]
                        o = b * 8 + (j * 2 + half) * 2
                        ins = e.matmul(sb_[:, o:o + 2], lhsT=kt, rhs=q, start=True, stop=True)
                return ins
            P.op("pe", mm, reads=[("KTb", bf_), "qsA", "qsB"], writes=[sbr])
        P.op("act", lambda e: e.activation(out=pes[:].rearrange("p b h -> p (b h)"), in_=sb_[:, 0:128], func=AF.Exp), reads=[sbr], writes=["pes"])
        P.op("dve", lambda e: e.tensor_tensor(out=Pts[:], in0=pes[:], in1=fap(E[:, 0, 0, 1, 0, 127:128], [[0, 16], [512, 4], [128, 2]]), op=ALU.mult),
             reads=["pes", "E"], writes=["Pts"])
        ob, obr = P.bank()
        db, dbr = P.bank()

        def mmv(e):
            ins = None
            for b in range(16):
                for j in range(2):
                    e.matmul(ob[:, b * 8 + j * 4:b * 8 + j * 4 + 4], lhsT=Vwd[:, b, j, :], rhs=Pts[:, b, j * 4:(j + 1) * 4], start=True, stop=True)
                ins = e.matmul(db[:, b * 8:(b + 1) * 8], lhsT=ones_bf[:], rhs=Pts[:, b, :], start=True, stop=True)
            return ins
        P.op("pe", mmv, reads=["Pts", "Vwd", "ones"], writes=[obr, dbr])
        P.op("dve", lambda e: e.tensor_tensor(out=rec_f[:, 0:128].rearrange("p (b h) -> p b h", b=16), in0=db[:, 0:128].rearrange("p (b h) -> p b h", b=16),
                                              in1=fap(sinkexp[:, 0, 0, 0, 0:1], [[0, 16], [128, 8]]), op=ALU.add), reads=[dbr, "sinkexp"], writes=["rec"])
        P.op("act", lambda e: e.activation(out=rec_f[:, 0:128], in_=rec_f[:, 0:128], func=AF.Ln), reads=["rec"], writes=["rec"])
        P.op("act", lambda e: e.activation(out=rec_f[:, 0:128], in_=rec_f[:, 0:128], func=AF.Exp, scale=-1.0), reads=["rec"], writes=["rec"])
        for j in range(2):
            for half in range(2):
                r0 = half * 64
                o = (j * 2 + half) * 2
                P.op("dve", lambda e, r0=r0, o=o, j=j: e.tensor_tensor(
                    out=mixT[r0:r0 + 64, 2 * j:2 * j + 2, 0:16],
                    in0=fap(ob[r0:r0 + 64, o:o + 1], [[1, 2], [8, 16]]),
                    in1=fap(rec_f[r0:r0 + 64, o:o + 1], [[1, 2], [8, 16]]), op=ALU.mult), reads=[obr, "rec"], writes=[("mixa", 0)])

    def sample_gla():
        ob, obr = P.bank(hold=True)
        for b in range(16):
            bf_ = b % 2
            P.dma("sp", Sb[:, bf_], sg[b].rearrange("(c u) d v -> (u d) c v", u=2), writes=[("Sb", bf_)])
            vb, vbr = P.bank()
            P.op("pe", lambda e, vb=vb, b=b: e.matmul(vb[:, 0:512], lhsT=sel_bf[:, b, :], rhs=vg_tok[:, 0, :], start=True, stop=True),
                 reads=["sel", ("vg", 0)], writes=[vbr])
            for c in range(2):
                for half in range(2):
                    r0 = half * 64
                    h = 2 * c + half
                    P.op("dve", lambda e, vb=vb, b=b, c=c, r0=r0, h=h, bf_=bf_: e.scalar_tensor_tensor(
                        out=Wt[r0:r0 + 64, bf_, c, :], in0=vb[r0:r0 + 64, h * 128:(h + 1) * 128], scalar=kgf[r0:r0 + 64, c, b:b + 1],
                        in1=Sb[r0:r0 + 64, bf_, c, :], op0=ALU.mult, op1=ALU.add), reads=[vbr, "kgf", ("Sb", bf_)], writes=[("Wt", bf_)])
            P.op("act", lambda e, bf_=bf_: e.activation(out=WA[0:64, bf_], in_=Wt[0:64, bf_], func=AF.Copy), reads=[("Wt", bf_), "zinit"], writes=[("WA", bf_)])
            P.op("act", lambda e, bf_=bf_: e.activation(out=WB[64:128, bf_], in_=Wt[64:128, bf_], func=AF.Copy), reads=[("Wt", bf_), "zinit"], writes=[("WB", bf_)])
            P.op("dve", lambda e, b=b, bf_=bf_: e.tensor_tensor(out=Sn[:, bf_], in0=Wt[:, bf_], in1=fap(elb[:, 0, b:b + 1], [[128, 2], [0, 128]]), op=ALU.mult),
                 reads=[("Wt", bf_), "elb"], writes=[("Sn", bf_)])
            outs.append(P.dma("sp", gss[b].rearrange("(c u) d v -> (u d) c v", u=2), Sn[:, bf_], reads=[("Sn", bf_)]))

            def mm(e, b=b, bf_=bf_):
                ins = None
                for h in range(4):
                    c, half = h // 2, h % 2
                    w = (WA if half == 0 else WB)[:, bf_, c, :]
                    ins = e.matmul(ob[:, h * 16 + b:h * 16 + b + 1], lhsT=w, rhs=qgT[:, c, b:b + 1], start=True, stop=True)
                return ins
            P.op("pe", mm, reads=[("WA", bf_), ("WB", bf_), "qgT"], writes=[obr])
        P.release(obr)
        gla_out_norm(ob, obr, 64, 16, 0)

    import os as _os
    _kb = _os.environ.get("KBAR", "")
    for t in range((0 if debug in ("sample", "scan") else (2 if debug == "two" else (4 if debug == "four" else 1))) if debug else 4):
        main_tile("prompt", t)
        if "t" in _kb:
            P.barrier()
    if debug and debug != "sample":
        outs.append(P.dma("pool", dbg_a, aT[:], reads=[("aT", m) for m in range(NKF)]))
    outs.append(P.dma("sp", gsp.rearrange("(c u) d v -> (u d) c v", u=2), S[:], reads=["S"]))
    P.barrier()
    if (not debug) or debug == "sample":
        main_tile("sample", 0)

    P.emit()
    P.stack.close()
    return nc


_CACHE = {}


def kernel(x_prompt, x_sample, cache_k, cache_v, state_gla, attn_norm_g, w_in, q_norm_g, k_norm_g, attn_sinks,
           rel_bias, w_gla_gate2, b_gla_gate, gla_norm_g, w_o, ffn_norm_g, w_gate, w_up, w_down):
    f = lambda a: np.ascontiguousarray(np.asarray(a, dtype=np.float32))
    xpr = f(x_prompt)[0]
    xsm = f(x_sample)[:, 0, :]
    ckk = f(cache_k)[0].reshape(128, 128, 128)
    cvv = f(cache_v)[0].reshape(128, 128, 128)
    sgg = f(state_gla)[0]
    consts = host_consts()
    shared = dict(w_in=f(w_in)[0], w_o=f(w_o)[0], w_gate=f(w_gate)[0], w_up=f(w_up)[0], w_down=f(w_down)[0],
                  attn_g=f(attn_norm_g)[0], ffn_g=f(ffn_norm_g)[0], qng=f(q_norm_g)[0], kng=f(k_norm_g)[0],
                  sinks=f(attn_sinks)[0], relb=f(rel_bias), w2=f(w_gla_gate2)[0], bgate=f(b_gla_gate)[0], glag=f(gla_norm_g)[0])
    for k, v in consts.items():
        shared["c_" + k] = v
    in_maps = []
    for c in range(NCORE):
        m = dict(shared)
        m["xp"] = xpr[c * TOK:(c + 1) * TOK]
        m["xhalo"] = xpr[c * TOK - 128:c * TOK] if c > 0 else np.zeros((128, D), np.float32)
        pre = np.zeros((NPRE, D), np.float32)
        if c > 0:
            pre[NPRE - c * TOK:] = xpr[:c * TOK]
        m["xpre"] = pre
        xs_ = np.zeros((128, D), np.float32)
        xs_[:16] = xsm[c * 16:(c + 1) * 16]
        m["xs"] = xs_
        m["ck"] = ckk[c * 16:(c + 1) * 16]
        m["cv"] = cvv[c * 16:(c + 1) * 16]
        m["sg"] = sgg[c * 16:(c + 1) * 16]
        m["flag"] = np.full((128, 1), 1.0 if c > 0 else 0.0, np.float32)
        in_maps.append(m)
    if "nc" not in _CACHE:
        _CACHE["nc"] = build_program()
    res = run_bass_kernel_spmd(_CACHE["nc"], in_maps, core_ids=list(range(NCORE)))
    R = res.results
    y_prompt = np.concatenate([R[c]["y_p"] for c in range(NCORE)], axis=0)[None]
    y_sample = np.concatenate([R[c]["y_s"][:16] for c in range(NCORE)], axis=0)[:, None, :]
    kwp = R[7]["kwp"].reshape(1, 1, 128, 2, 64)
    vwp = R[7]["vwp"].reshape(1, 1, 128, 2, 64)
    gsp = R[7]["gsp"].reshape(1, 1, 4, 64, 128)
    kws = np.concatenate([R[c]["kws"] for c in range(NCORE)], axis=0).reshape(1, 128, 128, 2, 64)
    vws = np.concatenate([R[c]["vws"] for c in range(NCORE)], axis=0).reshape(1, 128, 128, 2, 64)
    gss = np.concatenate([R[c]["gss"] for c in range(NCORE)], axis=0).reshape(1, 128, 4, 64, 128)
    return (y_prompt.astype(np.float32), y_sample.astype(np.float32), kwp, vwp, gsp, kws, vws, gss)
```

```python
import contextlib
import math
import numpy as np
import concourse.bass as bass
import concourse.mybir as mybir
from concourse.bass_utils import run_bass_kernel_spmd

F32 = mybir.dt.float32
BF16 = mybir.dt.bfloat16
AF = mybir.ActivationFunctionType
ALU = mybir.AluOpType

NCORE = 8
D = 1024
TOK = 2048
NPRE = 7 * 2048
DFF = 2816
NKF = DFF // 128
INW = 2320
ENGS = ("pe", "act", "dve", "pool", "sp")
NDMASEM = 12
NSLOT = 4
EPS = 1e-6
MASKV = -30000.0


class _Ins:
    def then_inc(self, *a, **k):
        return self


class _Mock:
    def __init__(self):
        self.cost = 0.0

    def _free(self, ap):
        n = 1
        for d in list(ap.shape)[1:]:
            n *= int(d)
        return n

    def matmul(self, out, lhsT=None, rhs=None, **k):
        n = max(self._free(rhs), 64)
        self.cost += (n * (4 if rhs.dtype == F32 else 1)) / 2400.0 + 0.01
        return _Ins()

    def transpose(self, out=None, in_=None, identity=None, **k):
        self.cost += (128 * (4 if in_.dtype == F32 else 1)) / 2400.0 + 0.01
        return _Ins()

    def __getattr__(self, name):
        def f(*a, **k):
            o = k.get("out", a[0] if a else None)
            n = self._free(o) if o is not None else 64
            self.cost += 0.2 + n / 1000.0
            return _Ins()
        return f


class Prog:
    def __init__(self, nc):
        self.nc = nc
        self.stack = contextlib.ExitStack()
        self.oplist = []
        self.last_w = {}
        self.readers = {}
        self.base = None
        self.sems = {}
        self.nbuf = 0
        self.banks = []
        self.bank_rr = 0
        self.held = set()
        self.finals = []

    def sb(self, shape, dt, name=None):
        self.nbuf += 1
        return self.stack.enter_context(self.nc.sbuf_tensor(name or f"sb{self.nbuf}", list(shape), dt))

    def ps(self, shape, dt, name=None):
        self.nbuf += 1
        return self.stack.enter_context(self.nc.psum_tensor(name or f"ps{self.nbuf}", list(shape), dt))

    def bank(self, hold=False):
        while True:
            i = self.bank_rr % len(self.banks)
            self.bank_rr += 1
            if i not in self.held:
                break
        if hold:
            self.held.add(i)
        return self.banks[i], ("ps", i)

    def release(self, res):
        self.held.discard(res[1])

    def _sem(self, key):
        if key not in self.sems:
            nm = "s_" + "_".join(str(k) for k in (key if isinstance(key, tuple) else (key,)))
            self.sems[key] = self.stack.enter_context(self.nc.semaphore(nm))
        return self.sems[key]

    def _add(self, eng, fn, reads, writes, kind, cost, dma=None):
        deps = set()
        for r in reads:
            t = self.last_w.get(r, self.base)
            if t is not None:
                deps.add(t)
        for w in writes:
            t = self.last_w.get(w, self.base)
            if t is not None:
                deps.add(t)
            deps.update(self.readers.get(w, ()))
        if not reads and not writes and self.base is not None:
            deps.add(self.base)
        i = len(self.oplist)
        deps.discard(i)
        self.oplist.append(dict(eng=eng, fn=fn, deps=deps, kind=kind, cost=cost, dma=dma))
        for r in reads:
            self.readers.setdefault(r, []).append(i)
        for w in writes:
            self.last_w[w] = i
            self.readers[w] = []
        return i

    def op(self, eng, fn, reads=(), writes=()):
        m = _Mock()
        fn(m)
        return self._add(eng, fn, reads, writes, "c", m.cost)

    def dma(self, eng, out, in_, reads=(), writes=()):
        n = 1
        for d in out.shape:
            n *= int(d)
        nbytes = n * (4 if out.dtype == F32 else 2)
        return self._add(eng, None, reads, writes, "d", 2.0 + nbytes / 150e3, dma=(out, in_))

    def barrier(self, keep=()):
        kept = {k: v for k, v in self.last_w.items() if (k in keep or (isinstance(k, tuple) and k and k[0] in keep))}
        skip = set(getattr(self, "nofence", ()))
        allprev = set(range(len(self.oplist))) - skip
        i = len(self.oplist)
        self.oplist.append(dict(eng="sp", fn=None, deps=allprev, kind="n", cost=0.05, dma=None))
        self.base = i
        self.bars = getattr(self, "bars", []) + [i]
        self.last_w = dict(kept)
        self.readers = {}

    def final_wait(self, eng, toks):
        pass

    def schedule(self):
        import heapq, os
        ops = self.oplist
        n = len(ops)
        succ = [[] for _ in range(n)]
        ndep = [0] * n
        for i, o in enumerate(ops):
            ndep[i] = len(o["deps"])
            for d in o["deps"]:
                succ[d].append(i)
        done = [0.0] * n
        ready_t = [0.0] * n
        efree = {e: 0.0 for e in ENGS}
        waiting = {e: [] for e in ENGS}
        avail = {e: [] for e in ENGS}
        order = {e: [] for e in ENGS}
        for i, o in enumerate(ops):
            if ndep[i] == 0:
                heapq.heappush(waiting[o["eng"]], (0.0, i))
        nsched = 0
        while nsched < n:
            best = None
            for e in ENGS:
                w, a = waiting[e], avail[e]
                while w and w[0][0] <= efree[e]:
                    heapq.heappush(a, heapq.heappop(w)[1])
                if a:
                    cand = (efree[e], a[0], e, True)
                elif w:
                    cand = (w[0][0], w[0][1], e, False)
                else:
                    continue
                if best is None or cand[:2] < best[:2]:
                    best = cand
            st, i, e, from_avail = best
            if from_avail:
                heapq.heappop(avail[e])
            else:
                heapq.heappop(waiting[e])
            o = ops[i]
            if o["kind"] == "d":
                efree[e] = st + 0.15
                done[i] = st + o["cost"]
            else:
                efree[e] = st + o["cost"]
                done[i] = st + o["cost"] + 0.15
            order[e].append(i)
            nsched += 1
            for sidx in succ[i]:
                ndep[sidx] -= 1
                if done[i] > ready_t[sidx]:
                    ready_t[sidx] = done[i]
                if ndep[sidx] == 0:
                    heapq.heappush(waiting[ops[sidx]["eng"]], (ready_t[sidx], sidx))
        self.sim_time = max(done) if n else 0.0
        self.sim_done = done
        if os.environ.get("KSIM"):
            print("SIM total", round(self.sim_time), "barriers", [round(done[b]) for b in getattr(self, "bars", [])])
        return order

    def emit(self):
        nc = self.nc
        import os
        order = self.schedule()
        km = os.environ.get("KSCHED", "mid")
        if km == "0":
            order = {e: [i for i, o in enumerate(self.oplist) if o["eng"] == e] for e in ENGS}
        elif km == "mid" and len(getattr(self, "bars", [])) >= 2:
            B2 = self.bars[-1]
            prog = {e: [i for i, o in enumerate(self.oplist) if o["eng"] == e] for e in ENGS}
            order = {e: [i for i in order[e] if i <= B2] + [i for i in prog[e] if i > B2] for e in ENGS}
        elif km in ("pre", "post") and self.base is not None:
            B = self.base
            prog = {e: [i for i, o in enumerate(self.oplist) if o["eng"] == e] for e in ENGS}
            if km == "post":
                order = {e: [i for i in prog[e] if i <= B] + [i for i in order[e] if i > B] for e in ENGS}
            else:
                order = {e: [i for i in order[e] if i <= B] + [i for i in prog[e] if i > B] for e in ENGS}
        ops = self.oplist
        tok = [None] * len(ops)
        ccnt = {e: 0 for e in ENGS}
        dcnt = {}
        drr = {e: 0 for e in ENGS}
        plan = {e: [] for e in ENGS}
        prevdma = {}
        for e in ENGS:
            for i in order[e]:
                o = ops[i]
                if o["kind"] == "d":
                    j = drr[e] % NDMASEM
                    drr[e] += 1
                    key = ("d", e, j)
                    c = dcnt.get(key, 0)
                    dcnt[key] = c + 1
                    tok[i] = (key, 16 * (c + 1))
                    prevdma[i] = (key, 16 * c) if c > 0 else None
                else:
                    ccnt[e] += 1
                    tok[i] = (e, ccnt[e])
        for e in ENGS:
            waited = {}
            for i in order[e]:
                o = ops[i]
                need = {}
                for d in o["deps"]:
                    k, v = tok[d]
                    if waited.get(k, 0) >= v:
                        continue
                    if need.get(k, 0) < v:
                        need[k] = v
                if o["kind"] == "d" and prevdma.get(i):
                    k, v = prevdma[i]
                    if waited.get(k, 0) < v and need.get(k, 0) < v:
                        need[k] = v
                for k, v in need.items():
                    waited[k] = v
                plan[e].append((list(need.items()), i))
            if e == "sp":
                fin = {}
                for i2, t in enumerate(tok):
                    if t is not None and fin.get(t[0], 0) < t[1]:
                        fin[t[0]] = t[1]
                plan[e].append(([(k, v) for k, v in fin.items() if waited.get(k, 0) < v], None))
        for e in ENGS:
            for (waits, i) in plan[e]:
                for (k, v) in waits:
                    self._sem(k)
                if i is not None:
                    self._sem(tok[i][0])
        if os.environ.get("KCHECK"):
            import collections
            sv = collections.defaultdict(int)
            ptr = {e: 0 for e in ENGS}
            while True:
                prog_ = False
                for e in ENGS:
                    while ptr[e] < len(plan[e]):
                        waits, i = plan[e][ptr[e]]
                        if all(sv[k] >= v for k, v in waits):
                            if i is not None:
                                k, v = tok[i]
                                inc = 16 if ops[i]["kind"] == "d" else 1
                                sv[k] += inc
                                assert sv[k] == v, ("token mismatch", e, i, k, v, sv[k])
                            ptr[e] += 1
                            prog_ = True
                        else:
                            break
                if all(ptr[e] == len(plan[e]) for e in ENGS):
                    print("KCHECK: ok, no deadlock")
                    break
                if not prog_:
                    for e in ENGS:
                        if ptr[e] < len(plan[e]):
                            waits, i = plan[e][ptr[e]]
                            print("KCHECK STUCK", e, ptr[e], i, [(k, v, sv[k]) for k, v in waits if sv[k] < v])
                    break
        block = self.stack.enter_context(nc.Block())

        def run(engname):
            def body(e):
                for (waits, i) in plan[engname]:
                    for (k, v) in waits:
                        e.wait_ge(self.sems[k], v)
                    if i is None:
                        continue
                    o = ops[i]
                    if o["kind"] == "d":
                        ins = e.dma_start(out=o["dma"][0], in_=o["dma"][1], allow_slow_non_contiguous=True)
                        ins.then_inc(self.sems[tok[i][0]], 16)
                    elif o["kind"] == "n":
                        ins = e.nop()
                        ins.then_inc(self.sems[tok[i][0]], 1)
                    else:
                        ins = o["fn"](e)
                        ins.then_inc(self.sems[tok[i][0]], 1)
            return body
        block.tensor(run("pe"))
        block.scalar(run("act"))
        block.vector(run("dve"))
        block.gpsimd(run("pool"))
        block.sync(run("sp"))


def fap(ap, dims):
    return bass.AP(tensor=ap.tensor, offset=ap.offset, ap=[list(ap.ap[0])] + [list(d) for d in dims])


def t5_bucket_np(n):
    n = np.maximum(n, 0)
    nf = np.maximum(n, 1).astype(np.float32)
    large = 16 + (np.log(nf / 16) / math.log(128 / 16) * 16).astype(np.int32)
    large = np.minimum(large, 31)
    return np.where(n < 16, n, large)


def host_consts():
    c = {}
    c["ident"] = np.eye(128, dtype=np.float32)
    s = np.arange(128)[:, None]
    t = np.arange(128)[None, :]
    c["tri"] = np.where(s <= t, -1.0 / 16, 0.0).astype(np.float32)
    c["tris"] = (np.eye(128) * (-1.0 / 16)).astype(np.float32)
    c["cmask"] = (s <= t).astype(np.float32)
    c["jx"] = np.eye(128, dtype=np.float32)[::-1].copy()
    bd = np.zeros((128, 128), np.float32)
    bd[:64, :64] = 1
    bd[64:, 64:] = 1
    c["bd"] = bd
    sw = np.zeros((128, 128), np.float32)
    for m in range(128):
        sw[(m + 64) % 128, m] = 1
    c["sw"] = sw
    oh = np.zeros((128, 2, 256), np.float32)
    for kb in range(2):
        off = 128 if kb == 0 else 0
        for i in range(255):
            dlt = 127 + off - i
            if 0 <= dlt <= 127:
                oh[int(t5_bucket_np(np.array(dlt))), kb, i] = 1.0
            else:
                oh[32, kb, i] = MASKV
        oh[32, kb, 255] = MASKV
    c["oh"] = oh
    sel = np.zeros((128, 16, 128), np.float32)
    for b in range(16):
        sel[b, b, :] = 1
    c["sel"] = sel
    return c


def build_program(debug=False):
    nc = bass.Bass("TRN2", target_bir_lowering=False)
    P = Prog(nc)

    def din(name, shape):
        return nc.dram_tensor(name, list(shape), F32, kind="ExternalInput").ap()

    def dout(name, shape):
        return nc.dram_tensor(name, list(shape), F32, kind="ExternalOutput").ap()

    xp = din("xp", [TOK, D]); xhalo = din("xhalo", [128, D]); xpre = din("xpre", [NPRE, D]); xs = din("xs", [128, D])
    ck = din("ck", [16, 128, 128]); cv = din("cv", [16, 128, 128]); sg = din("sg", [16, 4, 64, 128])
    w_in = din("w_in", [D, INW]); w_o = din("w_o", [D, D]); w_gate = din("w_gate", [D, DFF]); w_up = din("w_up", [D, DFF])
    w_down = din("w_down", [DFF, D])
    attn_g = din("attn_g", [D]); ffn_g = din("ffn_g", [D]); qg_in = din("qng", [64]); kg_in = din("kng", [64])
    sinks = din("sinks", [8]); relb = din("relb", [32, 8]); w2 = din("w2", [16, 256]); bgate = din("bgate", [256])
    glag = din("glag", [128]); flag = din("flag", [128, 1])
    cn = {k: din("c_" + k, v.shape) for k, v in host_consts().items()}

    y_p = dout("y_p", [TOK, D]); y_s = dout("y_s", [128, D])
    kwp = dout("kwp", [128, 128]); vwp = dout("vwp", [128, 128]); gsp = dout("gsp", [4, 64, 128])
    kws = dout("kws", [16, 128, 128]); vws = dout("vws", [16, 128, 128]); gss = dout("gss", [16, 4, 64, 128])
    fscr = nc.dram_tensor("fscr", [2, 8, 256], F32).ap()
    wb = {"w_in": nc.dram_tensor("wb_in", [D, INW], BF16).ap(), "w_o": nc.dram_tensor("wb_o", [D, D], BF16).ap(),
          "w_gate": nc.dram_tensor("wb_gate", [D, DFF], BF16).ap(), "w_up": nc.dram_tensor("wb_up", [D, DFF], BF16).ap(),
          "w_down": nc.dram_tensor("wb_down", [DFF, D], BF16).ap()}
    wf = {"w_in": w_in, "w_o": w_o, "w_gate": w_gate, "w_up": w_up, "w_down": w_down}
    if debug:
        dbg_mix = dout("dbg_mix", [128, 8, 512]); dbg_h = dout("dbg_h", [128, 4, D]); dbg_z = dout("dbg_z", [128, 8, 512]); dbg_a = dout("dbg_a", [128, NKF, 512])
        dbg_q = dout("dbg_q", [128, 4, 512]); dbg_rs = dout("dbg_rs", [128, 4, 512])

    for i in range(6):
        P.banks.append(P.ps([128, 512], F32, f"bank{i}"))
    psT = [P.ps([128, 1024], BF16, f"pst{i}") for i in range(2)]
    pst_rr = [0]

    def tbank():
        i = pst_rr[0] % 2
        pst_rr[0] += 1
        return psT[i], ("pst", i)

    ident_f = P.sb([128, 128], F32); ident_bf = P.sb([128, 128], BF16)
    tri_f = P.sb([128, 128], F32); tris_f = P.sb([128, 128], F32); cmask_bf = P.sb([128, 128], BF16)
    jx_f = P.sb([128, 128], F32); bd_bf = P.sb([128, 128], BF16); sw_bf = P.sb([128, 128], BF16)
    ones_bf = P.sb([128, 128], BF16); zeros_f = P.sb([128, 128], F32); scr_f = P.sb([128, 2048], F32)
    sel_bf = P.sb([128, 16, 128], BF16)
    gaT = P.sb([128, 8], F32); gfT = P.sb([128, 8], F32)
    gq_col = P.sb([128, 1], F32); gk_col = P.sb([128, 1], F32); glag_col = P.sb([128, 1], F32)
    eps_col = P.sb([128, 1], F32); ln8_col = P.sb([128, 1], F32); flag_col = P.sb([128, 1], F32)
    bgate_bc = P.sb([128, 256], F32); w2pad = P.sb([128, 256], BF16)
    relb_pad = P.sb([128, 128], F32)
    hank = scr_f[:, 0:1024].rearrange("p (h s) -> p h s", h=8)
    oh_sb = scr_f[:, 1024:1536].rearrange("p (a b) -> p a b", a=2)
    ftab = scr_f[0:8, 1536:2048].rearrange("p (a b) -> p a b", a=2)
    E = P.sb([128, 2, 2, 2, 2, 128], F32)
    sink_bc = P.sb([128, 8], F32); sinkexp = P.sb([128, 2, 2, 2, 128], F32)
    ring = P.sb([128, NSLOT, 8, 512], BF16)
    xb = P.sb([128, 4, D], F32)
    actT = P.sb([128, 8, 512], BF16)
    nbf = P.sb([128, D], BF16); nbf_b = P.sb([128, D], BF16)
    zt_b = P.sb([128, 256], F32); sp_b = P.sb([128, 256], F32); ktok_b = P.sb([128, 2, 2, 128], BF16)
    ss_c2 = P.sb([128, 2], F32); rs_c2 = P.sb([128, 2], F32)
    ss_c = P.sb([128, 1], F32); rs_c = P.sb([128, 1], F32)
    qhT = P.sb([128, 4, 512], BF16)
    kX = P.sb([128, 4, 640], BF16)
    khT_bf = P.sb([128, 512], BF16); khT_f = P.sb([128, 128], F32)
    Vdup = P.sb([128, 5, 2, 128], BF16)
    qgT = P.sb([128, 2, 512], BF16); kgA = P.sb([128, 2, 512], BF16); kgB = P.sb([128, 2, 512], BF16)
    kgf = P.sb([128, 2, 128], F32)
    ktok = P.sb([128, 2, 2, 128], BF16)
    vg_tok = P.sb([128, 4, 512], BF16)
    rsT = P.sb([128, 4, 512], BF16)
    ulrT = P.sb([128, 512], BF16)
    zt = P.sb([128, 256], F32); sp_t = P.sb([128, 256], F32)
    ebq = P.sb([128, 2, 512], F32); enb = P.sb([128, 2, 512], F32); elast = P.sb([128, 2, 4], F32)
    elb = P.sb([128, 2, 128], F32)
    S = P.sb([128, 2, 128], F32); Stmp = P.sb([128, 2, 128], F32); SA = P.sb([128, 2, 128], BF16); SB = P.sb([128, 2, 128], BF16)
    ATbf = P.sb([128, 4, 128], BF16)
    sq_bf = P.sb([128, 512], BF16); rstd_f = P.sb([128, 512], F32); tmp_f = P.sb([128, 512], F32)
    pe_f = P.sb([128, 512], F32); rec_f = P.sb([128, 512], F32)
    PT = P.sb([128, 2, 2, 2, 2, 128], BF16)
    mixT = P.sb([128, 8, 512], BF16)
    aT = P.sb([128, NKF, 512], BF16)
    sg_f = P.sb([128, 512], F32)
    vw_f = P.sb([128, 128], F32); kw_f = P.sb([128, 128], F32)
    Kw_bf = P.sb([128, 16, 128], BF16); Kwsw_bf = P.sb([128, 16, 128], BF16)
    KTb = P.sb([128, 2, 2, 128], BF16)
    Vwd = P.sb([128, 16, 2, 128], BF16)
    qsA = P.sb([128, 4, 16], BF16); qsB = P.sb([128, 4, 16], BF16)
    Pts = P.sb([128, 16, 8], BF16); pes = P.sb([128, 16, 8], F32)
    Sb = scr_f[:, 0:512].rearrange("p (a c v) -> p a c v", a=2, c=2)
    Wt = scr_f[:, 512:1024].rearrange("p (a c v) -> p a c v", a=2, c=2)
    Sn = scr_f[:, 1024:1536].rearrange("p (a c v) -> p a c v", a=2, c=2)
    WA = P.sb([128, 2, 2, 128], BF16); WB = P.sb([128, 2, 2, 128], BF16)

    CUR = {"p": 0}
    _vwd32 = Vwd[:].rearrange("p a b c -> p (a b c)").bitcast(F32)
    _kwf = Kw_bf[:].rearrange("p a b -> p (a b)")
    _kwswf = Kwsw_bf[:].rearrange("p a b -> p (a b)")
    nbfs2 = (nbf, nbf_b)

    def X(blk):
        if CUR["p"] == 0:
            return xb[:, blk, :]
        src = _vwd32 if blk < 2 else scr_f[:]
        o = (blk % 2) * 1024
        return src[:, o:o + 1024]

    def A(k):
        if CUR["p"] == 0:
            return actT[:, k, :]
        src = _kwf if k < 4 else _kwswf
        o = (k % 4) * 512
        return src[:, o:o + 512]

    def xr(blk):
        return ("x", CUR["p"], blk)

    def ar(blk):
        return ("actT", CUR["p"], blk)

    def ld(dst, src, name, eng="sp"):
        P.dma(eng, dst, src, writes=[name])

    ld(ident_f[:], cn["ident"], "ident_f"); ld(tri_f[:], cn["tri"], "tri"); ld(tris_f[:], cn["tris"], "tris")
    ld(jx_f[:], cn["jx"], "jx"); ld(oh_sb, cn["oh"], "oh")
    P.dma("pool", ident_bf[:], cn["ident"], writes=["ident_bf"])
    P.dma("pool", cmask_bf[:], cn["cmask"], writes=["cmask"])
    P.dma("pool", bd_bf[:], cn["bd"], writes=["bd"])
    P.dma("pool", sw_bf[:], cn["sw"], writes=["sw"])
    P.dma("pool", sel_bf[:], cn["sel"], writes=["sel"])
    ld(gaT[:], attn_g.rearrange("(k p) -> p k", p=128), "gaT"); ld(gfT[:], ffn_g.rearrange("(k p) -> p k", p=128), "gfT")
    for h in range(2):
        ld(gq_col[h * 64:(h + 1) * 64, :], qg_in.rearrange("(p o) -> p o", o=1), "gq")
        ld(gk_col[h * 64:(h + 1) * 64, :], kg_in.rearrange("(p o) -> p o", o=1), "gk")
    ld(glag_col[:], glag.rearrange("(p o) -> p o", o=1), "glag"); ld(flag_col[:], flag, "flag")
    ld(bgate_bc[:], bass.AP(tensor=bgate.tensor, offset=0, ap=[[0, 128], [1, 256]]), "bgate")
    ld(sink_bc[:], bass.AP(tensor=sinks.tensor, offset=0, ap=[[0, 128], [1, 8]]), "sink_bc")
    P.op("dve", lambda e: e.memset(ones_bf[:], 1.0), writes=["ones"])
    P.op("dve", lambda e: e.memset(zeros_f[:], 0.0), writes=["zeros"])
    P.op("dve", lambda e: e.memset(eps_col[:], EPS), writes=["eps"])
    P.op("dve", lambda e: e.memset(ln8_col[:], math.log(0.125)), writes=["ln8"])
    P.op("dve", lambda e: e.memset(w2pad[:], 0.0), writes=["w2pad"])
    P.dma("pool", w2pad[112:128, :], w2, reads=[], writes=["w2pad"])
    P.op("dve", lambda e: e.tensor_scalar(out=gq_col[:], in0=gq_col[:], scalar1=0.125, scalar2=None, op0=ALU.mult),
         reads=["gq"], writes=["gq"])
    for t_ in (kgA, kgB, SA, SB, qsA, qsB, WA, WB):
        P.op("dve", lambda e, t_=t_: e.memset(t_[:], 0.0), writes=["zinit"])
    P.op("dve", lambda e: e.memset(kX[:], 0.0), writes=["kX"])
    P.op("dve", lambda e: e.memset(S[:], 0.0), writes=["S"])
    P.op("dve", lambda e: e.memset(relb_pad[:], 0.0), writes=["relb_pad"])
    P.op("dve", lambda e: e.memset(relb_pad[32:33, :], 1.0), reads=[], writes=["relb_pad"])
    P.dma("sp", relb_pad[0:32, 0:8], relb, writes=["relb_pad"])

    bk, bkr = P.bank()
    P.op("pe", lambda e: e.matmul(bk[:, 0:512], lhsT=relb_pad[:], rhs=scr_f[:, 1024:1536], start=True, stop=True),
         reads=["relb_pad", "oh"], writes=[bkr])
    P.op("act", lambda e: e.activation(out=scr_f[0:8, 1536:2048], in_=bk[0:8, 0:512], func=AF.Copy), reads=[bkr], writes=["ftab"])
    P.dma("sp", fscr.rearrange("k h i -> h k i"), ftab, reads=["ftab"], writes=["fscr"])
    for kb in range(2):
        src = bass.AP(tensor=fscr.tensor, offset=kb * 8 * 256, ap=[[1, 128], [256, 8], [1, 128]])
        P.dma("sp", hank, src, reads=["fscr"], writes=["hank"])
        for hh in range(0, 8, 4):
            bk, bkr = P.bank()

            def mmj(e, bk=bk, hh=hh):
                ins = None
                for q in range(4):
                    ins = e.matmul(bk[:, q * 128:(q + 1) * 128], lhsT=hank[:, hh + q, :], rhs=jx_f[:], start=True, stop=True)
                return ins
            P.op("pe", mmj, reads=["hank", "jx"], writes=[bkr])
            for q in range(4):
                h = hh + q
                c_, half = h // 2, h % 2
                j, cl = c_ // 2, c_ % 2
                P.op("act", lambda e, bk=bk, q=q, j=j, half=half, kb=kb, cl=cl:
                     e.activation(out=E[:, j, half, kb, cl, :], in_=bk[:, q * 128:(q + 1) * 128], func=AF.Exp),
                     reads=[bkr], writes=["E"])
    P.op("act", lambda e: e.activation(out=sink_bc[:], in_=sink_bc[:], func=AF.Exp), reads=["sink_bc"], writes=["sink_bc"])
    for h in range(8):
        c_, half = h // 2, h % 2
        j, cl = c_ // 2, c_ % 2
        P.op("dve", lambda e, h=h, j=j, half=half, cl=cl: e.tensor_scalar(out=sinkexp[:, j, half, cl, :], in0=zeros_f[:], scalar1=sink_bc[:, h:h + 1],
                                                                         scalar2=None, op0=ALU.add), reads=["sink_bc", "zeros"], writes=["sinkexp"])

    wstate = {"n": 0}

    def wload(parts, fp32=False):
        s = wstate["n"] % NSLOT
        wstate["n"] += 1
        res = ("w", s)
        for (c0, (wn, r0, nrows, cc0, ncols_), nk, ncols) in parts:
            src = (wf if fp32 else wb)[wn][r0:r0 + nrows, cc0:cc0 + ncols_].rearrange("(k p) n -> p k n", p=128)
            q_ = "pool" if (fp32 or wstate["n"] % 2 == 0) else "sp"
            P.dma(q_, ring[:, s, 0:nk, c0:c0 + ncols], src, reads=([] if fp32 else [("wbf", wn)]), writes=[res])
        return s, res

    def wsrc(w, r0, nrows, c0, ncols):
        return (w, r0, nrows, c0, ncols)

    def front(src_fn, nb, gT):
        for blk in range(nb):
            P.dma("sp", X(blk), src_fn(blk), writes=[xr(blk)])
            norm_block(blk, gT)

    def norm_block(blk, gT):
        p = CUR["p"]
        xblk = X(blk); xres = xr(blk); ares = ar(blk)
        nb_ = nbfs2[p]; ss = ss_c2[:, p:p + 1]; rs = rs_c2[:, p:p + 1]
        P.op("act", lambda e: e.activation(out=nb_[:], in_=xblk, func=AF.Square, accum_out=ss),
             reads=[xres], writes=[("nbf", p), ("ss_c", p)])
        P.op("act", lambda e: e.activation(out=rs, in_=ss, func=AF.Ln, scale=1.0 / D, bias=eps_col[:, 0:1]),
             reads=[("ss_c", p), "eps"], writes=[("rs_c", p)])
        P.op("act", lambda e: e.activation(out=rs, in_=rs, func=AF.Exp, scale=-0.5), reads=[("rs_c", p)], writes=[("rs_c", p)])
        P.op("dve", lambda e: e.tensor_scalar(out=nb_[:], in0=xblk, scalar1=rs, scalar2=None, op0=ALU.mult),
             reads=[xres, ("rs_c", p)], writes=[("nbf", p)])
        tb, tbr = tbank()

        def tr(e):
            ins = None
            for k in range(8):
                ins = e.transpose(out=tb[:, k * 128:(k + 1) * 128], in_=nb_[:, k * 128:(k + 1) * 128], identity=ident_bf[:])
            return ins
        P.op("pe", tr, reads=[("nbf", p), "ident_bf"], writes=[tbr])
        if p == 0:
            P.op("dve", lambda e: e.tensor_tensor(out=actT[:, :, blk * 128:(blk + 1) * 128], in0=tb[:].rearrange("p (k t) -> p k t", k=8),
                                                  in1=fap(gT[:], [[1, 8], [0, 128]]), op=ALU.mult),
                 reads=[tbr, "gaT", "gfT"], writes=[ares])
        else:
            for kh, src in enumerate((_kwf, _kwswf)):
                P.op("dve", lambda e, kh=kh, src=src: e.tensor_tensor(
                    out=src.rearrange("p (k t) -> p k t", k=4)[:, :, blk * 128:(blk + 1) * 128],
                    in0=tb[:, kh * 512:(kh + 1) * 512].rearrange("p (k t) -> p k t", k=4),
                    in1=fap(gT[:, kh * 4:kh * 4 + 1], [[1, 4], [0, 128]]), op=ALU.mult),
                    reads=[tbr, "gaT", "gfT", ares], writes=[ares])

    def actT_reads(nb):
        return [ar(b) for b in range(nb)]

    def proj_fm(slot, res, col0, T, nb):
        bk, bkr = P.bank()
        acts = [A(k) for k in range(8)]

        def mm(e):
            ins = None
            for k in range(8):
                ins = e.matmul(bk[:, 0:T], lhsT=ring[:, slot, k, col0:col0 + 128], rhs=acts[k][:, 0:T], start=(k == 0), stop=(k == 7))
            return ins
        P.op("pe", mm, reads=[res] + actT_reads(nb), writes=[bkr])
        return bk, bkr

    def proj_tm(slot, res, col0, ncols, blk):
        bk, bkr = P.bank()
        acts = [A(k) for k in range(8)]

        def mm(e):
            ins = None
            for k in range(8):
                ins = e.matmul(bk[:, 0:ncols], lhsT=acts[k][:, blk * 128:(blk + 1) * 128], rhs=ring[:, slot, k, col0:col0 + ncols],
                               start=(k == 0), stop=(k == 7))
            return ins
        P.op("pe", mm, reads=[res, ar(blk)], writes=[bkr])
        return bk, bkr

    def rstd_fm(src_ap, T, lhs_ones, scale, reads_src):
        P.op("act", lambda e: e.activation(out=sq_bf[:, 0:T], in_=src_ap, func=AF.Square), reads=reads_src, writes=["sq"])
        b2, b2r = P.bank()
        P.op("pe", lambda e: e.matmul(b2[:, 0:T], lhsT=lhs_ones[:], rhs=sq_bf[:, 0:T], start=True, stop=True),
             reads=["sq", "bd", "ones"], writes=[b2r])
        P.op("act", lambda e: e.activation(out=rstd_f[:, 0:T], in_=b2[:, 0:T], func=AF.Ln, scale=scale, bias=eps_col[:, 0:1]),
             reads=[b2r, "eps"], writes=["rstd"])
        P.op("act", lambda e: e.activation(out=rstd_f[:, 0:T], in_=rstd_f[:, 0:T], func=AF.Exp, scale=-0.5), reads=["rstd"], writes=["rstd"])

    def gla_prep(slot_lr, res_lr, lrcol, nb, T, tri_ap, tri_res, full):
        bk, bkr = proj_fm(slot_lr, res_lr, lrcol, T, nb)
        P.op("act", lambda e: e.activation(out=ulrT[:, 0:T], in_=bk[:, 0:T], func=AF.Copy), reads=[bkr], writes=["ulrT"])
        bT = [P.bank(hold=True) for _ in range(2)]
        for blk in range(nb):
            zb, zbr = P.bank()
            P.op("pe", lambda e, zb=zb, blk=blk: e.matmul(zb[:, 0:256], lhsT=ulrT[:, blk * 128:(blk + 1) * 128], rhs=w2pad[:], start=True, stop=True),
                 reads=["ulrT", "w2pad"], writes=[zbr])
            P.op("dve", lambda e, zb=zb: e.tensor_tensor(out=zt[:], in0=zb[:, 0:256], in1=bgate_bc[:], op=ALU.add),
                 reads=[zbr, "bgate"], writes=["zt"])
            P.op("act", lambda e: e.activation(out=zt[:], in_=zt[:], func=AF.Exp, scale=-1.0), reads=["zt"], writes=["zt"])
            P.op("act", lambda e: e.activation(out=sp_t[:], in_=zt[:], func=AF.Ln, bias=1.0), reads=["zt"], writes=["sp_t"])
            for c in range(2):
                P.op("pe", lambda e, c=c, blk=blk: e.matmul(bT[c][0][:, blk * 128:(blk + 1) * 128], lhsT=sp_t[:, c * 128:(c + 1) * 128], rhs=tri_ap,
                                                           start=True, stop=True), reads=["sp_t", tri_res], writes=[bT[c][1]])
        for c in range(2):
            P.release(bT[c][1])
        for c in range(2):
            P.op("act", lambda e, c=c: e.activation(out=enb[:, c, 0:T], in_=bT[c][0][:, 0:T], func=AF.Exp, scale=-1.0), reads=[bT[c][1]], writes=["enb"])
            P.op("act", lambda e, c=c: e.activation(out=elast[:, c, 0:nb], in_=fap(bT[c][0][:, 127:128], [[128, nb]]), func=AF.Exp),
                 reads=[bT[c][1]], writes=["elast"])
            if full:
                P.op("act", lambda e, c=c: e.activation(out=ebq[:, c, 0:T], in_=bT[c][0][:, 0:T], func=AF.Exp, bias=ln8_col[:, 0:1]),
                     reads=[bT[c][1], "ln8"], writes=["ebq"])
                if T == 128:
                    P.op("act", lambda e, c=c: e.activation(out=elb[:, c, :], in_=bT[c][0][:, 0:128], func=AF.Exp), reads=[bT[c][1]], writes=["elb"])

    def kg_evac(slot, res, col0, nb, T, sample=False):
        for c in range(2):
            bk, bkr = proj_fm(slot, res, col0 + c * 128, T, nb)
            P.op("dve", lambda e, bk=bk, c=c: e.tensor_tensor(out=kgA[0:64, c, 0:T], in0=bk[0:64, 0:T], in1=enb[0:64, c, 0:T], op=ALU.mult),
                 reads=[bkr, "enb", "zinit"], writes=[("kgA", c)])
            P.op("dve", lambda e, bk=bk, c=c: e.tensor_tensor(out=kgB[64:128, c, 0:T], in0=bk[64:128, 0:T], in1=enb[64:128, c, 0:T], op=ALU.mult),
                 reads=[bkr, "enb", "zinit"], writes=[("kgB", c)])
            if sample:
                P.op("dve", lambda e, bk=bk, c=c: e.tensor_tensor(out=kgf[:, c, :], in0=bk[:, 0:128], in1=enb[:, c, 0:128], op=ALU.mult),
                     reads=[bkr, "enb"], writes=["kgf"])

    def vg_tm(slot, res, col0, nb):
        for blk in range(nb):
            bk, bkr = proj_tm(slot, res, col0, 512, blk)
            P.op("act", lambda e, bk=bk, blk=blk: e.activation(out=vg_tok[:, blk, :], in_=bk[:, 0:512], func=AF.Copy), reads=[bkr], writes=[("vg", blk)])

    def state_update(blk, masked):
        tb, tbr = tbank()

        def tr(e):
            ins = None
            for c in range(2):
                for ab, src in enumerate((kgA, kgB)):
                    o = (c * 2 + ab) * 128
                    ins = e.transpose(out=tb[:, o:o + 128], in_=src[:, c, blk * 128:(blk + 1) * 128], identity=ident_bf[:])
            return ins
        P.op("pe", tr, reads=[("kgA", 0), ("kgA", 1), ("kgB", 0), ("kgB", 1), "ident_bf"], writes=[tbr])
        P.op("act", lambda e: e.activation(out=ktok[:].rearrange("p c a f -> p (c a f)"), in_=tb[:, 0:512], func=AF.Copy), reads=[tbr], writes=["ktok"])
        ub, ubr = P.bank()

        def mm(e):
            ins = None
            for c in range(2):
                for ab in range(2):
                    h = 2 * c + ab
                    ins = e.matmul(ub[:, c * 128:(c + 1) * 128], lhsT=ktok[:, c, ab, :], rhs=vg_tok[:, blk, h * 128:(h + 1) * 128],
                                   start=(ab == 0), stop=(ab == 1))
            return ins
        P.op("pe", mm, reads=["ktok", ("vg", blk)], writes=[ubr])
        P.op("dve", lambda e: e.tensor_tensor(out=Stmp[:].rearrange("p c v -> p (c v)"), in0=ub[:, 0:256], in1=S[:].rearrange("p c v -> p (c v)"), op=ALU.add),
             reads=[ubr, "S"], writes=["Stmp"])
        P.op("dve", lambda e: e.tensor_tensor(out=S[:], in0=Stmp[:], in1=fap(elast[:, 0, blk:blk + 1], [[4, 2], [0, 128]]), op=ALU.mult),
             reads=["Stmp", "elast", "SA", "SB"], writes=["S"])
        if masked:
            P.op("act", lambda e: e.activation(out=SA[0:64], in_=S[0:64], func=AF.Copy), reads=["S", "zinit"], writes=["SA"])
            P.op("act", lambda e: e.activation(out=SB[64:128], in_=S[64:128], func=AF.Copy), reads=["S", "zinit"], writes=["SB"])

    def gla_out_norm(ob, obr, ncol, W, col0):
        rstd_fm(ob[:, 0:ncol], ncol, ones_bf, 1.0 / 128, [obr])
        P.op("dve", lambda e: e.scalar_tensor_tensor(out=tmp_f[:, 0:ncol], in0=ob[:, 0:ncol], scalar=glag_col[:, 0:1], in1=rstd_f[:, 0:ncol],
                                                     op0=ALU.mult, op1=ALU.mult), reads=[obr, "rstd", "glag"], writes=["tmp_f"])
        P.op("dve", lambda e: e.tensor_tensor(out=mixT[:, 4:8, col0:col0 + W], in0=tmp_f[:, 0:ncol].rearrange("p (h w) -> p h w", h=4),
                                              in1=rsT[:, :, col0:col0 + W], op=ALU.mult), reads=["tmp_f", "rsT"], writes=[("mixg", col0)])

    def gla_block(blk):
        ab_, abr = P.bank()

        def mm1(e):
            ins = None
            for h in range(4):
                c, half = h // 2, h % 2
                src = kgA if half == 0 else kgB
                ins = e.matmul(ab_[:, h * 128:(h + 1) * 128], lhsT=src[:, c, blk * 128:(blk + 1) * 128], rhs=qgT[:, c, blk * 128:(blk + 1) * 128],
                               start=True, stop=True)
            return ins
        P.op("pe", mm1, reads=[("kgA", 0), ("kgA", 1), ("kgB", 0), ("kgB", 1), "qgT"], writes=[abr])
        P.op("dve", lambda e: e.tensor_tensor(out=ATbf[:], in0=ab_[:, 0:512].rearrange("p (h t) -> p h t", h=4), in1=fap(cmask_bf[:], [[0, 4], [1, 128]]),
                                              op=ALU.mult), reads=[abr, "cmask"], writes=["ATbf"])
        ob, obr = P.bank()

        def mm2(e):
            ins = None
            for h in range(4):
                c, half = h // 2, h % 2
                sm = SA if half == 0 else SB
                e.matmul(ob[:, h * 128:(h + 1) * 128], lhsT=vg_tok[:, blk, h * 128:(h + 1) * 128], rhs=ATbf[:, h, :], start=True, stop=False)
                ins = e.matmul(ob[:, h * 128:(h + 1) * 128], lhsT=sm[:, c, :], rhs=qgT[:, c, blk * 128:(blk + 1) * 128], start=False, stop=True)
            return ins
        P.op("pe", mm2, reads=[("vg", blk), "ATbf", "SA", "SB", "qgT"], writes=[obr])
        gla_out_norm(ob, obr, 512, 128, blk * 128)
        state_update(blk, True)

    def attn_block(blk, Etab, Eres, useflag=False):
        for j in range(2):
            sbk = []
            for half in range(2):
                bk, bkr = P.bank()
                sbk.append((bk, bkr))

                def mm(e, bk=bk, half=half, j=j):
                    ins = None
                    for kb in range(2):
                        kc = (blk + kb) * 128
                        ins = e.matmul(bk[:, kb * 256:(kb + 1) * 256], lhsT=kX[:, 2 * j + half, kc:kc + 128],
                                       rhs=qhT[:, 2 * j:2 * j + 2, blk * 128:(blk + 1) * 128], start=True, stop=True)
                    return ins
                P.op("pe", mm, reads=["kX", "qhT"], writes=[bkr])
            for half in range(2):
                bk, bkr = sbk[half]
                P.op("act", lambda e, bk=bk: e.activation(out=pe_f[:], in_=bk[:, 0:512], func=AF.Exp), reads=[bkr], writes=["pe_f"])
                P.op("dve", lambda e, half=half, j=j: e.tensor_tensor(out=PT[:, j, half].rearrange("p a b q -> p (a b q)"), in0=pe_f[:],
                                                                     in1=Etab[:, j, half].rearrange("p a b q -> p (a b q)"), op=ALU.mult),
                     reads=["pe_f", Eres], writes=[("PT", j)])
                if useflag:
                    P.op("dve", lambda e, half=half, j=j: e.tensor_scalar(out=PT[:, j, half, 0], in0=PT[:, j, half, 0], scalar1=flag_col[:, 0:1],
                                                                          scalar2=None, op0=ALU.mult), reads=[("PT", j), "flag"], writes=[("PT", j)])
            ob, obr = P.bank()
            db, dbr = P.bank()

            def mmv(e, ob=ob, db=db, j=j):
                ins = None
                for kb in range(2):
                    rhs = PT[:, j, :, kb, :, :]
                    e.matmul(ob[:, 0:512], lhsT=Vdup[:, blk + kb, j, :], rhs=rhs, start=(kb == 0), stop=(kb == 1))
                for kb in range(2):
                    rhs = PT[:, j, :, kb, :, :]
                    ins = e.matmul(db[:, 0:512], lhsT=ones_bf[:], rhs=rhs, start=(kb == 0), stop=(kb == 1))
                return ins
            P.op("pe", mmv, reads=[("PT", j), "Vdup", "ones"], writes=[obr, dbr])
            P.op("dve", lambda e, db=db, j=j: e.tensor_tensor(out=rec_f[:], in0=db[:, 0:512], in1=sinkexp[:, j].rearrange("p a b q -> p (a b q)"), op=ALU.add),
                 reads=[dbr, "sinkexp"], writes=["rec"])
            P.op("act", lambda e: e.activation(out=rec_f[:], in_=rec_f[:], func=AF.Ln), reads=["rec"], writes=["rec"])
            P.op("act", lambda e: e.activation(out=rec_f[:], in_=rec_f[:], func=AF.Exp, scale=-1.0), reads=["rec"], writes=["rec"])
            for half in range(2):
                r0 = half * 64
                P.op("dve", lambda e, ob=ob, half=half, r0=r0, j=j: e.tensor_tensor(
                    out=mixT[r0:r0 + 64, 2 * j:2 * j + 2, blk * 128:(blk + 1) * 128],
                    in0=ob[r0:r0 + 64, half * 256:(half + 1) * 256].rearrange("p (c q) -> p c q", c=2),
                    in1=rec_f[r0:r0 + 64, half * 256:(half + 1) * 256].rearrange("p (c q) -> p c q", c=2), op=ALU.mult),
                    reads=[obr, "rec"], writes=[("mixa", blk)])

    def qk_norm_chunk(bk, bkr, T, gcol, gres, out_fn):
        rstd_fm(bk[:, 0:T], T, bd_bf, 1.0 / 64, [bkr])
        out_fn(bk, bkr)

    def wo_ffnnorm(nb, s0, r0, s1, r1):
        for blk in range(nb):
            for cg, (s, r) in enumerate(((s0, r0), (s1, r1))):
                bk, bkr = P.bank()

                def mm(e, bk=bk, s=s, blk=blk):
                    ins = None
                    for k in range(8):
                        ins = e.matmul(bk[:, 0:512], lhsT=mixT[:, k, blk * 128:(blk + 1) * 128], rhs=ring[:, s, k, 0:512], start=(k == 0), stop=(k == 7))
                    return ins
                P.op("pe", mm, reads=[r, ("mixa", blk), ("mixg", blk * 128)], writes=[bkr])
                xs_ = X(blk)[:, cg * 512:(cg + 1) * 512]
                P.op("dve", lambda e, bk=bk, xs_=xs_: e.tensor_tensor(out=xs_, in0=bk[:, 0:512], in1=xs_, op=ALU.add), reads=[bkr, xr(blk)], writes=[xr(blk)])
            norm_block(blk, gfT)

    def ffn(nb, T, ydst_fn):
        spec = []
        for s6 in range(6):
            ncols = 512 if s6 < 5 else 256
            spec.append([(0, wsrc("w_gate", 0, D, s6 * 512, ncols), 8, ncols)])
            spec.append([(0, wsrc("w_up", 0, D, s6 * 512, ncols), 8, ncols)])
        for cg in range(2):
            for kgp in range(3):
                nk = 8 if kgp < 2 else 6
                spec.append([(0, wsrc("w_down", kgp * 1024, nk * 128, cg * 512, 512), nk, 512)])
        W = []

        def need(i):
            while len(W) <= min(i + 2, len(spec) - 1):
                W.append(wload(spec[len(W)]))
            return W[i]
        for s6 in range(6):
            ncols = 512 if s6 < 5 else 256
            need(2 * s6 + 1)
            sg_, rg_ = W[2 * s6]
            su_, ru_ = W[2 * s6 + 1]
            for mi in range(ncols // 128):
                m = s6 * 4 + mi
                gb, gbr = proj_fm(sg_, rg_, mi * 128, T, nb)
                ubk, ubr = proj_fm(su_, ru_, mi * 128, T, nb)
                P.op("act", lambda e, gb=gb: e.activation(out=sg_f[:, 0:T], in_=gb[:, 0:T], func=AF.Silu), reads=[gbr], writes=["sg_f"])
                P.op("dve", lambda e, ubk=ubk, m=m: e.tensor_tensor(out=aT[:, m, 0:T], in0=ubk[:, 0:T], in1=sg_f[:, 0:T], op=ALU.mult),
                     reads=[ubr, "sg_f"], writes=[("aT", m)])
        for cg in range(2):
            bks = [P.bank(hold=True) for _ in range(nb)]
            for kgp in range(3):
                nk = 8 if kgp < 2 else 6
                sl = need(12 + cg * 3 + kgp)
                for blk in range(nb):
                    bk, bkr = bks[blk]

                    def mm(e, bk=bk, blk=blk, sl=sl, kgp=kgp, nk=nk):
                        ins = None
                        for kk in range(nk):
                            k = kgp * 8 + kk
                            ins = e.matmul(bk[:, 0:512], lhsT=aT[:, k, blk * 128:(blk + 1) * 128], rhs=ring[:, sl[0], kk, 0:512], start=(k == 0), stop=(k == NKF - 1))
                        return ins
                    P.op("pe", mm, reads=[sl[1]] + [("aT", kgp * 8 + kk) for kk in range(nk)], writes=[bkr])
            for blk in range(nb):
                bk, bkr = bks[blk]
                P.release(bkr)
                xs_ = X(blk)[:, cg * 512:(cg + 1) * 512]
                P.op("dve", lambda e, bk=bk, xs_=xs_: e.tensor_tensor(out=xs_, in0=bk[:, 0:512], in1=xs_, op=ALU.add), reads=[bkr, xr(blk)], writes=[xr(blk)])
                if cg == 1:
                    outs.append(P.dma("sp", ydst_fn(blk), X(blk), reads=[xr(blk)]))

    outs = []

    wstate["n"] = 0
    s_a, r_a = wload([(0, wsrc("w_in", 0, D, 1024, 256), 8, 256), (256, wsrc("w_in", 0, D, 2192, 128), 8, 128)], fp32=True)
    s_b, r_b = wload([(0, wsrc("w_in", 0, D, 1280, 512), 8, 512)], fp32=True)
    for k in range(8):
        P.op("dve", lambda e, k=k: e.tensor_scalar(out=ring[:, s_a, k, 0:384], in0=ring[:, s_a, k, 0:384], scalar1=gaT[:, k:k + 1], scalar2=None, op0=ALU.mult),
             reads=[r_a, "gaT"], writes=[r_a])
        P.op("act", lambda e, k=k: e.activation(out=ring[:, s_b, k, 0:512], in_=ring[:, s_b, k, 0:512], func=AF.Copy, scale=gaT[:, k:k + 1]),
             reads=[r_b, "gaT"], writes=[r_b])
    negs = P.sb([128, 2], BF16)
    P.op("dve", lambda e: e.memset(negs[:], -1.0 / 16), writes=["negs"])
    tri_bf = P.sb([128, 128], BF16)
    P.op("dve", lambda e: e.tensor_copy(out=tri_bf[:], in_=tri_f[:]), reads=["tri"], writes=["tri_bf"])
    sp_h = (P.sb([128, 256], BF16), P.sb([128, 256], BF16))
    _enbflat = enb[:].rearrange("p c t -> p (c t)")
    _qgflat = qgT[:].rearrange("p c t -> p (c t)")
    for wn in ("w_in", "w_o", "w_gate", "w_up", "w_down"):
        nr = wf[wn].shape[0]
        step = 256
        for r0 in range(0, nr, step):
            r1 = min(nr, r0 + step)
            ci = P.dma("pool", wb[wn][r0:r1, :], wf[wn][r0:r1, :], reads=["wbfchain"] + ([r_a, r_b] if (wn == "w_in" and r0 == 0) else []),
                       writes=[("wbf", wn), "wbfchain"])
            P.nofence = getattr(P, "nofence", set()) | {ci}
    NBLK = (16 if debug == "scan" else 0) if debug else NPRE // 128
    nbfs = (nbf, nbf_b); zts = (zt, zt_b); sps = (sp_t, sp_b); ktoks = (ktok, ktok_b)
    sbanks = {}

    def sc1(b):
        q = b % 4; p = b % 2
        P.dma("sp", xb[:, q, :], xpre[b * 128:(b + 1) * 128, :], writes=[("sx", q)])
        P.op("act", lambda e: e.activation(out=nbfs[p][:], in_=xb[:, q, :], func=AF.Square, accum_out=ss_c2[:, p:p + 1]),
             reads=[("sx", q)], writes=[("snbf", p), ("sss", p)])
        P.op("act", lambda e: e.activation(out=rs_c2[:, p:p + 1], in_=ss_c2[:, p:p + 1], func=AF.Ln, scale=1.0 / D, bias=eps_col[:, 0:1]),
             reads=[("sss", p), "eps"], writes=[("srs", p)])
        P.op("act", lambda e: e.activation(out=rs_c2[:, p:p + 1], in_=rs_c2[:, p:p + 1], func=AF.Exp, scale=-0.5), reads=[("srs", p)], writes=[("srs", p)])
        P.op("dve", lambda e: e.tensor_scalar(out=nbfs[p][:], in0=xb[:, q, :], scalar1=rs_c2[:, p:p + 1], scalar2=None, op0=ALU.mult),
             reads=[("sx", q), ("srs", p)], writes=[("snbf", p)])
        tb, tbr = tbank()

        def tr(e):
            ins = None
            for k in range(8):
                ins = e.transpose(out=tb[:, k * 128:(k + 1) * 128], in_=nbfs[p][:, k * 128:(k + 1) * 128], identity=ident_bf[:])
            return ins
        P.op("pe", tr, reads=[("snbf", p), "ident_bf"], writes=[tbr])
        if b % 2 == 0:
            P.op("act", lambda e: e.activation(out=actT[:, :, q * 128:(q + 1) * 128], in_=tb[:].rearrange("p (k t) -> p k t", k=8), func=AF.Copy),
                 reads=[tbr], writes=[("sact", q)])
        else:
            P.op("dve", lambda e: e.tensor_copy(out=actT[:, :, q * 128:(q + 1) * 128], in_=tb[:].rearrange("p (k t) -> p k t", k=8)),
                 reads=[tbr], writes=[("sact", q)])

    def sc2(b):
        q = b % 4
        ab, abr = P.bank()
        vb, vbr = P.bank()

        def mm(e):
            ins = None
            for k in range(8):
                ins = e.matmul(ab[:, 0:128], lhsT=ring[:, s_a, k, 256:384], rhs=actT[:, k, q * 128:(q + 1) * 128], start=(k == 0), stop=(k == 7))
            for k in range(8):
                ins = e.matmul(vb[:, 0:512], lhsT=actT[:, k, q * 128:(q + 1) * 128], rhs=ring[:, s_b, k, 0:512], start=(k == 0), stop=(k == 7))
            return ins
        P.op("pe", mm, reads=[r_a, r_b, ("sact", q)], writes=[abr, vbr])
        P.op("act", lambda e: e.activation(out=ulrT[:, q * 128:(q + 1) * 128], in_=ab[:, 0:128], func=AF.Copy), reads=[abr], writes=[("sulr", q)])
        P.op("act", lambda e: e.activation(out=vg_tok[:, q, :], in_=vb[:, 0:512], func=AF.Copy), reads=[vbr], writes=[("svg", q)])

    def sc3a(b):
        q = b % 4; p = b % 2
        zb, zbr = P.bank()
        sbanks[b] = (zb, zbr)
        P.op("pe", lambda e: e.matmul(zb[:, 0:256], lhsT=ulrT[:, q * 128:(q + 1) * 128], rhs=w2pad[:], start=True, stop=True),
             reads=[("sulr", q), "w2pad"], writes=[zbr])
        P.op("dve", lambda e: e.tensor_tensor(out=zts[p][:], in0=zb[:, 0:256], in1=bgate_bc[:], op=ALU.add), reads=[zbr, "bgate"], writes=[("szt", p)])
        P.op("act", lambda e: e.activation(out=zts[p][:], in_=zts[p][:], func=AF.Exp, scale=-1.0), reads=[("szt", p)], writes=[("szt", p)])
        P.op("act", lambda e: e.activation(out=sp_h[p][:], in_=zts[p][:], func=AF.Ln, bias=1.0), reads=[("szt", p)], writes=[("ssp", p)])

    def sc3b(b):
        q = b % 4; p = b % 2
        kb_, kbr = P.bank()
        cb, cbr = P.bank()
        en_ = _enbflat[:, q * 256:(q + 1) * 256]
        kt_ = _qgflat[:, q * 256:(q + 1) * 256]

        def mm(e):
            ins = None
            for k in range(8):
                ins = e.matmul(kb_[:, 0:256], lhsT=actT[:, k, q * 128:(q + 1) * 128], rhs=ring[:, s_a, k, 0:256], start=(k == 0), stop=(k == 7))
            return ins
        P.op("pe", mm, reads=[r_a, ("sact", q)], writes=[kbr])

        def mmc(e):
            e.matmul(cb[:, 0:256], lhsT=tri_bf[:], rhs=sp_h[p][:], start=True, stop=True)
            ins = None
            for c in range(2):
                ins = e.matmul(cb[:, 256 + 2 * c:258 + 2 * c], lhsT=sp_h[p][:, c * 128:(c + 1) * 128], rhs=negs[:], start=True, stop=True)
            return ins
        P.op("pe", mmc, reads=[("ssp", p), "tri_bf", "negs"], writes=[cbr])
        P.op("act", lambda e: e.activation(out=en_, in_=cb[:, 0:256], func=AF.Exp, scale=-1.0), reads=[cbr], writes=[("senb", q)])
        P.op("act", lambda e: e.activation(out=elast[:, :, q], in_=fap(cb[:, 256:257], [[2, 2]]), func=AF.Exp), reads=[cbr], writes=[("sel", q)])
        P.op("dve", lambda e: e.tensor_tensor(out=kt_, in0=kb_[:, 0:256], in1=en_, op=ALU.mult),
             reads=[kbr, ("senb", q)], writes=[("sktok", q)])

    def sc4(b):
        q = b % 4; p = b % 2
        kt = _qgflat[:, q * 256:(q + 1) * 256]
        ub, ubr = P.bank()

        def mm(e):
            ins = None
            for c in range(2):
                for ab in range(2):
                    h = 2 * c + ab
                    ins = e.matmul(ub[:, h * 128:(h + 1) * 128], lhsT=kt[:, c * 128:(c + 1) * 128], rhs=vg_tok[:, q, h * 128:(h + 1) * 128], start=True, stop=True)
            return ins
        P.op("pe", mm, reads=[("sktok", q), ("svg", q)], writes=[ubr])
        for ab in range(2):
            r0 = ab * 64
            P.op("dve", lambda e, ab=ab, r0=r0: e.tensor_tensor(out=Stmp[r0:r0 + 64], in0=fap(ub[r0:r0 + 64, ab * 128:ab * 128 + 1], [[256, 2], [1, 128]]),
                                                               in1=S[r0:r0 + 64], op=ALU.add), reads=[ubr, "S", "Stmp"], writes=["Stmp"])
        P.op("dve", lambda e: e.tensor_tensor(out=S[:], in0=Stmp[:], in1=fap(elast[:, 0, q:q + 1], [[4, 2], [0, 128]]), op=ALU.mult),
             reads=["Stmp", ("sel", q)], writes=["S"])

    stages = (sc1, sc2, sc3a, sc3b, sc4)
    for i in range(NBLK + len(stages) - 1):
        for si, st in enumerate(stages):
            b = i - si
            if 0 <= b < NBLK:
                st(b)
    P.barrier(keep=("wbf", "wbfchain"))
    P.op("act", lambda e: e.activation(out=SA[0:64], in_=S[0:64], func=AF.Copy), reads=["S", "zinit"], writes=["SA"])
    P.op("act", lambda e: e.activation(out=SB[64:128], in_=S[64:128], func=AF.Copy), reads=["S", "zinit"], writes=["SB"])

    def main_tile(kind, t):
        sample = kind == "sample"
        nb = 1 if sample else 4
        T = nb * 128
        CUR["p"] = 0 if sample else (t % 2)
        first = (kind == "prompt" and t == 0)
        last = (kind == "prompt" and t == 3)
        L0 = wload([(0, wsrc("w_in", 0, D, 0, 512), 8, 512)])
        L1 = wload([(0, wsrc("w_in", 0, D, 512, 512), 8, 512)])
        L2 = wload([(0, wsrc("w_in", 0, D, 1024, 256), 8, 256), (256, wsrc("w_in", 0, D, 2192, 128), 8, 128)])
        L3 = wload([(0, wsrc("w_in", 0, D, 1280, 512), 8, 512)])
        if first:
            front(lambda blk: xhalo, 1, gaT)
            halo_kv = True
            kv_part(L1, 1, 128, 0, False, False)
        if sample:
            front(lambda blk: xs, 1, gaT)
        else:
            front(lambda blk: xp[t * 512 + blk * 128: t * 512 + (blk + 1) * 128, :], 4, gaT)
        gla_prep(L2[0], L2[1], 256, nb, T, (tris_f if sample else tri_f)[:], "tris" if sample else "tri", True)
        for c in range(4):
            bk, bkr = proj_fm(L0[0], L0[1], c * 128, T, nb)
            rstd_fm(bk[:, 0:T], T, bd_bf, 1.0 / 64, [bkr])
            P.op("dve", lambda e, bk=bk, c=c: e.scalar_tensor_tensor(out=qhT[:, c, 0:T], in0=bk[:, 0:T], scalar=gq_col[:, 0:1], in1=rstd_f[:, 0:T],
                                                                    op0=ALU.mult, op1=ALU.mult), reads=[bkr, "rstd", "gq"], writes=["qhT"])
        L4 = wload([(0, wsrc("w_in", 0, D, 1792, 512), 8, 512)])
        kv_part(L1, nb, T, 1, last, sample)
        for c in range(2):
            bk, bkr = proj_fm(L1[0], L1[1], 256 + c * 128, T, nb)
            P.op("dve", lambda e, bk=bk, c=c: e.tensor_tensor(out=qgT[:, c, 0:T], in0=bk[:, 0:T], in1=ebq[:, c, 0:T], op=ALU.mult),
                 reads=[bkr, "ebq"], writes=["qgT"])
        kg_evac(L2[0], L2[1], 0, nb, T, sample)
        vg_tm(L3[0], L3[1], 0, nb)
        L5 = wload([(0, wsrc("w_o", 0, D, 0, 512), 8, 512)])
        for c in range(4):
            bk, bkr = proj_fm(L4[0], L4[1], c * 128, T, nb)
            P.op("act", lambda e, bk=bk, c=c: e.activation(out=rsT[:, c, 0:T], in_=bk[:, 0:T], func=AF.Silu), reads=[bkr], writes=["rsT"])
        if sample:
            sample_attn()
            sample_gla()
        else:
            for blk in range(nb):
                attn_block(blk, E, "E", first and blk == 0)
                gla_block(blk)
            P.op("pool", lambda e: e.tensor_copy(out=kX[:, :, 0:128], in_=kX[:, :, 512:640]), reads=["kX"], writes=["kX"])
            P.op("pool", lambda e: e.tensor_copy(out=Vdup[:, 0], in_=Vdup[:, 4]), reads=["Vdup"], writes=["Vdup"])
        if debug:
            outs.append(P.dma("pool", dbg_mix, mixT[:], reads=[("mixa", b_) for b_ in range(nb)] + [("mixg", b_ * 128) for b_ in range(nb)]))
            outs.append(P.dma("pool", dbg_q, qhT[:], reads=["qhT"]))
            outs.append(P.dma("pool", dbg_rs, rsT[:], reads=["rsT"]))
        L6 = wload([(0, wsrc("w_o", 0, D, 512, 512), 8, 512)])
        wo_ffnnorm(nb, L5[0], L5[1], L6[0], L6[1])
        if debug:
            outs.append(P.dma("sp", dbg_h, xb[:], reads=[("x", 0, b_) for b_ in range(nb)]))
            outs.append(P.dma("pool", dbg_z, actT[:], reads=[("actT", 0, b_) for b_ in range(nb)]))
        if sample:
            ffn(nb, T, lambda blk: y_s)
        else:
            ffn(nb, T, lambda blk: y_p[t * 512 + blk * 128: t * 512 + (blk + 1) * 128, :])

    def kv_part(L1, nb, T, kblk0, last, sample):
        bk, bkr = proj_fm(L1[0], L1[1], 0, T, nb)
        rstd_fm(bk[:, 0:T], T, bd_bf, 1.0 / 64, [bkr])
        c0 = kblk0 * 128
        P.op("dve", lambda e: e.scalar_tensor_tensor(out=khT_bf[:, 0:T], in0=bk[:, 0:T], scalar=gk_col[:, 0:1], in1=rstd_f[:, 0:T], op0=ALU.mult, op1=ALU.mult),
             reads=[bkr, "rstd", "gk"], writes=["khT"])
        if last or sample:
            lo = T - 128
            P.op("dve", lambda e: e.scalar_tensor_tensor(out=khT_f[:], in0=bk[:, lo:T], scalar=gk_col[:, 0:1], in1=rstd_f[:, lo:T], op0=ALU.mult, op1=ALU.mult),
                 reads=[bkr, "rstd", "gk"], writes=["khT_f"])
        P.op("act", lambda e: e.activation(out=kX[0:64, 0, c0:c0 + T], in_=khT_bf[0:64, 0:T], func=AF.Copy), reads=["khT", "kX"], writes=["kX"])
        P.op("act", lambda e: e.activation(out=kX[64:128, 3, c0:c0 + T], in_=khT_bf[64:128, 0:T], func=AF.Copy), reads=["khT", "kX"], writes=["kX"])
        b2, b2r = P.bank()
        P.op("pe", lambda e: e.matmul(b2[:, 0:T], lhsT=sw_bf[:], rhs=khT_bf[:, 0:T], start=True, stop=True), reads=["khT", "sw"], writes=[b2r])
        P.op("act", lambda e: e.activation(out=kX[0:64, 2, c0:c0 + T], in_=b2[0:64, 0:T], func=AF.Copy), reads=[b2r, "kX"], writes=["kX"])
        P.op("act", lambda e: e.activation(out=kX[64:128, 1, c0:c0 + T], in_=b2[64:128, 0:T], func=AF.Copy), reads=[b2r, "kX"], writes=["kX"])
        for blk in range(nb):
            vb, vbr = proj_tm(L1[0], L1[1], 128, 128, blk)
            P.op("act", lambda e, vb=vb, blk=blk: e.activation(out=Vdup[:, kblk0 + blk].rearrange("p j (u d) -> p j u d", u=2),
                                                               in_=fap(vb[:, 0:1], [[64, 2], [0, 2], [1, 64]]), func=AF.Copy),
                 reads=[vbr, "Vdup"], writes=["Vdup"])
            if (last and blk == nb - 1) or sample:
                P.op("dve", lambda e, vb=vb: e.tensor_copy(out=vw_f[:], in_=vb[:, 0:128]), reads=[vbr], writes=["vw_f"])
        if last or sample:
            kb_, kbr = P.bank()
            P.op("pe", lambda e: e.transpose(out=kb_[:, 0:128], in_=khT_f[:], identity=ident_f[:]), reads=["khT_f", "ident_f"], writes=[kbr])
            P.op("act", lambda e: e.activation(out=kw_f[:], in_=kb_[:, 0:128], func=AF.Copy), reads=[kbr], writes=["kw_f"])
        if last:
            outs.append(P.dma("sp", kwp, kw_f[:], reads=["kw_f"]))
            outs.append(P.dma("sp", vwp, vw_f[:], reads=["vw_f"]))

    def sample_attn():
        for (dst, src, new, nm) in ((kws, ck, kw_f, "kws"), (vws, cv, vw_f, "vws")):
            P.dma("sp", dst[:, 0:127, :], src[:, 1:128, :], writes=[nm])
            P.dma("sp", bass.AP(tensor=dst.tensor, offset=127 * 128, ap=[[128 * 128, 16], [1, 128]]), new[0:16, :], reads=["kw_f", "vw_f"], writes=[nm])
        t1 = P.dma("pool", Kw_bf[:], kws.rearrange("b k f -> k b f"), reads=["kws"], writes=["Kw"])
        for j in range(2):
            P.dma("pool", Kwsw_bf[:, :, (1 - j) * 64:(2 - j) * 64], kws[:, :, j * 64:(j + 1) * 64].rearrange("b k f -> k b f"), reads=["kws"], writes=["Kwsw"])
            for u in range(2):
                P.dma("pool", Vwd[:, :, j, u * 64:(u + 1) * 64], vws[:, :, j * 64:(j + 1) * 64].rearrange("b k f -> k b f"), reads=["vws"], writes=["Vwd"])
        outs.append(t1)
        P.op("dve", lambda e: e.tensor_copy(out=qsA[0:64], in_=qhT[0:64, :, 0:16]), reads=["qhT", "zinit"], writes=["qsA"])
        P.op("dve", lambda e: e.tensor_copy(out=qsB[64:128], in_=qhT[64:128, :, 0:16]), reads=["qhT", "zinit"], writes=["qsB"])
        sb_, sbr = P.bank()
        for b in range(16):
            tb, tbr = tbank()
            bf_ = b % 2

            def tr(e, tb=tb, b=b):
                e.transpose(out=tb[:, 0:128], in_=Kw_bf[:, b, :], identity=ident_bf[:])
                return e.transpose(out=tb[:, 128:256], in_=Kwsw_bf[:, b, :], identity=ident_bf[:])
            P.op("pe", tr, reads=["Kw", "Kwsw", "ident_bf"], writes=[tbr])
            P.op("act", lambda e, tb=tb, bf_=bf_: e.activation(out=KTb[:, bf_].rearrange("p a k -> p (a k)"), in_=tb[:, 0:256], func=AF.Copy),
                 reads=[tbr], writes=[("KTb", bf_)])

            def mm(e, b=b, bf_=bf_):
                ins = None
                for j in range(2):
                    for half in range(2):
                        kt = KTb[:, bf_, 0 if j == half else 1, :]
                        q = (qsA if half == 0 else qsB)[:, 2 * j:2 * j + 2, b]
                        o = b * 8 + (j * 2 + half) * 2
                        ins = e.matmul(sb_[:, o:o + 2], lhsT=kt, rhs=q, start=True, stop=True)
                return ins
            P.op("pe", mm, reads=[("KTb", bf_), "qsA", "qsB"], writes=[sbr])
        P.op("act", lambda e: e.activation(out=pes[:].rearrange("p b h -> p (b h)"), in_=sb_[:, 0:128], func=AF.Exp), reads=[sbr], writes=["pes"])
        P.op("dve", lambda e: e.tensor_tensor(out=Pts[:], in0=pes[:], in1=fap(E[:, 0, 0, 1, 0, 127:128], [[0, 16], [512, 4], [128, 2]]), op=ALU.mult),
             reads=["pes", "E"], writes=["Pts"])
        ob, obr = P.bank()
        db, dbr = P.bank()

        def mmv(e):
            ins = None
            for b in range(16):
                for j in range(2):
                    e.matmul(ob[:, b * 8 + j * 4:b * 8 + j * 4 + 4], lhsT=Vwd[:, b, j, :], rhs=Pts[:, b, j * 4:(j + 1) * 4], start=True, stop=True)
                ins = e.matmul(db[:, b * 8:(b + 1) * 8], lhsT=ones_bf[:], rhs=Pts[:, b, :], start=True, stop=True)
            return ins
        P.op("pe", mmv, reads=["Pts", "Vwd", "ones"], writes=[obr, dbr])
        P.op("dve", lambda e: e.tensor_tensor(out=rec_f[:, 0:128].rearrange("p (b h) -> p b h", b=16), in0=db[:, 0:128].rearrange("p (b h) -> p b h", b=16),
                                              in1=fap(sinkexp[:, 0, 0, 0, 0:1], [[0, 16], [128, 8]]), op=ALU.add), reads=[dbr, "sinkexp"], writes=["rec"])
        P.op("act", lambda e: e.activation(out=rec_f[:, 0:128], in_=rec_f[:, 0:128], func=AF.Ln), reads=["rec"], writes=["rec"])
        P.op("act", lambda e: e.activation(out=rec_f[:, 0:128], in_=rec_f[:, 0:128], func=AF.Exp, scale=-1.0), reads=["rec"], writes=["rec"])
        for j in range(2):
            for half in range(2):
                r0 = half * 64
                o = (j * 2 + half) * 2
                P.op("dve", lambda e, r0=r0, o=o, j=j: e.tensor_tensor(
                    out=mixT[r0:r0 + 64, 2 * j:2 * j + 2, 0:16],
                    in0=fap(ob[r0:r0 + 64, o:o + 1], [[1, 2], [8, 16]]),
                    in1=fap(rec_f[r0:r0 + 64, o:o + 1], [[1, 2], [8, 16]]), op=ALU.mult), reads=[obr, "rec"], writes=[("mixa", 0)])

    def sample_gla():
        ob, obr = P.bank(hold=True)
        for b in range(16):
            bf_ = b % 2
            P.dma("sp", Sb[:, bf_], sg[b].rearrange("(c u) d v -> (u d) c v", u=2), writes=[("Sb", bf_)])
            vb, vbr = P.bank()
            P.op("pe", lambda e, vb=vb, b=b: e.matmul(vb[:, 0:512], lhsT=sel_bf[:, b, :], rhs=vg_tok[:, 0, :], start=True, stop=True),
                 reads=["sel", ("vg", 0)], writes=[vbr])
            for c in range(2):
                for half in range(2):
                    r0 = half * 64
                    h = 2 * c + half
                    P.op("dve", lambda e, vb=vb, b=b, c=c, r0=r0, h=h, bf_=bf_: e.scalar_tensor_tensor(
                        out=Wt[r0:r0 + 64, bf_, c, :], in0=vb[r0:r0 + 64, h * 128:(h + 1) * 128], scalar=kgf[r0:r0 + 64, c, b:b + 1],
                        in1=Sb[r0:r0 + 64, bf_, c, :], op0=ALU.mult, op1=ALU.add), reads=[vbr, "kgf", ("Sb", bf_)], writes=[("Wt", bf_)])
            P.op("act", lambda e, bf_=bf_: e.activation(out=WA[0:64, bf_], in_=Wt[0:64, bf_], func=AF.Copy), reads=[("Wt", bf_), "zinit"], writes=[("WA", bf_)])
            P.op("act", lambda e, bf_=bf_: e.activation(out=WB[64:128, bf_], in_=Wt[64:128, bf_], func=AF.Copy), reads=[("Wt", bf_), "zinit"], writes=[("WB", bf_)])
            P.op("dve", lambda e, b=b, bf_=bf_: e.tensor_tensor(out=Sn[:, bf_], in0=Wt[:, bf_], in1=fap(elb[:, 0, b:b + 1], [[128, 2], [0, 128]]), op=ALU.mult),
                 reads=[("Wt", bf_), "elb"], writes=[("Sn", bf_)])
            outs.append(P.dma("sp", gss[b].rearrange("(c u) d v -> (u d) c v", u=2), Sn[:, bf_], reads=[("Sn", bf_)]))

            def mm(e, b=b, bf_=bf_):
                ins = None
                for h in range(4):
                    c, half = h // 2, h % 2
                    w = (WA if half == 0 else WB)[:, bf_, c, :]
                    ins = e.matmul(ob[:, h * 16 + b:h * 16 + b + 1], lhsT=w, rhs=qgT[:, c, b:b + 1], start=True, stop=True)
                return ins
            P.op("pe", mm, reads=[("WA", bf_), ("WB", bf_), "qgT"], writes=[obr])
        P.release(obr)
        gla_out_norm(ob, obr, 64, 16, 0)

    import os as _os
    _kb = _os.environ.get("KBAR", "")
    for t in range((0 if debug in ("sample", "scan") else (2 if debug == "two" else (4 if debug == "four" else 1))) if debug else 4):
        main_tile("prompt", t)
        if "t" in _kb:
            P.barrier()
    if debug and debug != "sample":
        outs.append(P.dma("pool", dbg_a, aT[:], reads=[("aT", m) for m in range(NKF)]))
    outs.append(P.dma("sp", gsp.rearrange("(c u) d v -> (u d) c v", u=2), S[:], reads=["S"]))
    P.barrier()
    if (not debug) or debug == "sample":
        main_tile("sample", 0)

    P.emit()
    P.stack.close()
    return nc


_CACHE = {}


def kernel(x_prompt, x_sample, cache_k, cache_v, state_gla, attn_norm_g, w_in, q_norm_g, k_norm_g, attn_sinks,
           rel_bias, w_gla_gate2, b_gla_gate, gla_norm_g, w_o, ffn_norm_g, w_gate, w_up, w_down):
    f = lambda a: np.ascontiguousarray(np.asarray(a, dtype=np.float32))
    xpr = f(x_prompt)[0]
    xsm = f(x_sample)[:, 0, :]
    ckk = f(cache_k)[0].reshape(128, 128, 128)
    cvv = f(cache_v)[0].reshape(128, 128, 128)
    sgg = f(state_gla)[0]
    consts = host_consts()
    shared = dict(w_in=f(w_in)[0], w_o=f(w_o)[0], w_gate=f(w_gate)[0], w_up=f(w_up)[0], w_down=f(w_down)[0],
                  attn_g=f(attn_norm_g)[0], ffn_g=f(ffn_norm_g)[0], qng=f(q_norm_g)[0], kng=f(k_norm_g)[0],
                  sinks=f(attn_sinks)[0], relb=f(rel_bias), w2=f(w_gla_gate2)[0], bgate=f(b_gla_gate)[0], glag=f(gla_norm_g)[0])
    for k, v in consts.items():
        shared["c_" + k] = v
    in_maps = []
    for c in range(NCORE):
        m = dict(shared)
        m["xp"] = xpr[c * TOK:(c + 1) * TOK]
        m["xhalo"] = xpr[c * TOK - 128:c * TOK] if c > 0 else np.zeros((128, D), np.float32)
        pre = np.zeros((NPRE, D), np.float32)
        if c > 0:
            pre[NPRE - c * TOK:] = xpr[:c * TOK]
        m["xpre"] = pre
        xs_ = np.zeros((128, D), np.float32)
        xs_[:16] = xsm[c * 16:(c + 1) * 16]
        m["xs"] = xs_
        m["ck"] = ckk[c * 16:(c + 1) * 16]
        m["cv"] = cvv[c * 16:(c + 1) * 16]
        m["sg"] = sgg[c * 16:(c + 1) * 16]
        m["flag"] = np.full((128, 1), 1.0 if c > 0 else 0.0, np.float32)
        in_maps.append(m)
    if "nc" not in _CACHE:
        _CACHE["nc"] = build_program()
    res = run_bass_kernel_spmd(_CACHE["nc"], in_maps, core_ids=list(range(NCORE)))
    R = res.results
    y_prompt = np.concatenate([R[c]["y_p"] for c in range(NCORE)], axis=0)[None]
    y_sample = np.concatenate([R[c]["y_s"][:16] for c in range(NCORE)], axis=0)[:, None, :]
    kwp = R[7]["kwp"].reshape(1, 1, 128, 2, 64)
    vwp = R[7]["vwp"].reshape(1, 1, 128, 2, 64)
    gsp = R[7]["gsp"].reshape(1, 1, 4, 64, 128)
    kws = np.concatenate([R[c]["kws"] for c in range(NCORE)], axis=0).reshape(1, 128, 128, 2, 64)
    vws = np.concatenate([R[c]["vws"] for c in range(NCORE)], axis=0).reshape(1, 128, 128, 2, 64)
    gss = np.concatenate([R[c]["gss"] for c in range(NCORE)], axis=0).reshape(1, 128, 4, 64, 128)
    return (y_prompt.astype(np.float32), y_sample.astype(np.float32), kwp, vwp, gsp, kws, vws, gss)
```

```python
import contextlib
import math
import numpy as np
import concourse.bass as bass
import concourse.mybir as mybir
from concourse.bass_utils import run_bass_kernel_spmd

F32 = mybir.dt.float32
BF16 = mybir.dt.bfloat16
AF = mybir.ActivationFunctionType
ALU = mybir.AluOpType

NCORE = 8
D = 1024
TOK = 2048
NPRE = 7 * 2048
DFF = 2816
NKF = DFF // 128
INW = 2320
ENGS = ("pe", "act", "dve", "pool", "sp")
NDMASEM = 12
NSLOT = 4
EPS = 1e-6
MASKV = -30000.0


class _Ins:
    def then_inc(self, *a, **k):
        return self


class _Mock:
    def __init__(self):
        self.cost = 0.0

    def _free(self, ap):
        n = 1
        for d in list(ap.shape)[1:]:
            n *= int(d)
        return n

    def matmul(self, out, lhsT=None, rhs=None, **k):
        n = max(self._free(rhs), 64)
        self.cost += (n * (4 if rhs.dtype == F32 else 1)) / 2400.0 + 0.01
        return _Ins()

    def transpose(self, out=None, in_=None, identity=None, **k):
        self.cost += (128 * (4 if in_.dtype == F32 else 1)) / 2400.0 + 0.01
        return _Ins()

    def __getattr__(self, name):
        def f(*a, **k):
            o = k.get("out", a[0] if a else None)
            n = self._free(o) if o is not None else 64
            self.cost += 0.2 + n / 1000.0
            return _Ins()
        return f


class Prog:
    def __init__(self, nc):
        self.nc = nc
        self.stack = contextlib.ExitStack()
        self.oplist = []
        self.last_w = {}
        self.readers = {}
        self.base = None
        self.sems = {}
        self.nbuf = 0
        self.banks = []
        self.bank_rr = 0
        self.held = set()
        self.finals = []

    def sb(self, shape, dt, name=None):
        self.nbuf += 1
        return self.stack.enter_context(self.nc.sbuf_tensor(name or f"sb{self.nbuf}", list(shape), dt))

    def ps(self, shape, dt, name=None):
        self.nbuf += 1
        return self.stack.enter_context(self.nc.psum_tensor(name or f"ps{self.nbuf}", list(shape), dt))

    def bank(self, hold=False):
        while True:
            i = self.bank_rr % len(self.banks)
            self.bank_rr += 1
            if i not in self.held:
                break
        if hold:
            self.held.add(i)
        return self.banks[i], ("ps", i)

    def release(self, res):
        self.held.discard(res[1])

    def _sem(self, key):
        if key not in self.sems:
            nm = "s_" + "_".join(str(k) for k in (key if isinstance(key, tuple) else (key,)))
            self.sems[key] = self.stack.enter_context(self.nc.semaphore(nm))
        return self.sems[key]

    def _add(self, eng, fn, reads, writes, kind, cost, dma=None):
        deps = set()
        for r in reads:
            t = self.last_w.get(r, self.base)
            if t is not None:
                deps.add(t)
        for w in writes:
            t = self.last_w.get(w, self.base)
            if t is not None:
                deps.add(t)
            deps.update(self.readers.get(w, ()))
        if not reads and not writes and self.base is not None:
            deps.add(self.base)
        i = len(self.oplist)
        deps.discard(i)
        self.oplist.append(dict(eng=eng, fn=fn, deps=deps, kind=kind, cost=cost, dma=dma))
        for r in reads:
            self.readers.setdefault(r, []).append(i)
        for w in writes:
            self.last_w[w] = i
            self.readers[w] = []
        return i

    def op(self, eng, fn, reads=(), writes=()):
        m = _Mock()
        fn(m)
        return self._add(eng, fn, reads, writes, "c", m.cost)

    def dma(self, eng, out, in_, reads=(), writes=()):
        n = 1
        for d in out.shape:
            n *= int(d)
        nbytes = n * (4 if out.dtype == F32 else 2)
        return self._add(eng, None, reads, writes, "d", 2.0 + nbytes / 150e3, dma=(out, in_))

    def barrier(self, keep=()):
        kept = {k: v for k, v in self.last_w.items() if (k in keep or (isinstance(k, tuple) and k and k[0] in keep))}
        skip = set(getattr(self, "nofence", ()))
        allprev = set(range(len(self.oplist))) - skip
        i = len(self.oplist)
        self.oplist.append(dict(eng="sp", fn=None, deps=allprev, kind="n", cost=0.05, dma=None))
        self.base = i
        self.bars = getattr(self, "bars", []) + [i]
        self.last_w = dict(kept)
        self.readers = {}

    def final_wait(self, eng, toks):
        pass

    def schedule(self):
        import heapq, os
        ops = self.oplist
        n = len(ops)
        succ = [[] for _ in range(n)]
        ndep = [0] * n
        for i, o in enumerate(ops):
            ndep[i] = len(o["deps"])
            for d in o["deps"]:
                succ[d].append(i)
        done = [0.0] * n
        ready_t = [0.0] * n
        efree = {e: 0.0 for e in ENGS}
        waiting = {e: [] for e in ENGS}
        avail = {e: [] for e in ENGS}
        order = {e: [] for e in ENGS}
        for i, o in enumerate(ops):
            if ndep[i] == 0:
                heapq.heappush(waiting[o["eng"]], (0.0, i))
        nsched = 0
        while nsched < n:
            best = None
            for e in ENGS:
                w, a = waiting[e], avail[e]
                while w and w[0][0] <= efree[e]:
                    heapq.heappush(a, heapq.heappop(w)[1])
                if a:
                    cand = (efree[e], a[0], e, True)
                elif w:
                    cand = (w[0][0], w[0][1], e, False)
                else:
                    continue
                if best is None or cand[:2] < best[:2]:
                    best = cand
            st, i, e, from_avail = best
            if from_avail:
                heapq.heappop(avail[e])
            else:
                heapq.heappop(waiting[e])
            o = ops[i]
            if o["kind"] == "d":
                efree[e] = st + 0.15
                done[i] = st + o["cost"]
            else:
                efree[e] = st + o["cost"]
                done[i] = st + o["cost"] + 0.15
            order[e].append(i)
            nsched += 1
            for sidx in succ[i]:
                ndep[sidx] -= 1
                if done[i] > ready_t[sidx]:
                    ready_t[sidx] = done[i]
                if ndep[sidx] == 0:
                    heapq.heappush(waiting[ops[sidx]["eng"]], (ready_t[sidx], sidx))
        self.sim_time = max(done) if n else 0.0
        self.sim_done = done
        if os.environ.get("KSIM"):
            print("SIM total", round(self.sim_time), "barriers", [round(done[b]) for b in getattr(self, "bars", [])])
        return order

    def emit(self):
        nc = self.nc
        import os
        order = self.schedule()
        km = os.environ.get("KSCHED", "mid")
        if km == "0":
            order = {e: [i for i, o in enumerate(self.oplist) if o["eng"] == e] for e in ENGS}
        elif km == "mid" and len(getattr(self, "bars", [])) >= 2:
            B2 = self.bars[-1]
            prog = {e: [i for i, o in enumerate(self.oplist) if o["eng"] == e] for e in ENGS}
            order = {e: [i for i in order[e] if i <= B2] + [i for i in prog[e] if i > B2] for e in ENGS}
        elif km in ("pre", "post") and self.base is not None:
            B = self.base
            prog = {e: [i for i, o in enumerate(self.oplist) if o["eng"] == e] for e in ENGS}
            if km == "post":
                order = {e: [i for i in prog[e] if i <= B] + [i for i in order[e] if i > B] for e in ENGS}
            else:
                order = {e: [i for i in order[e] if i <= B] + [i for i in prog[e] if i > B] for e in ENGS}
        ops = self.oplist
        tok = [None] * len(ops)
        ccnt = {e: 0 for e in ENGS}
        dcnt = {}
        drr = {e: 0 for e in ENGS}
        plan = {e: [] for e in ENGS}
        prevdma = {}
        for e in ENGS:
            for i in order[e]:
                o = ops[i]
                if o["kind"] == "d":
                    j = drr[e] % NDMASEM
                    drr[e] += 1
                    key = ("d", e, j)
                    c = dcnt.get(key, 0)
                    dcnt[key] = c + 1
                    tok[i] = (key, 16 * (c + 1))
                    prevdma[i] = (key, 16 * c) if c > 0 else None
                else:
                    ccnt[e] += 1
                    tok[i] = (e, ccnt[e])
        for e in ENGS:
            waited = {}
            for i in order[e]:
                o = ops[i]
                need = {}
                for d in o["deps"]:
                    k, v = tok[d]
                    if waited.get(k, 0) >= v:
                        continue
                    if need.get(k, 0) < v:
                        need[k] = v
                if o["kind"] == "d" and prevdma.get(i):
                    k, v = prevdma[i]
                    if waited.get(k, 0) < v and need.get(k, 0) < v:
                        need[k] = v
                for k, v in need.items():
                    waited[k] = v
                plan[e].append((list(need.items()), i))
            if e == "sp":
                fin = {}
                for i2, t in enumerate(tok):
                    if t is not None and fin.get(t[0], 0) < t[1]:
                        fin[t[0]] = t[1]
                plan[e].append(([(k, v) for k, v in fin.items() if waited.get(k, 0) < v], None))
        for e in ENGS:
            for (waits, i) in plan[e]:
                for (k, v) in waits:
                    self._sem(k)
                if i is not None:
                    self._sem(tok[i][0])
        if os.environ.get("KCHECK"):
            import collections
            sv = collections.defaultdict(int)
            ptr = {e: 0 for e in ENGS}
            while True:
                prog_ = False
                for e in ENGS:
                    while ptr[e] < len(plan[e]):
                        waits, i = plan[e][ptr[e]]
                        if all(sv[k] >= v for k, v in waits):
                            if i is not None:
                                k, v = tok[i]
                                inc = 16 if ops[i]["kind"] == "d" else 1
                                sv[k] += inc
                                assert sv[k] == v, ("token mismatch", e, i, k, v, sv[k])
                            ptr[e] += 1
                            prog_ = True
                        else:
                            break
                if all(ptr[e] == len(plan[e]) for e in ENGS):
                    print("KCHECK: ok, no deadlock")
                    break
                if not prog_:
                    for e in ENGS:
                        if ptr[e] < len(plan[e]):
                            waits, i = plan[e][ptr[e]]
                            print("KCHECK STUCK", e, ptr[e], i, [(k, v, sv[k]) for k, v in waits if sv[k] < v])
                    break
        block = self.stack.enter_context(nc.Block())

        def run(engname):
            def body(e):
                for (waits, i) in plan[engname]:
                    for (k, v) in waits:
                        e.wait_ge(self.sems[k], v)
                    if i is None:
                        continue
                    o = ops[i]
                    if o["kind"] == "d":
                        ins = e.dma_start(out=o["dma"][0], in_=o["dma"][1], allow_slow_non_contiguous=True)
                        ins.then_inc(self.sems[tok[i][0]], 16)
                    elif o["kind"] == "n":
                        ins = e.nop()
                        ins.then_inc(self.sems[tok[i][0]], 1)
                    else:
                        ins = o["fn"](e)
                        ins.then_inc(self.sems[tok[i][0]], 1)
            return body
        block.tensor(run("pe"))
        block.scalar(run("act"))
        block.vector(run("dve"))
        block.gpsimd(run("pool"))
        block.sync(run("sp"))


def fap(ap, dims):
    return bass.AP(tensor=ap.tensor, offset=ap.offset, ap=[list(ap.ap[0])] + [list(d) for d in dims])


def t5_bucket_np(n):
    n = np.maximum(n, 0)
    nf = np.maximum(n, 1).astype(np.float32)
    large = 16 + (np.log(nf / 16) / math.log(128 / 16) * 16).astype(np.int32)
    large = np.minimum(large, 31)
    return np.where(n < 16, n, large)


def host_consts():
    c = {}
    c["ident"] = np.eye(128, dtype=np.float32)
    s = np.arange(128)[:, None]
    t = np.arange(128)[None, :]
    c["tri"] = np.where(s <= t, -1.0 / 16, 0.0).astype(np.float32)
    c["tris"] = (np.eye(128) * (-1.0 / 16)).astype(np.float32)
    c["cmask"] = (s <= t).astype(np.float32)
    c["jx"] = np.eye(128, dtype=np.float32)[::-1].copy()
    bd = np.zeros((128, 128), np.float32)
    bd[:64, :64] = 1
    bd[64:, 64:] = 1
    c["bd"] = bd
    sw = np.zeros((128, 128), np.float32)
    for m in range(128):
        sw[(m + 64) % 128, m] = 1
    c["sw"] = sw
    oh = np.zeros((128, 2, 256), np.float32)
    for kb in range(2):
        off = 128 if kb == 0 else 0
        for i in range(255):
            dlt = 127 + off - i
            if 0 <= dlt <= 127:
                oh[int(t5_bucket_np(np.array(dlt))), kb, i] = 1.0
            else:
                oh[32, kb, i] = MASKV
        oh[32, kb, 255] = MASKV
    c["oh"] = oh
    sel = np.zeros((128, 16, 128), np.float32)
    for b in range(16):
        sel[b, b, :] = 1
    c["sel"] = sel
    return c


def build_program(debug=False):
    nc = bass.Bass("TRN2", target_bir_lowering=False)
    P = Prog(nc)

    def din(name, shape):
        return nc.dram_tensor(name, list(shape), F32, kind="ExternalInput").ap()

    def dout(name, shape):
        return nc.dram_tensor(name, list(shape), F32, kind="ExternalOutput").ap()

    xp = din("xp", [TOK, D]); xhalo = din("xhalo", [128, D]); xpre = din("xpre", [NPRE, D]); xs = din("xs", [128, D])
    ck = din("ck", [16, 128, 128]); cv = din("cv", [16, 128, 128]); sg = din("sg", [16, 4, 64, 128])
    w_in = din("w_in", [D, INW]); w_o = din("w_o", [D, D]); w_gate = din("w_gate", [D, DFF]); w_up = din("w_up", [D, DFF])
    w_down = din("w_down", [DFF, D])
    attn_g = din("attn_g", [D]); ffn_g = din("ffn_g", [D]); qg_in = din("qng", [64]); kg_in = din("kng", [64])
    sinks = din("sinks", [8]); relb = din("relb", [32, 8]); w2 = din("w2", [16, 256]); bgate = din("bgate", [256])
    glag = din("glag", [128]); flag = din("flag", [128, 1])
    cn = {k: din("c_" + k, v.shape) for k, v in host_consts().items()}

    y_p = dout("y_p", [TOK, D]); y_s = dout("y_s", [128, D])
    kwp = dout("kwp", [128, 128]); vwp = dout("vwp", [128, 128]); gsp = dout("gsp", [4, 64, 128])
    kws = dout("kws", [16, 128, 128]); vws = dout("vws", [16, 128, 128]); gss = dout("gss", [16, 4, 64, 128])
    fscr = nc.dram_tensor("fscr", [2, 8, 256], F32).ap()
    wb = {"w_in": nc.dram_tensor("wb_in", [D, INW], BF16).ap(), "w_o": nc.dram_tensor("wb_o", [D, D], BF16).ap(),
          "w_gate": nc.dram_tensor("wb_gate", [D, DFF], BF16).ap(), "w_up": nc.dram_tensor("wb_up", [D, DFF], BF16).ap(),
          "w_down": nc.dram_tensor("wb_down", [DFF, D], BF16).ap()}
    wf = {"w_in": w_in, "w_o": w_o, "w_gate": w_gate, "w_up": w_up, "w_down": w_down}
    if debug:
        dbg_mix = dout("dbg_mix", [128, 8, 512]); dbg_h = dout("dbg_h", [128, 4, D]); dbg_z = dout("dbg_z", [128, 8, 512]); dbg_a = dout("dbg_a", [128, NKF, 512])
        dbg_q = dout("dbg_q", [128, 4, 512]); dbg_rs = dout("dbg_rs", [128, 4, 512])

    for i in range(6):
        P.banks.append(P.ps([128, 512], F32, f"bank{i}"))
    psT = [P.ps([128, 1024], BF16, f"pst{i}") for i in range(2)]
    pst_rr = [0]

    def tbank():
        i = pst_rr[0] % 2
        pst_rr[0] += 1
        return psT[i], ("pst", i)

    ident_f = P.sb([128, 128], F32); ident_bf = P.sb([128, 128], BF16)
    tri_f = P.sb([128, 128], F32); tris_f = P.sb([128, 128], F32); cmask_bf = P.sb([128, 128], BF16)
    jx_f = P.sb([128, 128], F32); bd_bf = P.sb([128, 128], BF16); sw_bf = P.sb([128, 128], BF16)
    ones_bf = P.sb([128, 128], BF16); zeros_f = P.sb([128, 128], F32); scr_f = P.sb([128, 2048], F32)
    sel_bf = P.sb([128, 16, 128], BF16)
    gaT = P.sb([128, 8], F32); gfT = P.sb([128, 8], F32)
    gq_col = P.sb([128, 1], F32); gk_col = P.sb([128, 1], F32); glag_col = P.sb([128, 1], F32)
    eps_col = P.sb([128, 1], F32); ln8_col = P.sb([128, 1], F32); flag_col = P.sb([128, 1], F32)
    bgate_bc = P.sb([128, 256], F32); w2pad = P.sb([128, 256], BF16)
    relb_pad = P.sb([128, 128], F32)
    hank = scr_f[:, 0:1024].rearrange("p (h s) -> p h s", h=8)
    oh_sb = scr_f[:, 1024:1536].rearrange("p (a b) -> p a b", a=2)
    ftab = scr_f[0:8, 1536:2048].rearrange("p (a b) -> p a b", a=2)
    E = P.sb([128, 2, 2, 2, 2, 128], F32)
    sink_bc = P.sb([128, 8], F32); sinkexp = P.sb([128, 2, 2, 2, 128], F32)
    ring = P.sb([128, NSLOT, 8, 512], BF16)
    xb = P.sb([128, 4, D], F32)
    actT = P.sb([128, 8, 512], BF16)
    nbf = P.sb([128, D], BF16); nbf_b = P.sb([128, D], BF16)
    zt_b = P.sb([128, 256], F32); sp_b = P.sb([128, 256], F32); ktok_b = P.sb([128, 2, 2, 128], BF16)
    ss_c2 = P.sb([128, 2], F32); rs_c2 = P.sb([128, 2], F32)
    ss_c = P.sb([128, 1], F32); rs_c = P.sb([128, 1], F32)
    qhT = P.sb([128, 4, 512], BF16)
    kX = P.sb([128, 4, 640], BF16)
    khT_bf = P.sb([128, 512], BF16); khT_f = P.sb([128, 128], F32)
    Vdup = P.sb([128, 5, 2, 128], BF16)
    qgT = P.sb([128, 2, 512], BF16); kgA = P.sb([128, 2, 512], BF16); kgB = P.sb([128, 2, 512], BF16)
    kgf = P.sb([128, 2, 128], F32)
    ktok = P.sb([128, 2, 2, 128], BF16)
    vg_tok = P.sb([128, 4, 512], BF16)
    rsT = P.sb([128, 4, 512], BF16)
    ulrT = P.sb([128, 512], BF16)
    zt = P.sb([128, 256], F32); sp_t = P.sb([128, 256], F32)
    ebq = P.sb([128, 2, 512], F32); enb = P.sb([128, 2, 512], F32); elast = P.sb([128, 2, 4], F32)
    elb = P.sb([128, 2, 128], F32)
    S = P.sb([128, 2, 128], F32); Stmp = P.sb([128, 2, 128], F32); SA = P.sb([128, 2, 128], BF16); SB = P.sb([128, 2, 128], BF16)
    ATbf = P.sb([128, 4, 128], BF16)
    sq_bf = P.sb([128, 512], BF16); rstd_f = P.sb([128, 512], F32); tmp_f = P.sb([128, 512], F32)
    pe_f = P.sb([128, 512], F32); rec_f = P.sb([128, 512], F32)
    PT = P.sb([128, 2, 2, 2, 2, 128], BF16)
    mixT = P.sb([128, 8, 512], BF16)
    aT = P.sb([128, NKF, 512], BF16)
    sg_f = P.sb([128, 512], F32)
    vw_f = P.sb([128, 128], F32); kw_f = P.sb([128, 128], F32)
    Kw_bf = P.sb([128, 16, 128], BF16); Kwsw_bf = P.sb([128, 16, 128], BF16)
    KTb = P.sb([128, 2, 2, 128], BF16)
    Vwd = P.sb([128, 16, 2, 128], BF16)
    qsA = P.sb([128, 4, 16], BF16); qsB = P.sb([128, 4, 16], BF16)
    Pts = P.sb([128, 16, 8], BF16); pes = P.sb([128, 16, 8], F32)
    Sb = scr_f[:, 0:512].rearrange("p (a c v) -> p a c v", a=2, c=2)
    Wt = scr_f[:, 512:1024].rearrange("p (a c v) -> p a c v", a=2, c=2)
    Sn = scr_f[:, 1024:1536].rearrange("p (a c v) -> p a c v", a=2, c=2)
    WA = P.sb([128, 2, 2, 128], BF16); WB = P.sb([128, 2, 2, 128], BF16)

    CUR = {"p": 0}
    _vwd32 = Vwd[:].rearrange("p a b c -> p (a b c)").bitcast(F32)
    _kwf = Kw_bf[:].rearrange("p a b -> p (a b)")
    _kwswf = Kwsw_bf[:].rearrange("p a b -> p (a b)")
    nbfs2 = (nbf, nbf_b)

    def X(blk):
        if CUR["p"] == 0:
            return xb[:, blk, :]
        src = _vwd32 if blk < 2 else scr_f[:]
        o = (blk % 2) * 1024
        return src[:, o:o + 1024]

    def A(k):
        if CUR["p"] == 0:
            return actT[:, k, :]
        src = _kwf if k < 4 else _kwswf
        o = (k % 4) * 512
        return src[:, o:o + 512]

    def xr(blk):
        return ("x", CUR["p"], blk)

    def ar(blk):
        return ("actT", CUR["p"], blk)

    def ld(dst, src, name, eng="sp"):
        P.dma(eng, dst, src, writes=[name])

    ld(ident_f[:], cn["ident"], "ident_f"); ld(tri_f[:], cn["tri"], "tri"); ld(tris_f[:], cn["tris"], "tris")
    ld(jx_f[:], cn["jx"], "jx"); ld(oh_sb, cn["oh"], "oh")
    P.dma("pool", ident_bf[:], cn["ident"], writes=["ident_bf"])
    P.dma("pool", cmask_bf[:], cn["cmask"], writes=["cmask"])
    P.dma("pool", bd_bf[:], cn["bd"], writes=["bd"])
    P.dma("pool", sw_bf[:], cn["sw"], writes=["sw"])
    P.dma("pool", sel_bf[:], cn["sel"], writes=["sel"])
    ld(gaT[:], attn_g.rearrange("(k p) -> p k", p=128), "gaT"); ld(gfT[:], ffn_g.rearrange("(k p) -> p k", p=128), "gfT")
    for h in range(2):
        ld(gq_col[h * 64:(h + 1) * 64, :], qg_in.rearrange("(p o) -> p o", o=1), "gq")
        ld(gk_col[h * 64:(h + 1) * 64, :], kg_in.rearrange("(p o) -> p o", o=1), "gk")
    ld(glag_col[:], glag.rearrange("(p o) -> p o", o=1), "glag"); ld(flag_col[:], flag, "flag")
    ld(bgate_bc[:], bass.AP(tensor=bgate.tensor, offset=0, ap=[[0, 128], [1, 256]]), "bgate")
    ld(sink_bc[:], bass.AP(tensor=sinks.tensor, offset=0, ap=[[0, 128], [1, 8]]), "sink_bc")
    P.op("dve", lambda e: e.memset(ones_bf[:], 1.0), writes=["ones"])
    P.op("dve", lambda e: e.memset(zeros_f[:], 0.0), writes=["zeros"])
    P.op("dve", lambda e: e.memset(eps_col[:], EPS), writes=["eps"])
    P.op("dve", lambda e: e.memset(ln8_col[:], math.log(0.125)), writes=["ln8"])
    P.op("dve", lambda e: e.memset(w2pad[:], 0.0), writes=["w2pad"])
    P.dma("pool", w2pad[112:128, :], w2, reads=[], writes=["w2pad"])
    P.op("dve", lambda e: e.tensor_scalar(out=gq_col[:], in0=gq_col[:], scalar1=0.125, scalar2=None, op0=ALU.mult),
         reads=["gq"], writes=["gq"])
    for t_ in (kgA, kgB, SA, SB, qsA, qsB, WA, WB):
        P.op("dve", lambda e, t_=t_: e.memset(t_[:], 0.0), writes=["zinit"])
    P.op("dve", lambda e: e.memset(kX[:], 0.0), writes=["kX"])
    P.op("dve", lambda e: e.memset(S[:], 0.0), writes=["S"])
    P.op("dve", lambda e: e.memset(relb_pad[:], 0.0), writes=["relb_pad"])
    P.op("dve", lambda e: e.memset(relb_pad[32:33, :], 1.0), reads=[], writes=["relb_pad"])
    P.dma("sp", relb_pad[0:32, 0:8], relb, writes=["relb_pad"])

    bk, bkr = P.bank()
    P.op("pe", lambda e: e.matmul(bk[:, 0:512], lhsT=relb_pad[:], rhs=scr_f[:, 1024:1536], start=True, stop=True),
         reads=["relb_pad", "oh"], writes=[bkr])
    P.op("act", lambda e: e.activation(out=scr_f[0:8, 1536:2048], in_=bk[0:8, 0:512], func=AF.Copy), reads=[bkr], writes=["ftab"])
    P.dma("sp", fscr.rearrange("k h i -> h k i"), ftab, reads=["ftab"], writes=["fscr"])
    for kb in range(2):
        src = bass.AP(tensor=fscr.tensor, offset=kb * 8 * 256, ap=[[1, 128], [256, 8], [1, 128]])
        P.dma("sp", hank, src, reads=["fscr"], writes=["hank"])
        for hh in range(0, 8, 4):
            bk, bkr = P.bank()

            def mmj(e, bk=bk, hh=hh):
                ins = None
                for q in range(4):
                    ins = e.matmul(bk[:, q * 128:(q + 1) * 128], lhsT=hank[:, hh + q, :], rhs=jx_f[:], start=True, stop=True)
                return ins
            P.op("pe", mmj, reads=["hank", "jx"], writes=[bkr])
            for q in range(4):
                h = hh + q
                c_, half = h // 2, h % 2
                j, cl = c_ // 2, c_ % 2
                P.op("act", lambda e, bk=bk, q=q, j=j, half=half, kb=kb, cl=cl:
                     e.activation(out=E[:, j, half, kb, cl, :], in_=bk[:, q * 128:(q + 1) * 128], func=AF.Exp),
                     reads=[bkr], writes=["E"])
    P.op("act", lambda e: e.activation(out=sink_bc[:], in_=sink_bc[:], func=AF.Exp), reads=["sink_bc"], writes=["sink_bc"])
    for h in range(8):
        c_, half = h // 2, h % 2
        j, cl = c_ // 2, c_ % 2
        P.op("dve", lambda e, h=h, j=j, half=half, cl=cl: e.tensor_scalar(out=sinkexp[:, j, half, cl, :], in0=zeros_f[:], scalar1=sink_bc[:, h:h + 1],
                                                                         scalar2=None, op0=ALU.add), reads=["sink_bc", "zeros"], writes=["sinkexp"])

    wstate = {"n": 0}

    def wload(parts, fp32=False):
        s = wstate["n"] % NSLOT
        wstate["n"] += 1
        res = ("w", s)
        for (c0, (wn, r0, nrows, cc0, ncols_), nk, ncols) in parts:
            src = (wf if fp32 else wb)[wn][r0:r0 + nrows, cc0:cc0 + ncols_].rearrange("(k p) n -> p k n", p=128)
            q_ = "pool" if (fp32 or wstate["n"] % 2 == 0) else "sp"
            P.dma(q_, ring[:, s, 0:nk, c0:c0 + ncols], src, reads=([] if fp32 else [("wbf", wn)]), writes=[res])
        return s, res

    def wsrc(w, r0, nrows, c0, ncols):
        return (w, r0, nrows, c0, ncols)

    def front(src_fn, nb, gT):
        for blk in range(nb):
            P.dma("sp", X(blk), src_fn(blk), writes=[xr(blk)])
            norm_block(blk, gT)

    def norm_block(blk, gT):
        p = CUR["p"]
        xblk = X(blk); xres = xr(blk); ares = ar(blk)
        nb_ = nbfs2[p]; ss = ss_c2[:, p:p + 1]; rs = rs_c2[:, p:p + 1]
        P.op("act", lambda e: e.activation(out=nb_[:], in_=xblk, func=AF.Square, accum_out=ss),
             reads=[xres], writes=[("nbf", p), ("ss_c", p)])
        P.op("act", lambda e: e.activation(out=rs, in_=ss, func=AF.Ln, scale=1.0 / D, bias=eps_col[:, 0:1]),
             reads=[("ss_c", p), "eps"], writes=[("rs_c", p)])
        P.op("act", lambda e: e.activation(out=rs, in_=rs, func=AF.Exp, scale=-0.5), reads=[("rs_c", p)], writes=[("rs_c", p)])
        P.op("dve", lambda e: e.tensor_scalar(out=nb_[:], in0=xblk, scalar1=rs, scalar2=None, op0=ALU.mult),
             reads=[xres, ("rs_c", p)], writes=[("nbf", p)])
        tb, tbr = tbank()

        def tr(e):
            ins = None
            for k in range(8):
                ins = e.transpose(out=tb[:, k * 128:(k + 1) * 128], in_=nb_[:, k * 128:(k + 1) * 128], identity=ident_bf[:])
            return ins
        P.op("pe", tr, reads=[("nbf", p), "ident_bf"], writes=[tbr])
        if p == 0:
            P.op("dve", lambda e: e.tensor_tensor(out=actT[:, :, blk * 128:(blk + 1) * 128], in0=tb[:].rearrange("p (k t) -> p k t", k=8),
                                                  in1=fap(gT[:], [[1, 8], [0, 128]]), op=ALU.mult),
                 reads=[tbr, "gaT", "gfT"], writes=[ares])
        else:
            for kh, src in enumerate((_kwf, _kwswf)):
                P.op("dve", lambda e, kh=kh, src=src: e.tensor_tensor(
                    out=src.rearrange("p (k t) -> p k t", k=4)[:, :, blk * 128:(blk + 1) * 128],
                    in0=tb[:, kh * 512:(kh + 1) * 512].rearrange("p (k t) -> p k t", k=4),
                    in1=fap(gT[:, kh * 4:kh * 4 + 1], [[1, 4], [0, 128]]), op=ALU.mult),
                    reads=[tbr, "gaT", "gfT", ares], writes=[ares])

    def actT_reads(nb):
        return [ar(b) for b in range(nb)]

    def proj_fm(slot, res, col0, T, nb):
        bk, bkr = P.bank()
        acts = [A(k) for k in range(8)]

        def mm(e):
            ins = None
            for k in range(8):
                ins = e.matmul(bk[:, 0:T], lhsT=ring[:, slot, k, col0:col0 + 128], rhs=acts[k][:, 0:T], start=(k == 0), stop=(k == 7))
            return ins
        P.op("pe", mm, reads=[res] + actT_reads(nb), writes=[bkr])
        return bk, bkr

    def proj_tm(slot, res, col0, ncols, blk):
        bk, bkr = P.bank()
        acts = [A(k) for k in range(8)]

        def mm(e):
            ins = None
            for k in range(8):
                ins = e.matmul(bk[:, 0:ncols], lhsT=acts[k][:, blk * 128:(blk + 1) * 128], rhs=ring[:, slot, k, col0:col0 + ncols],
                               start=(k == 0), stop=(k == 7))
            return ins
        P.op("pe", mm, reads=[res, ar(blk)], writes=[bkr])
        return bk, bkr

    def rstd_fm(src_ap, T, lhs_ones, scale, reads_src):
        P.op("act", lambda e: e.activation(out=sq_bf[:, 0:T], in_=src_ap, func=AF.Square), reads=reads_src, writes=["sq"])
        b2, b2r = P.bank()
        P.op("pe", lambda e: e.matmul(b2[:, 0:T], lhsT=lhs_ones[:], rhs=sq_bf[:, 0:T], start=True, stop=True),
             reads=["sq", "bd", "ones"], writes=[b2r])
        P.op("act", lambda e: e.activation(out=rstd_f[:, 0:T], in_=b2[:, 0:T], func=AF.Ln, scale=scale, bias=eps_col[:, 0:1]),
             reads=[b2r, "eps"], writes=["rstd"])
        P.op("act", lambda e: e.activation(out=rstd_f[:, 0:T], in_=rstd_f[:, 0:T], func=AF.Exp, scale=-0.5), reads=["rstd"], writes=["rstd"])

    def gla_prep(slot_lr, res_lr, lrcol, nb, T, tri_ap, tri_res, full):
        bk, bkr = proj_fm(slot_lr, res_lr, lrcol, T, nb)
        P.op("act", lambda e: e.activation(out=ulrT[:, 0:T], in_=bk[:, 0:T], func=AF.Copy), reads=[bkr], writes=["ulrT"])
        bT = [P.bank(hold=True) for _ in range(2)]
        for blk in range(nb):
            zb, zbr = P.bank()
            P.op("pe", lambda e, zb=zb, blk=blk: e.matmul(zb[:, 0:256], lhsT=ulrT[:, blk * 128:(blk + 1) * 128], rhs=w2pad[:], start=True, stop=True),
                 reads=["ulrT", "w2pad"], writes=[zbr])
            P.op("dve", lambda e, zb=zb: e.tensor_tensor(out=zt[:], in0=zb[:, 0:256], in1=bgate_bc[:], op=ALU.add),
                 reads=[zbr, "bgate"], writes=["zt"])
            P.op("act", lambda e: e.activation(out=zt[:], in_=zt[:], func=AF.Exp, scale=-1.0), reads=["zt"], writes=["zt"])
            P.op("act", lambda e: e.activation(out=sp_t[:], in_=zt[:], func=AF.Ln, bias=1.0), reads=["zt"], writes=["sp_t"])
            for c in range(2):
                P.op("pe", lambda e, c=c, blk=blk: e.matmul(bT[c][0][:, blk * 128:(blk + 1) * 128], lhsT=sp_t[:, c * 128:(c + 1) * 128], rhs=tri_ap,
                                                           start=True, stop=True), reads=["sp_t", tri_res], writes=[bT[c][1]])
        for c in range(2):
            P.release(bT[c][1])
        for c in range(2):
            P.op("act", lambda e, c=c: e.activation(out=enb[:, c, 0:T], in_=bT[c][0][:, 0:T], func=AF.Exp, scale=-1.0), reads=[bT[c][1]], writes=["enb"])
            P.op("act", lambda e, c=c: e.activation(out=elast[:, c, 0:nb], in_=fap(bT[c][0][:, 127:128], [[128, nb]]), func=AF.Exp),
                 reads=[bT[c][1]], writes=["elast"])
            if full:
                P.op("act", lambda e, c=c: e.activation(out=ebq[:, c, 0:T], in_=bT[c][0][:, 0:T], func=AF.Exp, bias=ln8_col[:, 0:1]),
                     reads=[bT[c][1], "ln8"], writes=["ebq"])
                if T == 128:
                    P.op("act", lambda e, c=c: e.activation(out=elb[:, c, :], in_=bT[c][0][:, 0:128], func=AF.Exp), reads=[bT[c][1]], writes=["elb"])

    def kg_evac(slot, res, col0, nb, T, sample=False):
        for c in range(2):
            bk, bkr = proj_fm(slot, res, col0 + c * 128, T, nb)
            P.op("dve", lambda e, bk=bk, c=c: e.tensor_tensor(out=kgA[0:64, c, 0:T], in0=bk[0:64, 0:T], in1=enb[0:64, c, 0:T], op=ALU.mult),
                 reads=[bkr, "enb", "zinit"], writes=[("kgA", c)])
            P.op("dve", lambda e, bk=bk, c=c: e.tensor_tensor(out=kgB[64:128, c, 0:T], in0=bk[64:128, 0:T], in1=enb[64:128, c, 0:T], op=ALU.mult),
                 reads=[bkr, "enb", "zinit"], writes=[("kgB", c)])
            if sample:
                P.op("dve", lambda e, bk=bk, c=c: e.tensor_tensor(out=kgf[:, c, :], in0=bk[:, 0:128], in1=enb[:, c, 0:128], op=ALU.mult),
                     reads=[bkr, "enb"], writes=["kgf"])

    def vg_tm(slot, res, col0, nb):
        for blk in range(nb):
            bk, bkr = proj_tm(slot, res, col0, 512, blk)
            P.op("act", lambda e, bk=bk, blk=blk: e.activation(out=vg_tok[:, blk, :], in_=bk[:, 0:512], func=AF.Copy), reads=[bkr], writes=[("vg", blk)])

    def state_update(blk, masked):
        tb, tbr = tbank()

        def tr(e):
            ins = None
            for c in range(2):
                for ab, src in enumerate((kgA, kgB)):
                    o = (c * 2 + ab) * 128
                    ins = e.transpose(out=tb[:, o:o + 128], in_=src[:, c, blk * 128:(blk + 1) * 128], identity=ident_bf[:])
            return ins
        P.op("pe", tr, reads=[("kgA", 0), ("kgA", 1), ("kgB", 0), ("kgB", 1), "ident_bf"], writes=[tbr])
        P.op("act", lambda e: e.activation(out=ktok[:].rearrange("p c a f -> p (c a f)"), in_=tb[:, 0:512], func=AF.Copy), reads=[tbr], writes=["ktok"])
        ub, ubr = P.bank()

        def mm(e):
            ins = None
            for c in range(2):
                for ab in range(2):
                    h = 2 * c + ab
                    ins = e.matmul(ub[:, c * 128:(c + 1) * 128], lhsT=ktok[:, c, ab, :], rhs=vg_tok[:, blk, h * 128:(h + 1) * 128],
                                   start=(ab == 0), stop=(ab == 1))
            return ins
        P.op("pe", mm, reads=["ktok", ("vg", blk)], writes=[ubr])
        P.op("dve", lambda e: e.tensor_tensor(out=Stmp[:].rearrange("p c v -> p (c v)"), in0=ub[:, 0:256], in1=S[:].rearrange("p c v -> p (c v)"), op=ALU.add),
             reads=[ubr, "S"], writes=["Stmp"])
        P.op("dve", lambda e: e.tensor_tensor(out=S[:], in0=Stmp[:], in1=fap(elast[:, 0, blk:blk + 1], [[4, 2], [0, 128]]), op=ALU.mult),
             reads=["Stmp", "elast", "SA", "SB"], writes=["S"])
        if masked:
            P.op("act", lambda e: e.activation(out=SA[0:64], in_=S[0:64], func=AF.Copy), reads=["S", "zinit"], writes=["SA"])
            P.op("act", lambda e: e.activation(out=SB[64:128], in_=S[64:128], func=AF.Copy), reads=["S", "zinit"], writes=["SB"])

    def gla_out_norm(ob, obr, ncol, W, col0):
        rstd_fm(ob[:, 0:ncol], ncol, ones_bf, 1.0 / 128, [obr])
        P.op("dve", lambda e: e.scalar_tensor_tensor(out=tmp_f[:, 0:ncol], in0=ob[:, 0:ncol], scalar=glag_col[:, 0:1], in1=rstd_f[:, 0:ncol],
                                                     op0=ALU.mult, op1=ALU.mult), reads=[obr, "rstd", "glag"], writes=["tmp_f"])
        P.op("dve", lambda e: e.tensor_tensor(out=mixT[:, 4:8, col0:col0 + W], in0=tmp_f[:, 0:ncol].rearrange("p (h w) -> p h w", h=4),
                                              in1=rsT[:, :, col0:col0 + W], op=ALU.mult), reads=["tmp_f", "rsT"], writes=[("mixg", col0)])

    def gla_block(blk):
        ab_, abr = P.bank()

        def mm1(e):
            ins = None
            for h in range(4):
                c, half = h // 2, h % 2
                src = kgA if half == 0 else kgB
                ins = e.matmul(ab_[:, h * 128:(h + 1) * 128], lhsT=src[:, c, blk * 128:(blk + 1) * 128], rhs=qgT[:, c, blk * 128:(blk + 1) * 128],
                               start=True, stop=True)
            return ins
        P.op("pe", mm1, reads=[("kgA", 0), ("kgA", 1), ("kgB", 0), ("kgB", 1), "qgT"], writes=[abr])
        P.op("dve", lambda e: e.tensor_tensor(out=ATbf[:], in0=ab_[:, 0:512].rearrange("p (h t) -> p h t", h=4), in1=fap(cmask_bf[:], [[0, 4], [1, 128]]),
                                              op=ALU.mult), reads=[abr, "cmask"], writes=["ATbf"])
        ob, obr = P.bank()

        def mm2(e):
            ins = None
            for h in range(4):
                c, half = h // 2, h % 2
                sm = SA if half == 0 else SB
                e.matmul(ob[:, h * 128:(h + 1) * 128], lhsT=vg_tok[:, blk, h * 128:(h + 1) * 128], rhs=ATbf[:, h, :], start=True, stop=False)
                ins = e.matmul(ob[:, h * 128:(h + 1) * 128], lhsT=sm[:, c, :], rhs=qgT[:, c, blk * 128:(blk + 1) * 128], start=False, stop=True)
            return ins
        P.op("pe", mm2, reads=[("vg", blk), "ATbf", "SA", "SB", "qgT"], writes=[obr])
        gla_out_norm(ob, obr, 512, 128, blk * 128)
        state_update(blk, True)

    def attn_block(blk, Etab, Eres, useflag=False):
        for j in range(2):
            sbk = []
            for half in range(2):
                bk, bkr = P.bank()
                sbk.append((bk, bkr))

                def mm(e, bk=bk, half=half, j=j):
                    ins = None
                    for kb in range(2):
                        kc = (blk + kb) * 128
                        ins = e.matmul(bk[:, kb * 256:(kb + 1) * 256], lhsT=kX[:, 2 * j + half, kc:kc + 128],
                                       rhs=qhT[:, 2 * j:2 * j + 2, blk * 128:(blk + 1) * 128], start=True, stop=True)
                    return ins
                P.op("pe", mm, reads=["kX", "qhT"], writes=[bkr])
            for half in range(2):
                bk, bkr = sbk[half]
                P.op("act", lambda e, bk=bk: e.activation(out=pe_f[:], in_=bk[:, 0:512], func=AF.Exp), reads=[bkr], writes=["pe_f"])
                P.op("dve", lambda e, half=half, j=j: e.tensor_tensor(out=PT[:, j, half].rearrange("p a b q -> p (a b q)"), in0=pe_f[:],
                                                                     in1=Etab[:, j, half].rearrange("p a b q -> p (a b q)"), op=ALU.mult),
                     reads=["pe_f", Eres], writes=[("PT", j)])
                if useflag:
                    P.op("dve", lambda e, half=half, j=j: e.tensor_scalar(out=PT[:, j, half, 0], in0=PT[:, j, half, 0], scalar1=flag_col[:, 0:1],
                                                                          scalar2=None, op0=ALU.mult), reads=[("PT", j), "flag"], writes=[("PT", j)])
            ob, obr = P.bank()
            db, dbr = P.bank()

            def mmv(e, ob=ob, db=db, j=j):
                ins = None
                for kb in range(2):
                    rhs = PT[:, j, :, kb, :, :]
                    e.matmul(ob[:, 0:512], lhsT=Vdup[:, blk + kb, j, :], rhs=rhs, start=(kb == 0), stop=(kb == 1))
                for kb in range(2):
                    rhs = PT[:, j, :, kb, :, :]
                    ins = e.matmul(db[:, 0:512], lhsT=ones_bf[:], rhs=rhs, start=(kb == 0), stop=(kb == 1))
                return ins
            P.op("pe", mmv, reads=[("PT", j), "Vdup", "ones"], writes=[obr, dbr])
            P.op("dve", lambda e, db=db, j=j: e.tensor_tensor(out=rec_f[:], in0=db[:, 0:512], in1=sinkexp[:, j].rearrange("p a b q -> p (a b q)"), op=ALU.add),
                 reads=[dbr, "sinkexp"], writes=["rec"])
            P.op("act", lambda e: e.activation(out=rec_f[:], in_=rec_f[:], func=AF.Ln), reads=["rec"], writes=["rec"])
            P.op("act", lambda e: e.activation(out=rec_f[:], in_=rec_f[:], func=AF.Exp, scale=-1.0), reads=["rec"], writes=["rec"])
            for half in range(2):
                r0 = half * 64
                P.op("dve", lambda e, ob=ob, half=half, r0=r0, j=j: e.tensor_tensor(
                    out=mixT[r0:r0 + 64, 2 * j:2 * j + 2, blk * 128:(blk + 1) * 128],
                    in0=ob[r0:r0 + 64, half * 256:(half + 1) * 256].rearrange("p (c q) -> p c q", c=2),
                    in1=rec_f[r0:r0 + 64, half * 256:(half + 1) * 256].rearrange("p (c q) -> p c q", c=2), op=ALU.mult),
                    reads=[obr, "rec"], writes=[("mixa", blk)])

    def qk_norm_chunk(bk, bkr, T, gcol, gres, out_fn):
        rstd_fm(bk[:, 0:T], T, bd_bf, 1.0 / 64, [bkr])
        out_fn(bk, bkr)

    def wo_ffnnorm(nb, s0, r0, s1, r1):
        for blk in range(nb):
            for cg, (s, r) in enumerate(((s0, r0), (s1, r1))):
                bk, bkr = P.bank()

                def mm(e, bk=bk, s=s, blk=blk):
                    ins = None
                    for k in range(8):
                        ins = e.matmul(bk[:, 0:512], lhsT=mixT[:, k, blk * 128:(blk + 1) * 128], rhs=ring[:, s, k, 0:512], start=(k == 0), stop=(k == 7))
                    return ins
                P.op("pe", mm, reads=[r, ("mixa", blk), ("mixg", blk * 128)], writes=[bkr])
                xs_ = X(blk)[:, cg * 512:(cg + 1) * 512]
                P.op("dve", lambda e, bk=bk, xs_=xs_: e.tensor_tensor(out=xs_, in0=bk[:, 0:512], in1=xs_, op=ALU.add), reads=[bkr, xr(blk)], writes=[xr(blk)])
            norm_block(blk, gfT)

    def ffn(nb, T, ydst_fn):
        for s6 in range(6):
            ncols = 512 if s6 < 5 else 256
            sg_, rg_ = wload([(0, wsrc("w_gate", 0, D, s6 * 512, ncols), 8, ncols)])
            su_, ru_ = wload([(0, wsrc("w_up", 0, D, s6 * 512, ncols), 8, ncols)])
            for mi in range(ncols // 128):
                m = s6 * 4 + mi
                gb, gbr = proj_fm(sg_, rg_, mi * 128, T, nb)
                ubk, ubr = proj_fm(su_, ru_, mi * 128, T, nb)
                P.op("act", lambda e, gb=gb: e.activation(out=sg_f[:, 0:T], in_=gb[:, 0:T], func=AF.Silu), reads=[gbr], writes=["sg_f"])
                P.op("dve", lambda e, ubk=ubk, m=m: e.tensor_tensor(out=aT[:, m, 0:T], in0=ubk[:, 0:T], in1=sg_f[:, 0:T], op=ALU.mult),
                     reads=[ubr, "sg_f"], writes=[("aT", m)])
        for cg in range(2):
            bks = [P.bank(hold=True) for _ in range(nb)]
            for kgp in range(3):
                nk = 8 if kgp < 2 else 6
                sl = wload([(0, wsrc("w_down", kgp * 1024, nk * 128, cg * 512, 512), nk, 512)])
                for blk in range(nb):
                    bk, bkr = bks[blk]

                    def mm(e, bk=bk, blk=blk, sl=sl, kgp=kgp, nk=nk):
                        ins = None
                        for kk in range(nk):
                            k = kgp * 8 + kk
                            ins = e.matmul(bk[:, 0:512], lhsT=aT[:, k, blk * 128:(blk + 1) * 128], rhs=ring[:, sl[0], kk, 0:512], start=(k == 0), stop=(k == NKF - 1))
                        return ins
                    P.op("pe", mm, reads=[sl[1]] + [("aT", kgp * 8 + kk) for kk in range(nk)], writes=[bkr])
            for blk in range(nb):
                bk, bkr = bks[blk]
                P.release(bkr)
                xs_ = X(blk)[:, cg * 512:(cg + 1) * 512]
                P.op("dve", lambda e, bk=bk, xs_=xs_: e.tensor_tensor(out=xs_, in0=bk[:, 0:512], in1=xs_, op=ALU.add), reads=[bkr, xr(blk)], writes=[xr(blk)])
                if cg == 1:
                    outs.append(P.dma("sp", ydst_fn(blk), X(blk), reads=[xr(blk)]))

    outs = []

    wstate["n"] = 0
    s_a, r_a = wload([(0, wsrc("w_in", 0, D, 1024, 256), 8, 256), (256, wsrc("w_in", 0, D, 2192, 128), 8, 128)], fp32=True)
    s_b, r_b = wload([(0, wsrc("w_in", 0, D, 1280, 512), 8, 512)], fp32=True)
    for k in range(8):
        P.op("dve", lambda e, k=k: e.tensor_scalar(out=ring[:, s_a, k, 0:384], in0=ring[:, s_a, k, 0:384], scalar1=gaT[:, k:k + 1], scalar2=None, op0=ALU.mult),
             reads=[r_a, "gaT"], writes=[r_a])
        P.op("act", lambda e, k=k: e.activation(out=ring[:, s_b, k, 0:512], in_=ring[:, s_b, k, 0:512], func=AF.Copy, scale=gaT[:, k:k + 1]),
             reads=[r_b, "gaT"], writes=[r_b])
    negs = P.sb([128, 2], BF16)
    P.op("dve", lambda e: e.memset(negs[:], -1.0 / 16), writes=["negs"])
    tri_bf = P.sb([128, 128], BF16)
    P.op("dve", lambda e: e.tensor_copy(out=tri_bf[:], in_=tri_f[:]), reads=["tri"], writes=["tri_bf"])
    sp_h = (P.sb([128, 256], BF16), P.sb([128, 256], BF16))
    _enbflat = enb[:].rearrange("p c t -> p (c t)")
    _qgflat = qgT[:].rearrange("p c t -> p (c t)")
    for wn in ("w_in", "w_o", "w_gate", "w_up", "w_down"):
        nr = wf[wn].shape[0]
        step = 256
        for r0 in range(0, nr, step):
            r1 = min(nr, r0 + step)
            ci = P.dma("pool", wb[wn][r0:r1, :], wf[wn][r0:r1, :], reads=["wbfchain"] + ([r_a, r_b] if (wn == "w_in" and r0 == 0) else []),
                       writes=[("wbf", wn), "wbfchain"])
            P.nofence = getattr(P, "nofence", set()) | {ci}
    NBLK = (16 if debug == "scan" else 0) if debug else NPRE // 128
    nbfs = (nbf, nbf_b); zts = (zt, zt_b); sps = (sp_t, sp_b); ktoks = (ktok, ktok_b)
    sbanks = {}

    def sc1(b):
        q = b % 4; p = b % 2
        P.dma("sp", xb[:, q, :], xpre[b * 128:(b + 1) * 128, :], writes=[("sx", q)])
        P.op("act", lambda e: e.activation(out=nbfs[p][:], in_=xb[:, q, :], func=AF.Square, accum_out=ss_c2[:, p:p + 1]),
             reads=[("sx", q)], writes=[("snbf", p), ("sss", p)])
        P.op("act", lambda e: e.activation(out=rs_c2[:, p:p + 1], in_=ss_c2[:, p:p + 1], func=AF.Ln, scale=1.0 / D, bias=eps_col[:, 0:1]),
             reads=[("sss", p), "eps"], writes=[("srs", p)])
        P.op("act", lambda e: e.activation(out=rs_c2[:, p:p + 1], in_=rs_c2[:, p:p + 1], func=AF.Exp, scale=-0.5), reads=[("srs", p)], writes=[("srs", p)])
        P.op("dve", lambda e: e.tensor_scalar(out=nbfs[p][:], in0=xb[:, q, :], scalar1=rs_c2[:, p:p + 1], scalar2=None, op0=ALU.mult),
             reads=[("sx", q), ("srs", p)], writes=[("snbf", p)])
        tb, tbr = tbank()

        def tr(e):
            ins = None
            for k in range(8):
                ins = e.transpose(out=tb[:, k * 128:(k + 1) * 128], in_=nbfs[p][:, k * 128:(k + 1) * 128], identity=ident_bf[:])
            return ins
        P.op("pe", tr, reads=[("snbf", p), "ident_bf"], writes=[tbr])
        if b % 2 == 0:
            P.op("act", lambda e: e.activation(out=actT[:, :, q * 128:(q + 1) * 128], in_=tb[:].rearrange("p (k t) -> p k t", k=8), func=AF.Copy),
                 reads=[tbr], writes=[("sact", q)])
        else:
            P.op("dve", lambda e: e.tensor_copy(out=actT[:, :, q * 128:(q + 1) * 128], in_=tb[:].rearrange("p (k t) -> p k t", k=8)),
                 reads=[tbr], writes=[("sact", q)])

    def sc2(b):
        q = b % 4
        ab, abr = P.bank()
        vb, vbr = P.bank()

        def mm(e):
            ins = None
            for k in range(8):
                ins = e.matmul(ab[:, 0:128], lhsT=ring[:, s_a, k, 256:384], rhs=actT[:, k, q * 128:(q + 1) * 128], start=(k == 0), stop=(k == 7))
            for k in range(8):
                ins = e.matmul(vb[:, 0:512], lhsT=actT[:, k, q * 128:(q + 1) * 128], rhs=ring[:, s_b, k, 0:512], start=(k == 0), stop=(k == 7))
            return ins
        P.op("pe", mm, reads=[r_a, r_b, ("sact", q)], writes=[abr, vbr])
        P.op("act", lambda e: e.activation(out=ulrT[:, q * 128:(q + 1) * 128], in_=ab[:, 0:128], func=AF.Copy), reads=[abr], writes=[("sulr", q)])
        P.op("act", lambda e: e.activation(out=vg_tok[:, q, :], in_=vb[:, 0:512], func=AF.Copy), reads=[vbr], writes=[("svg", q)])

    def sc3a(b):
        q = b % 4; p = b % 2
        zb, zbr = P.bank()
        sbanks[b] = (zb, zbr)
        P.op("pe", lambda e: e.matmul(zb[:, 0:256], lhsT=ulrT[:, q * 128:(q + 1) * 128], rhs=w2pad[:], start=True, stop=True),
             reads=[("sulr", q), "w2pad"], writes=[zbr])
        P.op("dve", lambda e: e.tensor_tensor(out=zts[p][:], in0=zb[:, 0:256], in1=bgate_bc[:], op=ALU.add), reads=[zbr, "bgate"], writes=[("szt", p)])
        P.op("act", lambda e: e.activation(out=zts[p][:], in_=zts[p][:], func=AF.Exp, scale=-1.0), reads=[("szt", p)], writes=[("szt", p)])
        P.op("act", lambda e: e.activation(out=sp_h[p][:], in_=zts[p][:], func=AF.Ln, bias=1.0), reads=[("szt", p)], writes=[("ssp", p)])

    def sc3b(b):
        q = b % 4; p = b % 2
        kb_, kbr = P.bank()
        cb, cbr = P.bank()
        en_ = _enbflat[:, q * 256:(q + 1) * 256]
        kt_ = _qgflat[:, q * 256:(q + 1) * 256]

        def mm(e):
            ins = None
            for k in range(8):
                ins = e.matmul(kb_[:, 0:256], lhsT=actT[:, k, q * 128:(q + 1) * 128], rhs=ring[:, s_a, k, 0:256], start=(k == 0), stop=(k == 7))
            return ins
        P.op("pe", mm, reads=[r_a, ("sact", q)], writes=[kbr])

        def mmc(e):
            e.matmul(cb[:, 0:256], lhsT=tri_bf[:], rhs=sp_h[p][:], start=True, stop=True)
            ins = None
            for c in range(2):
                ins = e.matmul(cb[:, 256 + 2 * c:258 + 2 * c], lhsT=sp_h[p][:, c * 128:(c + 1) * 128], rhs=negs[:], start=True, stop=True)
            return ins
        P.op("pe", mmc, reads=[("ssp", p), "tri_bf", "negs"], writes=[cbr])
        P.op("act", lambda e: e.activation(out=en_, in_=cb[:, 0:256], func=AF.Exp, scale=-1.0), reads=[cbr], writes=[("senb", q)])
        P.op("act", lambda e: e.activation(out=elast[:, :, q], in_=fap(cb[:, 256:257], [[2, 2]]), func=AF.Exp), reads=[cbr], writes=[("sel", q)])
        P.op("dve", lambda e: e.tensor_tensor(out=kt_, in0=kb_[:, 0:256], in1=en_, op=ALU.mult),
             reads=[kbr, ("senb", q)], writes=[("sktok", q)])

    def sc4(b):
        q = b % 4; p = b % 2
        kt = _qgflat[:, q * 256:(q + 1) * 256]
        ub, ubr = P.bank()

        def mm(e):
            ins = None
            for c in range(2):
                for ab in range(2):
                    h = 2 * c + ab
                    ins = e.matmul(ub[:, h * 128:(h + 1) * 128], lhsT=kt[:, c * 128:(c + 1) * 128], rhs=vg_tok[:, q, h * 128:(h + 1) * 128], start=True, stop=True)
            return ins
        P.op("pe", mm, reads=[("sktok", q), ("svg", q)], writes=[ubr])
        for ab in range(2):
            r0 = ab * 64
            P.op("dve", lambda e, ab=ab, r0=r0: e.tensor_tensor(out=Stmp[r0:r0 + 64], in0=fap(ub[r0:r0 + 64, ab * 128:ab * 128 + 1], [[256, 2], [1, 128]]),
                                                               in1=S[r0:r0 + 64], op=ALU.add), reads=[ubr, "S", "Stmp"], writes=["Stmp"])
        P.op("dve", lambda e: e.tensor_tensor(out=S[:], in0=Stmp[:], in1=fap(elast[:, 0, q:q + 1], [[4, 2], [0, 128]]), op=ALU.mult),
             reads=["Stmp", ("sel", q)], writes=["S"])

    stages = (sc1, sc2, sc3a, sc3b, sc4)
    for i in range(NBLK + len(stages) - 1):
        for si, st in enumerate(stages):
            b = i - si
            if 0 <= b < NBLK:
                st(b)
    P.barrier(keep=("wbf", "wbfchain"))
    P.op("act", lambda e: e.activation(out=SA[0:64], in_=S[0:64], func=AF.Copy), reads=["S", "zinit"], writes=["SA"])
    P.op("act", lambda e: e.activation(out=SB[64:128], in_=S[64:128], func=AF.Copy), reads=["S", "zinit"], writes=["SB"])

    def main_tile(kind, t):
        sample = kind == "sample"
        nb = 1 if sample else 4
        T = nb * 128
        CUR["p"] = 0 if sample else (t % 2)
        first = (kind == "prompt" and t == 0)
        last = (kind == "prompt" and t == 3)
        L0 = wload([(0, wsrc("w_in", 0, D, 0, 512), 8, 512)])
        L1 = wload([(0, wsrc("w_in", 0, D, 512, 512), 8, 512)])
        L2 = wload([(0, wsrc("w_in", 0, D, 1024, 256), 8, 256), (256, wsrc("w_in", 0, D, 2192, 128), 8, 128)])
        if first:
            front(lambda blk: xhalo, 1, gaT)
            halo_kv = True
            kv_part(L1, 1, 128, 0, False, False)
        if sample:
            front(lambda blk: xs, 1, gaT)
        else:
            front(lambda blk: xp[t * 512 + blk * 128: t * 512 + (blk + 1) * 128, :], 4, gaT)
        gla_prep(L2[0], L2[1], 256, nb, T, (tris_f if sample else tri_f)[:], "tris" if sample else "tri", True)
        for c in range(4):
            bk, bkr = proj_fm(L0[0], L0[1], c * 128, T, nb)
            rstd_fm(bk[:, 0:T], T, bd_bf, 1.0 / 64, [bkr])
            P.op("dve", lambda e, bk=bk, c=c: e.scalar_tensor_tensor(out=qhT[:, c, 0:T], in0=bk[:, 0:T], scalar=gq_col[:, 0:1], in1=rstd_f[:, 0:T],
                                                                    op0=ALU.mult, op1=ALU.mult), reads=[bkr, "rstd", "gq"], writes=["qhT"])
        kv_part(L1, nb, T, 1, last, sample)
        for c in range(2):
            bk, bkr = proj_fm(L1[0], L1[1], 256 + c * 128, T, nb)
            P.op("dve", lambda e, bk=bk, c=c: e.tensor_tensor(out=qgT[:, c, 0:T], in0=bk[:, 0:T], in1=ebq[:, c, 0:T], op=ALU.mult),
                 reads=[bkr, "ebq"], writes=["qgT"])
        kg_evac(L2[0], L2[1], 0, nb, T, sample)
        L3 = wload([(0, wsrc("w_in", 0, D, 1280, 512), 8, 512)])
        vg_tm(L3[0], L3[1], 0, nb)
        L4 = wload([(0, wsrc("w_in", 0, D, 1792, 512), 8, 512)])
        for c in range(4):
            bk, bkr = proj_fm(L4[0], L4[1], c * 128, T, nb)
            P.op("act", lambda e, bk=bk, c=c: e.activation(out=rsT[:, c, 0:T], in_=bk[:, 0:T], func=AF.Silu), reads=[bkr], writes=["rsT"])
        if sample:
            sample_attn()
            sample_gla()
        else:
            for blk in range(nb):
                attn_block(blk, E, "E", first and blk == 0)
                gla_block(blk)
            P.op("pool", lambda e: e.tensor_copy(out=kX[:, :, 0:128], in_=kX[:, :, 512:640]), reads=["kX"], writes=["kX"])
            P.op("pool", lambda e: e.tensor_copy(out=Vdup[:, 0], in_=Vdup[:, 4]), reads=["Vdup"], writes=["Vdup"])
        if debug:
            outs.append(P.dma("pool", dbg_mix, mixT[:], reads=[("mixa", b_) for b_ in range(nb)] + [("mixg", b_ * 128) for b_ in range(nb)]))
            outs.append(P.dma("pool", dbg_q, qhT[:], reads=["qhT"]))
            outs.append(P.dma("pool", dbg_rs, rsT[:], reads=["rsT"]))
        L5 = wload([(0, wsrc("w_o", 0, D, 0, 512), 8, 512)])
        L6 = wload([(0, wsrc("w_o", 0, D, 512, 512), 8, 512)])
        wo_ffnnorm(nb, L5[0], L5[1], L6[0], L6[1])
        if debug:
            outs.append(P.dma("sp", dbg_h, xb[:], reads=[("x", 0, b_) for b_ in range(nb)]))
            outs.append(P.dma("pool", dbg_z, actT[:], reads=[("actT", 0, b_) for b_ in range(nb)]))
        if sample:
            ffn(nb, T, lambda blk: y_s)
        else:
            ffn(nb, T, lambda blk: y_p[t * 512 + blk * 128: t * 512 + (blk + 1) * 128, :])

    def kv_part(L1, nb, T, kblk0, last, sample):
        bk, bkr = proj_fm(L1[0], L1[1], 0, T, nb)
        rstd_fm(bk[:, 0:T], T, bd_bf, 1.0 / 64, [bkr])
        c0 = kblk0 * 128
        P.op("dve", lambda e: e.scalar_tensor_tensor(out=khT_bf[:, 0:T], in0=bk[:, 0:T], scalar=gk_col[:, 0:1], in1=rstd_f[:, 0:T], op0=ALU.mult, op1=ALU.mult),
             reads=[bkr, "rstd", "gk"], writes=["khT"])
        if last or sample:
            lo = T - 128
            P.op("dve", lambda e: e.scalar_tensor_tensor(out=khT_f[:], in0=bk[:, lo:T], scalar=gk_col[:, 0:1], in1=rstd_f[:, lo:T], op0=ALU.mult, op1=ALU.mult),
                 reads=[bkr, "rstd", "gk"], writes=["khT_f"])
        P.op("act", lambda e: e.activation(out=kX[0:64, 0, c0:c0 + T], in_=khT_bf[0:64, 0:T], func=AF.Copy), reads=["khT", "kX"], writes=["kX"])
        P.op("act", lambda e: e.activation(out=kX[64:128, 3, c0:c0 + T], in_=khT_bf[64:128, 0:T], func=AF.Copy), reads=["khT", "kX"], writes=["kX"])
        b2, b2r = P.bank()
        P.op("pe", lambda e: e.matmul(b2[:, 0:T], lhsT=sw_bf[:], rhs=khT_bf[:, 0:T], start=True, stop=True), reads=["khT", "sw"], writes=[b2r])
        P.op("act", lambda e: e.activation(out=kX[0:64, 2, c0:c0 + T], in_=b2[0:64, 0:T], func=AF.Copy), reads=[b2r, "kX"], writes=["kX"])
        P.op("act", lambda e: e.activation(out=kX[64:128, 1, c0:c0 + T], in_=b2[64:128, 0:T], func=AF.Copy), reads=[b2r, "kX"], writes=["kX"])
        for blk in range(nb):
            vb, vbr = proj_tm(L1[0], L1[1], 128, 128, blk)
            P.op("act", lambda e, vb=vb, blk=blk: e.activation(out=Vdup[:, kblk0 + blk].rearrange("p j (u d) -> p j u d", u=2),
                                                               in_=fap(vb[:, 0:1], [[64, 2], [0, 2], [1, 64]]), func=AF.Copy),
                 reads=[vbr, "Vdup"], writes=["Vdup"])
            if (last and blk == nb - 1) or sample:
                P.op("dve", lambda e, vb=vb: e.tensor_copy(out=vw_f[:], in_=vb[:, 0:128]), reads=[vbr], writes=["vw_f"])
        if last or sample:
            kb_, kbr = P.bank()
            P.op("pe", lambda e: e.transpose(out=kb_[:, 0:128], in_=khT_f[:], identity=ident_f[:]), reads=["khT_f", "ident_f"], writes=[kbr])
            P.op("act", lambda e: e.activation(out=kw_f[:], in_=kb_[:, 0:128], func=AF.Copy), reads=[kbr], writes=["kw_f"])
        if last:
            outs.append(P.dma("sp", kwp, kw_f[:], reads=["kw_f"]))
            outs.append(P.dma("sp", vwp, vw_f[:], reads=["vw_f"]))

    def sample_attn():
        for (dst, src, new, nm) in ((kws, ck, kw_f, "kws"), (vws, cv, vw_f, "vws")):
            P.dma("sp", dst[:, 0:127, :], src[:, 1:128, :], writes=[nm])
            P.dma("sp", bass.AP(tensor=dst.tensor, offset=127 * 128, ap=[[128 * 128, 16], [1, 128]]), new[0:16, :], reads=["kw_f", "vw_f"], writes=[nm])
        t1 = P.dma("pool", Kw_bf[:], kws.rearrange("b k f -> k b f"), reads=["kws"], writes=["Kw"])
        for j in range(2):
            P.dma("pool", Kwsw_bf[:, :, (1 - j) * 64:(2 - j) * 64], kws[:, :, j * 64:(j + 1) * 64].rearrange("b k f -> k b f"), reads=["kws"], writes=["Kwsw"])
            for u in range(2):
                P.dma("pool", Vwd[:, :, j, u * 64:(u + 1) * 64], vws[:, :, j * 64:(j + 1) * 64].rearrange("b k f -> k b f"), reads=["vws"], writes=["Vwd"])
        outs.append(t1)
        P.op("dve", lambda e: e.tensor_copy(out=qsA[0:64], in_=qhT[0:64, :, 0:16]), reads=["qhT", "zinit"], writes=["qsA"])
        P.op("dve", lambda e: e.tensor_copy(out=qsB[64:128], in_=qhT[64:128, :, 0:16]), reads=["qhT", "zinit"], writes=["qsB"])
        sb_, sbr = P.bank()

        def a_tr(b):
            tb, tbr = tbank()
            bf_ = b % 2

            def tr(e, tb=tb, b=b):
                e.transpose(out=tb[:, 0:128], in_=Kw_bf[:, b, :], identity=ident_bf[:])
                return e.transpose(out=tb[:, 128:256], in_=Kwsw_bf[:, b, :], identity=ident_bf[:])
            P.op("pe", tr, reads=["Kw", "Kwsw", "ident_bf"], writes=[tbr])
            P.op("act", lambda e, tb=tb, bf_=bf_: e.activation(out=KTb[:, bf_].rearrange("p a k -> p (a k)"), in_=tb[:, 0:256], func=AF.Copy),
                 reads=[tbr], writes=[("KTb", bf_)])

        def a_mm(b):
            bf_ = b % 2

            def mm(e, b=b, bf_=bf_):
                ins = None
                for j in range(2):
                    for half in range(2):
                        kt = KTb[:, bf_, 0 if j == half else 1, :]
                        q = (qsA if half == 0 else qsB)[:, 2 * j:2 * j + 2, b]
                        o = b * 8 + (j * 2 + half) * 2
                        ins = e.matmul(sb_[:, o:o + 2], lhsT=kt, rhs=q, start=True, stop=True)
                return ins
            P.op("pe", mm, reads=[("KTb", bf_), "qsA", "qsB"], writes=[sbr])
        a_tr(0)
        for b in range(16):
            if b + 1 < 16:
                a_tr(b + 1)
            a_mm(b)
        P.op("act", lambda e: e.activation(out=pes[:].rearrange("p b h -> p (b h)"), in_=sb_[:, 0:128], func=AF.Exp), reads=[sbr], writes=["pes"])
        P.op("dve", lambda e: e.tensor_tensor(out=Pts[:], in0=pes[:], in1=fap(E[:, 0, 0, 1, 0, 127:128], [[0, 16], [512, 4], [128, 2]]), op=ALU.mult),
             reads=["pes", "E"], writes=["Pts"])
        ob, obr = P.bank()
        db, dbr = P.bank()

        def mmv(e):
            ins = None
            for b in range(16):
                for j in range(2):
                    e.matmul(ob[:, b * 8 + j * 4:b * 8 + j * 4 + 4], lhsT=Vwd[:, b, j, :], rhs=Pts[:, b, j * 4:(j + 1) * 4], start=True, stop=True)
                ins = e.matmul(db[:, b * 8:(b + 1) * 8], lhsT=ones_bf[:], rhs=Pts[:, b, :], start=True, stop=True)
            return ins
        P.op("pe", mmv, reads=["Pts", "Vwd", "ones"], writes=[obr, dbr])
        P.op("dve", lambda e: e.tensor_tensor(out=rec_f[:, 0:128].rearrange("p (b h) -> p b h", b=16), in0=db[:, 0:128].rearrange("p (b h) -> p b h", b=16),
                                              in1=fap(sinkexp[:, 0, 0, 0, 0:1], [[0, 16], [128, 8]]), op=ALU.add), reads=[dbr, "sinkexp"], writes=["rec"])
        P.op("act", lambda e: e.activation(out=rec_f[:, 0:128], in_=rec_f[:, 0:128], func=AF.Ln), reads=["rec"], writes=["rec"])
        P.op("act", lambda e: e.activation(out=rec_f[:, 0:128], in_=rec_f[:, 0:128], func=AF.Exp, scale=-1.0), reads=["rec"], writes=["rec"])
        for j in range(2):
            for half in range(2):
                r0 = half * 64
                o = (j * 2 + half) * 2
                P.op("dve", lambda e, r0=r0, o=o, j=j: e.tensor_tensor(
                    out=mixT[r0:r0 + 64, 2 * j:2 * j + 2, 0:16],
                    in0=fap(ob[r0:r0 + 64, o:o + 1], [[1, 2], [8, 16]]),
                    in1=fap(rec_f[r0:r0 + 64, o:o + 1], [[1, 2], [8, 16]]), op=ALU.mult), reads=[obr, "rec"], writes=[("mixa", 0)])

    def sample_gla():
        ob, obr = P.bank(hold=True)
        vbs = {}

        def g_a(b):
            bf_ = b % 2
            P.dma("sp", Sb[:, bf_], sg[b].rearrange("(c u) d v -> (u d) c v", u=2), writes=[("Sb", bf_)])
            vb, vbr = P.bank()
            vbs[b] = (vb, vbr)
            P.op("pe", lambda e, vb=vb, b=b: e.matmul(vb[:, 0:512], lhsT=sel_bf[:, b, :], rhs=vg_tok[:, 0, :], start=True, stop=True),
                 reads=["sel", ("vg", 0)], writes=[vbr])

        def g_b(b):
            bf_ = b % 2
            vb, vbr = vbs.pop(b)
            for c in range(2):
                for half in range(2):
                    r0 = half * 64
                    h = 2 * c + half
                    P.op("dve", lambda e, vb=vb, b=b, c=c, r0=r0, h=h, bf_=bf_: e.scalar_tensor_tensor(
                        out=Wt[r0:r0 + 64, bf_, c, :], in0=vb[r0:r0 + 64, h * 128:(h + 1) * 128], scalar=kgf[r0:r0 + 64, c, b:b + 1],
                        in1=Sb[r0:r0 + 64, bf_, c, :], op0=ALU.mult, op1=ALU.add), reads=[vbr, "kgf", ("Sb", bf_)], writes=[("Wt", bf_)])
            P.op("act", lambda e, bf_=bf_: e.activation(out=WA[0:64, bf_], in_=Wt[0:64, bf_], func=AF.Copy), reads=[("Wt", bf_), "zinit"], writes=[("WA", bf_)])
            P.op("act", lambda e, bf_=bf_: e.activation(out=WB[64:128, bf_], in_=Wt[64:128, bf_], func=AF.Copy), reads=[("Wt", bf_), "zinit"], writes=[("WB", bf_)])
            P.op("dve", lambda e, b=b, bf_=bf_: e.tensor_tensor(out=Sn[:, bf_], in0=Wt[:, bf_], in1=fap(elb[:, 0, b:b + 1], [[128, 2], [0, 128]]), op=ALU.mult),
                 reads=[("Wt", bf_), "elb"], writes=[("Sn", bf_)])
            outs.append(P.dma("sp", gss[b].rearrange("(c u) d v -> (u d) c v", u=2), Sn[:, bf_], reads=[("Sn", bf_)]))

        def g_c(b):
            bf_ = b % 2

            def mm(e, b=b, bf_=bf_):
                ins = None
                for h in range(4):
                    c, half = h // 2, h % 2
                    w = (WA if half == 0 else WB)[:, bf_, c, :]
                    ins = e.matmul(ob[:, h * 16 + b:h * 16 + b + 1], lhsT=w, rhs=qgT[:, c, b:b + 1], start=True, stop=True)
                return ins
            P.op("pe", mm, reads=[("WA", bf_), ("WB", bf_), "qgT"], writes=[obr])
        g_a(0)
        for b in range(16):
            if b + 1 < 16:
                g_a(b + 1)
            g_b(b)
            g_c(b)
        P.release(obr)
        gla_out_norm(ob, obr, 64, 16, 0)

    import os as _os
    _kb = _os.environ.get("KBAR", "")
    for t in range((0 if debug in ("sample", "scan") else (2 if debug == "two" else (4 if debug == "four" else 1))) if debug else 4):
        main_tile("prompt", t)
        if "t" in _kb:
            P.barrier()
    if debug and debug != "sample":
        outs.append(P.dma("pool", dbg_a, aT[:], reads=[("aT", m) for m in range(NKF)]))
    outs.append(P.dma("sp", gsp.rearrange("(c u) d v -> (u d) c v", u=2), S[:], reads=["S"]))
    P.barrier()
    if (not debug) or debug == "sample":
        main_tile("sample", 0)

    P.emit()
    P.stack.close()
    return nc


_CACHE = {}


def kernel(x_prompt, x_sample, cache_k, cache_v, state_gla, attn_norm_g, w_in, q_norm_g, k_norm_g, attn_sinks,
           rel_bias, w_gla_gate2, b_gla_gate, gla_norm_g, w_o, ffn_norm_g, w_gate, w_up, w_down):
    f = lambda a: np.ascontiguousarray(np.asarray(a, dtype=np.float32))
    xpr = f(x_prompt)[0]
    xsm = f(x_sample)[:, 0, :]
    ckk = f(cache_k)[0].reshape(128, 128, 128)
    cvv = f(cache_v)[0].reshape(128, 128, 128)
    sgg = f(state_gla)[0]
    consts = host_consts()
    shared = dict(w_in=f(w_in)[0], w_o=f(w_o)[0], w_gate=f(w_gate)[0], w_up=f(w_up)[0], w_down=f(w_down)[0],
                  attn_g=f(attn_norm_g)[0], ffn_g=f(ffn_norm_g)[0], qng=f(q_norm_g)[0], kng=f(k_norm_g)[0],
                  sinks=f(attn_sinks)[0], relb=f(rel_bias), w2=f(w_gla_gate2)[0], bgate=f(b_gla_gate)[0], glag=f(gla_norm_g)[0])
    for k, v in consts.items():
        shared["c_" + k] = v
    in_maps = []
    for c in range(NCORE):
        m = dict(shared)
        m["xp"] = xpr[c * TOK:(c + 1) * TOK]
        m["xhalo"] = xpr[c * TOK - 128:c * TOK] if c > 0 else np.zeros((128, D), np.float32)
        pre = np.zeros((NPRE, D), np.float32)
        if c > 0:
            pre[NPRE - c * TOK:] = xpr[:c * TOK]
        m["xpre"] = pre
        xs_ = np.zeros((128, D), np.float32)
        xs_[:16] = xsm[c * 16:(c + 1) * 16]
        m["xs"] = xs_
        m["ck"] = ckk[c * 16:(c + 1) * 16]
        m["cv"] = cvv[c * 16:(c + 1) * 16]
        m["sg"] = sgg[c * 16:(c + 1) * 16]
        m["flag"] = np.full((128, 1), 1.0 if c > 0 else 0.0, np.float32)
        in_maps.append(m)
    if "nc" not in _CACHE:
        _CACHE["nc"] = build_program()
    res = run_bass_kernel_spmd(_CACHE["nc"], in_maps, core_ids=list(range(NCORE)))
    R = res.results
    y_prompt = np.concatenate([R[c]["y_p"] for c in range(NCORE)], axis=0)[None]
    y_sample = np.concatenate([R[c]["y_s"][:16] for c in range(NCORE)], axis=0)[:, None, :]
    kwp = R[7]["kwp"].reshape(1, 1, 128, 2, 64)
    vwp = R[7]["vwp"].reshape(1, 1, 128, 2, 64)
    gsp = R[7]["gsp"].reshape(1, 1, 4, 64, 128)
    kws = np.concatenate([R[c]["kws"] for c in range(NCORE)], axis=0).reshape(1, 128, 128, 2, 64)
    vws = np.concatenate([R[c]["vws"] for c in range(NCORE)], axis=0).reshape(1, 128, 128, 2, 64)
    gss = np.concatenate([R[c]["gss"] for c in range(NCORE)], axis=0).reshape(1, 128, 4, 64, 128)
    return (y_prompt.astype(np.float32), y_sample.astype(np.float32), kwp, vwp, gsp, kws, vws, gss)
```

```python
import contextlib
import math
import numpy as np
import concourse.bass as bass
import concourse.mybir as mybir
from concourse.bass_utils import run_bass_kernel_spmd

F32 = mybir.dt.float32
BF16 = mybir.dt.bfloat16
AF = mybir.ActivationFunctionType
ALU = mybir.AluOpType

NCORE = 8
D = 1024
TOK = 2048
NPRE = 7 * 2048
DFF = 2816
NKF = DFF // 128
INW = 2320
ENGS = ("pe", "act", "dve", "pool", "sp")
NDMASEM = 12
NSLOT = 4
EPS = 1e-6
MASKV = -30000.0


class _Ins:
    def then_inc(self, *a, **k):
        return self


class _Mock:
    def __init__(self):
        self.cost = 0.0

    def _free(self, ap):
        n = 1
        for d in list(ap.shape)[1:]:
            n *= int(d)
        return n

    def matmul(self, out, lhsT=None, rhs=None, **k):
        n = max(self._free(rhs), 64)
        self.cost += (n * (4 if rhs.dtype == F32 else 1)) / 2400.0 + 0.01
        return _Ins()

    def transpose(self, out=None, in_=None, identity=None, **k):
        self.cost += (128 * (4 if in_.dtype == F32 else 1)) / 2400.0 + 0.01
        return _Ins()

    def __getattr__(self, name):
        def f(*a, **k):
            o = k.get("out", a[0] if a else None)
            n = self._free(o) if o is not None else 64
            self.cost += 0.2 + n / 1000.0
            return _Ins()
        return f


class Prog:
    def __init__(self, nc):
        self.nc = nc
        self.stack = contextlib.ExitStack()
        self.oplist = []
        self.last_w = {}
        self.readers = {}
        self.base = None
        self.sems = {}
        self.nbuf = 0
        self.banks = []
        self.bank_rr = 0
        self.held = set()
        self.finals = []

    def sb(self, shape, dt, name=None):
        self.nbuf += 1
        return self.stack.enter_context(self.nc.sbuf_tensor(name or f"sb{self.nbuf}", list(shape), dt))

    def ps(self, shape, dt, name=None):
        self.nbuf += 1
        return self.stack.enter_context(self.nc.psum_tensor(name or f"ps{self.nbuf}", list(shape), dt))

    def bank(self, hold=False):
        while True:
            i = self.bank_rr % len(self.banks)
            self.bank_rr += 1
            if i not in self.held:
                break
        if hold:
            self.held.add(i)
        return self.banks[i], ("ps", i)

    def release(self, res):
        self.held.discard(res[1])

    def _sem(self, key):
        if key not in self.sems:
            nm = "s_" + "_".join(str(k) for k in (key if isinstance(key, tuple) else (key,)))
            self.sems[key] = self.stack.enter_context(self.nc.semaphore(nm))
        return self.sems[key]

    def _add(self, eng, fn, reads, writes, kind, cost, dma=None):
        deps = set()
        for r in reads:
            t = self.last_w.get(r, self.base)
            if t is not None:
                deps.add(t)
        for w in writes:
            t = self.last_w.get(w, self.base)
            if t is not None:
                deps.add(t)
            deps.update(self.readers.get(w, ()))
        if not reads and not writes and self.base is not None:
            deps.add(self.base)
        i = len(self.oplist)
        deps.discard(i)
        self.oplist.append(dict(eng=eng, fn=fn, deps=deps, kind=kind, cost=cost, dma=dma))
        for r in reads:
            self.readers.setdefault(r, []).append(i)
        for w in writes:
            self.last_w[w] = i
            self.readers[w] = []
        return i

    def op(self, eng, fn, reads=(), writes=()):
        m = _Mock()
        fn(m)
        return self._add(eng, fn, reads, writes, "c", m.cost)

    def dma(self, eng, out, in_, reads=(), writes=()):
        n = 1
        for d in out.shape:
            n *= int(d)
        nbytes = n * (4 if out.dtype == F32 else 2)
        return self._add(eng, None, reads, writes, "d", 2.0 + nbytes / 150e3, dma=(out, in_))

    def barrier(self, keep=()):
        kept = {k: v for k, v in self.last_w.items() if (k in keep or (isinstance(k, tuple) and k and k[0] in keep))}
        skip = set(getattr(self, "nofence", ()))
        allprev = set(range(len(self.oplist))) - skip
        i = len(self.oplist)
        self.oplist.append(dict(eng="sp", fn=None, deps=allprev, kind="n", cost=0.05, dma=None))
        self.base = i
        self.bars = getattr(self, "bars", []) + [i]
        self.last_w = dict(kept)
        self.readers = {}

    def final_wait(self, eng, toks):
        pass

    def schedule(self):
        import heapq, os
        ops = self.oplist
        n = len(ops)
        succ = [[] for _ in range(n)]
        ndep = [0] * n
        for i, o in enumerate(ops):
            ndep[i] = len(o["deps"])
            for d in o["deps"]:
                succ[d].append(i)
        done = [0.0] * n
        ready_t = [0.0] * n
        efree = {e: 0.0 for e in ENGS}
        waiting = {e: [] for e in ENGS}
        avail = {e: [] for e in ENGS}
        order = {e: [] for e in ENGS}
        for i, o in enumerate(ops):
            if ndep[i] == 0:
                heapq.heappush(waiting[o["eng"]], (0.0, i))
        nsched = 0
        while nsched < n:
            best = None
            for e in ENGS:
                w, a = waiting[e], avail[e]
                while w and w[0][0] <= efree[e]:
                    heapq.heappush(a, heapq.heappop(w)[1])
                if a:
                    cand = (efree[e], a[0], e, True)
                elif w:
                    cand = (w[0][0], w[0][1], e, False)
                else:
                    continue
                if best is None or cand[:2] < best[:2]:
                    best = cand
            st, i, e, from_avail = best
            if from_avail:
                heapq.heappop(avail[e])
            else:
                heapq.heappop(waiting[e])
            o = ops[i]
            if o["kind"] == "d":
                efree[e] = st + 0.15
                done[i] = st + o["cost"]
            else:
                efree[e] = st + o["cost"]
                done[i] = st + o["cost"] + 0.15
            order[e].append(i)
            nsched += 1
            for sidx in succ[i]:
                ndep[sidx] -= 1
                if done[i] > ready_t[sidx]:
                    ready_t[sidx] = done[i]
                if ndep[sidx] == 0:
                    heapq.heappush(waiting[ops[sidx]["eng"]], (ready_t[sidx], sidx))
        self.sim_time = max(done) if n else 0.0
        self.sim_done = done
        if os.environ.get("KSIM"):
            print("SIM total", round(self.sim_time), "barriers", [round(done[b]) for b in getattr(self, "bars", [])])
        return order

    def emit(self):
        nc = self.nc
        import os
        order = self.schedule()
        km = os.environ.get("KSCHED", "mid")
        if km == "0":
            order = {e: [i for i, o in enumerate(self.oplist) if o["eng"] == e] for e in ENGS}
        elif km == "mid" and len(getattr(self, "bars", [])) >= 2:
            B2 = self.bars[-1]
            prog = {e: [i for i, o in enumerate(self.oplist) if o["eng"] == e] for e in ENGS}
            order = {e: [i for i in order[e] if i <= B2] + [i for i in prog[e] if i > B2] for e in ENGS}
        elif km in ("pre", "post") and self.base is not None:
            B = self.base
            prog = {e: [i for i, o in enumerate(self.oplist) if o["eng"] == e] for e in ENGS}
            if km == "post":
                order = {e: [i for i in prog[e] if i <= B] + [i for i in order[e] if i > B] for e in ENGS}
            else:
                order = {e: [i for i in order[e] if i <= B] + [i for i in prog[e] if i > B] for e in ENGS}
        ops = self.oplist
        tok = [None] * len(ops)
        ccnt = {e: 0 for e in ENGS}
        dcnt = {}
        drr = {e: 0 for e in ENGS}
        plan = {e: [] for e in ENGS}
        prevdma = {}
        for e in ENGS:
            for i in order[e]:
                o = ops[i]
                if o["kind"] == "d":
                    j = drr[e] % NDMASEM
                    drr[e] += 1
                    key = ("d", e, j)
                    c = dcnt.get(key, 0)
                    dcnt[key] = c + 1
                    tok[i] = (key, 16 * (c + 1))
                    prevdma[i] = (key, 16 * c) if c > 0 else None
                else:
                    ccnt[e] += 1
                    tok[i] = (e, ccnt[e])
        for e in ENGS:
            waited = {}
            for i in order[e]:
                o = ops[i]
                need = {}
                for d in o["deps"]:
                    k, v = tok[d]
                    if waited.get(k, 0) >= v:
                        continue
                    if need.get(k, 0) < v:
                        need[k] = v
                if o["kind"] == "d" and prevdma.get(i):
                    k, v = prevdma[i]
                    if waited.get(k, 0) < v and need.get(k, 0) < v:
                        need[k] = v
                for k, v in need.items():
                    waited[k] = v
                plan[e].append((list(need.items()), i))
            if e == "sp":
                fin = {}
                for i2, t in enumerate(tok):
                    if t is not None and fin.get(t[0], 0) < t[1]:
                        fin[t[0]] = t[1]
                plan[e].append(([(k, v) for k, v in fin.items() if waited.get(k, 0) < v], None))
        for e in ENGS:
            for (waits, i) in plan[e]:
                for (k, v) in waits:
                    self._sem(k)
                if i is not None:
                    self._sem(tok[i][0])
        if os.environ.get("KCHECK"):
            import collections
            sv = collections.defaultdict(int)
            ptr = {e: 0 for e in ENGS}
            while True:
                prog_ = False
                for e in ENGS:
                    while ptr[e] < len(plan[e]):
                        waits, i = plan[e][ptr[e]]
                        if all(sv[k] >= v for k, v in waits):
                            if i is not None:
                                k, v = tok[i]
                                inc = 16 if ops[i]["kind"] == "d" else 1
                                sv[k] += inc
                                assert sv[k] == v, ("token mismatch", e, i, k, v, sv[k])
                            ptr[e] += 1
                            prog_ = True
                        else:
                            break
                if all(ptr[e] == len(plan[e]) for e in ENGS):
                    print("KCHECK: ok, no deadlock")
                    break
                if not prog_:
                    for e in ENGS:
                        if ptr[e] < len(plan[e]):
                            waits, i = plan[e][ptr[e]]
                            print("KCHECK STUCK", e, ptr[e], i, [(k, v, sv[k]) for k, v in waits if sv[k] < v])
                    break
        block = self.stack.enter_context(nc.Block())

        def run(engname):
            def body(e):
                for (waits, i) in plan[engname]:
                    for (k, v) in waits:
                        e.wait_ge(self.sems[k], v)
                    if i is None:
                        continue
                    o = ops[i]
                    if o["kind"] == "d":
                        ins = e.dma_start(out=o["dma"][0], in_=o["dma"][1], allow_slow_non_contiguous=True)
                        ins.then_inc(self.sems[tok[i][0]], 16)
                    elif o["kind"] == "n":
                        ins = e.nop()
                        ins.then_inc(self.sems[tok[i][0]], 1)
                    else:
                        ins = o["fn"](e)
                        ins.then_inc(self.sems[tok[i][0]], 1)
            return body
        block.tensor(run("pe"))
        block.scalar(run("act"))
        block.vector(run("dve"))
        block.gpsimd(run("pool"))
        block.sync(run("sp"))


def fap(ap, dims):
    return bass.AP(tensor=ap.tensor, offset=ap.offset, ap=[list(ap.ap[0])] + [list(d) for d in dims])


def t5_bucket_np(n):
    n = np.maximum(n, 0)
    nf = np.maximum(n, 1).astype(np.float32)
    large = 16 + (np.log(nf / 16) / math.log(128 / 16) * 16).astype(np.int32)
    large = np.minimum(large, 31)
    return np.where(n < 16, n, large)


def host_consts():
    c = {}
    c["ident"] = np.eye(128, dtype=np.float32)
    s = np.arange(128)[:, None]
    t = np.arange(128)[None, :]
    c["tri"] = np.where(s <= t, -1.0 / 16, 0.0).astype(np.float32)
    c["tris"] = (np.eye(128) * (-1.0 / 16)).astype(np.float32)
    c["cmask"] = (s <= t).astype(np.float32)
    c["jx"] = np.eye(128, dtype=np.float32)[::-1].copy()
    bd = np.zeros((128, 128), np.float32)
    bd[:64, :64] = 1
    bd[64:, 64:] = 1
    c["bd"] = bd
    sw = np.zeros((128, 128), np.float32)
    for m in range(128):
        sw[(m + 64) % 128, m] = 1
    c["sw"] = sw
    oh = np.zeros((128, 2, 256), np.float32)
    for kb in range(2):
        off = 128 if kb == 0 else 0
        for i in range(255):
            dlt = 127 + off - i
            if 0 <= dlt <= 127:
                oh[int(t5_bucket_np(np.array(dlt))), kb, i] = 1.0
            else:
                oh[32, kb, i] = MASKV
        oh[32, kb, 255] = MASKV
    c["oh"] = oh
    sel = np.zeros((128, 16, 128), np.float32)
    for b in range(16):
        sel[b, b, :] = 1
    c["sel"] = sel
    return c


def build_program(debug=False):
    nc = bass.Bass("TRN2", target_bir_lowering=False)
    P = Prog(nc)

    def din(name, shape):
        return nc.dram_tensor(name, list(shape), F32, kind="ExternalInput").ap()

    def dout(name, shape):
        return nc.dram_tensor(name, list(shape), F32, kind="ExternalOutput").ap()

    xp = din("xp", [TOK, D]); xhalo = din("xhalo", [128, D]); xpre = din("xpre", [NPRE, D]); xs = din("xs", [128, D])
    ck = din("ck", [16, 128, 128]); cv = din("cv", [16, 128, 128]); sg = din("sg", [16, 4, 64, 128])
    w_in = din("w_in", [D, INW]); w_o = din("w_o", [D, D]); w_gate = din("w_gate", [D, DFF]); w_up = din("w_up", [D, DFF])
    w_down = din("w_down", [DFF, D])
    attn_g = din("attn_g", [D]); ffn_g = din("ffn_g", [D]); qg_in = din("qng", [64]); kg_in = din("kng", [64])
    sinks = din("sinks", [8]); relb = din("relb", [32, 8]); w2 = din("w2", [16, 256]); bgate = din("bgate", [256])
    glag = din("glag", [128]); flag = din("flag", [128, 1])
    cn = {k: din("c_" + k, v.shape) for k, v in host_consts().items()}

    y_p = dout("y_p", [TOK, D]); y_s = dout("y_s", [128, D])
    kwp = dout("kwp", [128, 128]); vwp = dout("vwp", [128, 128]); gsp = dout("gsp", [4, 64, 128])
    kws = dout("kws", [16, 128, 128]); vws = dout("vws", [16, 128, 128]); gss = dout("gss", [16, 4, 64, 128])
    fscr = nc.dram_tensor("fscr", [2, 8, 256], F32).ap()
    wb = {"w_in": nc.dram_tensor("wb_in", [D, INW], BF16).ap(), "w_o": nc.dram_tensor("wb_o", [D, D], BF16).ap(),
          "w_gate": nc.dram_tensor("wb_gate", [D, DFF], BF16).ap(), "w_up": nc.dram_tensor("wb_up", [D, DFF], BF16).ap(),
          "w_down": nc.dram_tensor("wb_down", [DFF, D], BF16).ap()}
    wf = {"w_in": w_in, "w_o": w_o, "w_gate": w_gate, "w_up": w_up, "w_down": w_down}
    if debug:
        dbg_mix = dout("dbg_mix", [128, 8, 512]); dbg_h = dout("dbg_h", [128, 4, D]); dbg_z = dout("dbg_z", [128, 8, 512]); dbg_a = dout("dbg_a", [128, NKF, 512])
        dbg_q = dout("dbg_q", [128, 4, 512]); dbg_rs = dout("dbg_rs", [128, 4, 512])

    for i in range(6):
        P.banks.append(P.ps([128, 512], F32, f"bank{i}"))
    psT = [P.ps([128, 1024], BF16, f"pst{i}") for i in range(2)]
    pst_rr = [0]

    def tbank():
        i = pst_rr[0] % 2
        pst_rr[0] += 1
        return psT[i], ("pst", i)

    ident_f = P.sb([128, 128], F32); ident_bf = P.sb([128, 128], BF16)
    tri_f = P.sb([128, 128], F32); tris_f = P.sb([128, 128], F32); cmask_bf = P.sb([128, 128], BF16)
    jx_f = P.sb([128, 128], F32); bd_bf = P.sb([128, 128], BF16); sw_bf = P.sb([128, 128], BF16)
    ones_bf = P.sb([128, 128], BF16); zeros_f = P.sb([128, 128], F32); scr_f = P.sb([128, 2048], F32)
    sel_bf = P.sb([128, 16, 128], BF16)
    gaT = P.sb([128, 8], F32); gfT = P.sb([128, 8], F32)
    gq_col = P.sb([128, 1], F32); gk_col = P.sb([128, 1], F32); glag_col = P.sb([128, 1], F32)
    eps_col = P.sb([128, 1], F32); ln8_col = P.sb([128, 1], F32); flag_col = P.sb([128, 1], F32)
    bgate_bc = P.sb([128, 256], F32); w2pad = P.sb([128, 256], BF16)
    relb_pad = P.sb([128, 128], F32)
    hank = scr_f[:, 0:1024].rearrange("p (h s) -> p h s", h=8)
    oh_sb = scr_f[:, 1024:1536].rearrange("p (a b) -> p a b", a=2)
    ftab = scr_f[0:8, 1536:2048].rearrange("p (a b) -> p a b", a=2)
    E = P.sb([128, 2, 2, 2, 2, 128], F32)
    sink_bc = P.sb([128, 8], F32); sinkexp = P.sb([128, 2, 2, 2, 128], F32)
    ring = P.sb([128, NSLOT, 8, 512], BF16)
    xb = P.sb([128, 4, D], F32)
    actT = P.sb([128, 8, 512], BF16)
    nbf = P.sb([128, D], BF16); nbf_b = P.sb([128, D], BF16)
    zt_b = P.sb([128, 256], F32); sp_b = P.sb([128, 256], F32); ktok_b = P.sb([128, 2, 2, 128], BF16)
    ss_c2 = P.sb([128, 2], F32); rs_c2 = P.sb([128, 2], F32)
    ss_c = P.sb([128, 1], F32); rs_c = P.sb([128, 1], F32)
    qhT = P.sb([128, 4, 512], BF16)
    kX = P.sb([128, 4, 640], BF16)
    khT_bf = P.sb([128, 512], BF16); khT_f = P.sb([128, 128], F32)
    Vdup = P.sb([128, 5, 2, 128], BF16)
    qgT = P.sb([128, 2, 512], BF16); kgA = P.sb([128, 2, 512], BF16); kgB = P.sb([128, 2, 512], BF16)
    kgf = P.sb([128, 2, 128], F32)
    ktok = P.sb([128, 2, 2, 128], BF16)
    vg_tok = P.sb([128, 4, 512], BF16)
    rsT = P.sb([128, 4, 512], BF16)
    ulrT = P.sb([128, 512], BF16)
    zt = P.sb([128, 256], F32); sp_t = P.sb([128, 256], F32)
    ebq = P.sb([128, 2, 512], F32); enb = P.sb([128, 2, 512], F32); elast = P.sb([128, 2, 4], F32)
    elb = P.sb([128, 2, 128], F32)
    S = P.sb([128, 2, 128], F32); Stmp = P.sb([128, 2, 128], F32); SA = P.sb([128, 2, 128], BF16); SB = P.sb([128, 2, 128], BF16)
    ATbf = P.sb([128, 4, 128], BF16)
    sq_bf = P.sb([128, 512], BF16); rstd_f = P.sb([128, 512], F32); tmp_f = P.sb([128, 512], F32)
    pe_f = P.sb([128, 512], F32); rec_f = P.sb([128, 512], F32)
    PT = P.sb([128, 2, 2, 2, 2, 128], BF16)
    mixT = P.sb([128, 8, 512], BF16)
    aT = P.sb([128, NKF, 512], BF16)
    sg_f = P.sb([128, 512], F32)
    vw_f = P.sb([128, 128], F32); kw_f = P.sb([128, 128], F32)
    Kw_bf = P.sb([128, 16, 128], BF16); Kwsw_bf = P.sb([128, 16, 128], BF16)
    KTb = P.sb([128, 2, 2, 128], BF16)
    Vwd = P.sb([128, 16, 2, 128], BF16)
    qsA = P.sb([128, 4, 16], BF16); qsB = P.sb([128, 4, 16], BF16)
    Pts = P.sb([128, 16, 8], BF16); pes = P.sb([128, 16, 8], F32)
    Sb = scr_f[:, 0:512].rearrange("p (a c v) -> p a c v", a=2, c=2)
    Wt = scr_f[:, 512:1024].rearrange("p (a c v) -> p a c v", a=2, c=2)
    Sn = scr_f[:, 1024:1536].rearrange("p (a c v) -> p a c v", a=2, c=2)
    WA = P.sb([128, 2, 2, 128], BF16); WB = P.sb([128, 2, 2, 128], BF16)

    CUR = {"p": 0}
    _vwd32 = Vwd[:].rearrange("p a b c -> p (a b c)").bitcast(F32)
    _kwf = Kw_bf[:].rearrange("p a b -> p (a b)")
    _kwswf = Kwsw_bf[:].rearrange("p a b -> p (a b)")
    nbfs2 = (nbf, nbf_b)

    def X(blk):
        if CUR["p"] == 0:
            return xb[:, blk, :]
        src = _vwd32 if blk < 2 else scr_f[:]
        o = (blk % 2) * 1024
        return src[:, o:o + 1024]

    def A(k):
        if CUR["p"] == 0:
            return actT[:, k, :]
        src = _kwf if k < 4 else _kwswf
        o = (k % 4) * 512
        return src[:, o:o + 512]

    def xr(blk):
        return ("x", CUR["p"], blk)

    def ar(blk):
        return ("actT", CUR["p"], blk)

    def ld(dst, src, name, eng="sp"):
        P.dma(eng, dst, src, writes=[name])

    ld(ident_f[:], cn["ident"], "ident_f"); ld(tri_f[:], cn["tri"], "tri"); ld(tris_f[:], cn["tris"], "tris")
    ld(jx_f[:], cn["jx"], "jx"); ld(oh_sb, cn["oh"], "oh")
    P.dma("pool", ident_bf[:], cn["ident"], writes=["ident_bf"])
    P.dma("pool", cmask_bf[:], cn["cmask"], writes=["cmask"])
    P.dma("pool", bd_bf[:], cn["bd"], writes=["bd"])
    P.dma("pool", sw_bf[:], cn["sw"], writes=["sw"])
    P.dma("pool", sel_bf[:], cn["sel"], writes=["sel"])
    ld(gaT[:], attn_g.rearrange("(k p) -> p k", p=128), "gaT"); ld(gfT[:], ffn_g.rearrange("(k p) -> p k", p=128), "gfT")
    for h in range(2):
        ld(gq_col[h * 64:(h + 1) * 64, :], qg_in.rearrange("(p o) -> p o", o=1), "gq")
        ld(gk_col[h * 64:(h + 1) * 64, :], kg_in.rearrange("(p o) -> p o", o=1), "gk")
    ld(glag_col[:], glag.rearrange("(p o) -> p o", o=1), "glag"); ld(flag_col[:], flag, "flag")
    ld(bgate_bc[:], bass.AP(tensor=bgate.tensor, offset=0, ap=[[0, 128], [1, 256]]), "bgate")
    ld(sink_bc[:], bass.AP(tensor=sinks.tensor, offset=0, ap=[[0, 128], [1, 8]]), "sink_bc")
    P.op("dve", lambda e: e.memset(ones_bf[:], 1.0), writes=["ones"])
    P.op("dve", lambda e: e.memset(zeros_f[:], 0.0), writes=["zeros"])
    P.op("dve", lambda e: e.memset(eps_col[:], EPS), writes=["eps"])
    P.op("dve", lambda e: e.memset(ln8_col[:], math.log(0.125)), writes=["ln8"])
    P.op("dve", lambda e: e.memset(w2pad[:], 0.0), writes=["w2pad"])
    P.dma("pool", w2pad[112:128, :], w2, reads=[], writes=["w2pad"])
    P.op("dve", lambda e: e.tensor_scalar(out=gq_col[:], in0=gq_col[:], scalar1=0.125, scalar2=None, op0=ALU.mult),
         reads=["gq"], writes=["gq"])
    for t_ in (kgA, kgB, SA, SB, qsA, qsB, WA, WB):
        P.op("dve", lambda e, t_=t_: e.memset(t_[:], 0.0), writes=["zinit"])
    P.op("dve", lambda e: e.memset(kX[:], 0.0), writes=["kX"])
    P.op("dve", lambda e: e.memset(S[:], 0.0), writes=["S"])
    P.op("dve", lambda e: e.memset(relb_pad[:], 0.0), writes=["relb_pad"])
    P.op("dve", lambda e: e.memset(relb_pad[32:33, :], 1.0), reads=[], writes=["relb_pad"])
    P.dma("sp", relb_pad[0:32, 0:8], relb, writes=["relb_pad"])

    bk, bkr = P.bank()
    P.op("pe", lambda e: e.matmul(bk[:, 0:512], lhsT=relb_pad[:], rhs=scr_f[:, 1024:1536], start=True, stop=True),
         reads=["relb_pad", "oh"], writes=[bkr])
    P.op("act", lambda e: e.activation(out=scr_f[0:8, 1536:2048], in_=bk[0:8, 0:512], func=AF.Copy), reads=[bkr], writes=["ftab"])
    P.dma("sp", fscr.rearrange("k h i -> h k i"), ftab, reads=["ftab"], writes=["fscr"])
    for kb in range(2):
        src = bass.AP(tensor=fscr.tensor, offset=kb * 8 * 256, ap=[[1, 128], [256, 8], [1, 128]])
        P.dma("sp", hank, src, reads=["fscr"], writes=["hank"])
        for hh in range(0, 8, 4):
            bk, bkr = P.bank()

            def mmj(e, bk=bk, hh=hh):
                ins = None
                for q in range(4):
                    ins = e.matmul(bk[:, q * 128:(q + 1) * 128], lhsT=hank[:, hh + q, :], rhs=jx_f[:], start=True, stop=True)
                return ins
            P.op("pe", mmj, reads=["hank", "jx"], writes=[bkr])
            for q in range(4):
                h = hh + q
                c_, half = h // 2, h % 2
                j, cl = c_ // 2, c_ % 2
                P.op("act", lambda e, bk=bk, q=q, j=j, half=half, kb=kb, cl=cl:
                     e.activation(out=E[:, j, half, kb, cl, :], in_=bk[:, q * 128:(q + 1) * 128], func=AF.Exp),
                     reads=[bkr], writes=["E"])
    P.op("act", lambda e: e.activation(out=sink_bc[:], in_=sink_bc[:], func=AF.Exp), reads=["sink_bc"], writes=["sink_bc"])
    for h in range(8):
        c_, half = h // 2, h % 2
        j, cl = c_ // 2, c_ % 2
        P.op("dve", lambda e, h=h, j=j, half=half, cl=cl: e.tensor_scalar(out=sinkexp[:, j, half, cl, :], in0=zeros_f[:], scalar1=sink_bc[:, h:h + 1],
                                                                         scalar2=None, op0=ALU.add), reads=["sink_bc", "zeros"], writes=["sinkexp"])

    wstate = {"n": 0}

    def wload(parts, fp32=False):
        s = wstate["n"] % NSLOT
        wstate["n"] += 1
        res = ("w", s)
        for (c0, (wn, r0, nrows, cc0, ncols_), nk, ncols) in parts:
            src = (wf if fp32 else wb)[wn][r0:r0 + nrows, cc0:cc0 + ncols_].rearrange("(k p) n -> p k n", p=128)
            q_ = "pool" if (fp32 or wstate["n"] % 2 == 0) else "sp"
            P.dma(q_, ring[:, s, 0:nk, c0:c0 + ncols], src, reads=([] if fp32 else [("wbf", wn)]), writes=[res])
        return s, res

    def wsrc(w, r0, nrows, c0, ncols):
        return (w, r0, nrows, c0, ncols)

    def front(src_fn, nb, gT):
        for blk in range(nb):
            P.dma("sp", X(blk), src_fn(blk), writes=[xr(blk)])
            norm_block(blk, gT)

    def norm_block(blk, gT):
        p = CUR["p"]
        xblk = X(blk); xres = xr(blk); ares = ar(blk)
        nb_ = nbfs2[p]; ss = ss_c2[:, p:p + 1]; rs = rs_c2[:, p:p + 1]
        P.op("act", lambda e: e.activation(out=nb_[:], in_=xblk, func=AF.Square, accum_out=ss),
             reads=[xres], writes=[("nbf", p), ("ss_c", p)])
        P.op("act", lambda e: e.activation(out=rs, in_=ss, func=AF.Ln, scale=1.0 / D, bias=eps_col[:, 0:1]),
             reads=[("ss_c", p), "eps"], writes=[("rs_c", p)])
        P.op("act", lambda e: e.activation(out=rs, in_=rs, func=AF.Exp, scale=-0.5), reads=[("rs_c", p)], writes=[("rs_c", p)])
        P.op("dve", lambda e: e.tensor_scalar(out=nb_[:], in0=xblk, scalar1=rs, scalar2=None, op0=ALU.mult),
             reads=[xres, ("rs_c", p)], writes=[("nbf", p)])
        tb, tbr = tbank()

        def tr(e):
            ins = None
            for k in range(8):
                ins = e.transpose(out=tb[:, k * 128:(k + 1) * 128], in_=nb_[:, k * 128:(k + 1) * 128], identity=ident_bf[:])
            return ins
        P.op("pe", tr, reads=[("nbf", p), "ident_bf"], writes=[tbr])
        if p == 0:
            P.op("dve", lambda e: e.tensor_tensor(out=actT[:, :, blk * 128:(blk + 1) * 128], in0=tb[:].rearrange("p (k t) -> p k t", k=8),
                                                  in1=fap(gT[:], [[1, 8], [0, 128]]), op=ALU.mult),
                 reads=[tbr, "gaT", "gfT"], writes=[ares])
        else:
            for kh, src in enumerate((_kwf, _kwswf)):
                P.op("dve", lambda e, kh=kh, src=src: e.tensor_tensor(
                    out=src.rearrange("p (k t) -> p k t", k=4)[:, :, blk * 128:(blk + 1) * 128],
                    in0=tb[:, kh * 512:(kh + 1) * 512].rearrange("p (k t) -> p k t", k=4),
                    in1=fap(gT[:, kh * 4:kh * 4 + 1], [[1, 4], [0, 128]]), op=ALU.mult),
                    reads=[tbr, "gaT", "gfT", ares], writes=[ares])

    def actT_reads(nb):
        return [ar(b) for b in range(nb)]

    def proj_fm(slot, res, col0, T, nb):
        bk, bkr = P.bank()
        acts = [A(k) for k in range(8)]

        def mm(e):
            ins = None
            for k in range(8):
                ins = e.matmul(bk[:, 0:T], lhsT=ring[:, slot, k, col0:col0 + 128], rhs=acts[k][:, 0:T], start=(k == 0), stop=(k == 7))
            return ins
        P.op("pe", mm, reads=[res] + actT_reads(nb), writes=[bkr])
        return bk, bkr

    def proj_tm(slot, res, col0, ncols, blk):
        bk, bkr = P.bank()
        acts = [A(k) for k in range(8)]

        def mm(e):
            ins = None
            for k in range(8):
                ins = e.matmul(bk[:, 0:ncols], lhsT=acts[k][:, blk * 128:(blk + 1) * 128], rhs=ring[:, slot, k, col0:col0 + ncols],
                               start=(k == 0), stop=(k == 7))
            return ins
        P.op("pe", mm, reads=[res, ar(blk)], writes=[bkr])
        return bk, bkr

    def rstd_fm(src_ap, T, lhs_ones, scale, reads_src):
        P.op("act", lambda e: e.activation(out=sq_bf[:, 0:T], in_=src_ap, func=AF.Square), reads=reads_src, writes=["sq"])
        b2, b2r = P.bank()
        P.op("pe", lambda e: e.matmul(b2[:, 0:T], lhsT=lhs_ones[:], rhs=sq_bf[:, 0:T], start=True, stop=True),
             reads=["sq", "bd", "ones"], writes=[b2r])
        P.op("act", lambda e: e.activation(out=rstd_f[:, 0:T], in_=b2[:, 0:T], func=AF.Ln, scale=scale, bias=eps_col[:, 0:1]),
             reads=[b2r, "eps"], writes=["rstd"])
        P.op("act", lambda e: e.activation(out=rstd_f[:, 0:T], in_=rstd_f[:, 0:T], func=AF.Exp, scale=-0.5), reads=["rstd"], writes=["rstd"])

    def gla_prep(slot_lr, res_lr, lrcol, nb, T, tri_ap, tri_res, full):
        bk, bkr = proj_fm(slot_lr, res_lr, lrcol, T, nb)
        P.op("act", lambda e: e.activation(out=ulrT[:, 0:T], in_=bk[:, 0:T], func=AF.Copy), reads=[bkr], writes=["ulrT"])
        bT = [P.bank(hold=True) for _ in range(2)]
        for blk in range(nb):
            zb, zbr = P.bank()
            P.op("pe", lambda e, zb=zb, blk=blk: e.matmul(zb[:, 0:256], lhsT=ulrT[:, blk * 128:(blk + 1) * 128], rhs=w2pad[:], start=True, stop=True),
                 reads=["ulrT", "w2pad"], writes=[zbr])
            P.op("dve", lambda e, zb=zb: e.tensor_tensor(out=zt[:], in0=zb[:, 0:256], in1=bgate_bc[:], op=ALU.add),
                 reads=[zbr, "bgate"], writes=["zt"])
            P.op("act", lambda e: e.activation(out=zt[:], in_=zt[:], func=AF.Exp, scale=-1.0), reads=["zt"], writes=["zt"])
            P.op("act", lambda e: e.activation(out=sp_t[:], in_=zt[:], func=AF.Ln, bias=1.0), reads=["zt"], writes=["sp_t"])
            for c in range(2):
                P.op("pe", lambda e, c=c, blk=blk: e.matmul(bT[c][0][:, blk * 128:(blk + 1) * 128], lhsT=sp_t[:, c * 128:(c + 1) * 128], rhs=tri_ap,
                                                           start=True, stop=True), reads=["sp_t", tri_res], writes=[bT[c][1]])
        for c in range(2):
            P.release(bT[c][1])
        for c in range(2):
            P.op("act", lambda e, c=c: e.activation(out=enb[:, c, 0:T], in_=bT[c][0][:, 0:T], func=AF.Exp, scale=-1.0), reads=[bT[c][1]], writes=["enb"])
            P.op("act", lambda e, c=c: e.activation(out=elast[:, c, 0:nb], in_=fap(bT[c][0][:, 127:128], [[128, nb]]), func=AF.Exp),
                 reads=[bT[c][1]], writes=["elast"])
            if full:
                P.op("act", lambda e, c=c: e.activation(out=ebq[:, c, 0:T], in_=bT[c][0][:, 0:T], func=AF.Exp, bias=ln8_col[:, 0:1]),
                     reads=[bT[c][1], "ln8"], writes=["ebq"])
                if T == 128:
                    P.op("act", lambda e, c=c: e.activation(out=elb[:, c, :], in_=bT[c][0][:, 0:128], func=AF.Exp), reads=[bT[c][1]], writes=["elb"])

    def kg_evac(slot, res, col0, nb, T, sample=False):
        for c in range(2):
            bk, bkr = proj_fm(slot, res, col0 + c * 128, T, nb)
            P.op("dve", lambda e, bk=bk, c=c: e.tensor_tensor(out=kgA[0:64, c, 0:T], in0=bk[0:64, 0:T], in1=enb[0:64, c, 0:T], op=ALU.mult),
                 reads=[bkr, "enb", "zinit"], writes=[("kgA", c)])
            P.op("dve", lambda e, bk=bk, c=c: e.tensor_tensor(out=kgB[64:128, c, 0:T], in0=bk[64:128, 0:T], in1=enb[64:128, c, 0:T], op=ALU.mult),
                 reads=[bkr, "enb", "zinit"], writes=[("kgB", c)])
            if sample:
                P.op("dve", lambda e, bk=bk, c=c: e.tensor_tensor(out=kgf[:, c, :], in0=bk[:, 0:128], in1=enb[:, c, 0:128], op=ALU.mult),
                     reads=[bkr, "enb"], writes=["kgf"])

    def vg_tm(slot, res, col0, nb):
        for blk in range(nb):
            bk, bkr = proj_tm(slot, res, col0, 512, blk)
            P.op("act", lambda e, bk=bk, blk=blk: e.activation(out=vg_tok[:, blk, :], in_=bk[:, 0:512], func=AF.Copy), reads=[bkr], writes=[("vg", blk)])

    def state_update(blk, masked):
        tb, tbr = tbank()

        def tr(e):
            ins = None
            for c in range(2):
                for ab, src in enumerate((kgA, kgB)):
                    o = (c * 2 + ab) * 128
                    ins = e.transpose(out=tb[:, o:o + 128], in_=src[:, c, blk * 128:(blk + 1) * 128], identity=ident_bf[:])
            return ins
        P.op("pe", tr, reads=[("kgA", 0), ("kgA", 1), ("kgB", 0), ("kgB", 1), "ident_bf"], writes=[tbr])
        P.op("act", lambda e: e.activation(out=ktok[:].rearrange("p c a f -> p (c a f)"), in_=tb[:, 0:512], func=AF.Copy), reads=[tbr], writes=["ktok"])
        ub, ubr = P.bank()

        def mm(e):
            ins = None
            for c in range(2):
                for ab in range(2):
                    h = 2 * c + ab
                    ins = e.matmul(ub[:, c * 128:(c + 1) * 128], lhsT=ktok[:, c, ab, :], rhs=vg_tok[:, blk, h * 128:(h + 1) * 128],
                                   start=(ab == 0), stop=(ab == 1))
            return ins
        P.op("pe", mm, reads=["ktok", ("vg", blk)], writes=[ubr])
        P.op("dve", lambda e: e.tensor_tensor(out=Stmp[:].rearrange("p c v -> p (c v)"), in0=ub[:, 0:256], in1=S[:].rearrange("p c v -> p (c v)"), op=ALU.add),
             reads=[ubr, "S"], writes=["Stmp"])
        P.op("dve", lambda e: e.tensor_tensor(out=S[:], in0=Stmp[:], in1=fap(elast[:, 0, blk:blk + 1], [[4, 2], [0, 128]]), op=ALU.mult),
             reads=["Stmp", "elast", "SA", "SB"], writes=["S"])
        if masked:
            P.op("act", lambda e: e.activation(out=SA[0:64], in_=S[0:64], func=AF.Copy), reads=["S", "zinit"], writes=["SA"])
            P.op("act", lambda e: e.activation(out=SB[64:128], in_=S[64:128], func=AF.Copy), reads=["S", "zinit"], writes=["SB"])

    def gla_out_norm(ob, obr, ncol, W, col0):
        rstd_fm(ob[:, 0:ncol], ncol, ones_bf, 1.0 / 128, [obr])
        P.op("dve", lambda e: e.scalar_tensor_tensor(out=tmp_f[:, 0:ncol], in0=ob[:, 0:ncol], scalar=glag_col[:, 0:1], in1=rstd_f[:, 0:ncol],
                                                     op0=ALU.mult, op1=ALU.mult), reads=[obr, "rstd", "glag"], writes=["tmp_f"])
        P.op("dve", lambda e: e.tensor_tensor(out=mixT[:, 4:8, col0:col0 + W], in0=tmp_f[:, 0:ncol].rearrange("p (h w) -> p h w", h=4),
                                              in1=rsT[:, :, col0:col0 + W], op=ALU.mult), reads=["tmp_f", "rsT"], writes=[("mixg", col0)])

    def gla_block(blk):
        ab_, abr = P.bank()

        def mm1(e):
            ins = None
            for h in range(4):
                c, half = h // 2, h % 2
                src = kgA if half == 0 else kgB
                ins = e.matmul(ab_[:, h * 128:(h + 1) * 128], lhsT=src[:, c, blk * 128:(blk + 1) * 128], rhs=qgT[:, c, blk * 128:(blk + 1) * 128],
                               start=True, stop=True)
            return ins
        P.op("pe", mm1, reads=[("kgA", 0), ("kgA", 1), ("kgB", 0), ("kgB", 1), "qgT"], writes=[abr])
        P.op("dve", lambda e: e.tensor_tensor(out=ATbf[:], in0=ab_[:, 0:512].rearrange("p (h t) -> p h t", h=4), in1=fap(cmask_bf[:], [[0, 4], [1, 128]]),
                                              op=ALU.mult), reads=[abr, "cmask"], writes=["ATbf"])
        ob, obr = P.bank()

        def mm2(e):
            ins = None
            for h in range(4):
                c, half = h // 2, h % 2
                sm = SA if half == 0 else SB
                e.matmul(ob[:, h * 128:(h + 1) * 128], lhsT=vg_tok[:, blk, h * 128:(h + 1) * 128], rhs=ATbf[:, h, :], start=True, stop=False)
                ins = e.matmul(ob[:, h * 128:(h + 1) * 128], lhsT=sm[:, c, :], rhs=qgT[:, c, blk * 128:(blk + 1) * 128], start=False, stop=True)
            return ins
        P.op("pe", mm2, reads=[("vg", blk), "ATbf", "SA", "SB", "qgT"], writes=[obr])
        gla_out_norm(ob, obr, 512, 128, blk * 128)
        state_update(blk, True)

    def attn_block(blk, Etab, Eres, useflag=False):
        for j in range(2):
            sbk = []
            for half in range(2):
                bk, bkr = P.bank()
                sbk.append((bk, bkr))

                def mm(e, bk=bk, half=half, j=j):
                    ins = None
                    for kb in range(2):
                        kc = (blk + kb) * 128
                        ins = e.matmul(bk[:, kb * 256:(kb + 1) * 256], lhsT=kX[:, 2 * j + half, kc:kc + 128],
                                       rhs=qhT[:, 2 * j:2 * j + 2, blk * 128:(blk + 1) * 128], start=True, stop=True)
                    return ins
                P.op("pe", mm, reads=["kX", "qhT"], writes=[bkr])
            for half in range(2):
                bk, bkr = sbk[half]
                P.op("act", lambda e, bk=bk: e.activation(out=pe_f[:], in_=bk[:, 0:512], func=AF.Exp), reads=[bkr], writes=["pe_f"])
                P.op("dve", lambda e, half=half, j=j: e.tensor_tensor(out=PT[:, j, half].rearrange("p a b q -> p (a b q)"), in0=pe_f[:],
                                                                     in1=Etab[:, j, half].rearrange("p a b q -> p (a b q)"), op=ALU.mult),
                     reads=["pe_f", Eres], writes=[("PT", j)])
                if useflag:
                    P.op("dve", lambda e, half=half, j=j: e.tensor_scalar(out=PT[:, j, half, 0], in0=PT[:, j, half, 0], scalar1=flag_col[:, 0:1],
                                                                          scalar2=None, op0=ALU.mult), reads=[("PT", j), "flag"], writes=[("PT", j)])
            ob, obr = P.bank()
            db, dbr = P.bank()

            def mmv(e, ob=ob, db=db, j=j):
                ins = None
                for kb in range(2):
                    rhs = PT[:, j, :, kb, :, :]
                    e.matmul(ob[:, 0:512], lhsT=Vdup[:, blk + kb, j, :], rhs=rhs, start=(kb == 0), stop=(kb == 1))
                for kb in range(2):
                    rhs = PT[:, j, :, kb, :, :]
                    ins = e.matmul(db[:, 0:512], lhsT=ones_bf[:], rhs=rhs, start=(kb == 0), stop=(kb == 1))
                return ins
            P.op("pe", mmv, reads=[("PT", j), "Vdup", "ones"], writes=[obr, dbr])
            P.op("dve", lambda e, db=db, j=j: e.tensor_tensor(out=rec_f[:], in0=db[:, 0:512], in1=sinkexp[:, j].rearrange("p a b q -> p (a b q)"), op=ALU.add),
                 reads=[dbr, "sinkexp"], writes=["rec"])
            P.op("act", lambda e: e.activation(out=rec_f[:], in_=rec_f[:], func=AF.Ln), reads=["rec"], writes=["rec"])
            P.op("act", lambda e: e.activation(out=rec_f[:], in_=rec_f[:], func=AF.Exp, scale=-1.0), reads=["rec"], writes=["rec"])
            for half in range(2):
                r0 = half * 64
                P.op("dve", lambda e, ob=ob, half=half, r0=r0, j=j: e.tensor_tensor(
                    out=mixT[r0:r0 + 64, 2 * j:2 * j + 2, blk * 128:(blk + 1) * 128],
                    in0=ob[r0:r0 + 64, half * 256:(half + 1) * 256].rearrange("p (c q) -> p c q", c=2),
                    in1=rec_f[r0:r0 + 64, half * 256:(half + 1) * 256].rearrange("p (c q) -> p c q", c=2), op=ALU.mult),
                    reads=[obr, "rec"], writes=[("mixa", blk)])

    def qk_norm_chunk(bk, bkr, T, gcol, gres, out_fn):
        rstd_fm(bk[:, 0:T], T, bd_bf, 1.0 / 64, [bkr])
        out_fn(bk, bkr)

    def wo_ffnnorm(nb, s0, r0, s1, r1):
        for blk in range(nb):
            for cg, (s, r) in enumerate(((s0, r0), (s1, r1))):
                bk, bkr = P.bank()

                def mm(e, bk=bk, s=s, blk=blk):
                    ins = None
                    for k in range(8):
                        ins = e.matmul(bk[:, 0:512], lhsT=mixT[:, k, blk * 128:(blk + 1) * 128], rhs=ring[:, s, k, 0:512], start=(k == 0), stop=(k == 7))
                    return ins
                P.op("pe", mm, reads=[r, ("mixa", blk), ("mixg", blk * 128)], writes=[bkr])
                xs_ = X(blk)[:, cg * 512:(cg + 1) * 512]
                P.op("dve", lambda e, bk=bk, xs_=xs_: e.tensor_tensor(out=xs_, in0=bk[:, 0:512], in1=xs_, op=ALU.add), reads=[bkr, xr(blk)], writes=[xr(blk)])
            norm_block(blk, gfT)

    def ffn(nb, T, ydst_fn):
        for s6 in range(6):
            ncols = 512 if s6 < 5 else 256
            sg_, rg_ = wload([(0, wsrc("w_gate", 0, D, s6 * 512, ncols), 8, ncols)])
            su_, ru_ = wload([(0, wsrc("w_up", 0, D, s6 * 512, ncols), 8, ncols)])
            for mi in range(ncols // 128):
                m = s6 * 4 + mi
                gb, gbr = proj_fm(sg_, rg_, mi * 128, T, nb)
                ubk, ubr = proj_fm(su_, ru_, mi * 128, T, nb)
                P.op("act", lambda e, gb=gb: e.activation(out=sg_f[:, 0:T], in_=gb[:, 0:T], func=AF.Silu), reads=[gbr], writes=["sg_f"])
                P.op("dve", lambda e, ubk=ubk, m=m: e.tensor_tensor(out=aT[:, m, 0:T], in0=ubk[:, 0:T], in1=sg_f[:, 0:T], op=ALU.mult),
                     reads=[ubr, "sg_f"], writes=[("aT", m)])
        for cg in range(2):
            bks = [P.bank(hold=True) for _ in range(nb)]
            for kgp in range(3):
                nk = 8 if kgp < 2 else 6
                sl = wload([(0, wsrc("w_down", kgp * 1024, nk * 128, cg * 512, 512), nk, 512)])
                for blk in range(nb):
                    bk, bkr = bks[blk]

                    def mm(e, bk=bk, blk=blk, sl=sl, kgp=kgp, nk=nk):
                        ins = None
                        for kk in range(nk):
                            k = kgp * 8 + kk
                            ins = e.matmul(bk[:, 0:512], lhsT=aT[:, k, blk * 128:(blk + 1) * 128], rhs=ring[:, sl[0], kk, 0:512], start=(k == 0), stop=(k == NKF - 1))
                        return ins
                    P.op("pe", mm, reads=[sl[1]] + [("aT", kgp * 8 + kk) for kk in range(nk)], writes=[bkr])
            for blk in range(nb):
                bk, bkr = bks[blk]
                P.release(bkr)
                xs_ = X(blk)[:, cg * 512:(cg + 1) * 512]
                P.op("dve", lambda e, bk=bk, xs_=xs_: e.tensor_tensor(out=xs_, in0=bk[:, 0:512], in1=xs_, op=ALU.add), reads=[bkr, xr(blk)], writes=[xr(blk)])
                if cg == 1:
                    outs.append(P.dma("sp", ydst_fn(blk), X(blk), reads=[xr(blk)]))

    outs = []

    wstate["n"] = 0
    s_a, r_a = wload([(0, wsrc("w_in", 0, D, 1024, 256), 8, 256), (256, wsrc("w_in", 0, D, 2192, 128), 8, 128)], fp32=True)
    s_b, r_b = wload([(0, wsrc("w_in", 0, D, 1280, 512), 8, 512)], fp32=True)
    for k in range(8):
        P.op("dve", lambda e, k=k: e.tensor_scalar(out=ring[:, s_a, k, 0:384], in0=ring[:, s_a, k, 0:384], scalar1=gaT[:, k:k + 1], scalar2=None, op0=ALU.mult),
             reads=[r_a, "gaT"], writes=[r_a])
        P.op("act", lambda e, k=k: e.activation(out=ring[:, s_b, k, 0:512], in_=ring[:, s_b, k, 0:512], func=AF.Copy, scale=gaT[:, k:k + 1]),
             reads=[r_b, "gaT"], writes=[r_b])
    negs = P.sb([128, 2], BF16)
    P.op("dve", lambda e: e.memset(negs[:], -1.0 / 16), writes=["negs"])
    tri_bf = P.sb([128, 128], BF16)
    P.op("dve", lambda e: e.tensor_copy(out=tri_bf[:], in_=tri_f[:]), reads=["tri"], writes=["tri_bf"])
    sp_h = (P.sb([128, 256], BF16), P.sb([128, 256], BF16))
    _enbflat = enb[:].rearrange("p c t -> p (c t)")
    _qgflat = qgT[:].rearrange("p c t -> p (c t)")
    for wn in ("w_in", "w_o", "w_gate", "w_up", "w_down"):
        nr = wf[wn].shape[0]
        step = 256
        for r0 in range(0, nr, step):
            r1 = min(nr, r0 + step)
            ci = P.dma("pool", wb[wn][r0:r1, :], wf[wn][r0:r1, :], reads=["wbfchain"] + ([r_a, r_b] if (wn == "w_in" and r0 == 0) else []),
                       writes=[("wbf", wn), "wbfchain"])
            P.nofence = getattr(P, "nofence", set()) | {ci}
    NBLK = (16 if debug == "scan" else 0) if debug else NPRE // 128
    nbfs = (nbf, nbf_b); zts = (zt, zt_b); sps = (sp_t, sp_b); ktoks = (ktok, ktok_b)
    sbanks = {}

    def sc1(b):
        q = b % 4; p = b % 2
        P.dma("sp", xb[:, q, :], xpre[b * 128:(b + 1) * 128, :], writes=[("sx", q)])
        P.op("act", lambda e: e.activation(out=nbfs[p][:], in_=xb[:, q, :], func=AF.Square, accum_out=ss_c2[:, p:p + 1]),
             reads=[("sx", q)], writes=[("snbf", p), ("sss", p)])
        P.op("act", lambda e: e.activation(out=rs_c2[:, p:p + 1], in_=ss_c2[:, p:p + 1], func=AF.Ln, scale=1.0 / D, bias=eps_col[:, 0:1]),
             reads=[("sss", p), "eps"], writes=[("srs", p)])
        P.op("act", lambda e: e.activation(out=rs_c2[:, p:p + 1], in_=rs_c2[:, p:p + 1], func=AF.Exp, scale=-0.5), reads=[("srs", p)], writes=[("srs", p)])
        P.op("dve", lambda e: e.tensor_scalar(out=nbfs[p][:], in0=xb[:, q, :], scalar1=rs_c2[:, p:p + 1], scalar2=None, op0=ALU.mult),
             reads=[("sx", q), ("srs", p)], writes=[("snbf", p)])
        tb, tbr = tbank()

        def tr(e):
            ins = None
            for k in range(8):
                ins = e.transpose(out=tb[:, k * 128:(k + 1) * 128], in_=nbfs[p][:, k * 128:(k + 1) * 128], identity=ident_bf[:])
            return ins
        P.op("pe", tr, reads=[("snbf", p), "ident_bf"], writes=[tbr])
        if b % 2 == 0:
            P.op("act", lambda e: e.activation(out=actT[:, :, q * 128:(q + 1) * 128], in_=tb[:].rearrange("p (k t) -> p k t", k=8), func=AF.Copy),
                 reads=[tbr], writes=[("sact", q)])
        else:
            P.op("dve", lambda e: e.tensor_copy(out=actT[:, :, q * 128:(q + 1) * 128], in_=tb[:].rearrange("p (k t) -> p k t", k=8)),
                 reads=[tbr], writes=[("sact", q)])

    def sc2(b):
        q = b % 4
        ab, abr = P.bank()
        vb, vbr = P.bank()

        def mm(e):
            ins = None
            for k in range(8):
                ins = e.matmul(ab[:, 0:128], lhsT=ring[:, s_a, k, 256:384], rhs=actT[:, k, q * 128:(q + 1) * 128], start=(k == 0), stop=(k == 7))
            for k in range(8):
                ins = e.matmul(vb[:, 0:512], lhsT=actT[:, k, q * 128:(q + 1) * 128], rhs=ring[:, s_b, k, 0:512], start=(k == 0), stop=(k == 7))
            return ins
        P.op("pe", mm, reads=[r_a, r_b, ("sact", q)], writes=[abr, vbr])
        P.op("act", lambda e: e.activation(out=ulrT[:, q * 128:(q + 1) * 128], in_=ab[:, 0:128], func=AF.Copy), reads=[abr], writes=[("sulr", q)])
        P.op("act", lambda e: e.activation(out=vg_tok[:, q, :], in_=vb[:, 0:512], func=AF.Copy), reads=[vbr], writes=[("svg", q)])

    def sc3a(b):
        q = b % 4; p = b % 2
        zb, zbr = P.bank()
        sbanks[b] = (zb, zbr)
        P.op("pe", lambda e: e.matmul(zb[:, 0:256], lhsT=ulrT[:, q * 128:(q + 1) * 128], rhs=w2pad[:], start=True, stop=True),
             reads=[("sulr", q), "w2pad"], writes=[zbr])
        P.op("dve", lambda e: e.tensor_tensor(out=zts[p][:], in0=zb[:, 0:256], in1=bgate_bc[:], op=ALU.add), reads=[zbr, "bgate"], writes=[("szt", p)])
        P.op("act", lambda e: e.activation(out=zts[p][:], in_=zts[p][:], func=AF.Exp, scale=-1.0), reads=[("szt", p)], writes=[("szt", p)])
        P.op("act", lambda e: e.activation(out=sp_h[p][:], in_=zts[p][:], func=AF.Ln, bias=1.0), reads=[("szt", p)], writes=[("ssp", p)])

    def sc3b(b):
        q = b % 4; p = b % 2
        kb_, kbr = P.bank()
        cb, cbr = P.bank()
        en_ = _enbflat[:, q * 256:(q + 1) * 256]
        kt_ = _qgflat[:, q * 256:(q + 1) * 256]

        def mm(e):
            ins = None
            for k in range(8):
                ins = e.matmul(kb_[:, 0:256], lhsT=actT[:, k, q * 128:(q + 1) * 128], rhs=ring[:, s_a, k, 0:256], start=(k == 0), stop=(k == 7))
            return ins
        P.op("pe", mm, reads=[r_a, ("sact", q)], writes=[kbr])

        def mmc(e):
            e.matmul(cb[:, 0:256], lhsT=tri_bf[:], rhs=sp_h[p][:], start=True, stop=True)
            ins = None
            for c in range(2):
                ins = e.matmul(cb[:, 256 + 2 * c:258 + 2 * c], lhsT=sp_h[p][:, c * 128:(c + 1) * 128], rhs=negs[:], start=True, stop=True)
            return ins
        P.op("pe", mmc, reads=[("ssp", p), "tri_bf", "negs"], writes=[cbr])
        P.op("act", lambda e: e.activation(out=en_, in_=cb[:, 0:256], func=AF.Exp, scale=-1.0), reads=[cbr], writes=[("senb", q)])
        P.op("act", lambda e: e.activation(out=elast[:, :, q], in_=fap(cb[:, 256:257], [[2, 2]]), func=AF.Exp), reads=[cbr], writes=[("sel", q)])
        P.op("dve", lambda e: e.tensor_tensor(out=kt_, in0=kb_[:, 0:256], in1=en_, op=ALU.mult),
             reads=[kbr, ("senb", q)], writes=[("sktok", q)])

    def sc4(b):
        q = b % 4; p = b % 2
        kt = _qgflat[:, q * 256:(q + 1) * 256]
        ub, ubr = P.bank()

        def mm(e):
            ins = None
            for c in range(2):
                for ab in range(2):
                    h = 2 * c + ab
                    ins = e.matmul(ub[:, h * 128:(h + 1) * 128], lhsT=kt[:, c * 128:(c + 1) * 128], rhs=vg_tok[:, q, h * 128:(h + 1) * 128], start=True, stop=True)
            return ins
        P.op("pe", mm, reads=[("sktok", q), ("svg", q)], writes=[ubr])
        for ab in range(2):
            r0 = ab * 64
            P.op("dve", lambda e, ab=ab, r0=r0: e.tensor_tensor(out=Stmp[r0:r0 + 64], in0=fap(ub[r0:r0 + 64, ab * 128:ab * 128 + 1], [[256, 2], [1, 128]]),
                                                               in1=S[r0:r0 + 64], op=ALU.add), reads=[ubr, "S", "Stmp"], writes=["Stmp"])
        P.op("dve", lambda e: e.tensor_tensor(out=S[:], in0=Stmp[:], in1=fap(elast[:, 0, q:q + 1], [[4, 2], [0, 128]]), op=ALU.mult),
             reads=["Stmp", ("sel", q)], writes=["S"])

    stages = (sc1, sc2, sc3a, sc3b, sc4)
    for i in range(NBLK + len(stages) - 1):
        for si, st in enumerate(stages):
            b = i - si
            if 0 <= b < NBLK:
                st(b)
    P.barrier(keep=("wbf", "wbfchain"))
    P.op("act", lambda e: e.activation(out=SA[0:64], in_=S[0:64], func=AF.Copy), reads=["S", "zinit"], writes=["SA"])
    P.op("act", lambda e: e.activation(out=SB[64:128], in_=S[64:128], func=AF.Copy), reads=["S", "zinit"], writes=["SB"])

    SPRE = {}

    def main_tile(kind, t):
        sample = kind == "sample"
        nb = 1 if sample else 4
        T = nb * 128
        CUR["p"] = 0 if sample else (t % 2)
        first = (kind == "prompt" and t == 0)
        last = (kind == "prompt" and t == 3)
        L0 = SPRE.pop("L0", None) if sample else None
        L1 = SPRE.pop("L1", None) if sample else None
        L2 = SPRE.pop("L2", None) if sample else None
        L0 = L0 or wload([(0, wsrc("w_in", 0, D, 0, 512), 8, 512)])
        L1 = L1 or wload([(0, wsrc("w_in", 0, D, 512, 512), 8, 512)])
        L2 = L2 or wload([(0, wsrc("w_in", 0, D, 1024, 256), 8, 256), (256, wsrc("w_in", 0, D, 2192, 128), 8, 128)])
        if first:
            front(lambda blk: xhalo, 1, gaT)
            halo_kv = True
            kv_part(L1, 1, 128, 0, False, False)
        if sample:
            if SPRE.pop("x", False):
                norm_block(0, gaT)
            else:
                front(lambda blk: xs, 1, gaT)
        else:
            front(lambda blk: xp[t * 512 + blk * 128: t * 512 + (blk + 1) * 128, :], 4, gaT)
        gla_prep(L2[0], L2[1], 256, nb, T, (tris_f if sample else tri_f)[:], "tris" if sample else "tri", True)
        for c in range(4):
            bk, bkr = proj_fm(L0[0], L0[1], c * 128, T, nb)
            rstd_fm(bk[:, 0:T], T, bd_bf, 1.0 / 64, [bkr])
            P.op("dve", lambda e, bk=bk, c=c: e.scalar_tensor_tensor(out=qhT[:, c, 0:T], in0=bk[:, 0:T], scalar=gq_col[:, 0:1], in1=rstd_f[:, 0:T],
                                                                    op0=ALU.mult, op1=ALU.mult), reads=[bkr, "rstd", "gq"], writes=["qhT"])
        kv_part(L1, nb, T, 1, last, sample)
        for c in range(2):
            bk, bkr = proj_fm(L1[0], L1[1], 256 + c * 128, T, nb)
            P.op("dve", lambda e, bk=bk, c=c: e.tensor_tensor(out=qgT[:, c, 0:T], in0=bk[:, 0:T], in1=ebq[:, c, 0:T], op=ALU.mult),
                 reads=[bkr, "ebq"], writes=["qgT"])
        kg_evac(L2[0], L2[1], 0, nb, T, sample)
        L3 = wload([(0, wsrc("w_in", 0, D, 1280, 512), 8, 512)])
        vg_tm(L3[0], L3[1], 0, nb)
        L4 = wload([(0, wsrc("w_in", 0, D, 1792, 512), 8, 512)])
        for c in range(4):
            bk, bkr = proj_fm(L4[0], L4[1], c * 128, T, nb)
            P.op("act", lambda e, bk=bk, c=c: e.activation(out=rsT[:, c, 0:T], in_=bk[:, 0:T], func=AF.Silu), reads=[bkr], writes=["rsT"])
        if sample:
            sample_attn()
            sample_gla()
        else:
            for blk in range(nb):
                attn_block(blk, E, "E", first and blk == 0)
                gla_block(blk)
            P.op("pool", lambda e: e.tensor_copy(out=kX[:, :, 0:128], in_=kX[:, :, 512:640]), reads=["kX"], writes=["kX"])
            P.op("pool", lambda e: e.tensor_copy(out=Vdup[:, 0], in_=Vdup[:, 4]), reads=["Vdup"], writes=["Vdup"])
        if debug:
            outs.append(P.dma("pool", dbg_mix, mixT[:], reads=[("mixa", b_) for b_ in range(nb)] + [("mixg", b_ * 128) for b_ in range(nb)]))
            outs.append(P.dma("pool", dbg_q, qhT[:], reads=["qhT"]))
            outs.append(P.dma("pool", dbg_rs, rsT[:], reads=["rsT"]))
        L5 = wload([(0, wsrc("w_o", 0, D, 0, 512), 8, 512)])
        L6 = wload([(0, wsrc("w_o", 0, D, 512, 512), 8, 512)])
        wo_ffnnorm(nb, L5[0], L5[1], L6[0], L6[1])
        if debug:
            outs.append(P.dma("sp", dbg_h, xb[:], reads=[("x", 0, b_) for b_ in range(nb)]))
            outs.append(P.dma("pool", dbg_z, actT[:], reads=[("actT", 0, b_) for b_ in range(nb)]))
        if sample:
            ffn(nb, T, lambda blk: y_s)
        else:
            ffn(nb, T, lambda blk: y_p[t * 512 + blk * 128: t * 512 + (blk + 1) * 128, :])

    def kv_part(L1, nb, T, kblk0, last, sample):
        bk, bkr = proj_fm(L1[0], L1[1], 0, T, nb)
        rstd_fm(bk[:, 0:T], T, bd_bf, 1.0 / 64, [bkr])
        c0 = kblk0 * 128
        P.op("dve", lambda e: e.scalar_tensor_tensor(out=khT_bf[:, 0:T], in0=bk[:, 0:T], scalar=gk_col[:, 0:1], in1=rstd_f[:, 0:T], op0=ALU.mult, op1=ALU.mult),
             reads=[bkr, "rstd", "gk"], writes=["khT"])
        if last or sample:
            lo = T - 128
            P.op("dve", lambda e: e.scalar_tensor_tensor(out=khT_f[:], in0=bk[:, lo:T], scalar=gk_col[:, 0:1], in1=rstd_f[:, lo:T], op0=ALU.mult, op1=ALU.mult),
                 reads=[bkr, "rstd", "gk"], writes=["khT_f"])
        P.op("act", lambda e: e.activation(out=kX[0:64, 0, c0:c0 + T], in_=khT_bf[0:64, 0:T], func=AF.Copy), reads=["khT", "kX"], writes=["kX"])
        P.op("act", lambda e: e.activation(out=kX[64:128, 3, c0:c0 + T], in_=khT_bf[64:128, 0:T], func=AF.Copy), reads=["khT", "kX"], writes=["kX"])
        b2, b2r = P.bank()
        P.op("pe", lambda e: e.matmul(b2[:, 0:T], lhsT=sw_bf[:], rhs=khT_bf[:, 0:T], start=True, stop=True), reads=["khT", "sw"], writes=[b2r])
        P.op("act", lambda e: e.activation(out=kX[0:64, 2, c0:c0 + T], in_=b2[0:64, 0:T], func=AF.Copy), reads=[b2r, "kX"], writes=["kX"])
        P.op("act", lambda e: e.activation(out=kX[64:128, 1, c0:c0 + T], in_=b2[64:128, 0:T], func=AF.Copy), reads=[b2r, "kX"], writes=["kX"])
        for blk in range(nb):
            vb, vbr = proj_tm(L1[0], L1[1], 128, 128, blk)
            P.op("act", lambda e, vb=vb, blk=blk: e.activation(out=Vdup[:, kblk0 + blk].rearrange("p j (u d) -> p j u d", u=2),
                                                               in_=fap(vb[:, 0:1], [[64, 2], [0, 2], [1, 64]]), func=AF.Copy),
                 reads=[vbr, "Vdup"], writes=["Vdup"])
            if (last and blk == nb - 1) or sample:
                P.op("dve", lambda e, vb=vb: e.tensor_copy(out=vw_f[:], in_=vb[:, 0:128]), reads=[vbr], writes=["vw_f"])
        if last or sample:
            kb_, kbr = P.bank()
            P.op("pe", lambda e: e.transpose(out=kb_[:, 0:128], in_=khT_f[:], identity=ident_f[:]), reads=["khT_f", "ident_f"], writes=[kbr])
            P.op("act", lambda e: e.activation(out=kw_f[:], in_=kb_[:, 0:128], func=AF.Copy), reads=[kbr], writes=["kw_f"])
        if last:
            outs.append(P.dma("sp", kwp, kw_f[:], reads=["kw_f"]))
            outs.append(P.dma("sp", vwp, vw_f[:], reads=["vw_f"]))

    def sample_attn():
        for (dst, src, new, nm) in ((kws, ck, kw_f, "kws"), (vws, cv, vw_f, "vws")):
            P.dma("sp", dst[:, 0:127, :], src[:, 1:128, :], writes=[nm])
            P.dma("sp", bass.AP(tensor=dst.tensor, offset=127 * 128, ap=[[128 * 128, 16], [1, 128]]), new[0:16, :], reads=["kw_f", "vw_f"], writes=[nm])
        t1 = P.dma("pool", Kw_bf[:], kws.rearrange("b k f -> k b f"), reads=["kws"], writes=["Kw"])
        for j in range(2):
            P.dma("pool", Kwsw_bf[:, :, (1 - j) * 64:(2 - j) * 64], kws[:, :, j * 64:(j + 1) * 64].rearrange("b k f -> k b f"), reads=["kws"], writes=["Kwsw"])
            for u in range(2):
                P.dma("pool", Vwd[:, :, j, u * 64:(u + 1) * 64], vws[:, :, j * 64:(j + 1) * 64].rearrange("b k f -> k b f"), reads=["vws"], writes=["Vwd"])
        outs.append(t1)
        P.op("dve", lambda e: e.tensor_copy(out=qsA[0:64], in_=qhT[0:64, :, 0:16]), reads=["qhT", "zinit"], writes=["qsA"])
        P.op("dve", lambda e: e.tensor_copy(out=qsB[64:128], in_=qhT[64:128, :, 0:16]), reads=["qhT", "zinit"], writes=["qsB"])
        sb_, sbr = P.bank()

        def a_tr(b):
            tb, tbr = tbank()
            bf_ = b % 2

            def tr(e, tb=tb, b=b):
                e.transpose(out=tb[:, 0:128], in_=Kw_bf[:, b, :], identity=ident_bf[:])
                return e.transpose(out=tb[:, 128:256], in_=Kwsw_bf[:, b, :], identity=ident_bf[:])
            P.op("pe", tr, reads=["Kw", "Kwsw", "ident_bf"], writes=[tbr])
            P.op("act", lambda e, tb=tb, bf_=bf_: e.activation(out=KTb[:, bf_].rearrange("p a k -> p (a k)"), in_=tb[:, 0:256], func=AF.Copy),
                 reads=[tbr], writes=[("KTb", bf_)])

        def a_mm(b):
            bf_ = b % 2

            def mm(e, b=b, bf_=bf_):
                ins = None
                for j in range(2):
                    for half in range(2):
                        kt = KTb[:, bf_, 0 if j == half else 1, :]
                        q = (qsA if half == 0 else qsB)[:, 2 * j:2 * j + 2, b]
                        o = b * 8 + (j * 2 + half) * 2
                        ins = e.matmul(sb_[:, o:o + 2], lhsT=kt, rhs=q, start=True, stop=True)
                return ins
            P.op("pe", mm, reads=[("KTb", bf_), "qsA", "qsB"], writes=[sbr])
        a_tr(0)
        for b in range(16):
            if b + 1 < 16:
                a_tr(b + 1)
            a_mm(b)
        P.op("act", lambda e: e.activation(out=pes[:].rearrange("p b h -> p (b h)"), in_=sb_[:, 0:128], func=AF.Exp), reads=[sbr], writes=["pes"])
        P.op("dve", lambda e: e.tensor_tensor(out=Pts[:], in0=pes[:], in1=fap(E[:, 0, 0, 1, 0, 127:128], [[0, 16], [512, 4], [128, 2]]), op=ALU.mult),
             reads=["pes", "E"], writes=["Pts"])
        ob, obr = P.bank()
        db, dbr = P.bank()

        def mmv(e):
            ins = None
            for b in range(16):
                for j in range(2):
                    e.matmul(ob[:, b * 8 + j * 4:b * 8 + j * 4 + 4], lhsT=Vwd[:, b, j, :], rhs=Pts[:, b, j * 4:(j + 1) * 4], start=True, stop=True)
                ins = e.matmul(db[:, b * 8:(b + 1) * 8], lhsT=ones_bf[:], rhs=Pts[:, b, :], start=True, stop=True)
            return ins
        P.op("pe", mmv, reads=["Pts", "Vwd", "ones"], writes=[obr, dbr])
        P.op("dve", lambda e: e.tensor_tensor(out=rec_f[:, 0:128].rearrange("p (b h) -> p b h", b=16), in0=db[:, 0:128].rearrange("p (b h) -> p b h", b=16),
                                              in1=fap(sinkexp[:, 0, 0, 0, 0:1], [[0, 16], [128, 8]]), op=ALU.add), reads=[dbr, "sinkexp"], writes=["rec"])
        P.op("act", lambda e: e.activation(out=rec_f[:, 0:128], in_=rec_f[:, 0:128], func=AF.Ln), reads=["rec"], writes=["rec"])
        P.op("act", lambda e: e.activation(out=rec_f[:, 0:128], in_=rec_f[:, 0:128], func=AF.Exp, scale=-1.0), reads=["rec"], writes=["rec"])
        for j in range(2):
            for half in range(2):
                r0 = half * 64
                o = (j * 2 + half) * 2
                P.op("dve", lambda e, r0=r0, o=o, j=j: e.tensor_tensor(
                    out=mixT[r0:r0 + 64, 2 * j:2 * j + 2, 0:16],
                    in0=fap(ob[r0:r0 + 64, o:o + 1], [[1, 2], [8, 16]]),
                    in1=fap(rec_f[r0:r0 + 64, o:o + 1], [[1, 2], [8, 16]]), op=ALU.mult), reads=[obr, "rec"], writes=[("mixa", 0)])

    def sample_gla():
        ob, obr = P.bank(hold=True)
        vbs = {}

        def g_a(b):
            bf_ = b % 2
            P.dma("sp", Sb[:, bf_], sg[b].rearrange("(c u) d v -> (u d) c v", u=2), writes=[("Sb", bf_)])
            vb, vbr = P.bank()
            vbs[b] = (vb, vbr)
            P.op("pe", lambda e, vb=vb, b=b: e.matmul(vb[:, 0:512], lhsT=sel_bf[:, b, :], rhs=vg_tok[:, 0, :], start=True, stop=True),
                 reads=["sel", ("vg", 0)], writes=[vbr])

        def g_b(b):
            bf_ = b % 2
            vb, vbr = vbs.pop(b)
            for c in range(2):
                for half in range(2):
                    r0 = half * 64
                    h = 2 * c + half
                    P.op("dve", lambda e, vb=vb, b=b, c=c, r0=r0, h=h, bf_=bf_: e.scalar_tensor_tensor(
                        out=Wt[r0:r0 + 64, bf_, c, :], in0=vb[r0:r0 + 64, h * 128:(h + 1) * 128], scalar=kgf[r0:r0 + 64, c, b:b + 1],
                        in1=Sb[r0:r0 + 64, bf_, c, :], op0=ALU.mult, op1=ALU.add), reads=[vbr, "kgf", ("Sb", bf_)], writes=[("Wt", bf_)])
            P.op("act", lambda e, bf_=bf_: e.activation(out=WA[0:64, bf_], in_=Wt[0:64, bf_], func=AF.Copy), reads=[("Wt", bf_), "zinit"], writes=[("WA", bf_)])
            P.op("act", lambda e, bf_=bf_: e.activation(out=WB[64:128, bf_], in_=Wt[64:128, bf_], func=AF.Copy), reads=[("Wt", bf_), "zinit"], writes=[("WB", bf_)])
            P.op("dve", lambda e, b=b, bf_=bf_: e.tensor_tensor(out=Sn[:, bf_], in0=Wt[:, bf_], in1=fap(elb[:, 0, b:b + 1], [[128, 2], [0, 128]]), op=ALU.mult),
                 reads=[("Wt", bf_), "elb"], writes=[("Sn", bf_)])
            outs.append(P.dma("sp", gss[b].rearrange("(c u) d v -> (u d) c v", u=2), Sn[:, bf_], reads=[("Sn", bf_)]))

        def g_c(b):
            bf_ = b % 2

            def mm(e, b=b, bf_=bf_):
                ins = None
                for h in range(4):
                    c, half = h // 2, h % 2
                    w = (WA if half == 0 else WB)[:, bf_, c, :]
                    ins = e.matmul(ob[:, h * 16 + b:h * 16 + b + 1], lhsT=w, rhs=qgT[:, c, b:b + 1], start=True, stop=True)
                return ins
            P.op("pe", mm, reads=[("WA", bf_), ("WB", bf_), "qgT"], writes=[obr])
        g_a(0)
        for b in range(16):
            if b + 1 < 16:
                g_a(b + 1)
            g_b(b)
            g_c(b)
        P.release(obr)
        gla_out_norm(ob, obr, 64, 16, 0)

    import os as _os
    _kb = _os.environ.get("KBAR", "")
    for t in range((0 if debug in ("sample", "scan") else (2 if debug == "two" else (4 if debug == "four" else 1))) if debug else 4):
        main_tile("prompt", t)
        if "t" in _kb:
            P.barrier()
    if debug and debug != "sample":
        outs.append(P.dma("pool", dbg_a, aT[:], reads=[("aT", m) for m in range(NKF)]))
    outs.append(P.dma("sp", gsp.rearrange("(c u) d v -> (u d) c v", u=2), S[:], reads=["S"]))
    if not debug:
        CUR["p"] = 0
        SPRE["L0"] = wload([(0, wsrc("w_in", 0, D, 0, 512), 8, 512)])
        SPRE["L1"] = wload([(0, wsrc("w_in", 0, D, 512, 512), 8, 512)])
        SPRE["L2"] = wload([(0, wsrc("w_in", 0, D, 1024, 256), 8, 256), (256, wsrc("w_in", 0, D, 2192, 128), 8, 128)])
        P.dma("sp", X(0), xs, writes=[xr(0)])
        SPRE["x"] = True
    P.barrier()
    if (not debug) or debug == "sample":
        main_tile("sample", 0)

    P.emit()
    P.stack.close()
    return nc


_CACHE = {}


def kernel(x_prompt, x_sample, cache_k, cache_v, state_gla, attn_norm_g, w_in, q_norm_g, k_norm_g, attn_sinks,
           rel_bias, w_gla_gate2, b_gla_gate, gla_norm_g, w_o, ffn_norm_g, w_gate, w_up, w_down):
    f = lambda a: np.ascontiguousarray(np.asarray(a, dtype=np.float32))
    xpr = f(x_prompt)[0]
    xsm = f(x_sample)[:, 0, :]
    ckk = f(cache_k)[0].reshape(128, 128, 128)
    cvv = f(cache_v)[0].reshape(128, 128, 128)
    sgg = f(state_gla)[0]
    consts = host_consts()
    shared = dict(w_in=f(w_in)[0], w_o=f(w_o)[0], w_gate=f(w_gate)[0], w_up=f(w_up)[0], w_down=f(w_down)[0],
                  attn_g=f(attn_norm_g)[0], ffn_g=f(ffn_norm_g)[0], qng=f(q_norm_g)[0], kng=f(k_norm_g)[0],
                  sinks=f(attn_sinks)[0], relb=f(rel_bias), w2=f(w_gla_gate2)[0], bgate=f(b_gla_gate)[0], glag=f(gla_norm_g)[0])
    for k, v in consts.items():
        shared["c_" + k] = v
    in_maps = []
    for c in range(NCORE):
        m = dict(shared)
        m["xp"] = xpr[c * TOK:(c + 1) * TOK]
        m["xhalo"] = xpr[c * TOK - 128:c * TOK] if c > 0 else np.zeros((128, D), np.float32)
        pre = np.zeros((NPRE, D), np.float32)
        if c > 0:
            pre[NPRE - c * TOK:] = xpr[:c * TOK]
        m["xpre"] = pre
        xs_ = np.zeros((128, D), np.float32)
        xs_[:16] = xsm[c * 16:(c + 1) * 16]
        m["xs"] = xs_
        m["ck"] = ckk[c * 16:(c + 1) * 16]
        m["cv"] = cvv[c * 16:(c + 1) * 16]
        m["sg"] = sgg[c * 16:(c + 1) * 16]
        m["flag"] = np.full((128, 1), 1.0 if c > 0 else 0.0, np.float32)
        in_maps.append(m)
    if "nc" not in _CACHE:
        _CACHE["nc"] = build_program()
    res = run_bass_kernel_spmd(_CACHE["nc"], in_maps, core_ids=list(range(NCORE)))
    R = res.results
    y_prompt = np.concatenate([R[c]["y_p"] for c in range(NCORE)], axis=0)[None]
    y_sample = np.concatenate([R[c]["y_s"][:16] for c in range(NCORE)], axis=0)[:, None, :]
    kwp = R[7]["kwp"].reshape(1, 1, 128, 2, 64)
    vwp = R[7]["vwp"].reshape(1, 1, 128, 2, 64)
    gsp = R[7]["gsp"].reshape(1, 1, 4, 64, 128)
    kws = np.concatenate([R[c]["kws"] for c in range(NCORE)], axis=0).reshape(1, 128, 128, 2, 64)
    vws = np.concatenate([R[c]["vws"] for c in range(NCORE)], axis=0).reshape(1, 128, 128, 2, 64)
    gss = np.concatenate([R[c]["gss"] for c in range(NCORE)], axis=0).reshape(1, 128, 4, 64, 128)
    return (y_prompt.astype(np.float32), y_sample.astype(np.float32), kwp, vwp, gsp, kws, vws, gss)
```
